# Optimizing a Trainium2 kernel written in Bass

```python
import math
import jax, jax.numpy as jnp
from jax import lax
import numpy as np

D_MODEL = 2048
BATCH = 4
SEQ = 2048
DEPTH = 2

GRID_W = 64
CTX_LEN = 256
N_MOD = 6
SSM_WIDTH = D_MODEL // 2
SSM_GROUP = 16
SSM_GROUPS = SSM_WIDTH // SSM_GROUP
SSM_STATE = 64
DT_MIN = 0.001
DT_MAX = 0.1
NA_HEADS = 16
NA_HEAD_DIM = 64
NA_WIDTH = NA_HEADS * NA_HEAD_DIM
WIN_R = 8
WIN_C = 16
ROPE_BASE = 10000.0
D_FF = 4 * D_MODEL
NORM_EPS = 1e-6
PROJ_WIDTH = SSM_WIDTH + 3 * NA_WIDTH + 2 * D_MODEL

kernel_name = "hybrid_s5_natten_prefix_dit_block"


def rms_norm(x, g):
    xf = x.astype(jnp.float32)
    y = xf * lax.rsqrt(jnp.mean(xf * xf, axis=-1, keepdims=True) + NORM_EPS)
    return (y * g.astype(jnp.float32)).astype(x.dtype)


def modulate(x, shift, scale):
    return x * (1.0 + scale) + shift


def axial_rope(x, rows, cols):
    nf = NA_HEAD_DIM // 4
    half = NA_HEAD_DIM // 2
    inv = ROPE_BASE ** (-jnp.arange(nf, dtype=jnp.float32) / nf)

    def rot(xp, pos):
        ang = pos.astype(jnp.float32)[:, None] * inv
        cos = jnp.cos(ang)[None, :, None, :]
        sin = jnp.sin(ang)[None, :, None, :]
        x1, x2 = xp[..., :nf], xp[..., nf:]
        return jnp.concatenate([x1 * cos - x2 * sin, x1 * sin + x2 * cos], axis=-1)

    xf = x.astype(jnp.float32)
    out = jnp.concatenate([rot(xf[..., :half], rows), rot(xf[..., half:], cols)], axis=-1)
    return out.astype(x.dtype)


def s5_discretise(lam_re, lam_im, log_dt, b_re, b_im):
    lam = lax.complex(lam_re.astype(jnp.float32), lam_im.astype(jnp.float32))
    dt = jnp.exp(log_dt.astype(jnp.float32))[:, None]
    lam_bar = jnp.exp(lam * dt)
    b = lax.complex(b_re.astype(jnp.float32), b_im.astype(jnp.float32))
    b_bar = ((lam_bar - 1.0) / lam)[..., None] * b
    return lam_bar, b_bar


def _lin_comb(e_i, e_j):
    a_i, b_i = e_i
    a_j, b_j = e_j
    return a_j * a_i, a_j * b_i + b_j


def s5_scan(u, lam_bar, b_bar, s0, reverse):
    bu = jnp.einsum('gpc,btgc->btgp', b_bar, u.astype(jnp.complex64))
    if s0 is not None:
        first = -1 if reverse else 0
        bu = bu.at[:, first].add(lam_bar * s0)
    a = jnp.broadcast_to(lam_bar, bu.shape)
    _, states = lax.associative_scan(_lin_comb, (a, bu), axis=1, reverse=reverse)
    return states


def s5_readout(states, c_re, c_im):
    cmat = lax.complex(c_re.astype(jnp.float32), c_im.astype(jnp.float32))
    return jnp.real(jnp.einsum('gcp,btgp->btgc', cmat, states))


def s5_mixer(u_lat, u_ctx, lam_re, lam_im, log_dt, b_re, b_im, c_re, c_im, d_skip, with_ctx_out):
    bsz, t_len, _ = u_lat.shape
    c_len = u_ctx.shape[1]
    ul = u_lat.astype(jnp.float32).reshape(bsz, t_len, SSM_GROUPS, SSM_GROUP)
    uc = u_ctx.astype(jnp.float32).reshape(bsz, c_len, SSM_GROUPS, SSM_GROUP)
    dg = d_skip.astype(jnp.float32).reshape(SSM_GROUPS, SSM_GROUP)
    y_lat = ul * dg
    y_ctx = uc * dg
    for dirn, reverse in ((0, False), (1, True)):
        lam_bar, b_bar = s5_discretise(lam_re[dirn], lam_im[dirn], log_dt[dirn], b_re[dirn], b_im[dirn])
        st_ctx = s5_scan(uc, lam_bar, b_bar, None, reverse)
        s0 = st_ctx[:, 0] if reverse else st_ctx[:, -1]
        st_lat = s5_scan(ul, lam_bar, b_bar, s0, reverse)
        y_lat = y_lat + s5_readout(st_lat, c_re[dirn], c_im[dirn])
        if with_ctx_out:
            y_ctx = y_ctx + s5_readout(st_ctx, c_re[dirn], c_im[dirn])
    y_lat = y_lat.reshape(bsz, t_len, SSM_WIDTH).astype(u_lat.dtype)
    y_ctx = y_ctx.reshape(bsz, c_len, SSM_WIDTH).astype(u_ctx.dtype)
    return y_lat, y_ctx


def s5_glu(y, w_val, w_glu):
    a = jax.nn.gelu(y)
    return (a @ w_val) * jax.nn.sigmoid(a @ w_glu)


def neighbourhood_attention(q, k, v, k_ctx, v_ctx, rpb):
    bsz, t_len, n_h, dh = q.shape
    rows = t_len // GRID_W
    kr = min(WIN_R, rows)
    kc = min(WIN_C, GRID_W)
    nk = kr * kc
    qg = q.reshape(bsz, rows, GRID_W, n_h, dh)
    kg = k.reshape(bsz, rows, GRID_W, n_h, dh)
    vg = v.reshape(bsz, rows, GRID_W, n_h, dh)
    cols = np.arange(GRID_W)
    c0 = np.clip(cols - kc // 2, 0, GRID_W - kc)
    col_idx = c0[:, None] + np.arange(kc)[None, :]
    col_off = col_idx - cols[:, None] + (WIN_C - 1)
    bias_c = rpb.astype(jnp.float32)[:, :, col_off]
    scale = dh ** -0.5

    def one_row(r):
        r0 = jnp.clip(r - kr // 2, 0, rows - kr)
        q_r = lax.dynamic_index_in_dim(qg, r, axis=1, keepdims=False)
        k_rows = lax.dynamic_slice_in_dim(kg, r0, kr, axis=1)
        v_rows = lax.dynamic_slice_in_dim(vg, r0, kr, axis=1)
        k_win = k_rows[:, :, col_idx].transpose(0, 2, 1, 3, 4, 5).reshape(bsz, GRID_W, nk, n_h, dh)
        v_win = v_rows[:, :, col_idx].transpose(0, 2, 1, 3, 4, 5).reshape(bsz, GRID_W, nk, n_h, dh)
        row_off = r0 + jnp.arange(kr) - r + (WIN_R - 1)
        bias = jnp.take(bias_c, row_off, axis=1)
        bias = bias.transpose(0, 2, 1, 3).reshape(n_h, GRID_W, nk)
        s_loc = jnp.einsum('bwhd,bwkhd->bhwk', q_r, k_win).astype(jnp.float32) * scale + bias
        s_ctx = jnp.einsum('bwhd,blhd->bhwl', q_r, k_ctx).astype(jnp.float32) * scale
        p = jax.nn.softmax(jnp.concatenate([s_loc, s_ctx], axis=-1), axis=-1).astype(v.dtype)
        return (jnp.einsum('bhwk,bwkhd->bwhd', p[..., :nk], v_win)
                + jnp.einsum('bhwl,blhd->bwhd', p[..., nk:], v_ctx))

    out = lax.map(one_row, jnp.arange(rows))
    return out.transpose(1, 0, 2, 3, 4).reshape(bsz, t_len, n_h * dh)


def context_attention(q, k, v):
    s = jnp.einsum('blhd,bmhd->bhlm', q, k).astype(jnp.float32) * (NA_HEAD_DIM ** -0.5)
    p = jax.nn.softmax(s, axis=-1).astype(v.dtype)
    o = jnp.einsum('bhlm,bmhd->blhd', p, v)
    return o.reshape(q.shape[0], q.shape[1], NA_WIDTH)


def mixer_sublayer(h, hc, w_in, lam_re, lam_im, log_dt, b_re, b_im, c_re, c_im, d_skip,
                   w_ssm_val, w_ssm_glu, rpb, w_na_proj, w_out, with_ctx_out):
    bsz, t_len, _ = h.shape
    c_len = hc.shape[1]
    s1 = SSM_WIDTH
    s2 = s1 + NA_WIDTH
    s3 = s2 + NA_WIDTH
    s4 = s3 + NA_WIDTH
    s5 = s4 + D_MODEL
    proj = h @ w_in
    projc = hc @ w_in
    u, q, k, v, g_s, g_n = (proj[..., :s1], proj[..., s1:s2], proj[..., s2:s3],
                            proj[..., s3:s4], proj[..., s4:s5], proj[..., s5:])
    uc, qc, kc_, vc, g_sc, g_nc = (projc[..., :s1], projc[..., s1:s2], projc[..., s2:s3],
                                   projc[..., s3:s4], projc[..., s4:s5], projc[..., s5:])
    y_s, y_sc = s5_mixer(u, uc, lam_re, lam_im, log_dt, b_re, b_im, c_re, c_im, d_skip, with_ctx_out)
    br_s = s5_glu(y_s, w_ssm_val, w_ssm_glu)
    pos = jnp.arange(t_len)
    rows_pos, cols_pos = pos // GRID_W, pos % GRID_W
    qh = axial_rope(q.reshape(bsz, t_len, NA_HEADS, NA_HEAD_DIM), rows_pos, cols_pos)
    kh = axial_rope(k.reshape(bsz, t_len, NA_HEADS, NA_HEAD_DIM), rows_pos, cols_pos)
    vh = v.reshape(bsz, t_len, NA_HEADS, NA_HEAD_DIM)
    kch = kc_.reshape(bsz, c_len, NA_HEADS, NA_HEAD_DIM)
    vch = vc.reshape(bsz, c_len, NA_HEADS, NA_HEAD_DIM)
    br_n = neighbourhood_attention(qh, kh, vh, kch, vch, rpb) @ w_na_proj
    out = (jax.nn.sigmoid(g_s) * br_s + jax.nn.sigmoid(g_n) * br_n) @ w_out
    if not with_ctx_out:
        return out, None
    br_sc = s5_glu(y_sc, w_ssm_val, w_ssm_glu)
    qch = qc.reshape(bsz, c_len, NA_HEADS, NA_HEAD_DIM)
    br_nc = context_attention(qch, kch, vch) @ w_na_proj
    outc = (jax.nn.sigmoid(g_sc) * br_sc + jax.nn.sigmoid(g_nc) * br_nc) @ w_out
    return out, outc


def sq_relu_mlp(h, w_fc1, w_fc2):
    a = jax.nn.relu(h @ w_fc1)
    return (a * a) @ w_fc2


def setup_inputs(seed: int = 0) -> dict:
    key = jax.random.key(seed)
    ks = jax.random.split(key, 32)
    f32 = jnp.float32
    G, P, CG = SSM_GROUPS, SSM_STATE, SSM_GROUP

    def nrm(k, shape, std):
        return jax.random.normal(k, shape, f32) * std

    n_idx = jnp.arange(P, dtype=f32)
    return {
        "x": nrm(ks[0], (BATCH, SEQ, D_MODEL), 1.0),
        "c": nrm(ks[1], (BATCH, D_MODEL), 1.0),
        "ctx": nrm(ks[2], (BATCH, CTX_LEN, D_MODEL), 1.0),
        "c_ctx": nrm(ks[3], (D_MODEL,), 1.0),
        "w_mod": nrm(ks[4], (DEPTH, D_MODEL, N_MOD * D_MODEL), 0.5 * D_MODEL ** -0.5),
        "b_mod": nrm(ks[5], (DEPTH, N_MOD * D_MODEL), 0.02),
        "g_pre_mix": 1.0 + nrm(ks[6], (DEPTH, D_MODEL), 0.02),
        "g_post_mix": 1.0 + nrm(ks[7], (DEPTH, D_MODEL), 0.02),
        "g_pre_mlp": 1.0 + nrm(ks[8], (DEPTH, D_MODEL), 0.02),
        "g_post_mlp": 1.0 + nrm(ks[9], (DEPTH, D_MODEL), 0.02),
        "w_in": nrm(ks[10], (DEPTH, D_MODEL, PROJ_WIDTH), D_MODEL ** -0.5),
        "ssm_lam_re": -0.5 + nrm(ks[11], (DEPTH, 2, G, P), 0.01),
        "ssm_lam_im": math.pi * n_idx + nrm(ks[12], (DEPTH, 2, G, P), 0.01),
        "ssm_log_dt": jax.random.uniform(ks[13], (DEPTH, 2, G), f32, math.log(DT_MIN), math.log(DT_MAX)),
        "ssm_b_re": nrm(ks[14], (DEPTH, 2, G, P, CG), (2 * CG) ** -0.5),
        "ssm_b_im": nrm(ks[15], (DEPTH, 2, G, P, CG), (2 * CG) ** -0.5),
        "ssm_c_re": nrm(ks[16], (DEPTH, 2, G, CG, P), (2 * P) ** -0.5),
        "ssm_c_im": nrm(ks[17], (DEPTH, 2, G, CG, P), (2 * P) ** -0.5),
        "ssm_d": nrm(ks[18], (DEPTH, SSM_WIDTH), 0.5),
        "w_ssm_val": nrm(ks[19], (DEPTH, SSM_WIDTH, D_MODEL), SSM_WIDTH ** -0.5),
        "w_ssm_glu": nrm(ks[20], (DEPTH, SSM_WIDTH, D_MODEL), SSM_WIDTH ** -0.5),
        "na_rpb": nrm(ks[21], (DEPTH, NA_HEADS, 2 * WIN_R - 1, 2 * WIN_C - 1), 0.1),
        "w_na_proj": nrm(ks[22], (DEPTH, NA_WIDTH, D_MODEL), NA_WIDTH ** -0.5),
        "w_out": nrm(ks[23], (DEPTH, D_MODEL, D_MODEL), D_MODEL ** -0.5),
        "w_fc1": nrm(ks[24], (DEPTH, D_MODEL, D_FF), D_MODEL ** -0.5),
        "w_fc2": nrm(ks[25], (DEPTH, D_FF, D_MODEL), D_FF ** -0.5),
    }


def reference(x, c, ctx, c_ctx, w_mod, b_mod, g_pre_mix, g_post_mix, g_pre_mlp, g_post_mlp, w_in,
              ssm_lam_re, ssm_lam_im, ssm_log_dt, ssm_b_re, ssm_b_im, ssm_c_re, ssm_c_im, ssm_d,
              w_ssm_val, w_ssm_glu, na_rpb, w_na_proj, w_out, w_fc1, w_fc2):
    xc = ctx
    for l in range(DEPTH):
        with_ctx_out = l < DEPTH - 1
        mod = jax.nn.silu(c) @ w_mod[l] + b_mod[l]
        sh1, sc1, gt1, sh2, sc2, gt2 = [m[:, None, :] for m in jnp.split(mod, N_MOD, axis=-1)]
        modc = jax.nn.silu(c_ctx) @ w_mod[l] + b_mod[l]
        csh1, csc1, cgt1, csh2, csc2, cgt2 = jnp.split(modc, N_MOD, axis=-1)
        h = modulate(rms_norm(x, g_pre_mix[l]), sh1, sc1)
        hc = modulate(rms_norm(xc, g_pre_mix[l]), csh1, csc1)
        out, outc = mixer_sublayer(h, hc, w_in[l], ssm_lam_re[l], ssm_lam_im[l], ssm_log_dt[l],
                                   ssm_b_re[l], ssm_b_im[l], ssm_c_re[l], ssm_c_im[l], ssm_d[l],
                                   w_ssm_val[l], w_ssm_glu[l], na_rpb[l], w_na_proj[l], w_out[l],
                                   with_ctx_out)
        x = x + gt1 * rms_norm(out, g_post_mix[l])
        h2 = modulate(rms_norm(x, g_pre_mlp[l]), sh2, sc2)
        x = x + gt2 * rms_norm(sq_relu_mlp(h2, w_fc1[l], w_fc2[l]), g_post_mlp[l])
        if with_ctx_out:
            xc = xc + cgt1 * rms_norm(outc, g_post_mix[l])
            h2c = modulate(rms_norm(xc, g_pre_mlp[l]), csh2, csc2)
            xc = xc + cgt2 * rms_norm(sq_relu_mlp(h2c, w_fc1[l], w_fc2[l]), g_post_mlp[l])
    return x
```

```python
import contextlib
import math
import numpy as np
import concourse.bass as bass
import concourse.mybir as mybir
from concourse.bass_utils import run_bass_kernel_spmd

F32 = mybir.dt.float32
BF16 = mybir.dt.bfloat16
AF = mybir.ActivationFunctionType
ALU = mybir.AluOpType
AX = mybir.AxisListType

D = 2048
KT = 16
NTOK = 2304
CTX = 256
LAT = 2048
DFF = 8192
EPS = 1e-6
CHUNKS = [(0, 256), (256, 512), (768, 512), (1280, 512), (1792, 512)]
NEG = -30000.0
NCH8 = NTOK // 8
VEC_L = 168
C_OFF = 336
NVEC = 368


class Buf:
    __slots__ = ("ap", "w", "r", "name")

    def __init__(self, ap=None, name=""):
        self.ap = ap
        self.w = []
        self.r = []
        self.name = name

    def __getitem__(self, idx):
        return self.ap[idx]


class KB:
    NTICK = 40

    def __init__(self):
        self.nc = bass.Bass("TRN2", target_bir_lowering=False)
        nc = self.nc
        self.es = contextlib.ExitStack()
        self.eng = {}
        self.semid = {}
        for name, e in (("pe", nc.tensor), ("act", nc.scalar), ("dve", nc.vector),
                        ("pool", nc.gpsimd), ("sp", nc.sync)):
            sem = self.es.enter_context(nc.semaphore("s_" + name))
            self.eng[name] = [e, sem, 0]
        self.known = {n: {} for n in self.eng}
        self.ticks = []
        for i in range(self.NTICK):
            sem = self.es.enter_context(nc.semaphore("tk%d" % i))
            self.ticks.append([sem, 0, "tk%d" % i])
        self.tki = {"sp": 0, "pool": 0}
        self.tkr = {"sp": (0, 26), "pool": (26, 14)}
        self.out_evs = []
        self.phase_stack = None
        self.dbufs = {}
        self.uid = 0

    def sb(self, shape, dt, name=None, stack=None):
        self.uid += 1
        nm = (name or "t") + "_%d" % self.uid
        t = (stack or self.es).enter_context(self.nc.sbuf_tensor(nm, list(shape), dt))
        return Buf(t, nm)

    def pst(self, shape, dt, name=None, stack=None):
        self.uid += 1
        nm = (name or "p") + "_%d" % self.uid
        t = (stack or self.es).enter_context(self.nc.psum_tensor(nm, list(shape), dt))
        return Buf(t, nm)

    def dram(self, name, shape, dt, kind="Internal"):
        return self.nc.dram_tensor(name, list(shape), dt, kind=kind).ap()

    def tok(self, key):
        b = self.dbufs.get(key)
        if b is None:
            b = Buf(None, str(key))
            self.dbufs[key] = b
        return b

    def _wait(self, en, evs):
        e = self.eng[en][0]
        kn = self.known[en]
        best = {}
        for (key, sem, val) in evs:
            if val > kn.get(key, 0) and val > best.get(key, (None, 0))[1]:
                best[key] = (sem, val)
        for key, (sem, val) in best.items():
            e.wait_ge(sem, val)
            kn[key] = val

    def _deps(self, en, reads, writes):
        evs = []
        for b in reads:
            evs += b.w
        for b in writes:
            evs += b.w
            evs += b.r
        if en == "pe":
            evs = [x for x in evs if x[0] != "pe"]
        return evs

    def _record(self, ev, reads, writes):
        for b in reads:
            b.r = [x for x in b.r if x[0] != ev[0]]
            b.r.append(ev)
        for b in writes:
            b.w = [ev]
            b.r = []

    def op(self, en, fn, reads=(), writes=(), inc=True):
        self._wait(en, self._deps(en, reads, writes))
        ent = self.eng[en]
        ins = fn(ent[0])
        if inc:
            ent[2] += 1
            ins.then_inc(ent[1], 1)
            ev = (en, ent[1], ent[2])
            self._record(ev, reads, writes)
        return ins

    def dma(self, q, out_ap, in_ap, reads=(), writes=(), is_out=False):
        base, cnt = self.tkr[q]
        tk = self.ticks[base + self.tki[q]]
        self.tki[q] = (self.tki[q] + 1) % cnt
        evs = self._deps(q, reads, writes)
        if tk[1] > 0:
            evs.append((tk[2], tk[0], tk[1]))
        self._wait(q, evs)
        ins = self.eng[q][0].dma_start(out=out_ap, in_=in_ap)
        tk[1] += 16
        ins.then_inc(tk[0], 16)
        ev = (tk[2], tk[0], tk[1])
        self._record(ev, reads, writes)
        if is_out:
            self.out_evs.append(ev)
        return ev

    def barrier(self):
        evs = []
        for n, ent in self.eng.items():
            if ent[2] > 0:
                evs.append((n, ent[1], ent[2]))
        for tk in self.ticks:
            if tk[1] > 0:
                evs.append((tk[2], tk[0], tk[1]))
        for n in self.eng:
            self._wait(n, [x for x in evs if x[0] != n])

    def finish(self):
        self.barrier()
        self.es.close()


class Prog:
    def __init__(self, dbg=None, nlayers=2):
        self.k = KB()
        self.dbg = dbg or {}
        self.nlayers = nlayers
        k = self.k
        self.xT_d = k.dram("xT", [KT, 128, NTOK], F32, "ExternalInput")
        self.vecs_d = k.dram("vecs", [128, NVEC], F32, "ExternalInput")
        self.wmod_d = k.dram("wmod", [2, 96, 128, KT * 128], F32, "ExternalInput")
        self.win_d = k.dram("win", [2, 72, 128, KT * 128], F32, "ExternalInput")
        self.wv_d = k.dram("wv", [2, 2, 128, KT * 512], F32, "ExternalInput")
        self.rope_d = k.dram("rope", [2, 128, NTOK], F32, "ExternalInput")
        self.rpb_d = k.dram("rpb", [2, 16, 15, 31], F32, "ExternalInput")
        self.wval_d = k.dram("wval", [2, 16, 128, 8 * 128], F32, "ExternalInput")
        self.wglu_d = k.dram("wglu", [2, 16, 128, 8 * 128], F32, "ExternalInput")
        self.wna_d = k.dram("wna", [2, 16, 128, 8 * 128], F32, "ExternalInput")
        self.wout_d = k.dram("wout", [2, 16, 128, KT * 128], F32, "ExternalInput")
        self.wfc1_d = k.dram("wfc1", [2, 64, 128, KT * 128], F32, "ExternalInput")
        self.wfc2_d = k.dram("wfc2", [2, 16, 128, 64 * 128], F32, "ExternalInput")
        self.s5c_d = k.dram("s5c", [128, 400], F32, "ExternalInput")
        self.drep_d = k.dram("drep", [2, 128, 64], F32, "ExternalInput")
        self.sC_d = k.dram("sC", [2, 2, 7, 128, 512], F32, "ExternalInput")
        self.sel_d = k.dram("sel", [128, 8192], F32, "ExternalInput")
        self.selT_d = k.dram("selT", [128, 8192], F32, "ExternalInput")
        self.yT_d = k.dram("yT", [KT, 128, LAT], F32, "ExternalOutput")
        self.x1_d = k.dram("x1s", [KT, 128, NTOK], F32)
        self.u_d = k.dram("us", [8, 128, NTOK], BF16)
        self.q_d = k.dram("qs", [8, 128, NTOK], BF16)
        self.k_d = k.dram("ks", [8, 128, NTOK], BF16)
        self.v_d = k.dram("vs", [2, 18, 128, 1024], BF16)
        self.gs_d = k.dram("gss", [16, 128, NTOK], BF16)
        self.gn_d = k.dram("gns", [16, 128, NTOK], BF16)
        self.o_d = k.dram("os", [8, 128, NTOK], BF16)
        self.a_d = k.dram("as", [8, 128, NTOK], BF16)
        self.m_d = k.dram("ms", [16, 128, NTOK], BF16)
        self.wc_d = k.dram("wcache", [36, 128, 8192], BF16)
        self.vecs = k.sb([128, NVEC], F32, "vecs")
        self.sT = k.sb([128, KT, 2], F32, "sT")
        self.modT = k.sb([128, 2, 96, 2], F32, "modT")
        self.A = k.sb([128, 2, 2, KT, 2], F32, "Amod")
        self.G = k.sb([128, 2, 2, KT, 2], F32, "Gmod")
        self.ones = k.sb([128, 128], BF16, "ones")
        self.identb = k.sb([128, 128], BF16, "identb")
        self.ident_d = k.dram("ident", [128, 128], F32, "ExternalInput")
        self.ps = [k.pst([128, 512], F32, "ps%d" % i) for i in range(6)]
        self.psb = [k.pst([128, 1024], BF16, "psb%d" % i) for i in range(2)]
        self.psi = 0
        self.wsl = None
        self.wsi = 0

    def nps(self):
        b = self.ps[self.psi % 6]
        self.psi += 1
        return b

    def alloc_ws(self, st):
        self.wsl = [self.k.sb([128, 8192], BF16, "wsl%d" % i, st) for i in range(3)]

    def nws(self):
        b = self.wsl[self.wsi % 3]
        self.wsi += 1
        return b

    def phase0(self):
        k = self.k
        k.dma("sp", self.vecs[:, :], self.vecs_d[:, :], writes=[self.vecs])
        idf = k.sb([128, 128], F32, "idf")
        k.dma("sp", idf[:, :], self.ident_d[:, :], writes=[idf])
        k.op("dve", lambda e: e.tensor_copy(out=self.identb[:, :], in_=idf[:, :]), reads=[idf], writes=[self.identb])
        k.op("pool", lambda e: e.memset(self.ones[:, :], 1.0), writes=[self.ones])
        for col in range(2):
            k.op("act", lambda e: e.activation(out=self.sT[:, :, col], in_=self.vecs[:, C_OFF + 16 * col:C_OFF + 16 * col + 16],
                                               func=AF.Silu), reads=[self.vecs], writes=[self.sT])
        st = contextlib.ExitStack()
        wm = [k.sb([128, KT * 128], F32, "wm%d" % i, st) for i in range(3)]
        for l in range(self.nlayers):
            for mt in range(96):
                slot = wm[mt % 3]
                k.dma("sp", slot[:, :], self.wmod_d[l, mt], writes=[slot])
                ps = self.nps()
                for kt in range(KT):
                    k.op("pe", lambda e: e.matmul(ps[:, 0:2], lhsT=slot[:, kt * 128:(kt + 1) * 128], rhs=self.sT[:, kt, :],
                                                  start=(kt == 0), stop=(kt == KT - 1)),
                         reads=[slot, self.sT], writes=[ps], inc=(kt == KT - 1))
                boff = l * VEC_L + 64 + mt
                k.op("dve", lambda e: e.tensor_scalar(out=self.modT[:, l, mt, :], in0=ps[:, 0:2],
                                                      scalar1=self.vecs[:, boff:boff + 1], scalar2=None, op0=ALU.add),
                     reads=[ps, self.vecs], writes=[self.modT])
            for s in range(2):
                sc0 = 16 + 48 * s
                gt0 = 32 + 48 * s
                gpre = l * VEC_L + (0 if s == 0 else 32)
                gpost = l * VEC_L + (16 if s == 0 else 48)
                for col in range(2):
                    k.op("dve", lambda e: e.scalar_tensor_tensor(out=self.A[:, l, s, :, col], in0=self.modT[:, l, sc0:sc0 + 16, col],
                                                                 scalar=1.0, in1=self.vecs[:, gpre:gpre + 16],
                                                                 op0=ALU.add, op1=ALU.mult),
                         reads=[self.modT, self.vecs], writes=[self.A])
                    k.op("dve", lambda e: e.tensor_tensor(out=self.G[:, l, s, :, col], in0=self.modT[:, l, gt0:gt0 + 16, col],
                                                          in1=self.vecs[:, gpost:gpost + 16], op=ALU.mult),
                         reads=[self.modT, self.vecs], writes=[self.G])
        k.barrier()
        st.close()

    def rstd_of(self, src, W, sq, rstd):
        k = self.k
        k.op("act", lambda e: e.activation(out=sq[:, :, :W], in_=src[:, :, :W], func=AF.Square), reads=[src], writes=[sq])
        ps = self.nps()
        for kt in range(KT):
            k.op("pe", lambda e: e.matmul(ps[:, :W], lhsT=self.ones[:, :], rhs=sq[:, kt, :W], start=(kt == 0), stop=(kt == KT - 1)),
                 reads=[self.ones, sq], writes=[ps], inc=(kt == KT - 1))
        k.op("act", lambda e: e.activation(out=rstd[:, :W], in_=ps[:, :W], func=AF.Sqrt, bias=self.epsb[:, 0:1], scale=1.0 / D),
             reads=[ps, self.epsb], writes=[rstd])
        k.op("dve", lambda e: e.reciprocal(out=rstd[:, :W], in_=rstd[:, :W]), reads=[rstd], writes=[rstd])

    def modulate(self, src, W, rstd, l, s, col, dst, dst_t0, tmps):
        k = self.k
        sh0 = 0 if s == 0 else 48
        for kt in range(KT):
            tmp = tmps[kt % 2]
            k.op("dve", lambda e: e.scalar_tensor_tensor(out=tmp[:, :W], in0=src[:, kt, :W], scalar=self.A[:, l, s, kt, col:col + 1],
                                                         in1=rstd[:, :W], op0=ALU.mult, op1=ALU.mult),
                 reads=[src, rstd, self.A], writes=[tmp])
            k.op("act", lambda e: e.activation(out=dst[:, kt, dst_t0:dst_t0 + W], in_=tmp[:, :W], func=AF.Identity,
                                               bias=self.modT[:, l, sh0 + kt, col:col + 1], scale=1.0),
                 reads=[tmp, self.modT], writes=[dst])

    def xsrc(self, l):
        return self.xT_d if l == 0 else self.x1_d

    def phaseAB(self, l):
        k = self.k
        st = contextlib.ExitStack()
        self.alloc_ws(st)
        hT = k.sb([128, KT, NTOK], BF16, "hT", st)
        xs = [k.sb([128, KT, 512], F32, "xs%d" % i, st) for i in range(1)]
        sq = k.sb([128, KT, 512], BF16, "sq", st)
        rstd = k.sb([128, 512], F32, "rstd", st)
        tmps = [k.sb([128, 512], F32, "tmp%d" % i, st) for i in range(4)]
        stg = [k.sb([128, 512], BF16, "stg%d" % i, st) for i in range(4)]
        rope = k.sb([128, 2, NTOK], F32, "rope", st)
        k.dma("sp", rope[:, 0, :], self.rope_d[0], writes=[rope])
        k.dma("sp", rope[:, 1, :], self.rope_d[1], writes=[rope])
        xd = self.xsrc(l)
        for ci, (t0, W) in enumerate(CHUNKS):
            col = 1 if ci == 0 else 0
            x = xs[0]
            k.dma("sp", x[:, :, :W], xd[:, :, t0:t0 + W].rearrange("k p t -> p k t"), reads=[k.tok(("x", l, ci))], writes=[x])
            self.rstd_of(x, W, sq, rstd)
            self.modulate(x, W, rstd, l, 0, col, hT, t0, tmps)
        if "hT" in self.dbg:
            k.dma("sp", self.dbg["hT"].rearrange("k p t -> p k t"), hT[:, :, :], reads=[hT])
        si = [0]

        def store(dst_ap, src_fn, reads_ps, tokkey, eng="act"):
            s = stg[si[0] % 4]
            si[0] += 1
            src_fn(s)
            k.dma("sp", dst_ap, s[:, :dst_ap.shape[-1]], reads=[s], writes=[k.tok(tokkey)])

        def mm_group(ps, wslot, woff, t0, W):
            for kt in range(KT):
                k.op("pe", lambda e: e.matmul(ps[:, :W], lhsT=wslot[:, woff + kt * 128:woff + (kt + 1) * 128], rhs=hT[:, kt, t0:t0 + W],
                                              start=(kt == 0), stop=(kt == KT - 1)),
                     reads=[wslot, hT], writes=[ps], inc=(kt == KT - 1))

        def simple(pbase, n, dst_d, key, func):
            for m in range(0, n, 4):
                ws = self.nws()
                k.dma("pool", ws[:, :4 * 2048].rearrange("p (m x) -> p m x", m=4),
                      self.win_d[l, pbase + m:pbase + m + 4].rearrange("m p x -> p m x"), writes=[ws])
                for mm in range(4):
                    for ci, (t0, W) in enumerate(CHUNKS):
                        ps = self.nps()
                        mm_group(ps, ws, mm * 2048, t0, W)
                        store(dst_d[m + mm, :, t0:t0 + W],
                              lambda s: k.op("act", lambda e: e.activation(out=s[:, :W], in_=ps[:, :W], func=func), reads=[ps], writes=[s]),
                              None, (key, l, m + mm, ci))

        simple(0, 8, self.u_d, "u", AF.Copy)
        simple(40, 16, self.gs_d, "gs", AF.Sigmoid)
        simple(56, 16, self.gn_d, "gn", AF.Sigmoid)

        def roped(pbase, dst_d, key):
            for m in range(0, 8, 2):
                ws = self.nws()
                k.dma("pool", ws[:, 0:4096].rearrange("p (m x) -> p m x", m=2),
                      self.win_d[l, pbase + m:pbase + m + 2].rearrange("m p x -> p m x"), writes=[ws])
                k.dma("pool", ws[:, 4096:8192].rearrange("p (m x) -> p m x", m=2),
                      self.win_d[l, pbase + 8 + m:pbase + 8 + m + 2].rearrange("m p x -> p m x"), writes=[ws])
                for mm in range(2):
                    for ci, (t0, W) in enumerate(CHUNKS):
                        ps1 = self.nps()
                        mm_group(ps1, ws, mm * 2048, t0, W)
                        ps2 = self.nps()
                        mm_group(ps2, ws, 4096 + mm * 2048, t0, W)
                        ta, tb = tmps[2], tmps[3]
                        k.op("dve", lambda e: e.tensor_tensor(out=ta[:, :W], in0=ps1[:, :W], in1=rope[:, 0, t0:t0 + W], op=ALU.mult),
                             reads=[ps1, rope], writes=[ta])
                        k.op("dve", lambda e: e.tensor_tensor(out=tb[:, :W], in0=ps2[:, :W], in1=rope[:, 1, t0:t0 + W], op=ALU.mult),
                             reads=[ps2, rope], writes=[tb])
                        store(dst_d[m + mm, :, t0:t0 + W],
                              lambda s: k.op("pool", lambda e: e.tensor_tensor(out=s[:, :W], in0=ta[:, :W], in1=tb[:, :W], op=ALU.add),
                                             reads=[ta, tb], writes=[s]),
                              None, (key, l, m + mm, ci))

        roped(8, self.q_d, "q")
        roped(24, self.k_d, "k")
        for nt in range(2):
            ws = self.nws()
            k.dma("pool", ws[:, :], self.wv_d[l, nt], writes=[ws])
            for sh in range(2):
                for tt in range(18):
                    tk0 = tt * 128 + 64 * sh
                    if tk0 + 128 > NTOK:
                        continue
                    ps = self.nps()
                    for kt in range(KT):
                        k.op("pe", lambda e: e.matmul(ps[:, :], lhsT=hT[:, kt, tk0:tk0 + 128], rhs=ws[:, kt * 512:(kt + 1) * 512],
                                                      start=(kt == 0), stop=(kt == KT - 1)),
                             reads=[ws, hT], writes=[ps], inc=(kt == KT - 1))
                    store(self.v_d[sh, tt, :, nt * 512:(nt + 1) * 512],
                          lambda s: k.op("act", lambda e: e.activation(out=s[:, :], in_=ps[:, :], func=AF.Copy), reads=[ps], writes=[s]),
                          None, ("v", l, sh, tt, nt))
        k.barrier()
        st.close()

    def phaseC(self, l):
        k = self.k
        st = contextlib.ExitStack()
        kT = k.sb([128, 8, NTOK], BF16, "kT", st)
        vA = k.sb([128, 2, 18, 1024], BF16, "vA", st)
        tab = k.sb([128, 8, 15, 64], F32, "tab", st)
        qT = [k.sb([128, NTOK], BF16, "qT%d" % i, st) for i in range(2)]
        ost = [k.sb([128, NTOK], BF16, "ost%d" % i, st) for i in range(2)]
        S = [k.sb([128, 768], F32, "S%d" % i, st) for i in range(2)]
        P = [k.sb([128, 768], F32, "P%d" % i, st) for i in range(2)]
        Pb = [k.sb([128, 768], BF16, "Pb%d" % i, st) for i in range(2)]
        PT = [k.sb([128, 768], BF16, "PT%d" % i, st) for i in range(2)]
        sm = [k.sb([128, 4], F32, "sm%d" % i, st) for i in range(2)]
        for h in range(8):
            k.dma("sp", kT[:, h, :], self.k_d[h], reads=[k.tok(("k", l, h, ci)) for ci in range(5)], writes=[kT])
        for sh in range(2):
            for tt in range(18):
                if tt * 128 + 64 * sh + 128 > NTOK:
                    continue
                k.dma("sp", vA[:, sh, tt, :], self.v_d[sh, tt], reads=[k.tok(("v", l, sh, tt, nt)) for nt in range(2)], writes=[vA])
        k.op("pool", lambda e: e.memset(tab[:, :, :, :], NEG), writes=[tab])
        rp = self.rpb_d[l].rearrange("(hp hh) r c -> hh hp r c", hh=2)
        with self.k.nc.allow_non_contiguous_dma(reason="rpb window gather (tiny)"):
            for hh in range(2):
                for q in range(64):
                    c0 = min(max(q - 8, 0), 48)
                    off = c0 - q + 15
                    k.dma("sp", tab[hh * 64 + q:hh * 64 + q + 1, :, :, c0:c0 + 16], rp[hh:hh + 1, :, :, off:off + 16], writes=[tab])
        psS = [self.ps[0], self.ps[1]]
        psC = [self.ps[2], self.ps[3]]
        psO = [self.ps[4], self.ps[5]]
        it = 0
        for hp in range(8):
            q = qT[hp % 2]
            o = ost[hp % 2]
            k.dma("sp", q[:, :], self.q_d[hp], reads=[k.tok(("q", l, hp, ci)) for ci in range(5)], writes=[q])
            rows = list(range(32)) + ([-1, -2, -3, -4] if l < self.nlayers_total - 1 else [])
            for r in rows:
                i2 = it % 2
                it += 1
                s_, p_, pb_, pt_, sm_ = S[i2], P[i2], Pb[i2], PT[i2], sm[i2]
                pS, pC, pO, pT = psS[i2], psC[i2], psO[i2], self.psb[i2]
                if r >= 0:
                    qc = CTX + r * 64
                    r0 = min(max(r - 4, 0), 24)
                    ro0 = r0 - r + 7
                    k0 = CTX + r0 * 64
                    NK = 768
                    for hh in range(2):
                        pr = slice(hh * 64, hh * 64 + 64)
                        k.op("pe", lambda e: e.matmul(pS[pr, :], lhsT=q[pr, qc:qc + 64], rhs=kT[pr, hp, k0:k0 + 512], start=True, stop=True),
                             reads=[q, kT], writes=[pS])
                        k.op("pe", lambda e: e.matmul(pC[pr, 0:256], lhsT=q[pr, qc:qc + 64], rhs=kT[pr, hp, 0:256], start=True, stop=True),
                             reads=[q, kT], writes=[pC])
                    k.op("dve", lambda e: e.scalar_tensor_tensor(out=s_[:, 0:512], in0=pS[:, :], scalar=0.125,
                                                                 in1=tab[:, hp, ro0:ro0 + 8, :].rearrange("p a b -> p (a b)"),
                                                                 op0=ALU.mult, op1=ALU.add),
                         reads=[pS, tab], writes=[s_])
                    k.op("act", lambda e: e.activation(out=s_[:, 512:768], in_=pC[:, 0:256], func=AF.Copy, scale=0.125),
                         reads=[pC], writes=[s_])
                else:
                    qc = (-r - 1) * 64
                    NK = 256
                    for hh in range(2):
                        pr = slice(hh * 64, hh * 64 + 64)
                        k.op("pe", lambda e: e.matmul(pC[pr, 0:256], lhsT=q[pr, qc:qc + 64], rhs=kT[pr, hp, 0:256], start=True, stop=True),
                             reads=[q, kT], writes=[pC])
                    k.op("act", lambda e: e.activation(out=s_[:, 0:256], in_=pC[:, 0:256], func=AF.Copy, scale=0.125),
                         reads=[pC], writes=[s_])
                k.op("dve", lambda e: e.reduce_max(out=sm_[:, 0:1], in_=s_[:, :NK], axis=AX.X), reads=[s_], writes=[sm_])
                k.op("dve", lambda e: e.tensor_scalar(out=sm_[:, 1:2], in0=sm_[:, 0:1], scalar1=-1.0, scalar2=None, op0=ALU.mult),
                     reads=[sm_], writes=[sm_])
                k.op("act", lambda e: e.activation(out=p_[:, :NK], in_=s_[:, :NK], func=AF.Exp, bias=sm_[:, 1:2], scale=1.0,
                                                   accum_out=sm_[:, 2:3]),
                     reads=[s_, sm_], writes=[p_, sm_])
                k.op("dve", lambda e: e.reciprocal(out=sm_[:, 3:4], in_=sm_[:, 2:3]), reads=[sm_], writes=[sm_])
                k.op("dve", lambda e: e.tensor_scalar(out=pb_[:, :NK], in0=p_[:, :NK], scalar1=sm_[:, 3:4], scalar2=None, op0=ALU.mult),
                     reads=[p_, sm_], writes=[pb_])
                nj = NK // 128
                for j in range(nj):
                    k.op("pe", lambda e: e.transpose(out=pT[:, j * 128:(j + 1) * 128], in_=pb_[:, j * 128:(j + 1) * 128], identity=self.identb[:, :]),
                         reads=[pb_, self.identb], writes=[pT], inc=(j == nj - 1))
                k.op("act", lambda e: e.activation(out=pt_[:, :NK], in_=pT[:, :NK], func=AF.Copy), reads=[pT], writes=[pt_])
                for hh in range(2):
                    pr = slice(hh * 64, hh * 64 + 64)
                    hc = (2 * hp + hh) * 64
                    for j in range(nj):
                        if r >= 0 and j < 4:
                            sh = (r0 % 2)
                            tt = (k0 - 64 * sh) // 128 + j
                            vv = vA[:, sh, tt, hc:hc + 64]
                        else:
                            jj = j - 4 if r >= 0 else j
                            vv = vA[:, 0, jj, hc:hc + 64]
                        k.op("pe", lambda e: e.matmul(pO[pr, 0:64], lhsT=vv, rhs=pt_[:, j * 128 + hh * 64:j * 128 + hh * 64 + 64],
                                                      start=(j == 0), stop=(j == nj - 1)),
                             reads=[vA, pt_], writes=[pO], inc=(j == nj - 1))
                k.op("dve", lambda e: e.tensor_copy(out=o[:, qc:qc + 64], in_=pO[:, 0:64]), reads=[pO], writes=[o])
            k.dma("sp", self.o_d[hp], o[:, :], reads=[o], writes=[k.tok(("o", l, hp))])
        k.barrier()
        st.close()

    def phaseE(self, l, t_lo):
        k = self.k
        st = contextlib.ExitStack()
        self.alloc_ws(st)
        aT = k.sb([128, 8, NTOK], BF16, "aT", st)
        oT = k.sb([128, 8, NTOK], BF16, "oT", st)
        gsb = [k.sb([128, 512], BF16, "gsb%d" % i, st) for i in range(2)]
        gnb = [k.sb([128, 512], BF16, "gnb%d" % i, st) for i in range(2)]
        t1 = [k.sb([128, 512], F32, "t1_%d" % i, st) for i in range(2)]
        t2 = [k.sb([128, 512], F32, "t2_%d" % i, st) for i in range(2)]
        t3 = [k.sb([128, 512], F32, "t3_%d" % i, st) for i in range(2)]
        stg = [k.sb([128, 512], BF16, "stgE%d" % i, st) for i in range(3)]
        for h in range(8):
            k.dma("sp", aT[:, h, :], self.a_d[h], reads=[k.tok(("a", l, h))], writes=[aT])
            k.dma("sp", oT[:, h, :], self.o_d[h], reads=[k.tok(("o", l, h))], writes=[oT])
        it = 0
        chunks = [c for c in CHUNKS if c[0] >= t_lo]
        for mt in range(16):
            ws = self.nws()
            k.dma("pool", ws[:, 0:1024], self.wval_d[l, mt], writes=[ws])
            k.dma("pool", ws[:, 1024:2048], self.wglu_d[l, mt], writes=[ws])
            k.dma("pool", ws[:, 2048:3072], self.wna_d[l, mt], writes=[ws])
            for (t0, W) in chunks:
                ci = [c[0] for c in CHUNKS].index(t0)
                i2 = it % 2
                it += 1
                pss = []
                for wi, src in ((0, aT), (1, aT), (2, oT)):
                    ps = self.nps()
                    for kt in range(8):
                        k.op("pe", lambda e: e.matmul(ps[:, :W], lhsT=ws[:, wi * 1024 + kt * 128:wi * 1024 + (kt + 1) * 128], rhs=src[:, kt, t0:t0 + W],
                                                      start=(kt == 0), stop=(kt == 7)),
                             reads=[ws, src], writes=[ps], inc=(kt == 7))
                    pss.append(ps)
                g1, g2 = gsb[i2], gnb[i2]
                k.dma("sp", g1[:, :W], self.gs_d[mt, :, t0:t0 + W], reads=[k.tok(("gs", l, mt, ci))], writes=[g1])
                k.dma("sp", g2[:, :W], self.gn_d[mt, :, t0:t0 + W], reads=[k.tok(("gn", l, mt, ci))], writes=[g2])
                a1, a2, a3 = t1[i2], t2[i2], t3[i2]
                k.op("act", lambda e: e.activation(out=a1[:, :W], in_=pss[1][:, :W], func=AF.Sigmoid), reads=[pss[1]], writes=[a1])
                k.op("dve", lambda e: e.tensor_tensor(out=a2[:, :W], in0=pss[0][:, :W], in1=a1[:, :W], op=ALU.mult), reads=[pss[0], a1], writes=[a2])
                k.op("pool", lambda e: e.tensor_tensor(out=a2[:, :W], in0=a2[:, :W], in1=g1[:, :W], op=ALU.mult), reads=[a2, g1], writes=[a2])
                k.op("dve", lambda e: e.tensor_tensor(out=a3[:, :W], in0=pss[2][:, :W], in1=g2[:, :W], op=ALU.mult), reads=[pss[2], g2], writes=[a3])
                s = stg[it % 3]
                k.op("pool", lambda e: e.tensor_tensor(out=s[:, :W], in0=a2[:, :W], in1=a3[:, :W], op=ALU.add), reads=[a2, a3], writes=[s])
                k.dma("sp", self.m_d[mt, :, t0:t0 + W], s[:, :W], reads=[s], writes=[k.tok(("m", l, mt, ci))])
        k.barrier()
        st.close()

    def phaseF(self, l, t_lo, last):
        k = self.k
        st = contextlib.ExitStack()
        self.alloc_ws(st)
        x = k.sb([128, KT, 512], F32, "xF", st)
        ob = k.sb([128, KT, 512], F32, "oF", st)
        mh = k.sb([128, KT, 512], BF16, "mh", st)
        ab = k.sb([128, 64, 512], BF16, "ab", st)
        rstd = k.sb([128, 512], F32, "rstdF", st)
        tmps = [k.sb([128, 512], F32, "tmpF%d" % i, st) for i in range(2)]
        sq = Buf(ab.ap, "sqalias")
        xd = self.xsrc(l)
        chunks = [c for c in CHUNKS if c[0] >= t_lo]

        def wload(first, fill, src_ap, rr=None):
            ws = self.nws()
            wtok = k.tok(("wc", fill))
            if first:
                k.dma("pool", ws[:, :].rearrange("p (m x) -> p m x", m=rr) if rr else ws[:, :], src_ap, writes=[ws])
                k.dma("sp", self.wc_d[fill], ws[:, :], reads=[ws], writes=[wtok])
            else:
                k.dma("sp", ws[:, :], self.wc_d[fill], reads=[wtok], writes=[ws])
            return ws

        for cidx, (t0, W) in enumerate(chunks):
            first = (cidx == 0)
            ci = [c[0] for c in CHUNKS].index(t0)
            col = 1 if ci == 0 else 0
            k.dma("sp", x[:, :, :W], xd[:, :, t0:t0 + W].rearrange("k p t -> p k t"), reads=[k.tok(("x", l, ci))], writes=[x])
            k.dma("sp", mh[:, :, :W], self.m_d[:, :, t0:t0 + W].rearrange("k p t -> p k t"),
                  reads=[k.tok(("m", l, mt, ci)) for mt in range(16)], writes=[mh])
            for m in range(0, 16, 4):
                ws = wload(first, m // 4, self.wout_d[l, m:m + 4].rearrange("m p x -> p m x"), 4)
                for mm in range(4):
                    ps = self.nps()
                    for kt in range(KT):
                        k.op("pe", lambda e: e.matmul(ps[:, :W], lhsT=ws[:, mm * 2048 + kt * 128:mm * 2048 + (kt + 1) * 128], rhs=mh[:, kt, :W],
                                                      start=(kt == 0), stop=(kt == KT - 1)),
                             reads=[ws, mh], writes=[ps], inc=(kt == KT - 1))
                    k.op("act", lambda e: e.activation(out=ob[:, m + mm, :W], in_=ps[:, :W], func=AF.Copy), reads=[ps], writes=[ob])
            self.residual(x, ob, W, sq, ab, rstd, l, 0, col, tmps)
            self.rstd_of(x, W, Buf(ab.ap, "sq2"), rstd) if False else None
            self._rstd_alias(x, W, ab, rstd)
            self.modulate(x, W, rstd, l, 1, col, mh, 0, tmps)
            for m in range(0, 64, 4):
                ws = wload(first, 4 + m // 4, self.wfc1_d[l, m:m + 4].rearrange("m p x -> p m x"), 4)
                for mm in range(4):
                    ps = self.nps()
                    for kt in range(KT):
                        k.op("pe", lambda e: e.matmul(ps[:, :W], lhsT=ws[:, mm * 2048 + kt * 128:mm * 2048 + (kt + 1) * 128], rhs=mh[:, kt, :W],
                                                      start=(kt == 0), stop=(kt == KT - 1)),
                             reads=[ws, mh], writes=[ps], inc=(kt == KT - 1))
                    tm = tmps[mm % 2]
                    k.op("act", lambda e: e.activation(out=tm[:, :W], in_=ps[:, :W], func=AF.Relu), reads=[ps], writes=[tm])
                    k.op("pool", lambda e: e.tensor_tensor(out=ab[:, m + mm, :W], in0=tm[:, :W], in1=tm[:, :W], op=ALU.mult), reads=[tm], writes=[ab])
            for m in range(16):
                ws = wload(first, 20 + m, self.wfc2_d[l, m])
                ps = self.nps()
                for kt in range(64):
                    k.op("pe", lambda e: e.matmul(ps[:, :W], lhsT=ws[:, kt * 128:(kt + 1) * 128], rhs=ab[:, kt, :W], start=(kt == 0), stop=(kt == 63)),
                         reads=[ws, ab], writes=[ps], inc=(kt == 63))
                k.op("act", lambda e: e.activation(out=ob[:, m, :W], in_=ps[:, :W], func=AF.Copy), reads=[ps], writes=[ob])
            self.residual(x, ob, W, sq, ab, rstd, l, 1, col, tmps)
            if last:
                k.dma("sp", self.yT_d[:, :, t0 - CTX:t0 - CTX + W].rearrange("k p t -> p k t"), x[:, :, :W], reads=[x],
                      writes=[k.tok(("y", ci))], is_out=True)
            else:
                k.dma("sp", self.x1_d[:, :, t0:t0 + W].rearrange("k p t -> p k t"), x[:, :, :W], reads=[x], writes=[k.tok(("x", l + 1, ci))])
        k.barrier()
        st.close()

    def _rstd_alias(self, src, W, ab, rstd):
        k = self.k
        sqv = ab.ap[:, 0:KT, :]
        k.op("act", lambda e: e.activation(out=sqv[:, :, :W], in_=src[:, :, :W], func=AF.Square), reads=[src], writes=[ab])
        ps = self.nps()
        for kt in range(KT):
            k.op("pe", lambda e: e.matmul(ps[:, :W], lhsT=self.ones[:, :], rhs=sqv[:, kt, :W], start=(kt == 0), stop=(kt == KT - 1)),
                 reads=[self.ones, ab], writes=[ps], inc=(kt == KT - 1))
        k.op("act", lambda e: e.activation(out=rstd[:, :W], in_=ps[:, :W], func=AF.Sqrt, bias=self.epsb[:, 0:1], scale=1.0 / D),
             reads=[ps, self.epsb], writes=[rstd])
        k.op("dve", lambda e: e.reciprocal(out=rstd[:, :W], in_=rstd[:, :W]), reads=[rstd], writes=[rstd])

    def residual(self, x, ob, W, sq, ab, rstd, l, s, col, tmps):
        k = self.k
        self._rstd_alias(ob, W, ab, rstd)
        for kt in range(KT):
            tmp = tmps[kt % 2]
            k.op("dve", lambda e: e.scalar_tensor_tensor(out=tmp[:, :W], in0=ob[:, kt, :W], scalar=self.G[:, l, s, kt, col:col + 1],
                                                         in1=rstd[:, :W], op0=ALU.mult, op1=ALU.mult),
                 reads=[ob, rstd, self.G], writes=[tmp])
            k.op("pool", lambda e: e.tensor_tensor(out=x[:, kt, :W], in0=x[:, kt, :W], in1=tmp[:, :W], op=ALU.add), reads=[x, tmp], writes=[x])

    def build(self, upto=None):
        k = self.k
        self.nlayers_total = 2
        self.epsb = k.sb([128, 1], F32, "epsb")
        k.op("pool", lambda e: e.memset(self.epsb[:, :], EPS), writes=[self.epsb])
        self.hpib = k.sb([128, 1], F32, "hpib")
        k.op("pool", lambda e: e.memset(self.hpib[:, :], float(np.pi / 2)), writes=[self.hpib])
        self.phase0()
        for l in range(self.nlayers):
            self.phaseAB(l)
            if upto == "AB":
                break
            self.phaseC(l)
            if upto == "C":
                break
            self.phaseD(l)
            last = (l == 1)
            t_lo = CTX if last else 0
            self.phaseE(l, t_lo)
            self.phaseF(l, t_lo, last)
        k.finish()
        return k.nc


def _fm(v):
    return np.ascontiguousarray(np.asarray(v, np.float32).reshape(-1, 128).T)


def _panels(W):
    K, N = W.shape
    kt, mt = K // 128, N // 128
    return np.ascontiguousarray(W.reshape(kt, 128, mt, 128).transpose(2, 1, 0, 3).reshape(mt, 128, kt * 128))


def _rope_tables():
    nf = 16
    inv = (10000.0 ** (-np.arange(nf, dtype=np.float32) / nf)).astype(np.float32)
    pos = np.arange(LAT)
    rows = (pos // 64).astype(np.float32)
    cols = (pos % 64).astype(np.float32)
    cos = np.ones((64, NTOK), np.float32)
    sin = np.zeros((64, NTOK), np.float32)
    for d in range(64):
        blk = d // 16
        p = rows if blk < 2 else cols
        ang = (p * inv[d % 16]).astype(np.float32)
        cos[d, CTX:] = np.cos(ang)
        sg = -1.0 if blk % 2 == 0 else 1.0
        sin[d, CTX:] = sg * np.sin(ang)
    return np.ascontiguousarray(np.stack([np.concatenate([cos, cos], 0), np.concatenate([sin, sin], 0)], 0))


def prep_shared(inp):
    f = lambda a: np.asarray(a, np.float32)
    w_in = f(inp["w_in"])
    partner = np.array([d + 16 if (d % 32) < 16 else d - 16 for d in range(64)])
    perm = (np.arange(16)[:, None] * 64 + partner[None, :]).reshape(-1)
    wins, wvs = [], []
    for l in range(2):
        W = w_in[l]
        u, q, kk, v, gs, gn = W[:, :1024], W[:, 1024:2048], W[:, 2048:3072], W[:, 3072:4096], W[:, 4096:6144], W[:, 6144:]
        comb = np.concatenate([u, q, q[:, perm], kk, kk[:, perm], gs, gn], axis=1)
        wins.append(_panels(comb))
        wvs.append(np.ascontiguousarray(v.reshape(KT, 128, 2, 512).transpose(2, 1, 0, 3).reshape(2, 128, KT * 512)))
    sh = {
        "wmod": np.stack([_panels(f(inp["w_mod"])[l]) for l in range(2)]),
        "win": np.stack(wins),
        "wv": np.stack(wvs),
        "rope": _rope_tables(),
        "rpb": np.ascontiguousarray(f(inp["na_rpb"])),
        "wval": np.stack([_panels(f(inp["w_ssm_val"])[l]) for l in range(2)]),
        "wglu": np.stack([_panels(f(inp["w_ssm_glu"])[l]) for l in range(2)]),
        "wna": np.stack([_panels(f(inp["w_na_proj"])[l]) for l in range(2)]),
        "wout": np.stack([_panels(f(inp["w_out"])[l]) for l in range(2)]),
        "wfc1": np.stack([_panels(f(inp["w_fc1"])[l]) for l in range(2)]),
        "wfc2": np.stack([_panels(f(inp["w_fc2"])[l]) for l in range(2)]),
        "ident": np.eye(128, dtype=np.float32),
    }
    jj = np.arange(128) // 16
    cc = np.arange(128) % 16
    s5c = np.zeros((128, 400), np.float32)
    for kk in range(8):
        s5c[:, kk] = (jj == kk)
        s5c[:, 8 + kk] = (7 - jj == kk)
    s5c[:, 16:144] = (jj[:, None] <= jj[None, :])
    s5c[:, 144:272] = (jj[:, None] >= jj[None, :])
    s5c[:, 272:400] = np.eye(128)
    sel = np.zeros((128, 8, 8, 128), np.float32)
    selT = np.zeros((128, 8, 8, 128), np.float32)
    for gi in range(8):
        for j in range(8):
            for c in range(16):
                sel[gi * 16 + c, gi, j, j * 16 + c] = 1.0
                selT[j * 16 + c, gi, j, gi * 16 + c] = 1.0
    sh["s5c"] = s5c
    sh["sel"] = sel.reshape(128, 8192)
    sh["selT"] = selT.reshape(128, 8192)
    lre, lim, ldt = f(inp["ssm_lam_re"]), f(inp["ssm_lam_im"]), f(inp["ssm_log_dt"])
    bre, bim, cre, cim = f(inp["ssm_b_re"]), f(inp["ssm_b_im"]), f(inp["ssm_c_re"]), f(inp["ssm_c_im"])
    sC = np.empty((2, 2, 7, 128, 512), np.float32)
    for l in range(2):
        for d in range(2):
            def cl(A):
                t = A.reshape(2, 32, 64).transpose(0, 2, 1)
                return np.repeat(t.reshape(128, 32), 16, axis=1)
            sC[l, d, 0] = cl(lre[l, d])
            sC[l, d, 1] = cl(lim[l, d])
            sC[l, d, 2] = cl(np.repeat(ldt[l, d][:, None], 64, axis=1))
            for a, Cc in ((3, cre), (4, cim)):
                sC[l, d, a] = Cc[l, d].reshape(2, 32, 16, 64).transpose(0, 3, 1, 2).reshape(128, 512)
            for a, B in ((5, bre), (6, bim)):
                sC[l, d, a] = B[l, d].reshape(2, 32, 64, 16).transpose(0, 2, 1, 3).reshape(128, 512)
    sh["sC"] = sC
    dd = f(inp["ssm_d"])
    sh["drep"] = np.stack([np.tile(dd[l].reshape(64, 16).T, (8, 1)) for l in range(2)]).astype(np.float32)
    return sh


def prep_core(inp, b):
    f = lambda a: np.asarray(a, np.float32)
    X = np.concatenate([f(inp["ctx"])[b], f(inp["x"])[b]], axis=0)
    xT = np.ascontiguousarray(X.T.reshape(KT, 128, NTOK))
    cols = []
    for l in range(2):
        cols += [_fm(inp["g_pre_mix"][l]), _fm(inp["g_post_mix"][l]), _fm(inp["g_pre_mlp"][l]), _fm(inp["g_post_mlp"][l]),
                 _fm(inp["b_mod"][l]), _fm(inp["ssm_d"][l])]
    cols += [_fm(inp["c"][b]), _fm(inp["c_ctx"])]
    vecs = np.ascontiguousarray(np.concatenate(cols, axis=1))
    assert vecs.shape == (128, NVEC)
    return {"xT": xT, "vecs": vecs}


def _cmul(k, eng, o_re, o_im, a_re, a_im, b_re, b_im, t1, t2, rd=(), wr=()):
    E = k.op
    E(eng, lambda e: e.tensor_tensor(out=t1, in0=a_re, in1=b_re, op=ALU.mult), reads=rd, writes=wr)
    E(eng, lambda e: e.tensor_tensor(out=t2, in0=a_im, in1=b_im, op=ALU.mult), reads=rd, writes=wr)
    E(eng, lambda e: e.tensor_tensor(out=t2, in0=t1, in1=t2, op=ALU.subtract), reads=rd, writes=wr)
    E(eng, lambda e: e.tensor_tensor(out=t1, in0=a_re, in1=b_im, op=ALU.mult), reads=rd, writes=wr)
    E(eng, lambda e: e.tensor_tensor(out=o_im, in0=a_im, in1=b_re, op=ALU.mult), reads=rd, writes=wr)
    E(eng, lambda e: e.tensor_tensor(out=o_im, in0=o_im, in1=t1, op=ALU.add), reads=rd, writes=wr)
    E(eng, lambda e: e.tensor_copy(out=o_re, in_=t2), reads=rd, writes=wr)


def _lam_base(P, st, src, F, eng):
    k = P.k
    nb = lambda nm: k.sb([128, F], F32, nm, st)
    W = Buf(None, "Wtok")
    tok = [W, src]
    dt, ar, ai, c, s, m, t1, t2 = [nb(n) for n in ("dt", "ar", "ai", "c", "s", "m", "t1", "t2")]
    o = {n: nb(n) for n in ("lbr", "lbi", "lir", "lii", "gr", "gi")}
    E = lambda en, fn: k.op(en, fn, reads=tok + [P.hpib], writes=[W])
    E("act", lambda e: e.activation(out=dt[:, :], in_=src[:, 2, :], func=AF.Exp))
    E(eng, lambda e: e.tensor_tensor(out=ar[:, :], in0=src[:, 0, :], in1=dt[:, :], op=ALU.mult))
    E(eng, lambda e: e.tensor_tensor(out=ai[:, :], in0=src[:, 1, :], in1=dt[:, :], op=ALU.mult))
    E("act", lambda e: e.activation(out=s[:, :], in_=ai[:, :], func=AF.Sin, scale=1.0 / 16))
    E("act", lambda e: e.activation(out=c[:, :], in_=ai[:, :], func=AF.Sin, scale=1.0 / 16, bias=P.hpib[:, 0:1]))
    for _ in range(4):
        E(eng, lambda e: e.tensor_tensor(out=t1[:, :], in0=c[:, :], in1=s[:, :], op=ALU.mult))
        E(eng, lambda e: e.tensor_tensor(out=c[:, :], in0=c[:, :], in1=c[:, :], op=ALU.mult))
        E(eng, lambda e: e.tensor_tensor(out=s[:, :], in0=s[:, :], in1=s[:, :], op=ALU.mult))
        E(eng, lambda e: e.tensor_tensor(out=c[:, :], in0=c[:, :], in1=s[:, :], op=ALU.subtract))
        E(eng, lambda e: e.tensor_scalar(out=s[:, :], in0=t1[:, :], scalar1=2.0, scalar2=None, op0=ALU.mult))
    E("act", lambda e: e.activation(out=m[:, :], in_=ar[:, :], func=AF.Exp))
    E(eng, lambda e: e.tensor_tensor(out=o["lbr"][:, :], in0=m[:, :], in1=c[:, :], op=ALU.mult))
    E(eng, lambda e: e.tensor_tensor(out=o["lbi"][:, :], in0=m[:, :], in1=s[:, :], op=ALU.mult))
    E("act", lambda e: e.activation(out=m[:, :], in_=ar[:, :], func=AF.Exp, scale=-1.0))
    E(eng, lambda e: e.tensor_tensor(out=o["lir"][:, :], in0=m[:, :], in1=c[:, :], op=ALU.mult))
    E("dve", lambda e: e.scalar_tensor_tensor(out=o["lii"][:, :], in0=m[:, :], scalar=-1.0, in1=s[:, :], op0=ALU.mult, op1=ALU.mult))
    lr, li = src[:, 0, :], src[:, 1, :]
    E(eng, lambda e: e.tensor_tensor(out=t1[:, :], in0=lr, in1=lr, op=ALU.mult))
    E(eng, lambda e: e.tensor_tensor(out=t2[:, :], in0=li, in1=li, op=ALU.mult))
    E(eng, lambda e: e.tensor_tensor(out=t1[:, :], in0=t1[:, :], in1=t2[:, :], op=ALU.add))
    E("dve", lambda e: e.reciprocal(out=t1[:, :], in_=t1[:, :]))
    E(eng, lambda e: e.tensor_scalar(out=c[:, :], in0=o["lbr"][:, :], scalar1=-1.0, scalar2=None, op0=ALU.add))
    E(eng, lambda e: e.tensor_tensor(out=t2[:, :], in0=c[:, :], in1=lr, op=ALU.mult))
    E(eng, lambda e: e.tensor_tensor(out=s[:, :], in0=o["lbi"][:, :], in1=li, op=ALU.mult))
    E(eng, lambda e: e.tensor_tensor(out=t2[:, :], in0=t2[:, :], in1=s[:, :], op=ALU.add))
    E(eng, lambda e: e.tensor_tensor(out=o["gr"][:, :], in0=t2[:, :], in1=t1[:, :], op=ALU.mult))
    E(eng, lambda e: e.tensor_tensor(out=t2[:, :], in0=o["lbi"][:, :], in1=lr, op=ALU.mult))
    E(eng, lambda e: e.tensor_tensor(out=s[:, :], in0=c[:, :], in1=li, op=ALU.mult))
    E(eng, lambda e: e.tensor_tensor(out=t2[:, :], in0=t2[:, :], in1=s[:, :], op=ALU.subtract))
    E(eng, lambda e: e.tensor_tensor(out=o["gi"][:, :], in0=t2[:, :], in1=t1[:, :], op=ALU.mult))
    return o, W, (t1, t2, c, s, m, dt)


def phaseD(self, l):
    k = self.k
    st = contextlib.ExitStack()
    NG = 64
    R = k.sb([128, 17408], BF16, "Rraw", st)
    self._R = R
    BS = [Buf(R.ap[:, d * 8192:(d + 1) * 8192].rearrange("q (g m) -> q g m", m=128), "BS%d" % d) for d in range(2)]
    T = k.sb([128, NG, 128], BF16, "Tg", st)
    CS = [k.sb([128, 2, 32, 128], BF16, "CS%d" % d, st) for d in range(2)]
    MU = k.sb([128, 2, 2, 64], F32, "MU", st)
    cst = k.sb([128, 400], F32, "s5c", st)
    k.dma("sp", cst[:, :], self.s5c_d[:, :], writes=[cst])
    Drep = k.sb([128, NG], F32, "Drep", st)
    k.dma("sp", Drep[:, :], self.drep_d[l], writes=[Drep])
    F = NG * 64
    for d in range(2):
        st2 = contextlib.ExitStack()
        Fc = 512
        src = k.sb([128, 7, Fc], F32, "srcC", st2)
        for a in range(7):
            k.dma("sp", src[:, a, :], self.sC_d[l, d, a], writes=[src])
        LT = k.sb([128, 2, 32, 128], BF16, "LT", st2)
        LB = k.sb([128, 2, 32, 128], BF16, "LB", st2)
        o, W, (t1, t2, c, s, m, dt) = _lam_base(self, st2, src, Fc, "pool")
        tok = [W, src]
        E = lambda en, fn, wr=(): k.op(en, fn, reads=tok, writes=[W] + list(wr))
        bbr, bbi = k.sb([128, Fc], F32, "bbr", st2), k.sb([128, Fc], F32, "bbi", st2)
        _cmul(k, "pool", bbr[:, :], bbi[:, :], o["gr"][:, :], o["gi"][:, :], src[:, 5, :], src[:, 6, :], t1[:, :], t2[:, :], rd=tok, wr=[W])
        pr, pi = o["gr"], o["gi"]
        qr, qi = k.sb([128, Fc], F32, "qr", st2), k.sb([128, Fc], F32, "qi", st2)
        E("pool", lambda e: e.tensor_copy(out=pr[:, :], in_=o["lbr"][:, :]))
        E("pool", lambda e: e.tensor_copy(out=pi[:, :], in_=o["lbi"][:, :]))
        E("pool", lambda e: e.tensor_copy(out=qr[:, :], in_=o["lir"][:, :]))
        E("pool", lambda e: e.tensor_copy(out=qi[:, :], in_=o["lii"][:, :]))
        cs5 = CS[d].ap.rearrange("q r g (j c) -> q r g j c", c=16)
        lt5 = LT.ap.rearrange("q r g (j c) -> q r g j c", c=16)
        lb5 = LB.ap.rearrange("q r g (j c) -> q r g j c", c=16)
        g3 = lambda ap: ap.rearrange("q (g c) -> q g c", c=16)
        jb0 = 7 if d == 0 else 0
        E("act", lambda e: e.activation(out=lb5[:, 0, :, jb0, :], in_=g3(bbr[:, :]), func=AF.Copy), wr=[LB])
        E("act", lambda e: e.activation(out=lb5[:, 1, :, jb0, :], in_=g3(bbi[:, :]), func=AF.Copy), wr=[LB])
        for kk in range(1, 9):
            if kk > 1:
                _cmul(k, "pool", pr[:, :], pi[:, :], pr[:, :], pi[:, :], o["lbr"][:, :], o["lbi"][:, :], t1[:, :], t2[:, :], rd=tok, wr=[W])
                _cmul(k, "pool", qr[:, :], qi[:, :], qr[:, :], qi[:, :], o["lir"][:, :], o["lii"][:, :], t1[:, :], t2[:, :], rd=tok, wr=[W])
            jj = kk - 1 if d == 0 else 8 - kk
            _cmul(k, "pool", c[:, :], s[:, :], src[:, 3, :], src[:, 4, :], pr[:, :], pi[:, :], t1[:, :], t2[:, :], rd=tok, wr=[W])
            E("act", lambda e: e.activation(out=cs5[:, 0, :, jj, :], in_=g3(c[:, :]), func=AF.Copy), wr=[CS[d]])
            E("act", lambda e: e.activation(out=cs5[:, 1, :, jj, :], in_=g3(s[:, :]), func=AF.Copy, scale=-1.0), wr=[CS[d]])
            _cmul(k, "pool", c[:, :], s[:, :], bbr[:, :], bbi[:, :], qr[:, :], qi[:, :], t1[:, :], t2[:, :], rd=tok, wr=[W])
            E("act", lambda e: e.activation(out=lt5[:, 0, :, jj, :], in_=g3(c[:, :]), func=AF.Copy), wr=[LT])
            E("act", lambda e: e.activation(out=lt5[:, 1, :, jj, :], in_=g3(s[:, :]), func=AF.Copy), wr=[LT])
            if kk <= 7:
                jb = 7 - kk if d == 0 else kk
                _cmul(k, "pool", c[:, :], s[:, :], bbr[:, :], bbi[:, :], pr[:, :], pi[:, :], t1[:, :], t2[:, :], rd=tok, wr=[W])
                E("act", lambda e: e.activation(out=lb5[:, 0, :, jb, :], in_=g3(c[:, :]), func=AF.Copy), wr=[LB])
                E("act", lambda e: e.activation(out=lb5[:, 1, :, jb, :], in_=g3(s[:, :]), func=AF.Copy), wr=[LB])
        prg = g3(pr[:, :])[:, :, 0]
        pig = g3(pi[:, :])[:, :, 0]
        gsl = slice(d * 32, d * 32 + 32)
        E("pool", lambda e: e.tensor_copy(out=MU[:, 0, 0, gsl], in_=prg), wr=[MU])
        E("pool", lambda e: e.tensor_copy(out=MU[:, 0, 1, gsl], in_=prg), wr=[MU])
        E("pool", lambda e: e.tensor_scalar(out=MU[:, 1, 0, gsl], in0=pig, scalar1=-1.0, scalar2=None, op0=ALU.mult), wr=[MU])
        E("pool", lambda e: e.tensor_copy(out=MU[:, 1, 1, gsl], in_=pig), wr=[MU])
        for g in range(64):
            hb, gl = (g // 32) * 64, g % 32
            pT = self.psb[g % 2]
            for ri in range(2):
                k.op("pe", lambda e: e.transpose(out=pT[:, ri * 64:(ri + 1) * 64], in_=LB[hb:hb + 64, ri, gl, :], identity=self.identb[hb:hb + 64, hb:hb + 64]),
                     reads=[LB, self.identb], writes=[pT], inc=(ri == 1))
            if g % 2:
                k.op("act", lambda e: e.activation(out=BS[d][:, g, :], in_=pT[:, 0:128], func=AF.Copy), reads=[pT], writes=[BS[d]])
            else:
                k.op("dve", lambda e: e.tensor_copy(out=BS[d][:, g, :], in_=pT[:, 0:128]), reads=[pT], writes=[BS[d]])
        tf = [k.sb([128, 128], F32, "tfT%d" % i, st2) for i in range(2)]
        mask = cst[:, 16 + 128 * d:16 + 128 * d + 128]
        ident = cst[:, 272:400]
        for g in range(64):
            hb, gl = (g // 32) * 64, g % 32
            ps = self.nps()
            for ri in range(2):
                k.op("pe", lambda e: e.matmul(ps[:, 0:128], lhsT=LT[hb:hb + 64, ri, gl, :], rhs=CS[d][hb:hb + 64, ri, gl, :],
                                              start=(ri == 0), stop=(ri == 1)),
                     reads=[LT, CS[d]], writes=[ps], inc=(ri == 1))
            t = tf[g % 2]
            k.op("dve", lambda e: e.tensor_tensor(out=t[:, :], in0=ps[:, 0:128], in1=mask, op=ALU.mult), reads=[ps, cst], writes=[t])
            if d == 0:
                k.op("dve", lambda e: e.scalar_tensor_tensor(out=T[:, g, :], in0=ident, scalar=Drep[:, g:g + 1], in1=t[:, :],
                                                              op0=ALU.mult, op1=ALU.add), reads=[t, cst, Drep], writes=[T])
            else:
                k.op("pool", lambda e: e.tensor_tensor(out=T[:, g, :], in0=T[:, g, :], in1=t[:, :], op=ALU.add), reads=[t, T], writes=[T])
        k.barrier()
        st2.close()
    self._s5_run(l, BS, T, CS, MU, st)
    k.barrier()
    st.close()


def _s5_run(self, l, BS, T, CS, MU, st):
    k = self.k
    NG = 64
    Xall = k.sb([128, NG, NCH8], BF16, "Xall", st)
    st1 = contextlib.ExitStack()
    uT = k.sb([128, 8, NTOK], BF16, "uT", st1)
    Sel = k.sb([128, 8, 8, 128], BF16, "Sel", st1)
    for h in range(8):
        k.dma("sp", uT[:, h, :], self.u_d[h], reads=[k.tok(("u", l, h, ci)) for ci in range(5)], writes=[uT])
    k.dma("pool", Sel[:, :, :, :].rearrange("q a b m -> q (a b m)"), self.sel_d[:, :], writes=[Sel])
    for g in range(NG):
        ps = self.nps()
        for j in range(8):
            k.op("pe", lambda e: e.matmul(ps[:, 0:NCH8], lhsT=Sel[:, g % 8, j, :], rhs=uT[:, g // 8, j:NTOK:8], start=(j == 0), stop=(j == 7)),
                 reads=[Sel, uT], writes=[ps], inc=(j == 7))
        k.op("act" if g % 2 else "dve", (lambda e: e.activation(out=Xall[:, g, :], in_=ps[:, 0:NCH8], func=AF.Copy)) if g % 2 else
             (lambda e: e.tensor_copy(out=Xall[:, g, :], in_=ps[:, 0:NCH8])), reads=[ps], writes=[Xall])
    k.barrier()
    st1.close()
    S = [k.sb([128, 2, 32, NCH8], BF16, "S%d" % d, st) for d in range(2)]
    Srd = [Buf(S[d].ap, "Srd%d" % d) for d in range(2)]
    Swr = [Buf(S[d].ap, "Swr%d" % d) for d in range(2)]
    for d in range(2):
        for g in range(NG):
            hb, gl = (g // 32) * 64, g % 32
            ps = self.nps()
            ps2 = self.nps()
            for ri in range(2):
                k.op("pe", lambda e: e.matmul(ps[hb:hb + 64, ri * 256:(ri + 1) * 256], lhsT=BS[d][:, g, ri * 64:(ri + 1) * 64], rhs=Xall[:, g, 0:256],
                                              start=True, stop=True),
                     reads=[BS[d], Xall], writes=[ps], inc=(ri == 1))
            for ri in range(2):
                k.op("pe", lambda e: e.matmul(ps2[hb:hb + 64, ri * 32:(ri + 1) * 32], lhsT=BS[d][:, g, ri * 64:(ri + 1) * 64], rhs=Xall[:, g, 256:NCH8],
                                              start=True, stop=True),
                     reads=[BS[d], Xall], writes=[ps2], inc=(ri == 1))
            if g % 2:
                k.op("act", lambda e: e.activation(out=S[d][hb:hb + 64, :, gl, 0:256], in_=ps[hb:hb + 64, 0:512].rearrange("q (r n) -> q r n", r=2), func=AF.Copy),
                     reads=[ps], writes=[Srd[d]])
                k.op("act", lambda e: e.activation(out=S[d][hb:hb + 64, :, gl, 256:NCH8], in_=ps2[hb:hb + 64, 0:64].rearrange("q (r n) -> q r n", r=2), func=AF.Copy),
                     reads=[ps2], writes=[Srd[d]])
            else:
                k.op("dve", lambda e: e.tensor_copy(out=S[d][hb:hb + 64, :, gl, 0:256], in_=ps[hb:hb + 64, 0:512].rearrange("q (r n) -> q r n", r=2)),
                     reads=[ps], writes=[Srd[d]])
                k.op("dve", lambda e: e.tensor_copy(out=S[d][hb:hb + 64, :, gl, 256:NCH8], in_=ps2[hb:hb + 64, 0:64].rearrange("q (r n) -> q r n", r=2)),
                     reads=[ps2], writes=[Srd[d]])
    H = [k.sb([128, 2, 64], F32, "H%d" % i, st) for i in range(2)]
    t1 = k.sb([128, 2, 64], F32, "rt1", st)
    t2 = k.sb([128, 2, 64], F32, "rt2", st)
    k.op("pool", lambda e: e.memset(H[0][:, :, :], 0.0), writes=[H[0]])
    for i in range(NCH8):
        nf = i
        nb = (31 - i) if i < 32 else (319 - i)
        Ho, Hn = H[i % 2], H[(i + 1) % 2]
        k.op("dve", lambda e: e.tensor_tensor(out=t1[:, :, :], in0=MU[:, 0, :, :], in1=Ho[:, :, :], op=ALU.mult), reads=[MU, Ho], writes=[t1])
        k.op("pool", lambda e: e.tensor_tensor(out=t2[:, 0, :], in0=MU[:, 1, 0, :], in1=Ho[:, 1, :], op=ALU.mult), reads=[MU, Ho], writes=[t2])
        k.op("pool", lambda e: e.tensor_tensor(out=t2[:, 1, :], in0=MU[:, 1, 1, :], in1=Ho[:, 0, :], op=ALU.mult), reads=[MU, Ho], writes=[t2])
        k.op("dve", lambda e: e.tensor_tensor(out=t1[:, :, :], in0=t1[:, :, :], in1=t2[:, :, :], op=ALU.add), reads=[t1, t2], writes=[t1])
        stk = Buf(None, "stk")
        k.op("dve", lambda e: e.tensor_tensor(out=Hn[:, :, 0:32], in0=t1[:, :, 0:32], in1=S[0][:, :, :, nf], op=ALU.add), reads=[t1, Srd[0]], writes=[Hn, stk])
        k.op("pool", lambda e: e.tensor_tensor(out=Hn[:, :, 32:64], in0=t1[:, :, 32:64], in1=S[1][:, :, :, nb], op=ALU.add), reads=[t1, Srd[1]], writes=[Hn, stk])
        k.op("act", lambda e: e.activation(out=S[0][:, :, :, nf], in_=Ho[:, :, 0:32], func=AF.Copy), reads=[Ho, stk], writes=[Swr[0]])
        k.op("act", lambda e: e.activation(out=S[1][:, :, :, nb], in_=Ho[:, :, 32:64], func=AF.Copy), reads=[Ho, stk], writes=[Swr[1]])
    k.barrier()
    R = self._R
    SelT = Buf(R.ap[:, 0:8192].rearrange("q (a b m) -> q a b m", a=8, b=8), "SelT")
    k.dma("pool", R.ap[:, 0:8192], self.selT_d[:, :], writes=[SelT])
    Ag = [Buf(R.ap[:, 8192 + i * 2304:8192 + (i + 1) * 2304].rearrange("q (a n) -> q a n", a=8), "Ag%d" % i) for i in range(2)]
    ast = [Buf(R.ap[:, 12800 + i * 2304:12800 + (i + 1) * 2304], "ast%d" % i) for i in range(2)]
    ga = [k.sb([128, NCH8], F32, "ga%d" % i, st) for i in range(2)]
    gb = [k.sb([128, NCH8], F32, "gb%d" % i, st) for i in range(2)]
    for tl in range(8):
        A = Ag[tl % 2]
        for gi in range(8):
            g = tl * 8 + gi
            hb, gl = (g // 32) * 64, g % 32
            ps = self.nps()
            k.op("pe", lambda e: e.matmul(ps[:, 0:NCH8], lhsT=T[:, g, :], rhs=Xall[:, g, :], start=True, stop=False), reads=[T, Xall], writes=[ps], inc=False)
            for d in range(2):
                for ri in range(2):
                    last = (d == 1 and ri == 1)
                    k.op("pe", lambda e: e.matmul(ps[:, 0:NCH8], lhsT=CS[d][hb:hb + 64, ri, gl, :], rhs=S[d][hb:hb + 64, ri, gl, :], start=False, stop=last),
                         reads=[CS[d], Swr[d]], writes=[ps], inc=last)
            a_, b_ = ga[gi % 2], gb[gi % 2]
            k.op("act", lambda e: e.activation(out=a_[:, :], in_=ps[:, 0:NCH8], func=AF.Square), reads=[ps], writes=[a_])
            k.op("dve", lambda e: e.tensor_scalar(out=a_[:, :], in0=a_[:, :], scalar1=0.044715, scalar2=1.0, op0=ALU.mult, op1=ALU.add), reads=[a_], writes=[a_])
            k.op("dve", lambda e: e.tensor_tensor(out=b_[:, :], in0=ps[:, 0:NCH8], in1=a_[:, :], op=ALU.mult), reads=[ps, a_], writes=[b_])
            k.op("act", lambda e: e.activation(out=b_[:, :], in_=b_[:, :], func=AF.Sigmoid, scale=1.5957691216057308), reads=[b_], writes=[b_])
            k.op("dve", lambda e: e.tensor_tensor(out=A[:, gi, :], in0=ps[:, 0:NCH8], in1=b_[:, :], op=ALU.mult), reads=[ps, b_], writes=[A])
        o = ast[tl % 2]
        for j in range(8):
            ps = self.nps()
            for gi in range(8):
                k.op("pe", lambda e: e.matmul(ps[:, 0:NCH8], lhsT=SelT[:, gi, j, :], rhs=A[:, gi, :], start=(gi == 0), stop=(gi == 7)),
                     reads=[SelT, A], writes=[ps], inc=(gi == 7))
            k.op("act" if j % 2 else "dve", (lambda e: e.activation(out=o[:, j:NTOK:8], in_=ps[:, 0:NCH8], func=AF.Copy)) if j % 2 else
                 (lambda e: e.tensor_copy(out=o[:, j:NTOK:8], in_=ps[:, 0:NCH8])), reads=[ps], writes=[o])
        k.dma("sp", self.a_d[tl], o[:, :], reads=[o], writes=[k.tok(("a", l, tl))])


def _psl(self, ps, hb, ri):
    return ps[hb:hb + 64, ri * 256:(ri + 1) * 256]


Prog.phaseD = phaseD
Prog._s5_run = _s5_run
Prog._psl = _psl


_CACHE = {}


def kernel(**inputs):
    if "nc" not in _CACHE:
        _CACHE["nc"] = Prog().build()
    nc = _CACHE["nc"]
    sh = prep_shared(inputs)
    in_maps = []
    for core in range(8):
        m = dict(sh)
        m.update(prep_core(inputs, core % 4))
        in_maps.append(m)
    res = run_bass_kernel_spmd(nc, in_maps, core_ids=list(range(8)))
    out = np.empty((4, LAT, D), np.float32)
    for b in range(4):
        yT = np.asarray(res.results[b]["yT"]).reshape(D, LAT)
        out[b] = yT.T
    return out
```

```python
import contextlib
import math
import numpy as np
import concourse.bass as bass
import concourse.mybir as mybir
from concourse.bass_utils import run_bass_kernel_spmd

F32 = mybir.dt.float32
BF16 = mybir.dt.bfloat16
AF = mybir.ActivationFunctionType
ALU = mybir.AluOpType
AX = mybir.AxisListType

D = 2048
KT = 16
NTOK = 2304
CTX = 256
LAT = 2048
DFF = 8192
EPS = 1e-6
CHUNKS = [(0, 256), (256, 512), (768, 512), (1280, 512), (1792, 512)]
NEG = -30000.0
NCH8 = NTOK // 8
VEC_L = 168
C_OFF = 336
NVEC = 368


class Buf:
    __slots__ = ("ap", "w", "r", "name")

    def __init__(self, ap=None, name=""):
        self.ap = ap
        self.w = []
        self.r = []
        self.name = name

    def __getitem__(self, idx):
        return self.ap[idx]


class KB:
    NTICK = 40

    def __init__(self):
        self.nc = bass.Bass("TRN2", target_bir_lowering=False)
        nc = self.nc
        self.es = contextlib.ExitStack()
        self.eng = {}
        self.semid = {}
        for name, e in (("pe", nc.tensor), ("act", nc.scalar), ("dve", nc.vector),
                        ("pool", nc.gpsimd), ("sp", nc.sync)):
            sem = self.es.enter_context(nc.semaphore("s_" + name))
            self.eng[name] = [e, sem, 0]
        self.known = {n: {} for n in self.eng}
        self.ticks = []
        for i in range(self.NTICK):
            sem = self.es.enter_context(nc.semaphore("tk%d" % i))
            self.ticks.append([sem, 0, "tk%d" % i])
        self.tki = {"sp": 0, "pool": 0}
        self.tkr = {"sp": (0, 26), "pool": (26, 14)}
        self.out_evs = []
        self.phase_stack = None
        self.dbufs = {}
        self.uid = 0

    def sb(self, shape, dt, name=None, stack=None):
        self.uid += 1
        nm = (name or "t") + "_%d" % self.uid
        t = (stack or self.es).enter_context(self.nc.sbuf_tensor(nm, list(shape), dt))
        return Buf(t, nm)

    def pst(self, shape, dt, name=None, stack=None):
        self.uid += 1
        nm = (name or "p") + "_%d" % self.uid
        t = (stack or self.es).enter_context(self.nc.psum_tensor(nm, list(shape), dt))
        return Buf(t, nm)

    def dram(self, name, shape, dt, kind="Internal"):
        return self.nc.dram_tensor(name, list(shape), dt, kind=kind).ap()

    def tok(self, key):
        b = self.dbufs.get(key)
        if b is None:
            b = Buf(None, str(key))
            self.dbufs[key] = b
        return b

    def _wait(self, en, evs):
        e = self.eng[en][0]
        kn = self.known[en]
        best = {}
        for (key, sem, val) in evs:
            if val > kn.get(key, 0) and val > best.get(key, (None, 0))[1]:
                best[key] = (sem, val)
        for key, (sem, val) in best.items():
            e.wait_ge(sem, val)
            kn[key] = val

    def _deps(self, en, reads, writes):
        evs = []
        for b in reads:
            evs += b.w
        for b in writes:
            evs += b.w
            evs += b.r
        if en == "pe":
            evs = [x for x in evs if x[0] != "pe"]
        return evs

    def _record(self, ev, reads, writes):
        for b in reads:
            b.r = [x for x in b.r if x[0] != ev[0]]
            b.r.append(ev)
        for b in writes:
            b.w = [ev]
            b.r = []

    def op(self, en, fn, reads=(), writes=(), inc=True):
        self._wait(en, self._deps(en, reads, writes))
        ent = self.eng[en]
        ins = fn(ent[0])
        if inc:
            ent[2] += 1
            ins.then_inc(ent[1], 1)
            ev = (en, ent[1], ent[2])
            self._record(ev, reads, writes)
        return ins

    def dma(self, q, out_ap, in_ap, reads=(), writes=(), is_out=False):
        base, cnt = self.tkr[q]
        tk = self.ticks[base + self.tki[q]]
        self.tki[q] = (self.tki[q] + 1) % cnt
        evs = self._deps(q, reads, writes)
        if tk[1] > 0:
            evs.append((tk[2], tk[0], tk[1]))
        self._wait(q, evs)
        ins = self.eng[q][0].dma_start(out=out_ap, in_=in_ap)
        tk[1] += 16
        ins.then_inc(tk[0], 16)
        ev = (tk[2], tk[0], tk[1])
        self._record(ev, reads, writes)
        if is_out:
            self.out_evs.append(ev)
        return ev

    def pe_drain(self):
        ent = self.eng["pe"]
        if ent[2] > self.known["pe"].get("pe", 0):
            ent[0].wait_ge(ent[1], ent[2])
            self.known["pe"]["pe"] = ent[2]

    def barrier(self):
        evs = []
        for n, ent in self.eng.items():
            if ent[2] > 0:
                evs.append((n, ent[1], ent[2]))
        for tk in self.ticks:
            if tk[1] > 0:
                evs.append((tk[2], tk[0], tk[1]))
        for n in self.eng:
            self._wait(n, [x for x in evs if x[0] != n])

    def finish(self):
        self.barrier()
        self.es.close()


class Prog:
    def __init__(self, dbg=None, nlayers=2):
        self.k = KB()
        self.dbg = dbg or {}
        self.nlayers = nlayers
        k = self.k
        self.xT_d = k.dram("xT", [KT, 128, NTOK], F32, "ExternalInput")
        self.vecs_d = k.dram("vecs", [128, NVEC], F32, "ExternalInput")
        self.wmod_d = k.dram("wmod", [2, 96, 128, KT * 128], F32, "ExternalInput")
        self.win_d = k.dram("win", [2, 72, 128, KT * 128], F32, "ExternalInput")
        self.wv_d = k.dram("wv", [2, 2, 128, KT * 512], F32, "ExternalInput")
        self.rope_d = k.dram("rope", [2, 128, NTOK], F32, "ExternalInput")
        self.rpb_d = k.dram("rpb", [2, 16, 15, 31], F32, "ExternalInput")
        self.wval_d = k.dram("wval", [2, 16, 128, 8 * 128], F32, "ExternalInput")
        self.wglu_d = k.dram("wglu", [2, 16, 128, 8 * 128], F32, "ExternalInput")
        self.wna_d = k.dram("wna", [2, 16, 128, 8 * 128], F32, "ExternalInput")
        self.wout_d = k.dram("wout", [2, 16, 128, KT * 128], F32, "ExternalInput")
        self.wfc1_d = k.dram("wfc1", [2, 64, 128, KT * 128], F32, "ExternalInput")
        self.wfc2_d = k.dram("wfc2", [2, 16, 128, 64 * 128], F32, "ExternalInput")
        self.s5c_d = k.dram("s5c", [128, 400], F32, "ExternalInput")
        self.drep_d = k.dram("drep", [2, 128, 64], F32, "ExternalInput")
        self.sC_d = k.dram("sC", [2, 2, 7, 128, 512], F32, "ExternalInput")
        self.sel_d = k.dram("sel", [128, 8192], F32, "ExternalInput")
        self.selT_d = k.dram("selT", [128, 8192], F32, "ExternalInput")
        self.yT_d = k.dram("yT", [KT, 128, LAT], F32, "ExternalOutput")
        self.x1_d = k.dram("x1s", [KT, 128, NTOK], F32)
        self.u_d = k.dram("us", [8, 128, NTOK], BF16)
        self.q_d = k.dram("qs", [8, 128, NTOK], BF16)
        self.k_d = k.dram("ks", [8, 128, NTOK], BF16)
        self.v_d = k.dram("vs", [2, 18, 128, 1024], BF16)
        self.gs_d = k.dram("gss", [16, 128, NTOK], BF16)
        self.gn_d = k.dram("gns", [16, 128, NTOK], BF16)
        self.o_d = k.dram("os", [8, 128, NTOK], BF16)
        self.a_d = k.dram("as", [8, 128, NTOK], BF16)
        self.m_d = k.dram("ms", [16, 128, NTOK], BF16)
        self.wc_d = k.dram("wcache", [36, 128, 8192], BF16)
        self.vecs = k.sb([128, NVEC], F32, "vecs")
        self.sT = k.sb([128, KT, 2], F32, "sT")
        self.modT = k.sb([128, 2, 96, 2], F32, "modT")
        self.A = k.sb([128, 2, 2, KT, 2], F32, "Amod")
        self.G = k.sb([128, 2, 2, KT, 2], F32, "Gmod")
        self.ones = k.sb([128, 128], BF16, "ones")
        self.identb = k.sb([128, 128], BF16, "identb")
        self.ident_d = k.dram("ident", [128, 128], F32, "ExternalInput")
        self.ps = [k.pst([128, 512], F32, "ps%d" % i) for i in range(6)]
        self.psb = [k.pst([128, 1024], BF16, "psb%d" % i) for i in range(2)]
        self.psi = 0
        self.wsl = None
        self.wsi = 0

    def nps(self):
        b = self.ps[self.psi % 6]
        self.psi += 1
        return b

    def alloc_ws(self, st):
        self.wsl = [self.k.sb([128, 8192], BF16, "wsl%d" % i, st) for i in range(3)]

    def nws(self):
        b = self.wsl[self.wsi % 3]
        self.wsi += 1
        return b

    def phase0(self):
        k = self.k
        k.dma("sp", self.vecs[:, :], self.vecs_d[:, :], writes=[self.vecs])
        idf = k.sb([128, 128], F32, "idf")
        k.dma("sp", idf[:, :], self.ident_d[:, :], writes=[idf])
        k.op("dve", lambda e: e.tensor_copy(out=self.identb[:, :], in_=idf[:, :]), reads=[idf], writes=[self.identb])
        k.op("pool", lambda e: e.memset(self.ones[:, :], 1.0), writes=[self.ones])
        for col in range(2):
            k.op("act", lambda e: e.activation(out=self.sT[:, :, col], in_=self.vecs[:, C_OFF + 16 * col:C_OFF + 16 * col + 16],
                                               func=AF.Silu), reads=[self.vecs], writes=[self.sT])
        st = contextlib.ExitStack()
        wm = [k.sb([128, KT * 128], F32, "wm%d" % i, st) for i in range(3)]
        for l in range(self.nlayers):
            for mt in range(96):
                slot = wm[mt % 3]
                k.dma("sp", slot[:, :], self.wmod_d[l, mt], writes=[slot])
                ps = self.nps()
                for kt in range(KT):
                    k.op("pe", lambda e: e.matmul(ps[:, 0:2], lhsT=slot[:, kt * 128:(kt + 1) * 128], rhs=self.sT[:, kt, :],
                                                  start=(kt == 0), stop=(kt == KT - 1)),
                         reads=[slot, self.sT], writes=[ps], inc=(kt == KT - 1))
                boff = l * VEC_L + 64 + mt
                k.op("dve", lambda e: e.tensor_scalar(out=self.modT[:, l, mt, :], in0=ps[:, 0:2],
                                                      scalar1=self.vecs[:, boff:boff + 1], scalar2=None, op0=ALU.add),
                     reads=[ps, self.vecs], writes=[self.modT])
            for s in range(2):
                sc0 = 16 + 48 * s
                gt0 = 32 + 48 * s
                gpre = l * VEC_L + (0 if s == 0 else 32)
                gpost = l * VEC_L + (16 if s == 0 else 48)
                for col in range(2):
                    k.op("dve", lambda e: e.scalar_tensor_tensor(out=self.A[:, l, s, :, col], in0=self.modT[:, l, sc0:sc0 + 16, col],
                                                                 scalar=1.0, in1=self.vecs[:, gpre:gpre + 16],
                                                                 op0=ALU.add, op1=ALU.mult),
                         reads=[self.modT, self.vecs], writes=[self.A])
                    k.op("dve", lambda e: e.tensor_tensor(out=self.G[:, l, s, :, col], in0=self.modT[:, l, gt0:gt0 + 16, col],
                                                          in1=self.vecs[:, gpost:gpost + 16], op=ALU.mult),
                         reads=[self.modT, self.vecs], writes=[self.G])
        k.barrier()
        st.close()

    def rstd_of(self, src, W, sq, rstd):
        k = self.k
        k.op("act", lambda e: e.activation(out=sq[:, :, :W], in_=src[:, :, :W], func=AF.Square), reads=[src], writes=[sq])
        ps = self.nps()
        for kt in range(KT):
            k.op("pe", lambda e: e.matmul(ps[:, :W], lhsT=self.ones[:, :], rhs=sq[:, kt, :W], start=(kt == 0), stop=(kt == KT - 1)),
                 reads=[self.ones, sq], writes=[ps], inc=(kt == KT - 1))
        k.op("act", lambda e: e.activation(out=rstd[:, :W], in_=ps[:, :W], func=AF.Sqrt, bias=self.epsb[:, 0:1], scale=1.0 / D),
             reads=[ps, self.epsb], writes=[rstd])
        k.op("dve", lambda e: e.reciprocal(out=rstd[:, :W], in_=rstd[:, :W]), reads=[rstd], writes=[rstd])

    def modulate(self, src, W, rstd, l, s, col, dst, dst_t0, tmps):
        k = self.k
        sh0 = 0 if s == 0 else 48
        for kt in range(KT):
            tmp = tmps[kt % 2]
            k.op("dve", lambda e: e.scalar_tensor_tensor(out=tmp[:, :W], in0=src[:, kt, :W], scalar=self.A[:, l, s, kt, col:col + 1],
                                                         in1=rstd[:, :W], op0=ALU.mult, op1=ALU.mult),
                 reads=[src, rstd, self.A], writes=[tmp])
            k.op("act", lambda e: e.activation(out=dst[:, kt, dst_t0:dst_t0 + W], in_=tmp[:, :W], func=AF.Identity,
                                               bias=self.modT[:, l, sh0 + kt, col:col + 1], scale=1.0),
                 reads=[tmp, self.modT], writes=[dst])

    def xsrc(self, l):
        return self.xT_d if l == 0 else self.x1_d

    def phaseAB(self, l):
        k = self.k
        st = contextlib.ExitStack()
        self.alloc_ws(st)
        hT = k.sb([128, KT, NTOK], BF16, "hT", st)
        xs = [k.sb([128, KT, 512], F32, "xs%d" % i, st) for i in range(1)]
        sq = k.sb([128, KT, 512], BF16, "sq", st)
        rstd = k.sb([128, 512], F32, "rstd", st)
        tmps = [k.sb([128, 512], F32, "tmp%d" % i, st) for i in range(4)]
        stg = [k.sb([128, 512], BF16, "stg%d" % i, st) for i in range(4)]
        rope = k.sb([128, 2, NTOK], F32, "rope", st)
        k.dma("sp", rope[:, 0, :], self.rope_d[0], writes=[rope])
        k.dma("sp", rope[:, 1, :], self.rope_d[1], writes=[rope])
        xd = self.xsrc(l)
        for ci, (t0, W) in enumerate(CHUNKS):
            col = 1 if ci == 0 else 0
            x = xs[0]
            k.dma("sp", x[:, :, :W], xd[:, :, t0:t0 + W].rearrange("k p t -> p k t"), reads=[k.tok(("x", l, ci))], writes=[x])
            self.rstd_of(x, W, sq, rstd)
            self.modulate(x, W, rstd, l, 0, col, hT, t0, tmps)
        if "hT" in self.dbg:
            k.dma("sp", self.dbg["hT"].rearrange("k p t -> p k t"), hT[:, :, :], reads=[hT])
        si = [0]

        def store(dst_ap, src_fn, reads_ps, tokkey, eng="act"):
            s = stg[si[0] % 4]
            si[0] += 1
            src_fn(s)
            k.dma("sp", dst_ap, s[:, :dst_ap.shape[-1]], reads=[s], writes=[k.tok(tokkey)])

        def mm_group(ps, wslot, woff, t0, W):
            for kt in range(KT):
                k.op("pe", lambda e: e.matmul(ps[:, :W], lhsT=wslot[:, woff + kt * 128:woff + (kt + 1) * 128], rhs=hT[:, kt, t0:t0 + W],
                                              start=(kt == 0), stop=(kt == KT - 1)),
                     reads=[wslot, hT], writes=[ps], inc=(kt == KT - 1))

        def simple(pbase, n, dst_d, key, func):
            for m in range(0, n, 4):
                ws = self.nws()
                k.dma("pool", ws[:, :4 * 2048].rearrange("p (m x) -> p m x", m=4),
                      self.win_d[l, pbase + m:pbase + m + 4].rearrange("m p x -> p m x"), writes=[ws])
                for mm in range(4):
                    for ci, (t0, W) in enumerate(CHUNKS):
                        ps = self.nps()
                        mm_group(ps, ws, mm * 2048, t0, W)
                        store(dst_d[m + mm, :, t0:t0 + W],
                              lambda s: k.op("act", lambda e: e.activation(out=s[:, :W], in_=ps[:, :W], func=func), reads=[ps], writes=[s]),
                              None, (key, l, m + mm, ci))

        simple(0, 8, self.u_d, "u", AF.Copy)
        simple(40, 16, self.gs_d, "gs", AF.Sigmoid)
        simple(56, 16, self.gn_d, "gn", AF.Sigmoid)

        def roped(pbase, dst_d, key):
            for m in range(0, 8, 2):
                ws = self.nws()
                k.dma("pool", ws[:, 0:4096].rearrange("p (m x) -> p m x", m=2),
                      self.win_d[l, pbase + m:pbase + m + 2].rearrange("m p x -> p m x"), writes=[ws])
                k.dma("pool", ws[:, 4096:8192].rearrange("p (m x) -> p m x", m=2),
                      self.win_d[l, pbase + 8 + m:pbase + 8 + m + 2].rearrange("m p x -> p m x"), writes=[ws])
                for mm in range(2):
                    for ci, (t0, W) in enumerate(CHUNKS):
                        ps1 = self.nps()
                        mm_group(ps1, ws, mm * 2048, t0, W)
                        ps2 = self.nps()
                        mm_group(ps2, ws, 4096 + mm * 2048, t0, W)
                        ta, tb = tmps[2], tmps[3]
                        k.op("dve", lambda e: e.tensor_tensor(out=ta[:, :W], in0=ps1[:, :W], in1=rope[:, 0, t0:t0 + W], op=ALU.mult),
                             reads=[ps1, rope], writes=[ta])
                        k.op("dve", lambda e: e.tensor_tensor(out=tb[:, :W], in0=ps2[:, :W], in1=rope[:, 1, t0:t0 + W], op=ALU.mult),
                             reads=[ps2, rope], writes=[tb])
                        store(dst_d[m + mm, :, t0:t0 + W],
                              lambda s: k.op("pool", lambda e: e.tensor_tensor(out=s[:, :W], in0=ta[:, :W], in1=tb[:, :W], op=ALU.add),
                                             reads=[ta, tb], writes=[s]),
                              None, (key, l, m + mm, ci))

        roped(8, self.q_d, "q")
        roped(24, self.k_d, "k")
        for nt in range(2):
            ws = self.nws()
            k.dma("pool", ws[:, :], self.wv_d[l, nt], writes=[ws])
            for sh in range(2):
                for tt in range(18):
                    tk0 = tt * 128 + 64 * sh
                    if tk0 + 128 > NTOK:
                        continue
                    ps = self.nps()
                    for kt in range(KT):
                        k.op("pe", lambda e: e.matmul(ps[:, :], lhsT=hT[:, kt, tk0:tk0 + 128], rhs=ws[:, kt * 512:(kt + 1) * 512],
                                                      start=(kt == 0), stop=(kt == KT - 1)),
                             reads=[ws, hT], writes=[ps], inc=(kt == KT - 1))
                    store(self.v_d[sh, tt, :, nt * 512:(nt + 1) * 512],
                          lambda s: k.op("act", lambda e: e.activation(out=s[:, :], in_=ps[:, :], func=AF.Copy), reads=[ps], writes=[s]),
                          None, ("v", l, sh, tt, nt))
        k.barrier()
        st.close()

    def phaseC(self, l):
        k = self.k
        st = contextlib.ExitStack()
        kT = k.sb([128, 8, NTOK], BF16, "kT", st)
        vA = k.sb([128, 2, 18, 1024], BF16, "vA", st)
        tab = k.sb([128, 8, 15, 64], F32, "tab", st)
        qT = [k.sb([128, NTOK], BF16, "qT%d" % i, st) for i in range(2)]
        ost = [k.sb([128, NTOK], BF16, "ost%d" % i, st) for i in range(2)]
        S = [k.sb([128, 768], F32, "S%d" % i, st) for i in range(2)]
        P = [k.sb([128, 768], F32, "P%d" % i, st) for i in range(2)]
        Pb = [k.sb([128, 768], BF16, "Pb%d" % i, st) for i in range(2)]
        PT = [k.sb([128, 768], BF16, "PT%d" % i, st) for i in range(2)]
        sm = [k.sb([128, 4], F32, "sm%d" % i, st) for i in range(2)]
        for h in range(8):
            k.dma("sp", kT[:, h, :], self.k_d[h], reads=[k.tok(("k", l, h, ci)) for ci in range(5)], writes=[kT])
        for sh in range(2):
            for tt in range(18):
                if tt * 128 + 64 * sh + 128 > NTOK:
                    continue
                k.dma("sp", vA[:, sh, tt, :], self.v_d[sh, tt], reads=[k.tok(("v", l, sh, tt, nt)) for nt in range(2)], writes=[vA])
        k.op("pool", lambda e: e.memset(tab[:, :, :, :], NEG), writes=[tab])
        rp = self.rpb_d[l].rearrange("(hp hh) r c -> hh hp r c", hh=2)
        with self.k.nc.allow_non_contiguous_dma(reason="rpb window gather (tiny)"):
            for hh in range(2):
                for q in range(64):
                    c0 = min(max(q - 8, 0), 48)
                    off = c0 - q + 15
                    k.dma("sp", tab[hh * 64 + q:hh * 64 + q + 1, :, :, c0:c0 + 16], rp[hh:hh + 1, :, :, off:off + 16], writes=[tab])
        psS = [self.ps[0], self.ps[1]]
        psC = [self.ps[2], self.ps[3]]
        psO = [self.ps[4], self.ps[5]]
        items = []
        for hp in range(8):
            rows = list(range(32)) + ([-1, -2, -3, -4] if l < self.nlayers_total - 1 else [])
            for r in rows:
                items.append((hp, r))
        NI = len(items)

        def geo(idx):
            hp, r = items[idx]
            i2 = idx % 2
            d = dict(hp=hp, r=r, i2=i2, q=qT[hp % 2], o=ost[hp % 2], s=S[i2], p=P[i2], pb=Pb[i2], pt=PT[i2], sm=sm[i2],
                     pS=psS[i2], pC=psC[i2], pO=psO[i2], pT=self.psb[i2])
            if r >= 0:
                r0 = min(max(r - 4, 0), 24)
                d.update(qc=CTX + r * 64, r0=r0, ro0=r0 - r + 7, k0=CTX + r0 * 64, NK=768)
            else:
                d.update(qc=(-r - 1) * 64, NK=256)
            return d

        def stA(idx):
            g = geo(idx)
            hp, r, q = g["hp"], g["r"], g["q"]
            k.pe_drain()
            if idx == 0 or items[idx - 1][0] != hp:
                k.dma("sp", q[:, :], self.q_d[hp], reads=[k.tok(("q", l, hp, ci)) for ci in range(5)], writes=[q])
            for hh in range(2):
                pr = slice(hh * 64, hh * 64 + 64)
                if r >= 0:
                    k.op("pe", lambda e: e.matmul(g["pS"][pr, :], lhsT=q[pr, g["qc"]:g["qc"] + 64], rhs=kT[pr, hp, g["k0"]:g["k0"] + 512], start=True, stop=True),
                         reads=[q, kT], writes=[g["pS"]])
                k.op("pe", lambda e: e.matmul(g["pC"][pr, 0:256], lhsT=q[pr, g["qc"]:g["qc"] + 64], rhs=kT[pr, hp, 0:256], start=True, stop=True),
                     reads=[q, kT], writes=[g["pC"]])

        def stB(idx):
            g = geo(idx)
            hp, r, s_, p_, pb_, sm_, NK = g["hp"], g["r"], g["s"], g["p"], g["pb"], g["sm"], g["NK"]
            if r >= 0:
                ro0 = g["ro0"]
                k.op("dve", lambda e: e.scalar_tensor_tensor(out=s_[:, 0:512], in0=g["pS"][:, :], scalar=0.125,
                                                             in1=tab[:, hp, ro0:ro0 + 8, :].rearrange("p a b -> p (a b)"),
                                                             op0=ALU.mult, op1=ALU.add),
                     reads=[g["pS"], tab], writes=[s_])
                k.op("act", lambda e: e.activation(out=s_[:, 512:768], in_=g["pC"][:, 0:256], func=AF.Copy, scale=0.125),
                     reads=[g["pC"]], writes=[s_])
            else:
                k.op("act", lambda e: e.activation(out=s_[:, 0:256], in_=g["pC"][:, 0:256], func=AF.Copy, scale=0.125),
                     reads=[g["pC"]], writes=[s_])
            k.op("dve", lambda e: e.reduce_max(out=sm_[:, 0:1], in_=s_[:, :NK], axis=AX.X), reads=[s_], writes=[sm_])
            k.op("dve", lambda e: e.tensor_scalar(out=sm_[:, 1:2], in0=sm_[:, 0:1], scalar1=-1.0, scalar2=None, op0=ALU.mult),
                 reads=[sm_], writes=[sm_])
            k.op("act", lambda e: e.activation(out=p_[:, :NK], in_=s_[:, :NK], func=AF.Exp, bias=sm_[:, 1:2], scale=1.0,
                                               accum_out=sm_[:, 2:3]),
                 reads=[s_, sm_], writes=[p_, sm_])
            k.op("dve", lambda e: e.reciprocal(out=sm_[:, 3:4], in_=sm_[:, 2:3]), reads=[sm_], writes=[sm_])
            k.op("dve", lambda e: e.tensor_scalar(out=pb_[:, :NK], in0=p_[:, :NK], scalar1=sm_[:, 3:4], scalar2=None, op0=ALU.mult),
                 reads=[p_, sm_], writes=[pb_])

        def stC(idx):
            g = geo(idx)
            nj = g["NK"] // 128
            k.pe_drain()
            for j in range(nj):
                k.op("pe", lambda e: e.transpose(out=g["pT"][:, j * 128:(j + 1) * 128], in_=g["pb"][:, j * 128:(j + 1) * 128], identity=self.identb[:, :]),
                     reads=[g["pb"], self.identb], writes=[g["pT"]], inc=(j == nj - 1))

        def stD(idx):
            g = geo(idx)
            NK = g["NK"]
            k.op("act", lambda e: e.activation(out=g["pt"][:, :NK], in_=g["pT"][:, :NK], func=AF.Copy), reads=[g["pT"]], writes=[g["pt"]])

        def stE(idx):
            g = geo(idx)
            hp, r = g["hp"], g["r"]
            nj = g["NK"] // 128
            k.pe_drain()
            for hh in range(2):
                pr = slice(hh * 64, hh * 64 + 64)
                hc = (2 * hp + hh) * 64
                for j in range(nj):
                    if r >= 0 and j < 4:
                        sh = (g["r0"] % 2)
                        tt = (g["k0"] - 64 * sh) // 128 + j
                        vv = vA[:, sh, tt, hc:hc + 64]
                    else:
                        jj = j - 4 if r >= 0 else j
                        vv = vA[:, 0, jj, hc:hc + 64]
                    k.op("pe", lambda e: e.matmul(g["pO"][pr, 0:64], lhsT=vv, rhs=g["pt"][:, j * 128 + hh * 64:j * 128 + hh * 64 + 64],
                                                  start=(j == 0), stop=(j == nj - 1)),
                         reads=[vA, g["pt"]], writes=[g["pO"]], inc=(j == nj - 1))

        def stF(idx):
            g = geo(idx)
            hp, o = g["hp"], g["o"]
            k.op("dve", lambda e: e.tensor_copy(out=o[:, g["qc"]:g["qc"] + 64], in_=g["pO"][:, 0:64]), reads=[g["pO"]], writes=[o])
            if idx == NI - 1 or items[idx + 1][0] != hp:
                k.dma("sp", self.o_d[hp], o[:, :], reads=[o], writes=[k.tok(("o", l, hp))])

        for t in range(NI + 2):
            if t < NI:
                stA(t)
                stB(t)
            if 0 <= t - 1 < NI:
                stC(t - 1)
                stD(t - 1)
            if 0 <= t - 2 < NI:
                stE(t - 2)
                stF(t - 2)
        k.barrier()
        st.close()

    def phaseE(self, l, t_lo):
        k = self.k
        st = contextlib.ExitStack()
        self.alloc_ws(st)
        aT = k.sb([128, 8, NTOK], BF16, "aT", st)
        oT = k.sb([128, 8, NTOK], BF16, "oT", st)
        gsb = [k.sb([128, 512], BF16, "gsb%d" % i, st) for i in range(2)]
        gnb = [k.sb([128, 512], BF16, "gnb%d" % i, st) for i in range(2)]
        t1 = [k.sb([128, 512], F32, "t1_%d" % i, st) for i in range(2)]
        t2 = [k.sb([128, 512], F32, "t2_%d" % i, st) for i in range(2)]
        t3 = [k.sb([128, 512], F32, "t3_%d" % i, st) for i in range(2)]
        stg = [k.sb([128, 512], BF16, "stgE%d" % i, st) for i in range(3)]
        for h in range(8):
            k.dma("sp", aT[:, h, :], self.a_d[h], reads=[k.tok(("a", l, h))], writes=[aT])
            k.dma("sp", oT[:, h, :], self.o_d[h], reads=[k.tok(("o", l, h))], writes=[oT])
        it = 0
        chunks = [c for c in CHUNKS if c[0] >= t_lo]
        for mt in range(16):
            ws = self.nws()
            k.dma("pool", ws[:, 0:1024], self.wval_d[l, mt], writes=[ws])
            k.dma("pool", ws[:, 1024:2048], self.wglu_d[l, mt], writes=[ws])
            k.dma("pool", ws[:, 2048:3072], self.wna_d[l, mt], writes=[ws])
            for (t0, W) in chunks:
                ci = [c[0] for c in CHUNKS].index(t0)
                i2 = it % 2
                it += 1
                pss = []
                for wi, src in ((0, aT), (1, aT), (2, oT)):
                    ps = self.nps()
                    for kt in range(8):
                        k.op("pe", lambda e: e.matmul(ps[:, :W], lhsT=ws[:, wi * 1024 + kt * 128:wi * 1024 + (kt + 1) * 128], rhs=src[:, kt, t0:t0 + W],
                                                      start=(kt == 0), stop=(kt == 7)),
                             reads=[ws, src], writes=[ps], inc=(kt == 7))
                    pss.append(ps)
                g1, g2 = gsb[i2], gnb[i2]
                k.dma("sp", g1[:, :W], self.gs_d[mt, :, t0:t0 + W], reads=[k.tok(("gs", l, mt, ci))], writes=[g1])
                k.dma("sp", g2[:, :W], self.gn_d[mt, :, t0:t0 + W], reads=[k.tok(("gn", l, mt, ci))], writes=[g2])
                a1, a2, a3 = t1[i2], t2[i2], t3[i2]
                k.op("act", lambda e: e.activation(out=a1[:, :W], in_=pss[1][:, :W], func=AF.Sigmoid), reads=[pss[1]], writes=[a1])
                k.op("dve", lambda e: e.tensor_tensor(out=a2[:, :W], in0=pss[0][:, :W], in1=a1[:, :W], op=ALU.mult), reads=[pss[0], a1], writes=[a2])
                k.op("pool", lambda e: e.tensor_tensor(out=a2[:, :W], in0=a2[:, :W], in1=g1[:, :W], op=ALU.mult), reads=[a2, g1], writes=[a2])
                k.op("dve", lambda e: e.tensor_tensor(out=a3[:, :W], in0=pss[2][:, :W], in1=g2[:, :W], op=ALU.mult), reads=[pss[2], g2], writes=[a3])
                s = stg[it % 3]
                k.op("pool", lambda e: e.tensor_tensor(out=s[:, :W], in0=a2[:, :W], in1=a3[:, :W], op=ALU.add), reads=[a2, a3], writes=[s])
                k.dma("sp", self.m_d[mt, :, t0:t0 + W], s[:, :W], reads=[s], writes=[k.tok(("m", l, mt, ci))])
        k.barrier()
        st.close()

    def phaseF(self, l, t_lo, last):
        k = self.k
        st = contextlib.ExitStack()
        self.alloc_ws(st)
        x = k.sb([128, KT, 512], F32, "xF", st)
        ob = k.sb([128, KT, 512], F32, "oF", st)
        mh = k.sb([128, KT, 512], BF16, "mh", st)
        ab = k.sb([128, 64, 512], BF16, "ab", st)
        rstd = k.sb([128, 512], F32, "rstdF", st)
        tmps = [k.sb([128, 512], F32, "tmpF%d" % i, st) for i in range(2)]
        sq = Buf(ab.ap, "sqalias")
        xd = self.xsrc(l)
        chunks = [c for c in CHUNKS if c[0] >= t_lo]

        def wload(first, fill, src_ap, rr=None):
            ws = self.nws()
            wtok = k.tok(("wc", fill))
            if first:
                k.dma("pool", ws[:, :].rearrange("p (m x) -> p m x", m=rr) if rr else ws[:, :], src_ap, writes=[ws])
                k.dma("sp", self.wc_d[fill], ws[:, :], reads=[ws], writes=[wtok])
            else:
                k.dma("sp", ws[:, :], self.wc_d[fill], reads=[wtok], writes=[ws])
            return ws

        for cidx, (t0, W) in enumerate(chunks):
            first = (cidx == 0)
            ci = [c[0] for c in CHUNKS].index(t0)
            col = 1 if ci == 0 else 0
            k.dma("sp", x[:, :, :W], xd[:, :, t0:t0 + W].rearrange("k p t -> p k t"), reads=[k.tok(("x", l, ci))], writes=[x])
            k.dma("sp", mh[:, :, :W], self.m_d[:, :, t0:t0 + W].rearrange("k p t -> p k t"),
                  reads=[k.tok(("m", l, mt, ci)) for mt in range(16)], writes=[mh])
            for m in range(0, 16, 4):
                ws = wload(first, m // 4, self.wout_d[l, m:m + 4].rearrange("m p x -> p m x"), 4)
                for mm in range(4):
                    ps = self.nps()
                    for kt in range(KT):
                        k.op("pe", lambda e: e.matmul(ps[:, :W], lhsT=ws[:, mm * 2048 + kt * 128:mm * 2048 + (kt + 1) * 128], rhs=mh[:, kt, :W],
                                                      start=(kt == 0), stop=(kt == KT - 1)),
                             reads=[ws, mh], writes=[ps], inc=(kt == KT - 1))
                    k.op("act", lambda e: e.activation(out=ob[:, m + mm, :W], in_=ps[:, :W], func=AF.Copy), reads=[ps], writes=[ob])
            self.residual(x, ob, W, sq, ab, rstd, l, 0, col, tmps)
            self.rstd_of(x, W, Buf(ab.ap, "sq2"), rstd) if False else None
            self._rstd_alias(x, W, ab, rstd)
            self.modulate(x, W, rstd, l, 1, col, mh, 0, tmps)
            for m in range(0, 64, 4):
                ws = wload(first, 4 + m // 4, self.wfc1_d[l, m:m + 4].rearrange("m p x -> p m x"), 4)
                for mm in range(4):
                    ps = self.nps()
                    for kt in range(KT):
                        k.op("pe", lambda e: e.matmul(ps[:, :W], lhsT=ws[:, mm * 2048 + kt * 128:mm * 2048 + (kt + 1) * 128], rhs=mh[:, kt, :W],
                                                      start=(kt == 0), stop=(kt == KT - 1)),
                             reads=[ws, mh], writes=[ps], inc=(kt == KT - 1))
                    tm = tmps[mm % 2]
                    k.op("act", lambda e: e.activation(out=tm[:, :W], in_=ps[:, :W], func=AF.Relu), reads=[ps], writes=[tm])
                    k.op("pool", lambda e: e.tensor_tensor(out=ab[:, m + mm, :W], in0=tm[:, :W], in1=tm[:, :W], op=ALU.mult), reads=[tm], writes=[ab])
            for m in range(16):
                ws = wload(first, 20 + m, self.wfc2_d[l, m])
                ps = self.nps()
                for kt in range(64):
                    k.op("pe", lambda e: e.matmul(ps[:, :W], lhsT=ws[:, kt * 128:(kt + 1) * 128], rhs=ab[:, kt, :W], start=(kt == 0), stop=(kt == 63)),
                         reads=[ws, ab], writes=[ps], inc=(kt == 63))
                k.op("act", lambda e: e.activation(out=ob[:, m, :W], in_=ps[:, :W], func=AF.Copy), reads=[ps], writes=[ob])
            self.residual(x, ob, W, sq, ab, rstd, l, 1, col, tmps)
            if last:
                k.dma("sp", self.yT_d[:, :, t0 - CTX:t0 - CTX + W].rearrange("k p t -> p k t"), x[:, :, :W], reads=[x],
                      writes=[k.tok(("y", ci))], is_out=True)
            else:
                k.dma("sp", self.x1_d[:, :, t0:t0 + W].rearrange("k p t -> p k t"), x[:, :, :W], reads=[x], writes=[k.tok(("x", l + 1, ci))])
        k.barrier()
        st.close()

    def _rstd_alias(self, src, W, ab, rstd):
        k = self.k
        sqv = ab.ap[:, 0:KT, :]
        k.op("act", lambda e: e.activation(out=sqv[:, :, :W], in_=src[:, :, :W], func=AF.Square), reads=[src], writes=[ab])
        ps = self.nps()
        for kt in range(KT):
            k.op("pe", lambda e: e.matmul(ps[:, :W], lhsT=self.ones[:, :], rhs=sqv[:, kt, :W], start=(kt == 0), stop=(kt == KT - 1)),
                 reads=[self.ones, ab], writes=[ps], inc=(kt == KT - 1))
        k.op("act", lambda e: e.activation(out=rstd[:, :W], in_=ps[:, :W], func=AF.Sqrt, bias=self.epsb[:, 0:1], scale=1.0 / D),
             reads=[ps, self.epsb], writes=[rstd])
        k.op("dve", lambda e: e.reciprocal(out=rstd[:, :W], in_=rstd[:, :W]), reads=[rstd], writes=[rstd])

    def residual(self, x, ob, W, sq, ab, rstd, l, s, col, tmps):
        k = self.k
        self._rstd_alias(ob, W, ab, rstd)
        for kt in range(KT):
            tmp = tmps[kt % 2]
            k.op("dve", lambda e: e.scalar_tensor_tensor(out=tmp[:, :W], in0=ob[:, kt, :W], scalar=self.G[:, l, s, kt, col:col + 1],
                                                         in1=rstd[:, :W], op0=ALU.mult, op1=ALU.mult),
                 reads=[ob, rstd, self.G], writes=[tmp])
            k.op("pool", lambda e: e.tensor_tensor(out=x[:, kt, :W], in0=x[:, kt, :W], in1=tmp[:, :W], op=ALU.add), reads=[x, tmp], writes=[x])

    def build(self, upto=None):
        k = self.k
        self.nlayers_total = 2
        self.epsb = k.sb([128, 1], F32, "epsb")
        k.op("pool", lambda e: e.memset(self.epsb[:, :], EPS), writes=[self.epsb])
        self.hpib = k.sb([128, 1], F32, "hpib")
        k.op("pool", lambda e: e.memset(self.hpib[:, :], float(np.pi / 2)), writes=[self.hpib])
        self.phase0()
        for l in range(self.nlayers):
            self.phaseAB(l)
            if upto == "AB":
                break
            self.phaseC(l)
            if upto == "C":
                break
            self.phaseD(l)
            last = (l == 1)
            t_lo = CTX if last else 0
            self.phaseE(l, t_lo)
            self.phaseF(l, t_lo, last)
        k.finish()
        return k.nc


def _fm(v):
    return np.ascontiguousarray(np.asarray(v, np.float32).reshape(-1, 128).T)


def _panels(W):
    K, N = W.shape
    kt, mt = K // 128, N // 128
    return np.ascontiguousarray(W.reshape(kt, 128, mt, 128).transpose(2, 1, 0, 3).reshape(mt, 128, kt * 128))


def _rope_tables():
    nf = 16
    inv = (10000.0 ** (-np.arange(nf, dtype=np.float32) / nf)).astype(np.float32)
    pos = np.arange(LAT)
    rows = (pos // 64).astype(np.float32)
    cols = (pos % 64).astype(np.float32)
    cos = np.ones((64, NTOK), np.float32)
    sin = np.zeros((64, NTOK), np.float32)
    for d in range(64):
        blk = d // 16
        p = rows if blk < 2 else cols
        ang = (p * inv[d % 16]).astype(np.float32)
        cos[d, CTX:] = np.cos(ang)
        sg = -1.0 if blk % 2 == 0 else 1.0
        sin[d, CTX:] = sg * np.sin(ang)
    return np.ascontiguousarray(np.stack([np.concatenate([cos, cos], 0), np.concatenate([sin, sin], 0)], 0))


def prep_shared(inp):
    f = lambda a: np.asarray(a, np.float32)
    w_in = f(inp["w_in"])
    partner = np.array([d + 16 if (d % 32) < 16 else d - 16 for d in range(64)])
    perm = (np.arange(16)[:, None] * 64 + partner[None, :]).reshape(-1)
    wins, wvs = [], []
    for l in range(2):
        W = w_in[l]
        u, q, kk, v, gs, gn = W[:, :1024], W[:, 1024:2048], W[:, 2048:3072], W[:, 3072:4096], W[:, 4096:6144], W[:, 6144:]
        comb = np.concatenate([u, q, q[:, perm], kk, kk[:, perm], gs, gn], axis=1)
        wins.append(_panels(comb))
        wvs.append(np.ascontiguousarray(v.reshape(KT, 128, 2, 512).transpose(2, 1, 0, 3).reshape(2, 128, KT * 512)))
    sh = {
        "wmod": np.stack([_panels(f(inp["w_mod"])[l]) for l in range(2)]),
        "win": np.stack(wins),
        "wv": np.stack(wvs),
        "rope": _rope_tables(),
        "rpb": np.ascontiguousarray(f(inp["na_rpb"])),
        "wval": np.stack([_panels(f(inp["w_ssm_val"])[l]) for l in range(2)]),
        "wglu": np.stack([_panels(f(inp["w_ssm_glu"])[l]) for l in range(2)]),
        "wna": np.stack([_panels(f(inp["w_na_proj"])[l]) for l in range(2)]),
        "wout": np.stack([_panels(f(inp["w_out"])[l]) for l in range(2)]),
        "wfc1": np.stack([_panels(f(inp["w_fc1"])[l]) for l in range(2)]),
        "wfc2": np.stack([_panels(f(inp["w_fc2"])[l]) for l in range(2)]),
        "ident": np.eye(128, dtype=np.float32),
    }
    jj = np.arange(128) // 16
    cc = np.arange(128) % 16
    s5c = np.zeros((128, 400), np.float32)
    for kk in range(8):
        s5c[:, kk] = (jj == kk)
        s5c[:, 8 + kk] = (7 - jj == kk)
    s5c[:, 16:144] = (jj[:, None] <= jj[None, :])
    s5c[:, 144:272] = (jj[:, None] >= jj[None, :])
    s5c[:, 272:400] = np.eye(128)
    sel = np.zeros((128, 8, 8, 128), np.float32)
    selT = np.zeros((128, 8, 8, 128), np.float32)
    for gi in range(8):
        for j in range(8):
            for c in range(16):
                sel[gi * 16 + c, gi, j, j * 16 + c] = 1.0
                selT[j * 16 + c, gi, j, gi * 16 + c] = 1.0
    sh["s5c"] = s5c
    sh["sel"] = sel.reshape(128, 8192)
    sh["selT"] = selT.reshape(128, 8192)
    lre, lim, ldt = f(inp["ssm_lam_re"]), f(inp["ssm_lam_im"]), f(inp["ssm_log_dt"])
    bre, bim, cre, cim = f(inp["ssm_b_re"]), f(inp["ssm_b_im"]), f(inp["ssm_c_re"]), f(inp["ssm_c_im"])
    sC = np.empty((2, 2, 7, 128, 512), np.float32)
    for l in range(2):
        for d in range(2):
            def cl(A):
                t = A.reshape(2, 32, 64).transpose(0, 2, 1)
                return np.repeat(t.reshape(128, 32), 16, axis=1)
            sC[l, d, 0] = cl(lre[l, d])
            sC[l, d, 1] = cl(lim[l, d])
            sC[l, d, 2] = cl(np.repeat(ldt[l, d][:, None], 64, axis=1))
            for a, Cc in ((3, cre), (4, cim)):
                sC[l, d, a] = Cc[l, d].reshape(2, 32, 16, 64).transpose(0, 3, 1, 2).reshape(128, 512)
            for a, B in ((5, bre), (6, bim)):
                sC[l, d, a] = B[l, d].reshape(2, 32, 64, 16).transpose(0, 2, 1, 3).reshape(128, 512)
    sh["sC"] = sC
    dd = f(inp["ssm_d"])
    sh["drep"] = np.stack([np.tile(dd[l].reshape(64, 16).T, (8, 1)) for l in range(2)]).astype(np.float32)
    return sh


def prep_core(inp, b):
    f = lambda a: np.asarray(a, np.float32)
    X = np.concatenate([f(inp["ctx"])[b], f(inp["x"])[b]], axis=0)
    xT = np.ascontiguousarray(X.T.reshape(KT, 128, NTOK))
    cols = []
    for l in range(2):
        cols += [_fm(inp["g_pre_mix"][l]), _fm(inp["g_post_mix"][l]), _fm(inp["g_pre_mlp"][l]), _fm(inp["g_post_mlp"][l]),
                 _fm(inp["b_mod"][l]), _fm(inp["ssm_d"][l])]
    cols += [_fm(inp["c"][b]), _fm(inp["c_ctx"])]
    vecs = np.ascontiguousarray(np.concatenate(cols, axis=1))
    assert vecs.shape == (128, NVEC)
    return {"xT": xT, "vecs": vecs}


def _cmul(k, eng, o_re, o_im, a_re, a_im, b_re, b_im, t1, t2, rd=(), wr=()):
    E = k.op
    E(eng, lambda e: e.tensor_tensor(out=t1, in0=a_re, in1=b_re, op=ALU.mult), reads=rd, writes=wr)
    E(eng, lambda e: e.tensor_tensor(out=t2, in0=a_im, in1=b_im, op=ALU.mult), reads=rd, writes=wr)
    E(eng, lambda e: e.tensor_tensor(out=t2, in0=t1, in1=t2, op=ALU.subtract), reads=rd, writes=wr)
    E(eng, lambda e: e.tensor_tensor(out=t1, in0=a_re, in1=b_im, op=ALU.mult), reads=rd, writes=wr)
    E(eng, lambda e: e.tensor_tensor(out=o_im, in0=a_im, in1=b_re, op=ALU.mult), reads=rd, writes=wr)
    E(eng, lambda e: e.tensor_tensor(out=o_im, in0=o_im, in1=t1, op=ALU.add), reads=rd, writes=wr)
    E(eng, lambda e: e.tensor_copy(out=o_re, in_=t2), reads=rd, writes=wr)


def _lam_base(P, st, src, F, eng):
    k = P.k
    nb = lambda nm: k.sb([128, F], F32, nm, st)
    W = Buf(None, "Wtok")
    tok = [W, src]
    dt, ar, ai, c, s, m, t1, t2 = [nb(n) for n in ("dt", "ar", "ai", "c", "s", "m", "t1", "t2")]
    o = {n: nb(n) for n in ("lbr", "lbi", "lir", "lii", "gr", "gi")}
    E = lambda en, fn: k.op(en, fn, reads=tok + [P.hpib], writes=[W])
    E("act", lambda e: e.activation(out=dt[:, :], in_=src[:, 2, :], func=AF.Exp))
    E(eng, lambda e: e.tensor_tensor(out=ar[:, :], in0=src[:, 0, :], in1=dt[:, :], op=ALU.mult))
    E(eng, lambda e: e.tensor_tensor(out=ai[:, :], in0=src[:, 1, :], in1=dt[:, :], op=ALU.mult))
    E("act", lambda e: e.activation(out=s[:, :], in_=ai[:, :], func=AF.Sin, scale=1.0 / 16))
    E("act", lambda e: e.activation(out=c[:, :], in_=ai[:, :], func=AF.Sin, scale=1.0 / 16, bias=P.hpib[:, 0:1]))
    for _ in range(4):
        E(eng, lambda e: e.tensor_tensor(out=t1[:, :], in0=c[:, :], in1=s[:, :], op=ALU.mult))
        E(eng, lambda e: e.tensor_tensor(out=c[:, :], in0=c[:, :], in1=c[:, :], op=ALU.mult))
        E(eng, lambda e: e.tensor_tensor(out=s[:, :], in0=s[:, :], in1=s[:, :], op=ALU.mult))
        E(eng, lambda e: e.tensor_tensor(out=c[:, :], in0=c[:, :], in1=s[:, :], op=ALU.subtract))
        E(eng, lambda e: e.tensor_scalar(out=s[:, :], in0=t1[:, :], scalar1=2.0, scalar2=None, op0=ALU.mult))
    E("act", lambda e: e.activation(out=m[:, :], in_=ar[:, :], func=AF.Exp))
    E(eng, lambda e: e.tensor_tensor(out=o["lbr"][:, :], in0=m[:, :], in1=c[:, :], op=ALU.mult))
    E(eng, lambda e: e.tensor_tensor(out=o["lbi"][:, :], in0=m[:, :], in1=s[:, :], op=ALU.mult))
    E("act", lambda e: e.activation(out=m[:, :], in_=ar[:, :], func=AF.Exp, scale=-1.0))
    E(eng, lambda e: e.tensor_tensor(out=o["lir"][:, :], in0=m[:, :], in1=c[:, :], op=ALU.mult))
    E("dve", lambda e: e.scalar_tensor_tensor(out=o["lii"][:, :], in0=m[:, :], scalar=-1.0, in1=s[:, :], op0=ALU.mult, op1=ALU.mult))
    lr, li = src[:, 0, :], src[:, 1, :]
    E(eng, lambda e: e.tensor_tensor(out=t1[:, :], in0=lr, in1=lr, op=ALU.mult))
    E(eng, lambda e: e.tensor_tensor(out=t2[:, :], in0=li, in1=li, op=ALU.mult))
    E(eng, lambda e: e.tensor_tensor(out=t1[:, :], in0=t1[:, :], in1=t2[:, :], op=ALU.add))
    E("dve", lambda e: e.reciprocal(out=t1[:, :], in_=t1[:, :]))
    E(eng, lambda e: e.tensor_scalar(out=c[:, :], in0=o["lbr"][:, :], scalar1=-1.0, scalar2=None, op0=ALU.add))
    E(eng, lambda e: e.tensor_tensor(out=t2[:, :], in0=c[:, :], in1=lr, op=ALU.mult))
    E(eng, lambda e: e.tensor_tensor(out=s[:, :], in0=o["lbi"][:, :], in1=li, op=ALU.mult))
    E(eng, lambda e: e.tensor_tensor(out=t2[:, :], in0=t2[:, :], in1=s[:, :], op=ALU.add))
    E(eng, lambda e: e.tensor_tensor(out=o["gr"][:, :], in0=t2[:, :], in1=t1[:, :], op=ALU.mult))
    E(eng, lambda e: e.tensor_tensor(out=t2[:, :], in0=o["lbi"][:, :], in1=lr, op=ALU.mult))
    E(eng, lambda e: e.tensor_tensor(out=s[:, :], in0=c[:, :], in1=li, op=ALU.mult))
    E(eng, lambda e: e.tensor_tensor(out=t2[:, :], in0=t2[:, :], in1=s[:, :], op=ALU.subtract))
    E(eng, lambda e: e.tensor_tensor(out=o["gi"][:, :], in0=t2[:, :], in1=t1[:, :], op=ALU.mult))
    return o, W, (t1, t2, c, s, m, dt)


def phaseD(self, l):
    k = self.k
    st = contextlib.ExitStack()
    NG = 64
    R = k.sb([128, 17408], BF16, "Rraw", st)
    self._R = R
    BS = [Buf(R.ap[:, d * 8192:(d + 1) * 8192].rearrange("q (g m) -> q g m", m=128), "BS%d" % d) for d in range(2)]
    T = k.sb([128, NG, 128], BF16, "Tg", st)
    CS = [k.sb([128, 2, 32, 128], BF16, "CS%d" % d, st) for d in range(2)]
    MU = k.sb([128, 2, 2, 64], F32, "MU", st)
    cst = k.sb([128, 400], F32, "s5c", st)
    k.dma("sp", cst[:, :], self.s5c_d[:, :], writes=[cst])
    Drep = k.sb([128, NG], F32, "Drep", st)
    k.dma("sp", Drep[:, :], self.drep_d[l], writes=[Drep])
    F = NG * 64
    for d in range(2):
        st2 = contextlib.ExitStack()
        Fc = 512
        src = k.sb([128, 7, Fc], F32, "srcC", st2)
        for a in range(7):
            k.dma("sp", src[:, a, :], self.sC_d[l, d, a], writes=[src])
        LT = k.sb([128, 2, 32, 128], BF16, "LT", st2)
        LB = k.sb([128, 2, 32, 128], BF16, "LB", st2)
        o, W, (t1, t2, c, s, m, dt) = _lam_base(self, st2, src, Fc, "pool")
        tok = [W, src]
        E = lambda en, fn, wr=(): k.op(en, fn, reads=tok, writes=[W] + list(wr))
        bbr, bbi = k.sb([128, Fc], F32, "bbr", st2), k.sb([128, Fc], F32, "bbi", st2)
        _cmul(k, "pool", bbr[:, :], bbi[:, :], o["gr"][:, :], o["gi"][:, :], src[:, 5, :], src[:, 6, :], t1[:, :], t2[:, :], rd=tok, wr=[W])
        pr, pi = o["gr"], o["gi"]
        qr, qi = k.sb([128, Fc], F32, "qr", st2), k.sb([128, Fc], F32, "qi", st2)
        E("pool", lambda e: e.tensor_copy(out=pr[:, :], in_=o["lbr"][:, :]))
        E("pool", lambda e: e.tensor_copy(out=pi[:, :], in_=o["lbi"][:, :]))
        E("pool", lambda e: e.tensor_copy(out=qr[:, :], in_=o["lir"][:, :]))
        E("pool", lambda e: e.tensor_copy(out=qi[:, :], in_=o["lii"][:, :]))
        cs5 = CS[d].ap.rearrange("q r g (j c) -> q r g j c", c=16)
        lt5 = LT.ap.rearrange("q r g (j c) -> q r g j c", c=16)
        lb5 = LB.ap.rearrange("q r g (j c) -> q r g j c", c=16)
        g3 = lambda ap: ap.rearrange("q (g c) -> q g c", c=16)
        jb0 = 7 if d == 0 else 0
        E("act", lambda e: e.activation(out=lb5[:, 0, :, jb0, :], in_=g3(bbr[:, :]), func=AF.Copy), wr=[LB])
        E("act", lambda e: e.activation(out=lb5[:, 1, :, jb0, :], in_=g3(bbi[:, :]), func=AF.Copy), wr=[LB])
        for kk in range(1, 9):
            if kk > 1:
                _cmul(k, "pool", pr[:, :], pi[:, :], pr[:, :], pi[:, :], o["lbr"][:, :], o["lbi"][:, :], t1[:, :], t2[:, :], rd=tok, wr=[W])
                _cmul(k, "pool", qr[:, :], qi[:, :], qr[:, :], qi[:, :], o["lir"][:, :], o["lii"][:, :], t1[:, :], t2[:, :], rd=tok, wr=[W])
            jj = kk - 1 if d == 0 else 8 - kk
            _cmul(k, "pool", c[:, :], s[:, :], src[:, 3, :], src[:, 4, :], pr[:, :], pi[:, :], t1[:, :], t2[:, :], rd=tok, wr=[W])
            E("act", lambda e: e.activation(out=cs5[:, 0, :, jj, :], in_=g3(c[:, :]), func=AF.Copy), wr=[CS[d]])
            E("act", lambda e: e.activation(out=cs5[:, 1, :, jj, :], in_=g3(s[:, :]), func=AF.Copy, scale=-1.0), wr=[CS[d]])
            _cmul(k, "pool", c[:, :], s[:, :], bbr[:, :], bbi[:, :], qr[:, :], qi[:, :], t1[:, :], t2[:, :], rd=tok, wr=[W])
            E("act", lambda e: e.activation(out=lt5[:, 0, :, jj, :], in_=g3(c[:, :]), func=AF.Copy), wr=[LT])
            E("act", lambda e: e.activation(out=lt5[:, 1, :, jj, :], in_=g3(s[:, :]), func=AF.Copy), wr=[LT])
            if kk <= 7:
                jb = 7 - kk if d == 0 else kk
                _cmul(k, "pool", c[:, :], s[:, :], bbr[:, :], bbi[:, :], pr[:, :], pi[:, :], t1[:, :], t2[:, :], rd=tok, wr=[W])
                E("act", lambda e: e.activation(out=lb5[:, 0, :, jb, :], in_=g3(c[:, :]), func=AF.Copy), wr=[LB])
                E("act", lambda e: e.activation(out=lb5[:, 1, :, jb, :], in_=g3(s[:, :]), func=AF.Copy), wr=[LB])
        prg = g3(pr[:, :])[:, :, 0]
        pig = g3(pi[:, :])[:, :, 0]
        gsl = slice(d * 32, d * 32 + 32)
        E("pool", lambda e: e.tensor_copy(out=MU[:, 0, 0, gsl], in_=prg), wr=[MU])
        E("pool", lambda e: e.tensor_copy(out=MU[:, 0, 1, gsl], in_=prg), wr=[MU])
        E("pool", lambda e: e.tensor_scalar(out=MU[:, 1, 0, gsl], in0=pig, scalar1=-1.0, scalar2=None, op0=ALU.mult), wr=[MU])
        E("pool", lambda e: e.tensor_copy(out=MU[:, 1, 1, gsl], in_=pig), wr=[MU])
        for g in range(64):
            hb, gl = (g // 32) * 64, g % 32
            pT = self.psb[g % 2]
            for ri in range(2):
                k.op("pe", lambda e: e.transpose(out=pT[:, ri * 64:(ri + 1) * 64], in_=LB[hb:hb + 64, ri, gl, :], identity=self.identb[hb:hb + 64, hb:hb + 64]),
                     reads=[LB, self.identb], writes=[pT], inc=(ri == 1))
            if g % 2:
                k.op("act", lambda e: e.activation(out=BS[d][:, g, :], in_=pT[:, 0:128], func=AF.Copy), reads=[pT], writes=[BS[d]])
            else:
                k.op("dve", lambda e: e.tensor_copy(out=BS[d][:, g, :], in_=pT[:, 0:128]), reads=[pT], writes=[BS[d]])
        tf = [k.sb([128, 128], F32, "tfT%d" % i, st2) for i in range(2)]
        mask = cst[:, 16 + 128 * d:16 + 128 * d + 128]
        ident = cst[:, 272:400]
        for g in range(64):
            hb, gl = (g // 32) * 64, g % 32
            ps = self.nps()
            for ri in range(2):
                k.op("pe", lambda e: e.matmul(ps[:, 0:128], lhsT=LT[hb:hb + 64, ri, gl, :], rhs=CS[d][hb:hb + 64, ri, gl, :],
                                              start=(ri == 0), stop=(ri == 1)),
                     reads=[LT, CS[d]], writes=[ps], inc=(ri == 1))
            t = tf[g % 2]
            k.op("dve", lambda e: e.tensor_tensor(out=t[:, :], in0=ps[:, 0:128], in1=mask, op=ALU.mult), reads=[ps, cst], writes=[t])
            if d == 0:
                k.op("dve", lambda e: e.scalar_tensor_tensor(out=T[:, g, :], in0=ident, scalar=Drep[:, g:g + 1], in1=t[:, :],
                                                              op0=ALU.mult, op1=ALU.add), reads=[t, cst, Drep], writes=[T])
            else:
                k.op("pool", lambda e: e.tensor_tensor(out=T[:, g, :], in0=T[:, g, :], in1=t[:, :], op=ALU.add), reads=[t, T], writes=[T])
        k.barrier()
        st2.close()
    self._s5_run(l, BS, T, CS, MU, st)
    k.barrier()
    st.close()


def _s5_run(self, l, BS, T, CS, MU, st):
    k = self.k
    NG = 64
    Xall = k.sb([128, NG, NCH8], BF16, "Xall", st)
    st1 = contextlib.ExitStack()
    uT = k.sb([128, 8, NTOK], BF16, "uT", st1)
    Sel = k.sb([128, 8, 8, 128], BF16, "Sel", st1)
    for h in range(8):
        k.dma("sp", uT[:, h, :], self.u_d[h], reads=[k.tok(("u", l, h, ci)) for ci in range(5)], writes=[uT])
    k.dma("pool", Sel[:, :, :, :].rearrange("q a b m -> q (a b m)"), self.sel_d[:, :], writes=[Sel])
    for g in range(NG):
        ps = self.nps()
        for j in range(8):
            k.op("pe", lambda e: e.matmul(ps[:, 0:NCH8], lhsT=Sel[:, g % 8, j, :], rhs=uT[:, g // 8, j:NTOK:8], start=(j == 0), stop=(j == 7)),
                 reads=[Sel, uT], writes=[ps], inc=(j == 7))
        k.op("act" if g % 2 else "dve", (lambda e: e.activation(out=Xall[:, g, :], in_=ps[:, 0:NCH8], func=AF.Copy)) if g % 2 else
             (lambda e: e.tensor_copy(out=Xall[:, g, :], in_=ps[:, 0:NCH8])), reads=[ps], writes=[Xall])
    k.barrier()
    st1.close()
    S = [k.sb([128, 2, 32, NCH8], BF16, "S%d" % d, st) for d in range(2)]
    Srd = [Buf(S[d].ap, "Srd%d" % d) for d in range(2)]
    Swr = [Buf(S[d].ap, "Swr%d" % d) for d in range(2)]
    for d in range(2):
        for g in range(NG):
            hb, gl = (g // 32) * 64, g % 32
            ps = self.nps()
            ps2 = self.nps()
            for ri in range(2):
                k.op("pe", lambda e: e.matmul(ps[hb:hb + 64, ri * 256:(ri + 1) * 256], lhsT=BS[d][:, g, ri * 64:(ri + 1) * 64], rhs=Xall[:, g, 0:256],
                                              start=True, stop=True),
                     reads=[BS[d], Xall], writes=[ps], inc=(ri == 1))
            for ri in range(2):
                k.op("pe", lambda e: e.matmul(ps2[hb:hb + 64, ri * 32:(ri + 1) * 32], lhsT=BS[d][:, g, ri * 64:(ri + 1) * 64], rhs=Xall[:, g, 256:NCH8],
                                              start=True, stop=True),
                     reads=[BS[d], Xall], writes=[ps2], inc=(ri == 1))
            if g % 2:
                k.op("act", lambda e: e.activation(out=S[d][hb:hb + 64, :, gl, 0:256], in_=ps[hb:hb + 64, 0:512].rearrange("q (r n) -> q r n", r=2), func=AF.Copy),
                     reads=[ps], writes=[Srd[d]])
                k.op("act", lambda e: e.activation(out=S[d][hb:hb + 64, :, gl, 256:NCH8], in_=ps2[hb:hb + 64, 0:64].rearrange("q (r n) -> q r n", r=2), func=AF.Copy),
                     reads=[ps2], writes=[Srd[d]])
            else:
                k.op("dve", lambda e: e.tensor_copy(out=S[d][hb:hb + 64, :, gl, 0:256], in_=ps[hb:hb + 64, 0:512].rearrange("q (r n) -> q r n", r=2)),
                     reads=[ps], writes=[Srd[d]])
                k.op("dve", lambda e: e.tensor_copy(out=S[d][hb:hb + 64, :, gl, 256:NCH8], in_=ps2[hb:hb + 64, 0:64].rearrange("q (r n) -> q r n", r=2)),
                     reads=[ps2], writes=[Srd[d]])
    H = [k.sb([128, 2, 64], F32, "H%d" % i, st) for i in range(2)]
    t1 = k.sb([128, 2, 64], F32, "rt1", st)
    t2 = k.sb([128, 2, 64], F32, "rt2", st)
    k.op("pool", lambda e: e.memset(H[0][:, :, :], 0.0), writes=[H[0]])
    for i in range(NCH8):
        nf = i
        nb = (31 - i) if i < 32 else (319 - i)
        Ho, Hn = H[i % 2], H[(i + 1) % 2]
        k.op("dve", lambda e: e.tensor_tensor(out=t1[:, :, :], in0=MU[:, 0, :, :], in1=Ho[:, :, :], op=ALU.mult), reads=[MU, Ho], writes=[t1])
        k.op("pool", lambda e: e.tensor_tensor(out=t2[:, 0, :], in0=MU[:, 1, 0, :], in1=Ho[:, 1, :], op=ALU.mult), reads=[MU, Ho], writes=[t2])
        k.op("pool", lambda e: e.tensor_tensor(out=t2[:, 1, :], in0=MU[:, 1, 1, :], in1=Ho[:, 0, :], op=ALU.mult), reads=[MU, Ho], writes=[t2])
        k.op("dve", lambda e: e.tensor_tensor(out=t1[:, :, :], in0=t1[:, :, :], in1=t2[:, :, :], op=ALU.add), reads=[t1, t2], writes=[t1])
        stk = Buf(None, "stk")
        k.op("dve", lambda e: e.tensor_tensor(out=Hn[:, :, 0:32], in0=t1[:, :, 0:32], in1=S[0][:, :, :, nf], op=ALU.add), reads=[t1, Srd[0]], writes=[Hn, stk])
        k.op("pool", lambda e: e.tensor_tensor(out=Hn[:, :, 32:64], in0=t1[:, :, 32:64], in1=S[1][:, :, :, nb], op=ALU.add), reads=[t1, Srd[1]], writes=[Hn, stk])
        k.op("act", lambda e: e.activation(out=S[0][:, :, :, nf], in_=Ho[:, :, 0:32], func=AF.Copy), reads=[Ho, stk], writes=[Swr[0]])
        k.op("act", lambda e: e.activation(out=S[1][:, :, :, nb], in_=Ho[:, :, 32:64], func=AF.Copy), reads=[Ho, stk], writes=[Swr[1]])
    k.barrier()
    R = self._R
    SelT = Buf(R.ap[:, 0:8192].rearrange("q (a b m) -> q a b m", a=8, b=8), "SelT")
    k.dma("pool", R.ap[:, 0:8192], self.selT_d[:, :], writes=[SelT])
    Ag = [Buf(R.ap[:, 8192 + i * 2304:8192 + (i + 1) * 2304].rearrange("q (a n) -> q a n", a=8), "Ag%d" % i) for i in range(2)]
    ast = [Buf(R.ap[:, 12800 + i * 2304:12800 + (i + 1) * 2304], "ast%d" % i) for i in range(2)]
    ga = [k.sb([128, NCH8], F32, "ga%d" % i, st) for i in range(2)]
    gb = [k.sb([128, NCH8], F32, "gb%d" % i, st) for i in range(2)]
    for tl in range(8):
        A = Ag[tl % 2]
        for gi in range(8):
            g = tl * 8 + gi
            hb, gl = (g // 32) * 64, g % 32
            ps = self.nps()
            k.op("pe", lambda e: e.matmul(ps[:, 0:NCH8], lhsT=T[:, g, :], rhs=Xall[:, g, :], start=True, stop=False), reads=[T, Xall], writes=[ps], inc=False)
            for d in range(2):
                for ri in range(2):
                    last = (d == 1 and ri == 1)
                    k.op("pe", lambda e: e.matmul(ps[:, 0:NCH8], lhsT=CS[d][hb:hb + 64, ri, gl, :], rhs=S[d][hb:hb + 64, ri, gl, :], start=False, stop=last),
                         reads=[CS[d], Swr[d]], writes=[ps], inc=last)
            a_, b_ = ga[gi % 2], gb[gi % 2]
            k.op("act", lambda e: e.activation(out=a_[:, :], in_=ps[:, 0:NCH8], func=AF.Square), reads=[ps], writes=[a_])
            k.op("dve", lambda e: e.tensor_scalar(out=a_[:, :], in0=a_[:, :], scalar1=0.044715, scalar2=1.0, op0=ALU.mult, op1=ALU.add), reads=[a_], writes=[a_])
            k.op("dve", lambda e: e.tensor_tensor(out=b_[:, :], in0=ps[:, 0:NCH8], in1=a_[:, :], op=ALU.mult), reads=[ps, a_], writes=[b_])
            k.op("act", lambda e: e.activation(out=b_[:, :], in_=b_[:, :], func=AF.Sigmoid, scale=1.5957691216057308), reads=[b_], writes=[b_])
            k.op("dve", lambda e: e.tensor_tensor(out=A[:, gi, :], in0=ps[:, 0:NCH8], in1=b_[:, :], op=ALU.mult), reads=[ps, b_], writes=[A])
        o = ast[tl % 2]
        for j in range(8):
            ps = self.nps()
            for gi in range(8):
                k.op("pe", lambda e: e.matmul(ps[:, 0:NCH8], lhsT=SelT[:, gi, j, :], rhs=A[:, gi, :], start=(gi == 0), stop=(gi == 7)),
                     reads=[SelT, A], writes=[ps], inc=(gi == 7))
            k.op("act" if j % 2 else "dve", (lambda e: e.activation(out=o[:, j:NTOK:8], in_=ps[:, 0:NCH8], func=AF.Copy)) if j % 2 else
                 (lambda e: e.tensor_copy(out=o[:, j:NTOK:8], in_=ps[:, 0:NCH8])), reads=[ps], writes=[o])
        k.dma("sp", self.a_d[tl], o[:, :], reads=[o], writes=[k.tok(("a", l, tl))])


def _psl(self, ps, hb, ri):
    return ps[hb:hb + 64, ri * 256:(ri + 1) * 256]


Prog.phaseD = phaseD
Prog._s5_run = _s5_run
Prog._psl = _psl


_CACHE = {}


def kernel(**inputs):
    if "nc" not in _CACHE:
        _CACHE["nc"] = Prog().build()
    nc = _CACHE["nc"]
    sh = prep_shared(inputs)
    in_maps = []
    for core in range(8):
        m = dict(sh)
        m.update(prep_core(inputs, core % 4))
        in_maps.append(m)
    res = run_bass_kernel_spmd(nc, in_maps, core_ids=list(range(8)))
    out = np.empty((4, LAT, D), np.float32)
    for b in range(4):
        yT = np.asarray(res.results[b]["yT"]).reshape(D, LAT)
        out[b] = yT.T
    return out
```

```python
import contextlib
import math
import numpy as np
import concourse.bass as bass
import concourse.mybir as mybir
from concourse.bass_utils import run_bass_kernel_spmd

F32 = mybir.dt.float32
BF16 = mybir.dt.bfloat16
AF = mybir.ActivationFunctionType
ALU = mybir.AluOpType
AX = mybir.AxisListType

D = 2048
KT = 16
NTOK = 2304
CTX = 256
LAT = 2048
DFF = 8192
EPS = 1e-6
CHUNKS = [(0, 256), (256, 512), (768, 512), (1280, 512), (1792, 512)]
NEG = -30000.0
NCH8 = NTOK // 8
VEC_L = 168
C_OFF = 336
NVEC = 368


class Buf:
    __slots__ = ("ap", "w", "r", "name")

    def __init__(self, ap=None, name=""):
        self.ap = ap
        self.w = []
        self.r = []
        self.name = name

    def __getitem__(self, idx):
        return self.ap[idx]


class KB:
    NTICK = 40

    def __init__(self):
        self.nc = bass.Bass("TRN2", target_bir_lowering=False)
        nc = self.nc
        self.es = contextlib.ExitStack()
        self.eng = {}
        self.semid = {}
        for name, e in (("pe", nc.tensor), ("act", nc.scalar), ("dve", nc.vector),
                        ("pool", nc.gpsimd), ("sp", nc.sync)):
            sem = self.es.enter_context(nc.semaphore("s_" + name))
            self.eng[name] = [e, sem, 0]
        self.known = {n: {} for n in self.eng}
        self.ticks = []
        for i in range(self.NTICK):
            sem = self.es.enter_context(nc.semaphore("tk%d" % i))
            self.ticks.append([sem, 0, "tk%d" % i])
        self.tki = {"sp": 0, "pool": 0}
        self.tkr = {"sp": (0, 26), "pool": (26, 14)}
        self.out_evs = []
        self.phase_stack = None
        self.dbufs = {}
        self.uid = 0

    def sb(self, shape, dt, name=None, stack=None):
        self.uid += 1
        nm = (name or "t") + "_%d" % self.uid
        t = (stack or self.es).enter_context(self.nc.sbuf_tensor(nm, list(shape), dt))
        return Buf(t, nm)

    def pst(self, shape, dt, name=None, stack=None):
        self.uid += 1
        nm = (name or "p") + "_%d" % self.uid
        t = (stack or self.es).enter_context(self.nc.psum_tensor(nm, list(shape), dt))
        return Buf(t, nm)

    def dram(self, name, shape, dt, kind="Internal"):
        return self.nc.dram_tensor(name, list(shape), dt, kind=kind).ap()

    def tok(self, key):
        b = self.dbufs.get(key)
        if b is None:
            b = Buf(None, str(key))
            self.dbufs[key] = b
        return b

    def _wait(self, en, evs):
        e = self.eng[en][0]
        kn = self.known[en]
        best = {}
        for (key, sem, val) in evs:
            if val > kn.get(key, 0) and val > best.get(key, (None, 0))[1]:
                best[key] = (sem, val)
        for key, (sem, val) in best.items():
            e.wait_ge(sem, val)
            kn[key] = val

    def _deps(self, en, reads, writes):
        evs = []
        for b in reads:
            evs += b.w
        for b in writes:
            evs += b.w
            evs += b.r
        if en == "pe":
            evs = [x for x in evs if x[0] != "pe"]
        return evs

    def _record(self, ev, reads, writes):
        for b in reads:
            b.r = [x for x in b.r if x[0] != ev[0]]
            b.r.append(ev)
        for b in writes:
            b.w = [ev]
            b.r = []

    def op(self, en, fn, reads=(), writes=(), inc=True):
        self._wait(en, self._deps(en, reads, writes))
        ent = self.eng[en]
        ins = fn(ent[0])
        if inc:
            ent[2] += 1
            ins.then_inc(ent[1], 1)
            ev = (en, ent[1], ent[2])
            self._record(ev, reads, writes)
        return ins

    def dma(self, q, out_ap, in_ap, reads=(), writes=(), is_out=False):
        base, cnt = self.tkr[q]
        tk = self.ticks[base + self.tki[q]]
        self.tki[q] = (self.tki[q] + 1) % cnt
        evs = self._deps(q, reads, writes)
        if tk[1] > 0:
            evs.append((tk[2], tk[0], tk[1]))
        self._wait(q, evs)
        ins = self.eng[q][0].dma_start(out=out_ap, in_=in_ap)
        tk[1] += 16
        ins.then_inc(tk[0], 16)
        ev = (tk[2], tk[0], tk[1])
        self._record(ev, reads, writes)
        if is_out:
            self.out_evs.append(ev)
        return ev

    def pe_drain(self):
        ent = self.eng["pe"]
        if ent[2] > self.known["pe"].get("pe", 0):
            ent[0].wait_ge(ent[1], ent[2])
            self.known["pe"]["pe"] = ent[2]

    def barrier(self):
        evs = []
        for n, ent in self.eng.items():
            if ent[2] > 0:
                evs.append((n, ent[1], ent[2]))
        for tk in self.ticks:
            if tk[1] > 0:
                evs.append((tk[2], tk[0], tk[1]))
        for n in self.eng:
            self._wait(n, [x for x in evs if x[0] != n])

    def finish(self):
        self.barrier()
        self.es.close()


class Prog:
    def __init__(self, dbg=None, nlayers=2):
        self.k = KB()
        self.dbg = dbg or {}
        self.nlayers = nlayers
        k = self.k
        self.xT_d = k.dram("xT", [KT, 128, NTOK], F32, "ExternalInput")
        self.vecs_d = k.dram("vecs", [128, NVEC], F32, "ExternalInput")
        self.wmod_d = k.dram("wmod", [2, 96, 128, KT * 128], F32, "ExternalInput")
        self.win_d = k.dram("win", [2, 72, 128, KT * 128], F32, "ExternalInput")
        self.wv_d = k.dram("wv", [2, 2, 128, KT * 512], F32, "ExternalInput")
        self.rope_d = k.dram("rope", [2, 128, NTOK], F32, "ExternalInput")
        self.rpb_d = k.dram("rpb", [2, 16, 15, 31], F32, "ExternalInput")
        self.wval_d = k.dram("wval", [2, 16, 128, 8 * 128], F32, "ExternalInput")
        self.wglu_d = k.dram("wglu", [2, 16, 128, 8 * 128], F32, "ExternalInput")
        self.wna_d = k.dram("wna", [2, 16, 128, 8 * 128], F32, "ExternalInput")
        self.wout_d = k.dram("wout", [2, 16, 128, KT * 128], F32, "ExternalInput")
        self.wfc1_d = k.dram("wfc1", [2, 64, 128, KT * 128], F32, "ExternalInput")
        self.wfc2_d = k.dram("wfc2", [2, 16, 128, 64 * 128], F32, "ExternalInput")
        self.s5c_d = k.dram("s5c", [128, 400], F32, "ExternalInput")
        self.drep_d = k.dram("drep", [2, 128, 64], F32, "ExternalInput")
        self.sC_d = k.dram("sC", [2, 2, 7, 128, 512], F32, "ExternalInput")
        self.sel_d = k.dram("sel", [128, 8192], F32, "ExternalInput")
        self.selT_d = k.dram("selT", [128, 8192], F32, "ExternalInput")
        self.yT_d = k.dram("yT", [KT, 128, LAT], F32, "ExternalOutput")
        self.x1_d = k.dram("x1s", [KT, 128, NTOK], F32)
        self.u_d = k.dram("us", [8, 128, NTOK], BF16)
        self.q_d = k.dram("qs", [8, 128, NTOK], BF16)
        self.k_d = k.dram("ks", [8, 128, NTOK], BF16)
        self.v_d = k.dram("vs", [2, 18, 128, 1024], BF16)
        self.gs_d = k.dram("gss", [16, 128, NTOK], BF16)
        self.gn_d = k.dram("gns", [16, 128, NTOK], BF16)
        self.o_d = k.dram("os", [8, 128, NTOK], BF16)
        self.a_d = k.dram("as", [8, 128, NTOK], BF16)
        self.m_d = k.dram("ms", [16, 128, NTOK], BF16)
        self.wc_d = k.dram("wcache", [36, 128, 8192], BF16)
        self.vecs = k.sb([128, NVEC], F32, "vecs")
        self.sT = k.sb([128, KT, 2], F32, "sT")
        self.modT = k.sb([128, 2, 96, 2], F32, "modT")
        self.A = k.sb([128, 2, 2, KT, 2], F32, "Amod")
        self.G = k.sb([128, 2, 2, KT, 2], F32, "Gmod")
        self.ones = k.sb([128, 128], BF16, "ones")
        self.identb = k.sb([128, 128], BF16, "identb")
        self.ident_d = k.dram("ident", [128, 128], F32, "ExternalInput")
        self.ps = [k.pst([128, 512], F32, "ps%d" % i) for i in range(6)]
        self.psb = [k.pst([128, 1024], BF16, "psb%d" % i) for i in range(2)]
        self.psi = 0
        self.wsl = None
        self.wsi = 0

    def nps(self):
        b = self.ps[self.psi % 6]
        self.psi += 1
        return b

    def alloc_ws(self, st):
        self.wsl = [self.k.sb([128, 8192], BF16, "wsl%d" % i, st) for i in range(3)]

    def nws(self):
        b = self.wsl[self.wsi % 3]
        self.wsi += 1
        return b

    def phase0(self):
        k = self.k
        k.dma("sp", self.vecs[:, :], self.vecs_d[:, :], writes=[self.vecs])
        idf = k.sb([128, 128], F32, "idf")
        k.dma("sp", idf[:, :], self.ident_d[:, :], writes=[idf])
        k.op("dve", lambda e: e.tensor_copy(out=self.identb[:, :], in_=idf[:, :]), reads=[idf], writes=[self.identb])
        k.op("pool", lambda e: e.memset(self.ones[:, :], 1.0), writes=[self.ones])
        for col in range(2):
            k.op("act", lambda e: e.activation(out=self.sT[:, :, col], in_=self.vecs[:, C_OFF + 16 * col:C_OFF + 16 * col + 16],
                                               func=AF.Silu), reads=[self.vecs], writes=[self.sT])
        st = contextlib.ExitStack()
        wm = [k.sb([128, KT * 128], F32, "wm%d" % i, st) for i in range(3)]
        for l in range(self.nlayers):
            for mt in range(96):
                slot = wm[mt % 3]
                k.dma("sp", slot[:, :], self.wmod_d[l, mt], writes=[slot])
                ps = self.nps()
                for kt in range(KT):
                    k.op("pe", lambda e: e.matmul(ps[:, 0:2], lhsT=slot[:, kt * 128:(kt + 1) * 128], rhs=self.sT[:, kt, :],
                                                  start=(kt == 0), stop=(kt == KT - 1)),
                         reads=[slot, self.sT], writes=[ps], inc=(kt == KT - 1))
                boff = l * VEC_L + 64 + mt
                k.op("dve", lambda e: e.tensor_scalar(out=self.modT[:, l, mt, :], in0=ps[:, 0:2],
                                                      scalar1=self.vecs[:, boff:boff + 1], scalar2=None, op0=ALU.add),
                     reads=[ps, self.vecs], writes=[self.modT])
            for s in range(2):
                sc0 = 16 + 48 * s
                gt0 = 32 + 48 * s
                gpre = l * VEC_L + (0 if s == 0 else 32)
                gpost = l * VEC_L + (16 if s == 0 else 48)
                for col in range(2):
                    k.op("dve", lambda e: e.scalar_tensor_tensor(out=self.A[:, l, s, :, col], in0=self.modT[:, l, sc0:sc0 + 16, col],
                                                                 scalar=1.0, in1=self.vecs[:, gpre:gpre + 16],
                                                                 op0=ALU.add, op1=ALU.mult),
                         reads=[self.modT, self.vecs], writes=[self.A])
                    k.op("dve", lambda e: e.tensor_tensor(out=self.G[:, l, s, :, col], in0=self.modT[:, l, gt0:gt0 + 16, col],
                                                          in1=self.vecs[:, gpost:gpost + 16], op=ALU.mult),
                         reads=[self.modT, self.vecs], writes=[self.G])
        k.barrier()
        st.close()

    def rstd_of(self, src, W, sq, rstd):
        k = self.k
        k.op("act", lambda e: e.activation(out=sq[:, :, :W], in_=src[:, :, :W], func=AF.Square), reads=[src], writes=[sq])
        ps = self.nps()
        for kt in range(KT):
            k.op("pe", lambda e: e.matmul(ps[:, :W], lhsT=self.ones[:, :], rhs=sq[:, kt, :W], start=(kt == 0), stop=(kt == KT - 1)),
                 reads=[self.ones, sq], writes=[ps], inc=(kt == KT - 1))
        k.op("act", lambda e: e.activation(out=rstd[:, :W], in_=ps[:, :W], func=AF.Sqrt, bias=self.epsb[:, 0:1], scale=1.0 / D),
             reads=[ps, self.epsb], writes=[rstd])
        k.op("dve", lambda e: e.reciprocal(out=rstd[:, :W], in_=rstd[:, :W]), reads=[rstd], writes=[rstd])

    def modulate(self, src, W, rstd, l, s, col, dst, dst_t0, tmps):
        k = self.k
        sh0 = 0 if s == 0 else 48
        for kt in range(KT):
            tmp = tmps[kt % 2]
            k.op("dve", lambda e: e.scalar_tensor_tensor(out=tmp[:, :W], in0=src[:, kt, :W], scalar=self.A[:, l, s, kt, col:col + 1],
                                                         in1=rstd[:, :W], op0=ALU.mult, op1=ALU.mult),
                 reads=[src, rstd, self.A], writes=[tmp])
            k.op("act", lambda e: e.activation(out=dst[:, kt, dst_t0:dst_t0 + W], in_=tmp[:, :W], func=AF.Identity,
                                               bias=self.modT[:, l, sh0 + kt, col:col + 1], scale=1.0),
                 reads=[tmp, self.modT], writes=[dst])

    def xsrc(self, l):
        return self.xT_d if l == 0 else self.x1_d

    def phaseAB(self, l):
        k = self.k
        st = contextlib.ExitStack()
        self.alloc_ws(st)
        hT = k.sb([128, KT, NTOK], BF16, "hT", st)
        xs = [k.sb([128, KT, 512], F32, "xs%d" % i, st) for i in range(1)]
        sq = k.sb([128, KT, 512], BF16, "sq", st)
        rstd = k.sb([128, 512], F32, "rstd", st)
        tmps = [k.sb([128, 512], F32, "tmp%d" % i, st) for i in range(4)]
        stg = [k.sb([128, 512], BF16, "stg%d" % i, st) for i in range(4)]
        rope = k.sb([128, 2, NTOK], F32, "rope", st)
        k.dma("sp", rope[:, 0, :], self.rope_d[0], writes=[rope])
        k.dma("sp", rope[:, 1, :], self.rope_d[1], writes=[rope])
        xd = self.xsrc(l)
        for ci, (t0, W) in enumerate(CHUNKS):
            col = 1 if ci == 0 else 0
            x = xs[0]
            k.dma("sp", x[:, :, :W], xd[:, :, t0:t0 + W].rearrange("k p t -> p k t"), reads=[k.tok(("x", l, ci))], writes=[x])
            self.rstd_of(x, W, sq, rstd)
            self.modulate(x, W, rstd, l, 0, col, hT, t0, tmps)
        if "hT" in self.dbg:
            k.dma("sp", self.dbg["hT"].rearrange("k p t -> p k t"), hT[:, :, :], reads=[hT])
        si = [0]

        def store(dst_ap, src_fn, reads_ps, tokkey, eng="act"):
            s = stg[si[0] % 4]
            si[0] += 1
            src_fn(s)
            k.dma("sp", dst_ap, s[:, :dst_ap.shape[-1]], reads=[s], writes=[k.tok(tokkey)])

        def mm_group(ps, wslot, woff, t0, W):
            for kt in range(KT):
                k.op("pe", lambda e: e.matmul(ps[:, :W], lhsT=wslot[:, woff + kt * 128:woff + (kt + 1) * 128], rhs=hT[:, kt, t0:t0 + W],
                                              start=(kt == 0), stop=(kt == KT - 1)),
                     reads=[wslot, hT], writes=[ps], inc=(kt == KT - 1))

        def simple(pbase, n, dst_d, key, func, skip_ctx=False):
            for m in range(0, n, 4):
                ws = self.nws()
                k.dma("pool", ws[:, :4 * 2048].rearrange("p (m x) -> p m x", m=4),
                      self.win_d[l, pbase + m:pbase + m + 4].rearrange("m p x -> p m x"), writes=[ws])
                for mm in range(4):
                    for ci, (t0, W) in enumerate(CHUNKS):
                        if skip_ctx and ci == 0:
                            continue
                        ps = self.nps()
                        mm_group(ps, ws, mm * 2048, t0, W)
                        store(dst_d[m + mm, :, t0:t0 + W],
                              lambda s: k.op("act", lambda e: e.activation(out=s[:, :W], in_=ps[:, :W], func=func), reads=[ps], writes=[s]),
                              None, (key, l, m + mm, ci))

        simple(0, 8, self.u_d, "u", AF.Copy)
        simple(40, 16, self.gs_d, "gs", AF.Sigmoid, skip_ctx=(l == 1))
        simple(56, 16, self.gn_d, "gn", AF.Sigmoid, skip_ctx=(l == 1))

        def roped(pbase, dst_d, key):
            for m in range(0, 8, 2):
                ws = self.nws()
                k.dma("pool", ws[:, 0:4096].rearrange("p (m x) -> p m x", m=2),
                      self.win_d[l, pbase + m:pbase + m + 2].rearrange("m p x -> p m x"), writes=[ws])
                k.dma("pool", ws[:, 4096:8192].rearrange("p (m x) -> p m x", m=2),
                      self.win_d[l, pbase + 8 + m:pbase + 8 + m + 2].rearrange("m p x -> p m x"), writes=[ws])
                for mm in range(2):
                    for ci, (t0, W) in enumerate(CHUNKS):
                        ps1 = self.nps()
                        mm_group(ps1, ws, mm * 2048, t0, W)
                        ps2 = self.nps()
                        mm_group(ps2, ws, 4096 + mm * 2048, t0, W)
                        ta, tb = tmps[2], tmps[3]
                        k.op("dve", lambda e: e.tensor_tensor(out=ta[:, :W], in0=ps1[:, :W], in1=rope[:, 0, t0:t0 + W], op=ALU.mult),
                             reads=[ps1, rope], writes=[ta])
                        k.op("dve", lambda e: e.tensor_tensor(out=tb[:, :W], in0=ps2[:, :W], in1=rope[:, 1, t0:t0 + W], op=ALU.mult),
                             reads=[ps2, rope], writes=[tb])
                        store(dst_d[m + mm, :, t0:t0 + W],
                              lambda s: k.op("pool", lambda e: e.tensor_tensor(out=s[:, :W], in0=ta[:, :W], in1=tb[:, :W], op=ALU.add),
                                             reads=[ta, tb], writes=[s]),
                              None, (key, l, m + mm, ci))

        roped(8, self.q_d, "q")
        roped(24, self.k_d, "k")
        for nt in range(2):
            ws = self.nws()
            k.dma("pool", ws[:, :], self.wv_d[l, nt], writes=[ws])
            for sh in range(2):
                for tt in range(18):
                    tk0 = tt * 128 + 64 * sh
                    if tk0 + 128 > NTOK:
                        continue
                    ps = self.nps()
                    for kt in range(KT):
                        k.op("pe", lambda e: e.matmul(ps[:, :], lhsT=hT[:, kt, tk0:tk0 + 128], rhs=ws[:, kt * 512:(kt + 1) * 512],
                                                      start=(kt == 0), stop=(kt == KT - 1)),
                             reads=[ws, hT], writes=[ps], inc=(kt == KT - 1))
                    store(self.v_d[sh, tt, :, nt * 512:(nt + 1) * 512],
                          lambda s: k.op("act", lambda e: e.activation(out=s[:, :], in_=ps[:, :], func=AF.Copy), reads=[ps], writes=[s]),
                          None, ("v", l, sh, tt, nt))
        k.barrier()
        st.close()

    def phaseC(self, l):
        k = self.k
        st = contextlib.ExitStack()
        kT = k.sb([128, 8, NTOK], BF16, "kT", st)
        vA = k.sb([128, 2, 18, 1024], BF16, "vA", st)
        tab = k.sb([128, 8, 15, 64], F32, "tab", st)
        qT = [k.sb([128, NTOK], BF16, "qT%d" % i, st) for i in range(2)]
        ost = [k.sb([128, NTOK], BF16, "ost%d" % i, st) for i in range(2)]
        S = [k.sb([128, 768], F32, "S%d" % i, st) for i in range(2)]
        P = [k.sb([128, 768], F32, "P%d" % i, st) for i in range(2)]
        Pb = [k.sb([128, 768], BF16, "Pb%d" % i, st) for i in range(2)]
        PT = [k.sb([128, 768], BF16, "PT%d" % i, st) for i in range(2)]
        sm = [k.sb([128, 4], F32, "sm%d" % i, st) for i in range(2)]
        for h in range(8):
            k.dma("sp", kT[:, h, :], self.k_d[h], reads=[k.tok(("k", l, h, ci)) for ci in range(5)], writes=[kT])
        for sh in range(2):
            for tt in range(18):
                if tt * 128 + 64 * sh + 128 > NTOK:
                    continue
                k.dma("sp", vA[:, sh, tt, :], self.v_d[sh, tt], reads=[k.tok(("v", l, sh, tt, nt)) for nt in range(2)], writes=[vA])
        k.op("pool", lambda e: e.memset(tab[:, :, :, :], NEG), writes=[tab])
        rp = self.rpb_d[l].rearrange("(hp hh) r c -> hh hp r c", hh=2)
        with self.k.nc.allow_non_contiguous_dma(reason="rpb window gather (tiny)"):
            for hh in range(2):
                for q in range(64):
                    c0 = min(max(q - 8, 0), 48)
                    off = c0 - q + 15
                    k.dma("sp", tab[hh * 64 + q:hh * 64 + q + 1, :, :, c0:c0 + 16], rp[hh:hh + 1, :, :, off:off + 16], writes=[tab])
        psS = [self.ps[0], self.ps[1]]
        psC = [self.ps[2], self.ps[3]]
        psO = [self.ps[4], self.ps[5]]
        items = []
        for hp in range(8):
            rows = list(range(32)) + ([-1, -2, -3, -4] if l < self.nlayers_total - 1 else [])
            for r in rows:
                items.append((hp, r))
        NI = len(items)

        def geo(idx):
            hp, r = items[idx]
            i2 = idx % 2
            d = dict(hp=hp, r=r, i2=i2, q=qT[hp % 2], o=ost[hp % 2], s=S[i2], p=P[i2], pb=Pb[i2], pt=PT[i2], sm=sm[i2],
                     pS=psS[i2], pC=psC[i2], pO=psO[i2], pT=self.psb[i2])
            if r >= 0:
                r0 = min(max(r - 4, 0), 24)
                d.update(qc=CTX + r * 64, r0=r0, ro0=r0 - r + 7, k0=CTX + r0 * 64, NK=768)
            else:
                d.update(qc=(-r - 1) * 64, NK=256)
            return d

        def stA(idx):
            g = geo(idx)
            hp, r, q = g["hp"], g["r"], g["q"]
            k.pe_drain()
            if idx == 0 or items[idx - 1][0] != hp:
                k.dma("sp", q[:, :], self.q_d[hp], reads=[k.tok(("q", l, hp, ci)) for ci in range(5)], writes=[q])
            for hh in range(2):
                pr = slice(hh * 64, hh * 64 + 64)
                if r >= 0:
                    k.op("pe", lambda e: e.matmul(g["pS"][pr, :], lhsT=q[pr, g["qc"]:g["qc"] + 64], rhs=kT[pr, hp, g["k0"]:g["k0"] + 512], start=True, stop=True),
                         reads=[q, kT], writes=[g["pS"]])
                k.op("pe", lambda e: e.matmul(g["pC"][pr, 0:256], lhsT=q[pr, g["qc"]:g["qc"] + 64], rhs=kT[pr, hp, 0:256], start=True, stop=True),
                     reads=[q, kT], writes=[g["pC"]])

        def stB(idx):
            g = geo(idx)
            hp, r, s_, p_, pb_, sm_, NK = g["hp"], g["r"], g["s"], g["p"], g["pb"], g["sm"], g["NK"]
            if r >= 0:
                ro0 = g["ro0"]
                k.op("dve", lambda e: e.scalar_tensor_tensor(out=s_[:, 0:512], in0=g["pS"][:, :], scalar=0.125,
                                                             in1=tab[:, hp, ro0:ro0 + 8, :].rearrange("p a b -> p (a b)"),
                                                             op0=ALU.mult, op1=ALU.add),
                     reads=[g["pS"], tab], writes=[s_])
                k.op("act", lambda e: e.activation(out=s_[:, 512:768], in_=g["pC"][:, 0:256], func=AF.Copy, scale=0.125),
                     reads=[g["pC"]], writes=[s_])
            else:
                k.op("act", lambda e: e.activation(out=s_[:, 0:256], in_=g["pC"][:, 0:256], func=AF.Copy, scale=0.125),
                     reads=[g["pC"]], writes=[s_])
            k.op("dve", lambda e: e.reduce_max(out=sm_[:, 0:1], in_=s_[:, :NK], axis=AX.X), reads=[s_], writes=[sm_])
            k.op("dve", lambda e: e.tensor_scalar(out=sm_[:, 1:2], in0=sm_[:, 0:1], scalar1=-1.0, scalar2=None, op0=ALU.mult),
                 reads=[sm_], writes=[sm_])
            k.op("act", lambda e: e.activation(out=p_[:, :NK], in_=s_[:, :NK], func=AF.Exp, bias=sm_[:, 1:2], scale=1.0,
                                               accum_out=sm_[:, 2:3]),
                 reads=[s_, sm_], writes=[p_, sm_])
            k.op("dve", lambda e: e.reciprocal(out=sm_[:, 3:4], in_=sm_[:, 2:3]), reads=[sm_], writes=[sm_])
            k.op("dve", lambda e: e.tensor_scalar(out=pb_[:, :NK], in0=p_[:, :NK], scalar1=sm_[:, 3:4], scalar2=None, op0=ALU.mult),
                 reads=[p_, sm_], writes=[pb_])

        def stC(idx):
            g = geo(idx)
            nj = g["NK"] // 128
            k.pe_drain()
            for j in range(nj):
                k.op("pe", lambda e: e.transpose(out=g["pT"][:, j * 128:(j + 1) * 128], in_=g["pb"][:, j * 128:(j + 1) * 128], identity=self.identb[:, :]),
                     reads=[g["pb"], self.identb], writes=[g["pT"]], inc=(j == nj - 1))

        def stD(idx):
            g = geo(idx)
            NK = g["NK"]
            k.op("act", lambda e: e.activation(out=g["pt"][:, :NK], in_=g["pT"][:, :NK], func=AF.Copy), reads=[g["pT"]], writes=[g["pt"]])

        def stE(idx):
            g = geo(idx)
            hp, r = g["hp"], g["r"]
            nj = g["NK"] // 128
            k.pe_drain()
            for hh in range(2):
                pr = slice(hh * 64, hh * 64 + 64)
                hc = (2 * hp + hh) * 64
                for j in range(nj):
                    if r >= 0 and j < 4:
                        sh = (g["r0"] % 2)
                        tt = (g["k0"] - 64 * sh) // 128 + j
                        vv = vA[:, sh, tt, hc:hc + 64]
                    else:
                        jj = j - 4 if r >= 0 else j
                        vv = vA[:, 0, jj, hc:hc + 64]
                    k.op("pe", lambda e: e.matmul(g["pO"][pr, 0:64], lhsT=vv, rhs=g["pt"][:, j * 128 + hh * 64:j * 128 + hh * 64 + 64],
                                                  start=(j == 0), stop=(j == nj - 1)),
                         reads=[vA, g["pt"]], writes=[g["pO"]], inc=(j == nj - 1))

        def stF(idx):
            g = geo(idx)
            hp, o = g["hp"], g["o"]
            k.op("dve", lambda e: e.tensor_copy(out=o[:, g["qc"]:g["qc"] + 64], in_=g["pO"][:, 0:64]), reads=[g["pO"]], writes=[o])
            if idx == NI - 1 or items[idx + 1][0] != hp:
                k.dma("sp", self.o_d[hp], o[:, :], reads=[o], writes=[k.tok(("o", l, hp))])

        for t in range(NI + 2):
            if t < NI:
                stA(t)
                stB(t)
            if 0 <= t - 1 < NI:
                stC(t - 1)
                stD(t - 1)
            if 0 <= t - 2 < NI:
                stE(t - 2)
                stF(t - 2)
        k.barrier()
        st.close()

    def phaseE(self, l, t_lo):
        k = self.k
        st = contextlib.ExitStack()
        self.alloc_ws(st)
        aT = k.sb([128, 8, NTOK], BF16, "aT", st)
        oT = k.sb([128, 8, NTOK], BF16, "oT", st)
        gsb = [k.sb([128, 512], BF16, "gsb%d" % i, st) for i in range(2)]
        gnb = [k.sb([128, 512], BF16, "gnb%d" % i, st) for i in range(2)]
        t1 = [k.sb([128, 512], F32, "t1_%d" % i, st) for i in range(2)]
        t2 = [k.sb([128, 512], F32, "t2_%d" % i, st) for i in range(2)]
        t3 = [k.sb([128, 512], F32, "t3_%d" % i, st) for i in range(2)]
        stg = [k.sb([128, 512], BF16, "stgE%d" % i, st) for i in range(3)]
        for h in range(8):
            k.dma("sp", aT[:, h, :], self.a_d[h], reads=[k.tok(("a", l, h))], writes=[aT])
            k.dma("sp", oT[:, h, :], self.o_d[h], reads=[k.tok(("o", l, h))], writes=[oT])
        it = 0
        chunks = [c for c in CHUNKS if c[0] >= t_lo]
        for mt in range(16):
            ws = self.nws()
            k.dma("pool", ws[:, 0:1024], self.wval_d[l, mt], writes=[ws])
            k.dma("pool", ws[:, 1024:2048], self.wglu_d[l, mt], writes=[ws])
            k.dma("pool", ws[:, 2048:3072], self.wna_d[l, mt], writes=[ws])
            for (t0, W) in chunks:
                ci = [c[0] for c in CHUNKS].index(t0)
                i2 = it % 2
                it += 1
                pss = []
                for wi, src in ((0, aT), (1, aT), (2, oT)):
                    ps = self.nps()
                    for kt in range(8):
                        k.op("pe", lambda e: e.matmul(ps[:, :W], lhsT=ws[:, wi * 1024 + kt * 128:wi * 1024 + (kt + 1) * 128], rhs=src[:, kt, t0:t0 + W],
                                                      start=(kt == 0), stop=(kt == 7)),
                             reads=[ws, src], writes=[ps], inc=(kt == 7))
                    pss.append(ps)
                g1, g2 = gsb[i2], gnb[i2]
                k.dma("sp", g1[:, :W], self.gs_d[mt, :, t0:t0 + W], reads=[k.tok(("gs", l, mt, ci))], writes=[g1])
                k.dma("sp", g2[:, :W], self.gn_d[mt, :, t0:t0 + W], reads=[k.tok(("gn", l, mt, ci))], writes=[g2])
                a1, a2, a3 = t1[i2], t2[i2], t3[i2]
                k.op("act", lambda e: e.activation(out=a1[:, :W], in_=pss[1][:, :W], func=AF.Sigmoid), reads=[pss[1]], writes=[a1])
                k.op("dve", lambda e: e.tensor_tensor(out=a2[:, :W], in0=pss[0][:, :W], in1=a1[:, :W], op=ALU.mult), reads=[pss[0], a1], writes=[a2])
                k.op("pool", lambda e: e.tensor_tensor(out=a2[:, :W], in0=a2[:, :W], in1=g1[:, :W], op=ALU.mult), reads=[a2, g1], writes=[a2])
                k.op("dve", lambda e: e.tensor_tensor(out=a3[:, :W], in0=pss[2][:, :W], in1=g2[:, :W], op=ALU.mult), reads=[pss[2], g2], writes=[a3])
                s = stg[it % 3]
                k.op("pool", lambda e: e.tensor_tensor(out=s[:, :W], in0=a2[:, :W], in1=a3[:, :W], op=ALU.add), reads=[a2, a3], writes=[s])
                k.dma("sp", self.m_d[mt, :, t0:t0 + W], s[:, :W], reads=[s], writes=[k.tok(("m", l, mt, ci))])
        k.barrier()
        st.close()

    def phaseF(self, l, t_lo, last):
        k = self.k
        st = contextlib.ExitStack()
        self.alloc_ws(st)
        x = k.sb([128, KT, 512], F32, "xF", st)
        ob = k.sb([128, KT, 512], F32, "oF", st)
        mh = k.sb([128, KT, 512], BF16, "mh", st)
        ab = k.sb([128, 64, 512], BF16, "ab", st)
        rstd = k.sb([128, 512], F32, "rstdF", st)
        tmps = [k.sb([128, 512], F32, "tmpF%d" % i, st) for i in range(2)]
        sq = Buf(ab.ap, "sqalias")
        xd = self.xsrc(l)
        chunks = [c for c in CHUNKS if c[0] >= t_lo]

        def wload(first, fill, src_ap, rr=None):
            ws = self.nws()
            wtok = k.tok(("wc", fill))
            if first:
                k.dma("pool", ws[:, :].rearrange("p (m x) -> p m x", m=rr) if rr else ws[:, :], src_ap, writes=[ws])
                k.dma("sp", self.wc_d[fill], ws[:, :], reads=[ws], writes=[wtok])
            else:
                k.dma("sp", ws[:, :], self.wc_d[fill], reads=[wtok], writes=[ws])
            return ws

        for cidx, (t0, W) in enumerate(chunks):
            first = (cidx == 0)
            ci = [c[0] for c in CHUNKS].index(t0)
            col = 1 if ci == 0 else 0
            k.dma("sp", x[:, :, :W], xd[:, :, t0:t0 + W].rearrange("k p t -> p k t"), reads=[k.tok(("x", l, ci))], writes=[x])
            k.dma("sp", mh[:, :, :W], self.m_d[:, :, t0:t0 + W].rearrange("k p t -> p k t"),
                  reads=[k.tok(("m", l, mt, ci)) for mt in range(16)], writes=[mh])
            for m in range(0, 16, 4):
                ws = wload(first, m // 4, self.wout_d[l, m:m + 4].rearrange("m p x -> p m x"), 4)
                for mm in range(4):
                    ps = self.nps()
                    for kt in range(KT):
                        k.op("pe", lambda e: e.matmul(ps[:, :W], lhsT=ws[:, mm * 2048 + kt * 128:mm * 2048 + (kt + 1) * 128], rhs=mh[:, kt, :W],
                                                      start=(kt == 0), stop=(kt == KT - 1)),
                             reads=[ws, mh], writes=[ps], inc=(kt == KT - 1))
                    k.op("act", lambda e: e.activation(out=ob[:, m + mm, :W], in_=ps[:, :W], func=AF.Copy), reads=[ps], writes=[ob])
            self.residual(x, ob, W, sq, ab, rstd, l, 0, col, tmps)
            self.rstd_of(x, W, Buf(ab.ap, "sq2"), rstd) if False else None
            self._rstd_alias(x, W, ab, rstd)
            self.modulate(x, W, rstd, l, 1, col, mh, 0, tmps)
            for m in range(0, 64, 4):
                ws = wload(first, 4 + m // 4, self.wfc1_d[l, m:m + 4].rearrange("m p x -> p m x"), 4)
                for mm in range(4):
                    ps = self.nps()
                    for kt in range(KT):
                        k.op("pe", lambda e: e.matmul(ps[:, :W], lhsT=ws[:, mm * 2048 + kt * 128:mm * 2048 + (kt + 1) * 128], rhs=mh[:, kt, :W],
                                                      start=(kt == 0), stop=(kt == KT - 1)),
                             reads=[ws, mh], writes=[ps], inc=(kt == KT - 1))
                    tm = tmps[mm % 2]
                    k.op("act", lambda e: e.activation(out=tm[:, :W], in_=ps[:, :W], func=AF.Relu), reads=[ps], writes=[tm])
                    k.op("pool", lambda e: e.tensor_tensor(out=ab[:, m + mm, :W], in0=tm[:, :W], in1=tm[:, :W], op=ALU.mult), reads=[tm], writes=[ab])
            for m in range(16):
                ws = wload(first, 20 + m, self.wfc2_d[l, m])
                ps = self.nps()
                for kt in range(64):
                    k.op("pe", lambda e: e.matmul(ps[:, :W], lhsT=ws[:, kt * 128:(kt + 1) * 128], rhs=ab[:, kt, :W], start=(kt == 0), stop=(kt == 63)),
                         reads=[ws, ab], writes=[ps], inc=(kt == 63))
                k.op("act", lambda e: e.activation(out=ob[:, m, :W], in_=ps[:, :W], func=AF.Copy), reads=[ps], writes=[ob])
            self.residual(x, ob, W, sq, ab, rstd, l, 1, col, tmps)
            if last:
                k.dma("sp", self.yT_d[:, :, t0 - CTX:t0 - CTX + W].rearrange("k p t -> p k t"), x[:, :, :W], reads=[x],
                      writes=[k.tok(("y", ci))], is_out=True)
            else:
                k.dma("sp", self.x1_d[:, :, t0:t0 + W].rearrange("k p t -> p k t"), x[:, :, :W], reads=[x], writes=[k.tok(("x", l + 1, ci))])
        k.barrier()
        st.close()

    def _rstd_alias(self, src, W, ab, rstd):
        k = self.k
        sqv = ab.ap[:, 0:KT, :]
        k.op("act", lambda e: e.activation(out=sqv[:, :, :W], in_=src[:, :, :W], func=AF.Square), reads=[src], writes=[ab])
        ps = self.nps()
        for kt in range(KT):
            k.op("pe", lambda e: e.matmul(ps[:, :W], lhsT=self.ones[:, :], rhs=sqv[:, kt, :W], start=(kt == 0), stop=(kt == KT - 1)),
                 reads=[self.ones, ab], writes=[ps], inc=(kt == KT - 1))
        k.op("act", lambda e: e.activation(out=rstd[:, :W], in_=ps[:, :W], func=AF.Sqrt, bias=self.epsb[:, 0:1], scale=1.0 / D),
             reads=[ps, self.epsb], writes=[rstd])
        k.op("dve", lambda e: e.reciprocal(out=rstd[:, :W], in_=rstd[:, :W]), reads=[rstd], writes=[rstd])

    def residual(self, x, ob, W, sq, ab, rstd, l, s, col, tmps):
        k = self.k
        self._rstd_alias(ob, W, ab, rstd)
        for kt in range(KT):
            tmp = tmps[kt % 2]
            k.op("dve", lambda e: e.scalar_tensor_tensor(out=tmp[:, :W], in0=ob[:, kt, :W], scalar=self.G[:, l, s, kt, col:col + 1],
                                                         in1=rstd[:, :W], op0=ALU.mult, op1=ALU.mult),
                 reads=[ob, rstd, self.G], writes=[tmp])
            k.op("pool", lambda e: e.tensor_tensor(out=x[:, kt, :W], in0=x[:, kt, :W], in1=tmp[:, :W], op=ALU.add), reads=[x, tmp], writes=[x])

    def build(self, upto=None):
        k = self.k
        self.nlayers_total = 2
        self.epsb = k.sb([128, 1], F32, "epsb")
        k.op("pool", lambda e: e.memset(self.epsb[:, :], EPS), writes=[self.epsb])
        self.hpib = k.sb([128, 1], F32, "hpib")
        k.op("pool", lambda e: e.memset(self.hpib[:, :], float(np.pi / 2)), writes=[self.hpib])
        self.phase0()
        for l in range(self.nlayers):
            self.phaseAB(l)
            if upto == "AB":
                break
            self.phaseC(l)
            if upto == "C":
                break
            self.phaseD(l)
            last = (l == 1)
            t_lo = CTX if last else 0
            self.phaseE(l, t_lo)
            self.phaseF(l, t_lo, last)
        k.finish()
        return k.nc


def _fm(v):
    return np.ascontiguousarray(np.asarray(v, np.float32).reshape(-1, 128).T)


def _panels(W):
    K, N = W.shape
    kt, mt = K // 128, N // 128
    return np.ascontiguousarray(W.reshape(kt, 128, mt, 128).transpose(2, 1, 0, 3).reshape(mt, 128, kt * 128))


def _rope_tables():
    nf = 16
    inv = (10000.0 ** (-np.arange(nf, dtype=np.float32) / nf)).astype(np.float32)
    pos = np.arange(LAT)
    rows = (pos // 64).astype(np.float32)
    cols = (pos % 64).astype(np.float32)
    cos = np.ones((64, NTOK), np.float32)
    sin = np.zeros((64, NTOK), np.float32)
    for d in range(64):
        blk = d // 16
        p = rows if blk < 2 else cols
        ang = (p * inv[d % 16]).astype(np.float32)
        cos[d, CTX:] = np.cos(ang)
        sg = -1.0 if blk % 2 == 0 else 1.0
        sin[d, CTX:] = sg * np.sin(ang)
    return np.ascontiguousarray(np.stack([np.concatenate([cos, cos], 0), np.concatenate([sin, sin], 0)], 0))


def prep_shared(inp):
    f = lambda a: np.asarray(a, np.float32)
    w_in = f(inp["w_in"])
    partner = np.array([d + 16 if (d % 32) < 16 else d - 16 for d in range(64)])
    perm = (np.arange(16)[:, None] * 64 + partner[None, :]).reshape(-1)
    wins, wvs = [], []
    for l in range(2):
        W = w_in[l]
        u, q, kk, v, gs, gn = W[:, :1024], W[:, 1024:2048], W[:, 2048:3072], W[:, 3072:4096], W[:, 4096:6144], W[:, 6144:]
        comb = np.concatenate([u, q, q[:, perm], kk, kk[:, perm], gs, gn], axis=1)
        wins.append(_panels(comb))
        wvs.append(np.ascontiguousarray(v.reshape(KT, 128, 2, 512).transpose(2, 1, 0, 3).reshape(2, 128, KT * 512)))
    sh = {
        "wmod": np.stack([_panels(f(inp["w_mod"])[l]) for l in range(2)]),
        "win": np.stack(wins),
        "wv": np.stack(wvs),
        "rope": _rope_tables(),
        "rpb": np.ascontiguousarray(f(inp["na_rpb"])),
        "wval": np.stack([_panels(f(inp["w_ssm_val"])[l]) for l in range(2)]),
        "wglu": np.stack([_panels(f(inp["w_ssm_glu"])[l]) for l in range(2)]),
        "wna": np.stack([_panels(f(inp["w_na_proj"])[l]) for l in range(2)]),
        "wout": np.stack([_panels(f(inp["w_out"])[l]) for l in range(2)]),
        "wfc1": np.stack([_panels(f(inp["w_fc1"])[l]) for l in range(2)]),
        "wfc2": np.stack([_panels(f(inp["w_fc2"])[l]) for l in range(2)]),
        "ident": np.eye(128, dtype=np.float32),
    }
    jj = np.arange(128) // 16
    cc = np.arange(128) % 16
    s5c = np.zeros((128, 400), np.float32)
    for kk in range(8):
        s5c[:, kk] = (jj == kk)
        s5c[:, 8 + kk] = (7 - jj == kk)
    s5c[:, 16:144] = (jj[:, None] <= jj[None, :])
    s5c[:, 144:272] = (jj[:, None] >= jj[None, :])
    s5c[:, 272:400] = np.eye(128)
    sel = np.zeros((128, 8, 8, 128), np.float32)
    selT = np.zeros((128, 8, 8, 128), np.float32)
    for gi in range(8):
        for j in range(8):
            for c in range(16):
                sel[gi * 16 + c, gi, j, j * 16 + c] = 1.0
                selT[j * 16 + c, gi, j, gi * 16 + c] = 1.0
    sh["s5c"] = s5c
    sh["sel"] = sel.reshape(128, 8192)
    sh["selT"] = selT.reshape(128, 8192)
    lre, lim, ldt = f(inp["ssm_lam_re"]), f(inp["ssm_lam_im"]), f(inp["ssm_log_dt"])
    bre, bim, cre, cim = f(inp["ssm_b_re"]), f(inp["ssm_b_im"]), f(inp["ssm_c_re"]), f(inp["ssm_c_im"])
    sC = np.empty((2, 2, 7, 128, 512), np.float32)
    for l in range(2):
        for d in range(2):
            def cl(A):
                t = A.reshape(2, 32, 64).transpose(0, 2, 1)
                return np.repeat(t.reshape(128, 32), 16, axis=1)
            sC[l, d, 0] = cl(lre[l, d])
            sC[l, d, 1] = cl(lim[l, d])
            sC[l, d, 2] = cl(np.repeat(ldt[l, d][:, None], 64, axis=1))
            for a, Cc in ((3, cre), (4, cim)):
                sC[l, d, a] = Cc[l, d].reshape(2, 32, 16, 64).transpose(0, 3, 1, 2).reshape(128, 512)
            for a, B in ((5, bre), (6, bim)):
                sC[l, d, a] = B[l, d].reshape(2, 32, 64, 16).transpose(0, 2, 1, 3).reshape(128, 512)
    sh["sC"] = sC
    dd = f(inp["ssm_d"])
    sh["drep"] = np.stack([np.tile(dd[l].reshape(64, 16).T, (8, 1)) for l in range(2)]).astype(np.float32)
    return sh


def prep_core(inp, b):
    f = lambda a: np.asarray(a, np.float32)
    X = np.concatenate([f(inp["ctx"])[b], f(inp["x"])[b]], axis=0)
    xT = np.ascontiguousarray(X.T.reshape(KT, 128, NTOK))
    cols = []
    for l in range(2):
        cols += [_fm(inp["g_pre_mix"][l]), _fm(inp["g_post_mix"][l]), _fm(inp["g_pre_mlp"][l]), _fm(inp["g_post_mlp"][l]),
                 _fm(inp["b_mod"][l]), _fm(inp["ssm_d"][l])]
    cols += [_fm(inp["c"][b]), _fm(inp["c_ctx"])]
    vecs = np.ascontiguousarray(np.concatenate(cols, axis=1))
    assert vecs.shape == (128, NVEC)
    return {"xT": xT, "vecs": vecs}


def _cmul(k, eng, o_re, o_im, a_re, a_im, b_re, b_im, t1, t2, rd=(), wr=()):
    E = k.op
    E(eng, lambda e: e.tensor_tensor(out=t1, in0=a_re, in1=b_re, op=ALU.mult), reads=rd, writes=wr)
    E(eng, lambda e: e.tensor_tensor(out=t2, in0=a_im, in1=b_im, op=ALU.mult), reads=rd, writes=wr)
    E(eng, lambda e: e.tensor_tensor(out=t2, in0=t1, in1=t2, op=ALU.subtract), reads=rd, writes=wr)
    E(eng, lambda e: e.tensor_tensor(out=t1, in0=a_re, in1=b_im, op=ALU.mult), reads=rd, writes=wr)
    E(eng, lambda e: e.tensor_tensor(out=o_im, in0=a_im, in1=b_re, op=ALU.mult), reads=rd, writes=wr)
    E(eng, lambda e: e.tensor_tensor(out=o_im, in0=o_im, in1=t1, op=ALU.add), reads=rd, writes=wr)
    E(eng, lambda e: e.tensor_copy(out=o_re, in_=t2), reads=rd, writes=wr)


def _lam_base(P, st, src, F, eng):
    k = P.k
    nb = lambda nm: k.sb([128, F], F32, nm, st)
    W = Buf(None, "Wtok")
    tok = [W, src]
    dt, ar, ai, c, s, m, t1, t2 = [nb(n) for n in ("dt", "ar", "ai", "c", "s", "m", "t1", "t2")]
    o = {n: nb(n) for n in ("lbr", "lbi", "lir", "lii", "gr", "gi")}
    E = lambda en, fn: k.op(en, fn, reads=tok + [P.hpib], writes=[W])
    E("act", lambda e: e.activation(out=dt[:, :], in_=src[:, 2, :], func=AF.Exp))
    E(eng, lambda e: e.tensor_tensor(out=ar[:, :], in0=src[:, 0, :], in1=dt[:, :], op=ALU.mult))
    E(eng, lambda e: e.tensor_tensor(out=ai[:, :], in0=src[:, 1, :], in1=dt[:, :], op=ALU.mult))
    E("act", lambda e: e.activation(out=s[:, :], in_=ai[:, :], func=AF.Sin, scale=1.0 / 16))
    E("act", lambda e: e.activation(out=c[:, :], in_=ai[:, :], func=AF.Sin, scale=1.0 / 16, bias=P.hpib[:, 0:1]))
    for _ in range(4):
        E(eng, lambda e: e.tensor_tensor(out=t1[:, :], in0=c[:, :], in1=s[:, :], op=ALU.mult))
        E(eng, lambda e: e.tensor_tensor(out=c[:, :], in0=c[:, :], in1=c[:, :], op=ALU.mult))
        E(eng, lambda e: e.tensor_tensor(out=s[:, :], in0=s[:, :], in1=s[:, :], op=ALU.mult))
        E(eng, lambda e: e.tensor_tensor(out=c[:, :], in0=c[:, :], in1=s[:, :], op=ALU.subtract))
        E(eng, lambda e: e.tensor_scalar(out=s[:, :], in0=t1[:, :], scalar1=2.0, scalar2=None, op0=ALU.mult))
    E("act", lambda e: e.activation(out=m[:, :], in_=ar[:, :], func=AF.Exp))
    E(eng, lambda e: e.tensor_tensor(out=o["lbr"][:, :], in0=m[:, :], in1=c[:, :], op=ALU.mult))
    E(eng, lambda e: e.tensor_tensor(out=o["lbi"][:, :], in0=m[:, :], in1=s[:, :], op=ALU.mult))
    E("act", lambda e: e.activation(out=m[:, :], in_=ar[:, :], func=AF.Exp, scale=-1.0))
    E(eng, lambda e: e.tensor_tensor(out=o["lir"][:, :], in0=m[:, :], in1=c[:, :], op=ALU.mult))
    E("dve", lambda e: e.scalar_tensor_tensor(out=o["lii"][:, :], in0=m[:, :], scalar=-1.0, in1=s[:, :], op0=ALU.mult, op1=ALU.mult))
    lr, li = src[:, 0, :], src[:, 1, :]
    E(eng, lambda e: e.tensor_tensor(out=t1[:, :], in0=lr, in1=lr, op=ALU.mult))
    E(eng, lambda e: e.tensor_tensor(out=t2[:, :], in0=li, in1=li, op=ALU.mult))
    E(eng, lambda e: e.tensor_tensor(out=t1[:, :], in0=t1[:, :], in1=t2[:, :], op=ALU.add))
    E("dve", lambda e: e.reciprocal(out=t1[:, :], in_=t1[:, :]))
    E(eng, lambda e: e.tensor_scalar(out=c[:, :], in0=o["lbr"][:, :], scalar1=-1.0, scalar2=None, op0=ALU.add))
    E(eng, lambda e: e.tensor_tensor(out=t2[:, :], in0=c[:, :], in1=lr, op=ALU.mult))
    E(eng, lambda e: e.tensor_tensor(out=s[:, :], in0=o["lbi"][:, :], in1=li, op=ALU.mult))
    E(eng, lambda e: e.tensor_tensor(out=t2[:, :], in0=t2[:, :], in1=s[:, :], op=ALU.add))
    E(eng, lambda e: e.tensor_tensor(out=o["gr"][:, :], in0=t2[:, :], in1=t1[:, :], op=ALU.mult))
    E(eng, lambda e: e.tensor_tensor(out=t2[:, :], in0=o["lbi"][:, :], in1=lr, op=ALU.mult))
    E(eng, lambda e: e.tensor_tensor(out=s[:, :], in0=c[:, :], in1=li, op=ALU.mult))
    E(eng, lambda e: e.tensor_tensor(out=t2[:, :], in0=t2[:, :], in1=s[:, :], op=ALU.subtract))
    E(eng, lambda e: e.tensor_tensor(out=o["gi"][:, :], in0=t2[:, :], in1=t1[:, :], op=ALU.mult))
    return o, W, (t1, t2, c, s, m, dt)


def phaseD(self, l):
    k = self.k
    st = contextlib.ExitStack()
    NG = 64
    R = k.sb([128, 17408], BF16, "Rraw", st)
    self._R = R
    BS = [Buf(R.ap[:, d * 8192:(d + 1) * 8192].rearrange("q (g m) -> q g m", m=128), "BS%d" % d) for d in range(2)]
    T = k.sb([128, NG, 128], BF16, "Tg", st)
    CS = [k.sb([128, 2, 32, 128], BF16, "CS%d" % d, st) for d in range(2)]
    MU = k.sb([128, 2, 2, 64], F32, "MU", st)
    cst = k.sb([128, 400], F32, "s5c", st)
    k.dma("sp", cst[:, :], self.s5c_d[:, :], writes=[cst])
    Drep = k.sb([128, NG], F32, "Drep", st)
    k.dma("sp", Drep[:, :], self.drep_d[l], writes=[Drep])
    F = NG * 64
    for d in range(2):
        st2 = contextlib.ExitStack()
        Fc = 512
        src = k.sb([128, 7, Fc], F32, "srcC", st2)
        for a in range(7):
            k.dma("sp", src[:, a, :], self.sC_d[l, d, a], writes=[src])
        LT = k.sb([128, 2, 32, 128], BF16, "LT", st2)
        LB = k.sb([128, 2, 32, 128], BF16, "LB", st2)
        o, W, (t1, t2, c, s, m, dt) = _lam_base(self, st2, src, Fc, "pool")
        tok = [W, src]
        E = lambda en, fn, wr=(): k.op(en, fn, reads=tok, writes=[W] + list(wr))
        bbr, bbi = k.sb([128, Fc], F32, "bbr", st2), k.sb([128, Fc], F32, "bbi", st2)
        _cmul(k, "pool", bbr[:, :], bbi[:, :], o["gr"][:, :], o["gi"][:, :], src[:, 5, :], src[:, 6, :], t1[:, :], t2[:, :], rd=tok, wr=[W])
        pr, pi = o["gr"], o["gi"]
        qr, qi = k.sb([128, Fc], F32, "qr", st2), k.sb([128, Fc], F32, "qi", st2)
        E("pool", lambda e: e.tensor_copy(out=pr[:, :], in_=o["lbr"][:, :]))
        E("pool", lambda e: e.tensor_copy(out=pi[:, :], in_=o["lbi"][:, :]))
        E("pool", lambda e: e.tensor_copy(out=qr[:, :], in_=o["lir"][:, :]))
        E("pool", lambda e: e.tensor_copy(out=qi[:, :], in_=o["lii"][:, :]))
        cs5 = CS[d].ap.rearrange("q r g (j c) -> q r g j c", c=16)
        lt5 = LT.ap.rearrange("q r g (j c) -> q r g j c", c=16)
        lb5 = LB.ap.rearrange("q r g (j c) -> q r g j c", c=16)
        g3 = lambda ap: ap.rearrange("q (g c) -> q g c", c=16)
        jb0 = 7 if d == 0 else 0
        E("act", lambda e: e.activation(out=lb5[:, 0, :, jb0, :], in_=g3(bbr[:, :]), func=AF.Copy), wr=[LB])
        E("act", lambda e: e.activation(out=lb5[:, 1, :, jb0, :], in_=g3(bbi[:, :]), func=AF.Copy), wr=[LB])
        for kk in range(1, 9):
            if kk > 1:
                _cmul(k, "pool", pr[:, :], pi[:, :], pr[:, :], pi[:, :], o["lbr"][:, :], o["lbi"][:, :], t1[:, :], t2[:, :], rd=tok, wr=[W])
                _cmul(k, "pool", qr[:, :], qi[:, :], qr[:, :], qi[:, :], o["lir"][:, :], o["lii"][:, :], t1[:, :], t2[:, :], rd=tok, wr=[W])
            jj = kk - 1 if d == 0 else 8 - kk
            _cmul(k, "pool", c[:, :], s[:, :], src[:, 3, :], src[:, 4, :], pr[:, :], pi[:, :], t1[:, :], t2[:, :], rd=tok, wr=[W])
            E("act", lambda e: e.activation(out=cs5[:, 0, :, jj, :], in_=g3(c[:, :]), func=AF.Copy), wr=[CS[d]])
            E("act", lambda e: e.activation(out=cs5[:, 1, :, jj, :], in_=g3(s[:, :]), func=AF.Copy, scale=-1.0), wr=[CS[d]])
            _cmul(k, "pool", c[:, :], s[:, :], bbr[:, :], bbi[:, :], qr[:, :], qi[:, :], t1[:, :], t2[:, :], rd=tok, wr=[W])
            E("act", lambda e: e.activation(out=lt5[:, 0, :, jj, :], in_=g3(c[:, :]), func=AF.Copy), wr=[LT])
            E("act", lambda e: e.activation(out=lt5[:, 1, :, jj, :], in_=g3(s[:, :]), func=AF.Copy), wr=[LT])
            if kk <= 7:
                jb = 7 - kk if d == 0 else kk
                _cmul(k, "pool", c[:, :], s[:, :], bbr[:, :], bbi[:, :], pr[:, :], pi[:, :], t1[:, :], t2[:, :], rd=tok, wr=[W])
                E("act", lambda e: e.activation(out=lb5[:, 0, :, jb, :], in_=g3(c[:, :]), func=AF.Copy), wr=[LB])
                E("act", lambda e: e.activation(out=lb5[:, 1, :, jb, :], in_=g3(s[:, :]), func=AF.Copy), wr=[LB])
        prg = g3(pr[:, :])[:, :, 0]
        pig = g3(pi[:, :])[:, :, 0]
        gsl = slice(d * 32, d * 32 + 32)
        E("pool", lambda e: e.tensor_copy(out=MU[:, 0, 0, gsl], in_=prg), wr=[MU])
        E("pool", lambda e: e.tensor_copy(out=MU[:, 0, 1, gsl], in_=prg), wr=[MU])
        E("pool", lambda e: e.tensor_scalar(out=MU[:, 1, 0, gsl], in0=pig, scalar1=-1.0, scalar2=None, op0=ALU.mult), wr=[MU])
        E("pool", lambda e: e.tensor_copy(out=MU[:, 1, 1, gsl], in_=pig), wr=[MU])
        for g in range(64):
            hb, gl = (g // 32) * 64, g % 32
            pT = self.psb[g % 2]
            for ri in range(2):
                k.op("pe", lambda e: e.transpose(out=pT[:, ri * 64:(ri + 1) * 64], in_=LB[hb:hb + 64, ri, gl, :], identity=self.identb[hb:hb + 64, hb:hb + 64]),
                     reads=[LB, self.identb], writes=[pT], inc=(ri == 1))
            if g % 2:
                k.op("act", lambda e: e.activation(out=BS[d][:, g, :], in_=pT[:, 0:128], func=AF.Copy), reads=[pT], writes=[BS[d]])
            else:
                k.op("dve", lambda e: e.tensor_copy(out=BS[d][:, g, :], in_=pT[:, 0:128]), reads=[pT], writes=[BS[d]])
        tf = [k.sb([128, 128], F32, "tfT%d" % i, st2) for i in range(2)]
        mask = cst[:, 16 + 128 * d:16 + 128 * d + 128]
        ident = cst[:, 272:400]
        for g in range(64):
            hb, gl = (g // 32) * 64, g % 32
            ps = self.nps()
            for ri in range(2):
                k.op("pe", lambda e: e.matmul(ps[:, 0:128], lhsT=LT[hb:hb + 64, ri, gl, :], rhs=CS[d][hb:hb + 64, ri, gl, :],
                                              start=(ri == 0), stop=(ri == 1)),
                     reads=[LT, CS[d]], writes=[ps], inc=(ri == 1))
            t = tf[g % 2]
            k.op("dve", lambda e: e.tensor_tensor(out=t[:, :], in0=ps[:, 0:128], in1=mask, op=ALU.mult), reads=[ps, cst], writes=[t])
            if d == 0:
                k.op("dve", lambda e: e.scalar_tensor_tensor(out=T[:, g, :], in0=ident, scalar=Drep[:, g:g + 1], in1=t[:, :],
                                                              op0=ALU.mult, op1=ALU.add), reads=[t, cst, Drep], writes=[T])
            else:
                k.op("pool", lambda e: e.tensor_tensor(out=T[:, g, :], in0=T[:, g, :], in1=t[:, :], op=ALU.add), reads=[t, T], writes=[T])
        k.barrier()
        st2.close()
    self._s5_run(l, BS, T, CS, MU, st)
    k.barrier()
    st.close()


def _s5_run(self, l, BS, T, CS, MU, st):
    k = self.k
    NG = 64
    Xall = k.sb([128, NG, NCH8], BF16, "Xall", st)
    st1 = contextlib.ExitStack()
    uT = k.sb([128, 8, NTOK], BF16, "uT", st1)
    Sel = k.sb([128, 8, 8, 128], BF16, "Sel", st1)
    for h in range(8):
        k.dma("sp", uT[:, h, :], self.u_d[h], reads=[k.tok(("u", l, h, ci)) for ci in range(5)], writes=[uT])
    k.dma("pool", Sel[:, :, :, :].rearrange("q a b m -> q (a b m)"), self.sel_d[:, :], writes=[Sel])
    for g in range(NG):
        ps = self.nps()
        for j in range(8):
            k.op("pe", lambda e: e.matmul(ps[:, 0:NCH8], lhsT=Sel[:, g % 8, j, :], rhs=uT[:, g // 8, j:NTOK:8], start=(j == 0), stop=(j == 7)),
                 reads=[Sel, uT], writes=[ps], inc=(j == 7))
        k.op("act" if g % 2 else "dve", (lambda e: e.activation(out=Xall[:, g, :], in_=ps[:, 0:NCH8], func=AF.Copy)) if g % 2 else
             (lambda e: e.tensor_copy(out=Xall[:, g, :], in_=ps[:, 0:NCH8])), reads=[ps], writes=[Xall])
    k.barrier()
    st1.close()
    S = [k.sb([128, 2, 32, NCH8], BF16, "S%d" % d, st) for d in range(2)]
    Srd = [Buf(S[d].ap, "Srd%d" % d) for d in range(2)]
    Swr = [Buf(S[d].ap, "Swr%d" % d) for d in range(2)]
    for d in range(2):
        for g in range(NG):
            hb, gl = (g // 32) * 64, g % 32
            ps = self.nps()
            ps2 = self.nps()
            for ri in range(2):
                k.op("pe", lambda e: e.matmul(ps[hb:hb + 64, ri * 256:(ri + 1) * 256], lhsT=BS[d][:, g, ri * 64:(ri + 1) * 64], rhs=Xall[:, g, 0:256],
                                              start=True, stop=True),
                     reads=[BS[d], Xall], writes=[ps], inc=(ri == 1))
            for ri in range(2):
                k.op("pe", lambda e: e.matmul(ps2[hb:hb + 64, ri * 32:(ri + 1) * 32], lhsT=BS[d][:, g, ri * 64:(ri + 1) * 64], rhs=Xall[:, g, 256:NCH8],
                                              start=True, stop=True),
                     reads=[BS[d], Xall], writes=[ps2], inc=(ri == 1))
            if g % 2:
                k.op("act", lambda e: e.activation(out=S[d][hb:hb + 64, :, gl, 0:256], in_=ps[hb:hb + 64, 0:512].rearrange("q (r n) -> q r n", r=2), func=AF.Copy),
                     reads=[ps], writes=[Srd[d]])
                k.op("act", lambda e: e.activation(out=S[d][hb:hb + 64, :, gl, 256:NCH8], in_=ps2[hb:hb + 64, 0:64].rearrange("q (r n) -> q r n", r=2), func=AF.Copy),
                     reads=[ps2], writes=[Srd[d]])
            else:
                k.op("dve", lambda e: e.tensor_copy(out=S[d][hb:hb + 64, :, gl, 0:256], in_=ps[hb:hb + 64, 0:512].rearrange("q (r n) -> q r n", r=2)),
                     reads=[ps], writes=[Srd[d]])
                k.op("dve", lambda e: e.tensor_copy(out=S[d][hb:hb + 64, :, gl, 256:NCH8], in_=ps2[hb:hb + 64, 0:64].rearrange("q (r n) -> q r n", r=2)),
                     reads=[ps2], writes=[Srd[d]])
    H = [k.sb([128, 2, 64], F32, "H%d" % i, st) for i in range(2)]
    t1 = k.sb([128, 2, 64], F32, "rt1", st)
    t2 = k.sb([128, 2, 64], F32, "rt2", st)
    k.op("pool", lambda e: e.memset(H[0][:, :, :], 0.0), writes=[H[0]])
    for i in range(NCH8):
        nf = i
        nb = (31 - i) if i < 32 else (319 - i)
        Ho, Hn = H[i % 2], H[(i + 1) % 2]
        k.op("dve", lambda e: e.tensor_tensor(out=t1[:, :, :], in0=MU[:, 0, :, :], in1=Ho[:, :, :], op=ALU.mult), reads=[MU, Ho], writes=[t1])
        k.op("dve", lambda e: e.tensor_tensor(out=t2[:, 0, :], in0=MU[:, 1, 0, :], in1=Ho[:, 1, :], op=ALU.mult), reads=[MU, Ho], writes=[t2])
        k.op("dve", lambda e: e.tensor_tensor(out=t2[:, 1, :], in0=MU[:, 1, 1, :], in1=Ho[:, 0, :], op=ALU.mult), reads=[MU, Ho], writes=[t2])
        k.op("dve", lambda e: e.tensor_tensor(out=t1[:, :, :], in0=t1[:, :, :], in1=t2[:, :, :], op=ALU.add), reads=[t1, t2], writes=[t1])
        stk = Buf(None, "stk")
        k.op("dve", lambda e: e.tensor_tensor(out=Hn[:, :, 0:32], in0=t1[:, :, 0:32], in1=S[0][:, :, :, nf], op=ALU.add), reads=[t1, Srd[0]], writes=[Hn, stk])
        k.op("dve", lambda e: e.tensor_tensor(out=Hn[:, :, 32:64], in0=t1[:, :, 32:64], in1=S[1][:, :, :, nb], op=ALU.add), reads=[t1, Srd[1]], writes=[Hn, stk])
        k.op("act", lambda e: e.activation(out=S[0][:, :, :, nf], in_=Ho[:, :, 0:32], func=AF.Copy), reads=[Ho, stk], writes=[Swr[0]])
        k.op("act", lambda e: e.activation(out=S[1][:, :, :, nb], in_=Ho[:, :, 32:64], func=AF.Copy), reads=[Ho, stk], writes=[Swr[1]])
    k.barrier()
    R = self._R
    SelT = Buf(R.ap[:, 0:8192].rearrange("q (a b m) -> q a b m", a=8, b=8), "SelT")
    k.dma("pool", R.ap[:, 0:8192], self.selT_d[:, :], writes=[SelT])
    Ag = [Buf(R.ap[:, 8192 + i * 2304:8192 + (i + 1) * 2304].rearrange("q (a n) -> q a n", a=8), "Ag%d" % i) for i in range(2)]
    ast = [Buf(R.ap[:, 12800 + i * 2304:12800 + (i + 1) * 2304], "ast%d" % i) for i in range(2)]
    ga = [k.sb([128, NCH8], F32, "ga%d" % i, st) for i in range(2)]
    gb = [k.sb([128, NCH8], F32, "gb%d" % i, st) for i in range(2)]
    for tl in range(8):
        A = Ag[tl % 2]
        for gi in range(8):
            g = tl * 8 + gi
            hb, gl = (g // 32) * 64, g % 32
            ps = self.nps()
            k.op("pe", lambda e: e.matmul(ps[:, 0:NCH8], lhsT=T[:, g, :], rhs=Xall[:, g, :], start=True, stop=False), reads=[T, Xall], writes=[ps], inc=False)
            for d in range(2):
                for ri in range(2):
                    last = (d == 1 and ri == 1)
                    k.op("pe", lambda e: e.matmul(ps[:, 0:NCH8], lhsT=CS[d][hb:hb + 64, ri, gl, :], rhs=S[d][hb:hb + 64, ri, gl, :], start=False, stop=last),
                         reads=[CS[d], Swr[d]], writes=[ps], inc=last)
            a_, b_ = ga[gi % 2], gb[gi % 2]
            k.op("act", lambda e: e.activation(out=a_[:, :], in_=ps[:, 0:NCH8], func=AF.Square), reads=[ps], writes=[a_])
            k.op("dve", lambda e: e.tensor_scalar(out=a_[:, :], in0=a_[:, :], scalar1=0.044715, scalar2=1.0, op0=ALU.mult, op1=ALU.add), reads=[a_], writes=[a_])
            k.op("dve", lambda e: e.tensor_tensor(out=b_[:, :], in0=ps[:, 0:NCH8], in1=a_[:, :], op=ALU.mult), reads=[ps, a_], writes=[b_])
            k.op("act", lambda e: e.activation(out=b_[:, :], in_=b_[:, :], func=AF.Sigmoid, scale=1.5957691216057308), reads=[b_], writes=[b_])
            k.op("dve", lambda e: e.tensor_tensor(out=A[:, gi, :], in0=ps[:, 0:NCH8], in1=b_[:, :], op=ALU.mult), reads=[ps, b_], writes=[A])
        o = ast[tl % 2]
        for j in range(8):
            ps = self.nps()
            for gi in range(8):
                k.op("pe", lambda e: e.matmul(ps[:, 0:NCH8], lhsT=SelT[:, gi, j, :], rhs=A[:, gi, :], start=(gi == 0), stop=(gi == 7)),
                     reads=[SelT, A], writes=[ps], inc=(gi == 7))
            k.op("act" if j % 2 else "dve", (lambda e: e.activation(out=o[:, j:NTOK:8], in_=ps[:, 0:NCH8], func=AF.Copy)) if j % 2 else
                 (lambda e: e.tensor_copy(out=o[:, j:NTOK:8], in_=ps[:, 0:NCH8])), reads=[ps], writes=[o])
        k.dma("sp", self.a_d[tl], o[:, :], reads=[o], writes=[k.tok(("a", l, tl))])


def _psl(self, ps, hb, ri):
    return ps[hb:hb + 64, ri * 256:(ri + 1) * 256]


Prog.phaseD = phaseD
Prog._s5_run = _s5_run
Prog._psl = _psl


_CACHE = {}


def kernel(**inputs):
    if "nc" not in _CACHE:
        _CACHE["nc"] = Prog().build()
    nc = _CACHE["nc"]
    sh = prep_shared(inputs)
    in_maps = []
    for core in range(8):
        m = dict(sh)
        m.update(prep_core(inputs, core % 4))
        in_maps.append(m)
    res = run_bass_kernel_spmd(nc, in_maps, core_ids=list(range(8)))
    out = np.empty((4, LAT, D), np.float32)
    for b in range(4):
        yT = np.asarray(res.results[b]["yT"]).reshape(D, LAT)
        out[b] = yT.T
    return out
```

```python
import contextlib
import math
import numpy as np
import concourse.bass as bass
import concourse.mybir as mybir
from concourse.bass_utils import run_bass_kernel_spmd

F32 = mybir.dt.float32
BF16 = mybir.dt.bfloat16
AF = mybir.ActivationFunctionType
ALU = mybir.AluOpType
AX = mybir.AxisListType

D = 2048
KT = 16
NTOK = 2304
CTX = 256
LAT = 2048
DFF = 8192
EPS = 1e-6
CHUNKS = [(0, 256), (256, 512), (768, 512), (1280, 512), (1792, 512)]
NEG = -30000.0
NCH8 = NTOK // 8
VEC_L = 168
C_OFF = 336
NVEC = 368


class Buf:
    __slots__ = ("ap", "w", "r", "name")

    def __init__(self, ap=None, name=""):
        self.ap = ap
        self.w = []
        self.r = []
        self.name = name

    def __getitem__(self, idx):
        return self.ap[idx]


class KB:
    NTICK = 40

    def __init__(self):
        self.nc = bass.Bass("TRN2", target_bir_lowering=False)
        nc = self.nc
        self.es = contextlib.ExitStack()
        self.eng = {}
        self.semid = {}
        for name, e in (("pe", nc.tensor), ("act", nc.scalar), ("dve", nc.vector),
                        ("pool", nc.gpsimd), ("sp", nc.sync)):
            sem = self.es.enter_context(nc.semaphore("s_" + name))
            self.eng[name] = [e, sem, 0]
        self.known = {n: {} for n in self.eng}
        self.ticks = []
        for i in range(self.NTICK):
            sem = self.es.enter_context(nc.semaphore("tk%d" % i))
            self.ticks.append([sem, 0, "tk%d" % i])
        self.tki = {"sp": 0, "pool": 0}
        self.tkr = {"sp": (0, 26), "pool": (26, 14)}
        self.out_evs = []
        self.phase_stack = None
        self.dbufs = {}
        self.uid = 0

    def sb(self, shape, dt, name=None, stack=None):
        self.uid += 1
        nm = (name or "t") + "_%d" % self.uid
        t = (stack or self.es).enter_context(self.nc.sbuf_tensor(nm, list(shape), dt))
        return Buf(t, nm)

    def pst(self, shape, dt, name=None, stack=None):
        self.uid += 1
        nm = (name or "p") + "_%d" % self.uid
        t = (stack or self.es).enter_context(self.nc.psum_tensor(nm, list(shape), dt))
        return Buf(t, nm)

    def dram(self, name, shape, dt, kind="Internal"):
        return self.nc.dram_tensor(name, list(shape), dt, kind=kind).ap()

    def tok(self, key):
        b = self.dbufs.get(key)
        if b is None:
            b = Buf(None, str(key))
            self.dbufs[key] = b
        return b

    def _wait(self, en, evs):
        e = self.eng[en][0]
        kn = self.known[en]
        best = {}
        for (key, sem, val) in evs:
            if val > kn.get(key, 0) and val > best.get(key, (None, 0))[1]:
                best[key] = (sem, val)
        for key, (sem, val) in best.items():
            e.wait_ge(sem, val)
            kn[key] = val

    def _deps(self, en, reads, writes):
        evs = []
        for b in reads:
            evs += b.w
        for b in writes:
            evs += b.w
            evs += b.r
        if en == "pe":
            evs = [x for x in evs if x[0] != "pe"]
        return evs

    def _record(self, ev, reads, writes):
        for b in reads:
            b.r = [x for x in b.r if x[0] != ev[0]]
            b.r.append(ev)
        for b in writes:
            b.w = [ev]
            b.r = []

    def op(self, en, fn, reads=(), writes=(), inc=True):
        self._wait(en, self._deps(en, reads, writes))
        ent = self.eng[en]
        ins = fn(ent[0])
        if inc:
            ent[2] += 1
            ins.then_inc(ent[1], 1)
            ev = (en, ent[1], ent[2])
            self._record(ev, reads, writes)
        return ins

    def dma(self, q, out_ap, in_ap, reads=(), writes=(), is_out=False):
        base, cnt = self.tkr[q]
        tk = self.ticks[base + self.tki[q]]
        self.tki[q] = (self.tki[q] + 1) % cnt
        evs = self._deps(q, reads, writes)
        if tk[1] > 0:
            evs.append((tk[2], tk[0], tk[1]))
        self._wait(q, evs)
        ins = self.eng[q][0].dma_start(out=out_ap, in_=in_ap)
        tk[1] += 16
        ins.then_inc(tk[0], 16)
        ev = (tk[2], tk[0], tk[1])
        self._record(ev, reads, writes)
        if is_out:
            self.out_evs.append(ev)
        return ev

    def pe_drain(self):
        ent = self.eng["pe"]
        if ent[2] > self.known["pe"].get("pe", 0):
            ent[0].wait_ge(ent[1], ent[2])
            self.known["pe"]["pe"] = ent[2]

    def barrier(self):
        evs = []
        for n, ent in self.eng.items():
            if ent[2] > 0:
                evs.append((n, ent[1], ent[2]))
        for tk in self.ticks:
            if tk[1] > 0:
                evs.append((tk[2], tk[0], tk[1]))
        for n in self.eng:
            self._wait(n, [x for x in evs if x[0] != n])

    def finish(self):
        self.barrier()
        self.es.close()


class Prog:
    def __init__(self, dbg=None, nlayers=2):
        self.k = KB()
        self.dbg = dbg or {}
        self.nlayers = nlayers
        k = self.k
        self.xT_d = k.dram("xT", [KT, 128, NTOK], F32, "ExternalInput")
        self.vecs_d = k.dram("vecs", [128, NVEC], F32, "ExternalInput")
        self.wmod_d = k.dram("wmod", [2, 96, 128, KT * 128], F32, "ExternalInput")
        self.win_d = k.dram("win", [2, 72, 128, KT * 128], F32, "ExternalInput")
        self.wv_d = k.dram("wv", [2, 2, 128, KT * 512], F32, "ExternalInput")
        self.rope_d = k.dram("rope", [2, 128, NTOK], F32, "ExternalInput")
        self.rpb_d = k.dram("rpb", [2, 16, 15, 31], F32, "ExternalInput")
        self.wval_d = k.dram("wval", [2, 16, 128, 8 * 128], F32, "ExternalInput")
        self.wglu_d = k.dram("wglu", [2, 16, 128, 8 * 128], F32, "ExternalInput")
        self.wna_d = k.dram("wna", [2, 16, 128, 8 * 128], F32, "ExternalInput")
        self.wout_d = k.dram("wout", [2, 16, 128, KT * 128], F32, "ExternalInput")
        self.wfc1_d = k.dram("wfc1", [2, 64, 128, KT * 128], F32, "ExternalInput")
        self.wfc2_d = k.dram("wfc2", [2, 16, 128, 64 * 128], F32, "ExternalInput")
        self.s5c_d = k.dram("s5c", [128, 400], F32, "ExternalInput")
        self.drep_d = k.dram("drep", [2, 128, 64], F32, "ExternalInput")
        self.sC_d = k.dram("sC", [2, 2, 7, 128, 512], F32, "ExternalInput")
        self.sel_d = k.dram("sel", [128, 8192], F32, "ExternalInput")
        self.selT_d = k.dram("selT", [128, 8192], F32, "ExternalInput")
        self.yT_d = k.dram("yT", [KT, 128, LAT], F32, "ExternalOutput")
        self.x1_d = k.dram("x1s", [KT, 128, NTOK], F32)
        self.u_d = k.dram("us", [8, 128, NTOK], BF16)
        self.q_d = k.dram("qs", [8, 128, NTOK], BF16)
        self.k_d = k.dram("ks", [8, 128, NTOK], BF16)
        self.v_d = k.dram("vs", [2, 18, 128, 1024], BF16)
        self.gs_d = k.dram("gss", [16, 128, NTOK], BF16)
        self.gn_d = k.dram("gns", [16, 128, NTOK], BF16)
        self.o_d = k.dram("os", [8, 128, NTOK], BF16)
        self.a_d = k.dram("as", [8, 128, NTOK], BF16)
        self.m_d = k.dram("ms", [16, 128, NTOK], BF16)
        self.wc_d = k.dram("wcache", [36, 128, 8192], BF16)
        self.vecs = k.sb([128, NVEC], F32, "vecs")
        self.sT = k.sb([128, KT, 2], F32, "sT")
        self.modT = k.sb([128, 2, 96, 2], F32, "modT")
        self.A = k.sb([128, 2, 2, KT, 2], F32, "Amod")
        self.G = k.sb([128, 2, 2, KT, 2], F32, "Gmod")
        self.ones = k.sb([128, 128], BF16, "ones")
        self.identb = k.sb([128, 128], BF16, "identb")
        self.ident_d = k.dram("ident", [128, 128], F32, "ExternalInput")
        self.perm_d = k.dram("perm", [128, 128], F32, "ExternalInput")
        self.permb = k.sb([128, 128], BF16, "permb")
        self.ps = [k.pst([128, 512], F32, "ps%d" % i) for i in range(6)]
        self.psb = [k.pst([128, 1024], BF16, "psb%d" % i) for i in range(2)]
        self.psi = 0
        self.wsl = None
        self.wsi = 0

    def nps(self):
        b = self.ps[self.psi % 6]
        self.psi += 1
        return b

    def alloc_ws(self, st):
        self.wsl = [self.k.sb([128, 8192], BF16, "wsl%d" % i, st) for i in range(3)]

    def nws(self):
        b = self.wsl[self.wsi % 3]
        self.wsi += 1
        return b

    def phase0(self):
        k = self.k
        k.dma("sp", self.vecs[:, :], self.vecs_d[:, :], writes=[self.vecs])
        st = contextlib.ExitStack()
        idf = k.sb([128, 128], F32, "idf", st)
        k.dma("sp", idf[:, :], self.ident_d[:, :], writes=[idf])
        k.op("dve", lambda e: e.tensor_copy(out=self.identb[:, :], in_=idf[:, :]), reads=[idf], writes=[self.identb])
        pmf = k.sb([128, 128], F32, "pmf", st)
        k.dma("sp", pmf[:, :], self.perm_d[:, :], writes=[pmf])
        k.op("dve", lambda e: e.tensor_copy(out=self.permb[:, :], in_=pmf[:, :]), reads=[pmf], writes=[self.permb])
        k.op("pool", lambda e: e.memset(self.ones[:, :], 1.0), writes=[self.ones])
        for col in range(2):
            k.op("act", lambda e: e.activation(out=self.sT[:, :, col], in_=self.vecs[:, C_OFF + 16 * col:C_OFF + 16 * col + 16],
                                               func=AF.Silu), reads=[self.vecs], writes=[self.sT])
        wm = [k.sb([128, KT * 128], F32, "wm%d" % i, st) for i in range(3)]
        for l in range(self.nlayers):
            for mt in range(96):
                slot = wm[mt % 3]
                k.dma("sp", slot[:, :], self.wmod_d[l, mt], writes=[slot])
                ps = self.nps()
                for kt in range(KT):
                    k.op("pe", lambda e: e.matmul(ps[:, 0:2], lhsT=slot[:, kt * 128:(kt + 1) * 128], rhs=self.sT[:, kt, :],
                                                  start=(kt == 0), stop=(kt == KT - 1)),
                         reads=[slot, self.sT], writes=[ps], inc=(kt == KT - 1))
                boff = l * VEC_L + 64 + mt
                k.op("dve", lambda e: e.tensor_scalar(out=self.modT[:, l, mt, :], in0=ps[:, 0:2],
                                                      scalar1=self.vecs[:, boff:boff + 1], scalar2=None, op0=ALU.add),
                     reads=[ps, self.vecs], writes=[self.modT])
            for s in range(2):
                sc0 = 16 + 48 * s
                gt0 = 32 + 48 * s
                gpre = l * VEC_L + (0 if s == 0 else 32)
                gpost = l * VEC_L + (16 if s == 0 else 48)
                for col in range(2):
                    k.op("dve", lambda e: e.scalar_tensor_tensor(out=self.A[:, l, s, :, col], in0=self.modT[:, l, sc0:sc0 + 16, col],
                                                                 scalar=1.0, in1=self.vecs[:, gpre:gpre + 16],
                                                                 op0=ALU.add, op1=ALU.mult),
                         reads=[self.modT, self.vecs], writes=[self.A])
                    k.op("dve", lambda e: e.tensor_tensor(out=self.G[:, l, s, :, col], in0=self.modT[:, l, gt0:gt0 + 16, col],
                                                          in1=self.vecs[:, gpost:gpost + 16], op=ALU.mult),
                         reads=[self.modT, self.vecs], writes=[self.G])
        k.barrier()
        st.close()

    def rstd_of(self, src, W, sq, rstd):
        k = self.k
        k.op("act", lambda e: e.activation(out=sq[:, :, :W], in_=src[:, :, :W], func=AF.Square), reads=[src], writes=[sq])
        ps = self.nps()
        for kt in range(KT):
            k.op("pe", lambda e: e.matmul(ps[:, :W], lhsT=self.ones[:, :], rhs=sq[:, kt, :W], start=(kt == 0), stop=(kt == KT - 1)),
                 reads=[self.ones, sq], writes=[ps], inc=(kt == KT - 1))
        k.op("act", lambda e: e.activation(out=rstd[:, :W], in_=ps[:, :W], func=AF.Sqrt, bias=self.epsb[:, 0:1], scale=1.0 / D),
             reads=[ps, self.epsb], writes=[rstd])
        k.op("dve", lambda e: e.reciprocal(out=rstd[:, :W], in_=rstd[:, :W]), reads=[rstd], writes=[rstd])

    def modulate(self, src, W, rstd, l, s, col, dst, dst_t0, tmps):
        k = self.k
        sh0 = 0 if s == 0 else 48
        for kt in range(KT):
            tmp = tmps[kt % 2]
            k.op("dve", lambda e: e.scalar_tensor_tensor(out=tmp[:, :W], in0=src[:, kt, :W], scalar=self.A[:, l, s, kt, col:col + 1],
                                                         in1=rstd[:, :W], op0=ALU.mult, op1=ALU.mult),
                 reads=[src, rstd, self.A], writes=[tmp])
            k.op("act", lambda e: e.activation(out=dst[:, kt, dst_t0:dst_t0 + W], in_=tmp[:, :W], func=AF.Identity,
                                               bias=self.modT[:, l, sh0 + kt, col:col + 1], scale=1.0),
                 reads=[tmp, self.modT], writes=[dst])

    def xsrc(self, l):
        return self.xT_d if l == 0 else self.x1_d

    def phaseAB(self, l):
        k = self.k
        st = contextlib.ExitStack()
        self.alloc_ws(st)
        hT = k.sb([128, KT, NTOK], BF16, "hT", st)
        xs = [k.sb([128, KT, 512], F32, "xs%d" % i, st) for i in range(1)]
        sq = k.sb([128, KT, 512], BF16, "sq", st)
        rstd = k.sb([128, 512], F32, "rstd", st)
        tmps = [k.sb([128, 512], F32, "tmp%d" % i, st) for i in range(4)]
        stg = [k.sb([128, 512], BF16, "stg%d" % i, st) for i in range(3)]
        rope = k.sb([128, 2, NTOK], F32, "rope", st)
        k.dma("sp", rope[:, 0, :], self.rope_d[0], writes=[rope])
        k.dma("sp", rope[:, 1, :], self.rope_d[1], writes=[rope])
        xd = self.xsrc(l)
        for ci, (t0, W) in enumerate(CHUNKS):
            col = 1 if ci == 0 else 0
            x = xs[0]
            k.dma("sp", x[:, :, :W], xd[:, :, t0:t0 + W].rearrange("k p t -> p k t"), reads=[k.tok(("x", l, ci))], writes=[x])
            self.rstd_of(x, W, sq, rstd)
            self.modulate(x, W, rstd, l, 0, col, hT, t0, tmps)
        if "hT" in self.dbg:
            k.dma("sp", self.dbg["hT"].rearrange("k p t -> p k t"), hT[:, :, :], reads=[hT])
        si = [0]

        def store(dst_ap, src_fn, reads_ps, tokkey, eng="act"):
            s = stg[si[0] % 3]
            si[0] += 1
            src_fn(s)
            k.dma("sp", dst_ap, s[:, :dst_ap.shape[-1]], reads=[s], writes=[k.tok(tokkey)])

        def mm_group(ps, wslot, woff, t0, W):
            for kt in range(KT):
                k.op("pe", lambda e: e.matmul(ps[:, :W], lhsT=wslot[:, woff + kt * 128:woff + (kt + 1) * 128], rhs=hT[:, kt, t0:t0 + W],
                                              start=(kt == 0), stop=(kt == KT - 1)),
                     reads=[wslot, hT], writes=[ps], inc=(kt == KT - 1))

        def simple(pbase, n, dst_d, key, func, skip_ctx=False):
            for m in range(0, n, 4):
                ws = self.nws()
                k.dma("pool", ws[:, :4 * 2048].rearrange("p (m x) -> p m x", m=4),
                      self.win_d[l, pbase + m:pbase + m + 4].rearrange("m p x -> p m x"), writes=[ws])
                for mm in range(4):
                    for ci, (t0, W) in enumerate(CHUNKS):
                        if skip_ctx and ci == 0:
                            continue
                        ps = self.nps()
                        mm_group(ps, ws, mm * 2048, t0, W)
                        store(dst_d[m + mm, :, t0:t0 + W],
                              lambda s: k.op("act", lambda e: e.activation(out=s[:, :W], in_=ps[:, :W], func=func), reads=[ps], writes=[s]),
                              None, (key, l, m + mm, ci))

        simple(0, 8, self.u_d, "u", AF.Copy)
        simple(40, 16, self.gs_d, "gs", AF.Sigmoid, skip_ctx=(l == 1))
        simple(56, 16, self.gn_d, "gn", AF.Sigmoid, skip_ctx=(l == 1))

        qbs = [k.sb([128, 512], BF16, "qb%d" % i, st) for i in range(2)]
        qi = [0]

        def roped(pbase, dst_d, key, skip_ctx=False):
            for m in range(0, 8, 4):
                ws = self.nws()
                k.dma("pool", ws[:, :].rearrange("p (m x) -> p m x", m=4),
                      self.win_d[l, pbase + m:pbase + m + 4].rearrange("m p x -> p m x"), writes=[ws])
                for mm in range(4):
                    for ci, (t0, W) in enumerate(CHUNKS):
                        if skip_ctx and ci == 0:
                            continue
                        ps1 = self.nps()
                        mm_group(ps1, ws, mm * 2048, t0, W)
                        qb = qbs[qi[0] % 2]
                        qi[0] += 1
                        k.op("act", lambda e: e.activation(out=qb[:, :W], in_=ps1[:, :W], func=AF.Copy), reads=[ps1], writes=[qb])
                        ps2 = self.nps()
                        k.op("pe", lambda e: e.matmul(ps2[:, :W], lhsT=self.permb[:, :], rhs=qb[:, :W], start=True, stop=True),
                             reads=[self.permb, qb], writes=[ps2])
                        ta, tb = tmps[2], tmps[3]
                        k.op("dve", lambda e: e.tensor_tensor(out=ta[:, :W], in0=ps1[:, :W], in1=rope[:, 0, t0:t0 + W], op=ALU.mult),
                             reads=[ps1, rope, qb], writes=[ta])
                        k.op("dve", lambda e: e.tensor_tensor(out=tb[:, :W], in0=ps2[:, :W], in1=rope[:, 1, t0:t0 + W], op=ALU.mult),
                             reads=[ps2, rope], writes=[tb])
                        store(dst_d[m + mm, :, t0:t0 + W],
                              lambda s: k.op("pool", lambda e: e.tensor_tensor(out=s[:, :W], in0=ta[:, :W], in1=tb[:, :W], op=ALU.add),
                                             reads=[ta, tb], writes=[s]),
                              None, (key, l, m + mm, ci))

        roped(8, self.q_d, "q", skip_ctx=(l == 1))
        roped(24, self.k_d, "k")
        for nt in range(2):
            ws = self.nws()
            k.dma("pool", ws[:, :], self.wv_d[l, nt], writes=[ws])
            for sh in range(2):
                for tt in range(18):
                    tk0 = tt * 128 + 64 * sh
                    if tk0 + 128 > NTOK:
                        continue
                    ps = self.nps()
                    for kt in range(KT):
                        k.op("pe", lambda e: e.matmul(ps[:, :], lhsT=hT[:, kt, tk0:tk0 + 128], rhs=ws[:, kt * 512:(kt + 1) * 512],
                                                      start=(kt == 0), stop=(kt == KT - 1)),
                             reads=[ws, hT], writes=[ps], inc=(kt == KT - 1))
                    store(self.v_d[sh, tt, :, nt * 512:(nt + 1) * 512],
                          lambda s: k.op("act", lambda e: e.activation(out=s[:, :], in_=ps[:, :], func=AF.Copy), reads=[ps], writes=[s]),
                          None, ("v", l, sh, tt, nt))
        k.barrier()
        st.close()

    def phaseC(self, l):
        k = self.k
        st = contextlib.ExitStack()
        kT = k.sb([128, 8, NTOK], BF16, "kT", st)
        vA = k.sb([128, 2, 18, 1024], BF16, "vA", st)
        tab = k.sb([128, 8, 15, 64], F32, "tab", st)
        qT = [k.sb([128, NTOK], BF16, "qT%d" % i, st) for i in range(2)]
        ost = [k.sb([128, NTOK], BF16, "ost%d" % i, st) for i in range(2)]
        S = [k.sb([128, 768], F32, "S%d" % i, st) for i in range(2)]
        P = [k.sb([128, 768], F32, "P%d" % i, st) for i in range(2)]
        Pb = [k.sb([128, 768], BF16, "Pb%d" % i, st) for i in range(2)]
        PT = [k.sb([128, 768], BF16, "PT%d" % i, st) for i in range(2)]
        sm = [k.sb([128, 4], F32, "sm%d" % i, st) for i in range(2)]
        for h in range(8):
            k.dma("sp", kT[:, h, :], self.k_d[h], reads=[k.tok(("k", l, h, ci)) for ci in range(5)], writes=[kT])
        for sh in range(2):
            for tt in range(18):
                if tt * 128 + 64 * sh + 128 > NTOK:
                    continue
                k.dma("sp", vA[:, sh, tt, :], self.v_d[sh, tt], reads=[k.tok(("v", l, sh, tt, nt)) for nt in range(2)], writes=[vA])
        k.op("pool", lambda e: e.memset(tab[:, :, :, :], NEG), writes=[tab])
        rp = self.rpb_d[l].rearrange("(hp hh) r c -> hh hp r c", hh=2)
        with self.k.nc.allow_non_contiguous_dma(reason="rpb window gather (tiny)"):
            for hh in range(2):
                for q in range(64):
                    c0 = min(max(q - 8, 0), 48)
                    off = c0 - q + 15
                    k.dma("sp", tab[hh * 64 + q:hh * 64 + q + 1, :, :, c0:c0 + 16], rp[hh:hh + 1, :, :, off:off + 16], writes=[tab])
        psS = [self.ps[0], self.ps[1]]
        psC = [self.ps[2], self.ps[3]]
        psO = [self.ps[4], self.ps[5]]
        items = []
        for hp in range(8):
            rows = list(range(32)) + ([-1, -2, -3, -4] if l < self.nlayers_total - 1 else [])
            for r in rows:
                items.append((hp, r))
        NI = len(items)

        def geo(idx):
            hp, r = items[idx]
            i2 = idx % 2
            d = dict(hp=hp, r=r, i2=i2, q=qT[hp % 2], o=ost[hp % 2], s=S[i2], p=P[i2], pb=Pb[i2], pt=PT[i2], sm=sm[i2],
                     pS=psS[i2], pC=psC[i2], pO=psO[i2], pT=self.psb[i2])
            if r >= 0:
                r0 = min(max(r - 4, 0), 24)
                d.update(qc=CTX + r * 64, r0=r0, ro0=r0 - r + 7, k0=CTX + r0 * 64, NK=768)
            else:
                d.update(qc=(-r - 1) * 64, NK=256)
            return d

        def stA(idx):
            g = geo(idx)
            hp, r, q = g["hp"], g["r"], g["q"]
            k.pe_drain()
            if idx == 0 or items[idx - 1][0] != hp:
                k.dma("sp", q[:, :], self.q_d[hp], reads=[k.tok(("q", l, hp, ci)) for ci in range(5)], writes=[q])
            for hh in range(2):
                pr = slice(hh * 64, hh * 64 + 64)
                if r >= 0:
                    k.op("pe", lambda e: e.matmul(g["pS"][pr, :], lhsT=q[pr, g["qc"]:g["qc"] + 64], rhs=kT[pr, hp, g["k0"]:g["k0"] + 512], start=True, stop=True),
                         reads=[q, kT], writes=[g["pS"]])
                k.op("pe", lambda e: e.matmul(g["pC"][pr, 0:256], lhsT=q[pr, g["qc"]:g["qc"] + 64], rhs=kT[pr, hp, 0:256], start=True, stop=True),
                     reads=[q, kT], writes=[g["pC"]])

        def stB(idx):
            g = geo(idx)
            hp, r, s_, p_, pb_, sm_, NK = g["hp"], g["r"], g["s"], g["p"], g["pb"], g["sm"], g["NK"]
            if r >= 0:
                ro0 = g["ro0"]
                k.op("dve", lambda e: e.scalar_tensor_tensor(out=s_[:, 0:512], in0=g["pS"][:, :], scalar=0.125,
                                                             in1=tab[:, hp, ro0:ro0 + 8, :].rearrange("p a b -> p (a b)"),
                                                             op0=ALU.mult, op1=ALU.add),
                     reads=[g["pS"], tab], writes=[s_])
                k.op("act", lambda e: e.activation(out=s_[:, 512:768], in_=g["pC"][:, 0:256], func=AF.Copy, scale=0.125),
                     reads=[g["pC"]], writes=[s_])
            else:
                k.op("act", lambda e: e.activation(out=s_[:, 0:256], in_=g["pC"][:, 0:256], func=AF.Copy, scale=0.125),
                     reads=[g["pC"]], writes=[s_])
            k.op("dve", lambda e: e.reduce_max(out=sm_[:, 0:1], in_=s_[:, :NK], axis=AX.X), reads=[s_], writes=[sm_])
            k.op("dve", lambda e: e.tensor_scalar(out=sm_[:, 1:2], in0=sm_[:, 0:1], scalar1=-1.0, scalar2=None, op0=ALU.mult),
                 reads=[sm_], writes=[sm_])
            k.op("act", lambda e: e.activation(out=p_[:, :NK], in_=s_[:, :NK], func=AF.Exp, bias=sm_[:, 1:2], scale=1.0,
                                               accum_out=sm_[:, 2:3]),
                 reads=[s_, sm_], writes=[p_, sm_])
            k.op("dve", lambda e: e.reciprocal(out=sm_[:, 3:4], in_=sm_[:, 2:3]), reads=[sm_], writes=[sm_])
            k.op("dve", lambda e: e.tensor_scalar(out=pb_[:, :NK], in0=p_[:, :NK], scalar1=sm_[:, 3:4], scalar2=None, op0=ALU.mult),
                 reads=[p_, sm_], writes=[pb_])

        def stC(idx):
            g = geo(idx)
            nj = g["NK"] // 128
            k.pe_drain()
            for j in range(nj):
                k.op("pe", lambda e: e.transpose(out=g["pT"][:, j * 128:(j + 1) * 128], in_=g["pb"][:, j * 128:(j + 1) * 128], identity=self.identb[:, :]),
                     reads=[g["pb"], self.identb], writes=[g["pT"]], inc=(j == nj - 1))

        def stD(idx):
            g = geo(idx)
            NK = g["NK"]
            k.op("act", lambda e: e.activation(out=g["pt"][:, :NK], in_=g["pT"][:, :NK], func=AF.Copy), reads=[g["pT"]], writes=[g["pt"]])

        def stE(idx):
            g = geo(idx)
            hp, r = g["hp"], g["r"]
            nj = g["NK"] // 128
            k.pe_drain()
            for hh in range(2):
                pr = slice(hh * 64, hh * 64 + 64)
                hc = (2 * hp + hh) * 64
                for j in range(nj):
                    if r >= 0 and j < 4:
                        sh = (g["r0"] % 2)
                        tt = (g["k0"] - 64 * sh) // 128 + j
                        vv = vA[:, sh, tt, hc:hc + 64]
                    else:
                        jj = j - 4 if r >= 0 else j
                        vv = vA[:, 0, jj, hc:hc + 64]
                    k.op("pe", lambda e: e.matmul(g["pO"][pr, 0:64], lhsT=vv, rhs=g["pt"][:, j * 128 + hh * 64:j * 128 + hh * 64 + 64],
                                                  start=(j == 0), stop=(j == nj - 1)),
                         reads=[vA, g["pt"]], writes=[g["pO"]], inc=(j == nj - 1))

        def stF(idx):
            g = geo(idx)
            hp, o = g["hp"], g["o"]
            k.op("dve", lambda e: e.tensor_copy(out=o[:, g["qc"]:g["qc"] + 64], in_=g["pO"][:, 0:64]), reads=[g["pO"]], writes=[o])
            if idx == NI - 1 or items[idx + 1][0] != hp:
                k.dma("sp", self.o_d[hp], o[:, :], reads=[o], writes=[k.tok(("o", l, hp))])

        for t in range(NI + 2):
            if t < NI:
                stA(t)
                stB(t)
            if 0 <= t - 1 < NI:
                stC(t - 1)
                stD(t - 1)
            if 0 <= t - 2 < NI:
                stE(t - 2)
                stF(t - 2)
        k.barrier()
        st.close()

    def phaseE(self, l, t_lo):
        k = self.k
        st = contextlib.ExitStack()
        self.alloc_ws(st)
        aT = k.sb([128, 8, NTOK], BF16, "aT", st)
        oT = k.sb([128, 8, NTOK], BF16, "oT", st)
        gsb = [k.sb([128, 512], BF16, "gsb%d" % i, st) for i in range(2)]
        gnb = [k.sb([128, 512], BF16, "gnb%d" % i, st) for i in range(2)]
        t1 = [k.sb([128, 512], F32, "t1_%d" % i, st) for i in range(2)]
        t2 = [k.sb([128, 512], F32, "t2_%d" % i, st) for i in range(2)]
        t3 = [k.sb([128, 512], F32, "t3_%d" % i, st) for i in range(2)]
        stg = [k.sb([128, 512], BF16, "stgE%d" % i, st) for i in range(3)]
        for h in range(8):
            k.dma("sp", aT[:, h, :], self.a_d[h], reads=[k.tok(("a", l, h))], writes=[aT])
            k.dma("sp", oT[:, h, :], self.o_d[h], reads=[k.tok(("o", l, h))], writes=[oT])
        it = 0
        chunks = [c for c in CHUNKS if c[0] >= t_lo]
        for mt in range(16):
            ws = self.nws()
            k.dma("pool", ws[:, 0:1024], self.wval_d[l, mt], writes=[ws])
            k.dma("pool", ws[:, 1024:2048], self.wglu_d[l, mt], writes=[ws])
            k.dma("pool", ws[:, 2048:3072], self.wna_d[l, mt], writes=[ws])
            for (t0, W) in chunks:
                ci = [c[0] for c in CHUNKS].index(t0)
                i2 = it % 2
                it += 1
                pss = []
                for wi, src in ((0, aT), (1, aT), (2, oT)):
                    ps = self.nps()
                    for kt in range(8):
                        k.op("pe", lambda e: e.matmul(ps[:, :W], lhsT=ws[:, wi * 1024 + kt * 128:wi * 1024 + (kt + 1) * 128], rhs=src[:, kt, t0:t0 + W],
                                                      start=(kt == 0), stop=(kt == 7)),
                             reads=[ws, src], writes=[ps], inc=(kt == 7))
                    pss.append(ps)
                g1, g2 = gsb[i2], gnb[i2]
                k.dma("sp", g1[:, :W], self.gs_d[mt, :, t0:t0 + W], reads=[k.tok(("gs", l, mt, ci))], writes=[g1])
                k.dma("sp", g2[:, :W], self.gn_d[mt, :, t0:t0 + W], reads=[k.tok(("gn", l, mt, ci))], writes=[g2])
                a1, a2, a3 = t1[i2], t2[i2], t3[i2]
                k.op("act", lambda e: e.activation(out=a1[:, :W], in_=pss[1][:, :W], func=AF.Sigmoid), reads=[pss[1]], writes=[a1])
                k.op("dve", lambda e: e.tensor_tensor(out=a2[:, :W], in0=pss[0][:, :W], in1=a1[:, :W], op=ALU.mult), reads=[pss[0], a1], writes=[a2])
                k.op("pool", lambda e: e.tensor_tensor(out=a2[:, :W], in0=a2[:, :W], in1=g1[:, :W], op=ALU.mult), reads=[a2, g1], writes=[a2])
                k.op("dve", lambda e: e.tensor_tensor(out=a3[:, :W], in0=pss[2][:, :W], in1=g2[:, :W], op=ALU.mult), reads=[pss[2], g2], writes=[a3])
                s = stg[it % 3]
                k.op("pool", lambda e: e.tensor_tensor(out=s[:, :W], in0=a2[:, :W], in1=a3[:, :W], op=ALU.add), reads=[a2, a3], writes=[s])
                k.dma("sp", self.m_d[mt, :, t0:t0 + W], s[:, :W], reads=[s], writes=[k.tok(("m", l, mt, ci))])
        k.barrier()
        st.close()

    def phaseF(self, l, t_lo, last):
        k = self.k
        st = contextlib.ExitStack()
        self.alloc_ws(st)
        x = k.sb([128, KT, 512], F32, "xF", st)
        ob = k.sb([128, KT, 512], F32, "oF", st)
        mh = k.sb([128, KT, 512], BF16, "mh", st)
        ab = k.sb([128, 64, 512], BF16, "ab", st)
        rstd = k.sb([128, 512], F32, "rstdF", st)
        tmps = [k.sb([128, 512], F32, "tmpF%d" % i, st) for i in range(2)]
        sq = Buf(ab.ap, "sqalias")
        xd = self.xsrc(l)
        chunks = [c for c in CHUNKS if c[0] >= t_lo]

        def wload(first, fill, src_ap, rr=None):
            ws = self.nws()
            wtok = k.tok(("wc", fill))
            if first:
                k.dma("pool", ws[:, :].rearrange("p (m x) -> p m x", m=rr) if rr else ws[:, :], src_ap, writes=[ws])
                k.dma("sp", self.wc_d[fill], ws[:, :], reads=[ws], writes=[wtok])
            else:
                k.dma("sp", ws[:, :], self.wc_d[fill], reads=[wtok], writes=[ws])
            return ws

        for cidx, (t0, W) in enumerate(chunks):
            first = (cidx == 0)
            ci = [c[0] for c in CHUNKS].index(t0)
            col = 1 if ci == 0 else 0
            k.dma("sp", x[:, :, :W], xd[:, :, t0:t0 + W].rearrange("k p t -> p k t"), reads=[k.tok(("x", l, ci))], writes=[x])
            k.dma("sp", mh[:, :, :W], self.m_d[:, :, t0:t0 + W].rearrange("k p t -> p k t"),
                  reads=[k.tok(("m", l, mt, ci)) for mt in range(16)], writes=[mh])
            for m in range(0, 16, 4):
                ws = wload(first, m // 4, self.wout_d[l, m:m + 4].rearrange("m p x -> p m x"), 4)
                for mm in range(4):
                    ps = self.nps()
                    for kt in range(KT):
                        k.op("pe", lambda e: e.matmul(ps[:, :W], lhsT=ws[:, mm * 2048 + kt * 128:mm * 2048 + (kt + 1) * 128], rhs=mh[:, kt, :W],
                                                      start=(kt == 0), stop=(kt == KT - 1)),
                             reads=[ws, mh], writes=[ps], inc=(kt == KT - 1))
                    k.op("act", lambda e: e.activation(out=ob[:, m + mm, :W], in_=ps[:, :W], func=AF.Copy), reads=[ps], writes=[ob])
            self.residual(x, ob, W, sq, ab, rstd, l, 0, col, tmps)
            self.rstd_of(x, W, Buf(ab.ap, "sq2"), rstd) if False else None
            self._rstd_alias(x, W, ab, rstd)
            self.modulate(x, W, rstd, l, 1, col, mh, 0, tmps)
            for m in range(0, 64, 4):
                ws = wload(first, 4 + m // 4, self.wfc1_d[l, m:m + 4].rearrange("m p x -> p m x"), 4)
                for mm in range(4):
                    ps = self.nps()
                    for kt in range(KT):
                        k.op("pe", lambda e: e.matmul(ps[:, :W], lhsT=ws[:, mm * 2048 + kt * 128:mm * 2048 + (kt + 1) * 128], rhs=mh[:, kt, :W],
                                                      start=(kt == 0), stop=(kt == KT - 1)),
                             reads=[ws, mh], writes=[ps], inc=(kt == KT - 1))
                    tm = tmps[mm % 2]
                    k.op("act", lambda e: e.activation(out=tm[:, :W], in_=ps[:, :W], func=AF.Relu), reads=[ps], writes=[tm])
                    k.op("pool", lambda e: e.tensor_tensor(out=ab[:, m + mm, :W], in0=tm[:, :W], in1=tm[:, :W], op=ALU.mult), reads=[tm], writes=[ab])
            for m in range(16):
                ws = wload(first, 20 + m, self.wfc2_d[l, m])
                ps = self.nps()
                for kt in range(64):
                    k.op("pe", lambda e: e.matmul(ps[:, :W], lhsT=ws[:, kt * 128:(kt + 1) * 128], rhs=ab[:, kt, :W], start=(kt == 0), stop=(kt == 63)),
                         reads=[ws, ab], writes=[ps], inc=(kt == 63))
                k.op("act", lambda e: e.activation(out=ob[:, m, :W], in_=ps[:, :W], func=AF.Copy), reads=[ps], writes=[ob])
            self.residual(x, ob, W, sq, ab, rstd, l, 1, col, tmps)
            if last:
                k.dma("sp", self.yT_d[:, :, t0 - CTX:t0 - CTX + W].rearrange("k p t -> p k t"), x[:, :, :W], reads=[x],
                      writes=[k.tok(("y", ci))], is_out=True)
            else:
                k.dma("sp", self.x1_d[:, :, t0:t0 + W].rearrange("k p t -> p k t"), x[:, :, :W], reads=[x], writes=[k.tok(("x", l + 1, ci))])
        k.barrier()
        st.close()

    def _rstd_alias(self, src, W, ab, rstd):
        k = self.k
        sqv = ab.ap[:, 0:KT, :]
        k.op("act", lambda e: e.activation(out=sqv[:, :, :W], in_=src[:, :, :W], func=AF.Square), reads=[src], writes=[ab])
        ps = self.nps()
        for kt in range(KT):
            k.op("pe", lambda e: e.matmul(ps[:, :W], lhsT=self.ones[:, :], rhs=sqv[:, kt, :W], start=(kt == 0), stop=(kt == KT - 1)),
                 reads=[self.ones, ab], writes=[ps], inc=(kt == KT - 1))
        k.op("act", lambda e: e.activation(out=rstd[:, :W], in_=ps[:, :W], func=AF.Sqrt, bias=self.epsb[:, 0:1], scale=1.0 / D),
             reads=[ps, self.epsb], writes=[rstd])
        k.op("dve", lambda e: e.reciprocal(out=rstd[:, :W], in_=rstd[:, :W]), reads=[rstd], writes=[rstd])

    def residual(self, x, ob, W, sq, ab, rstd, l, s, col, tmps):
        k = self.k
        self._rstd_alias(ob, W, ab, rstd)
        for kt in range(KT):
            tmp = tmps[kt % 2]
            k.op("dve", lambda e: e.scalar_tensor_tensor(out=tmp[:, :W], in0=ob[:, kt, :W], scalar=self.G[:, l, s, kt, col:col + 1],
                                                         in1=rstd[:, :W], op0=ALU.mult, op1=ALU.mult),
                 reads=[ob, rstd, self.G], writes=[tmp])
            k.op("pool", lambda e: e.tensor_tensor(out=x[:, kt, :W], in0=x[:, kt, :W], in1=tmp[:, :W], op=ALU.add), reads=[x, tmp], writes=[x])

    def build(self, upto=None):
        k = self.k
        self.nlayers_total = 2
        self.epsb = k.sb([128, 1], F32, "epsb")
        k.op("pool", lambda e: e.memset(self.epsb[:, :], EPS), writes=[self.epsb])
        self.hpib = k.sb([128, 1], F32, "hpib")
        k.op("pool", lambda e: e.memset(self.hpib[:, :], float(np.pi / 2)), writes=[self.hpib])
        self.phase0()
        for l in range(self.nlayers):
            self.phaseAB(l)
            if upto == "AB":
                break
            self.phaseC(l)
            if upto == "C":
                break
            self.phaseD(l)
            last = (l == 1)
            t_lo = CTX if last else 0
            self.phaseE(l, t_lo)
            self.phaseF(l, t_lo, last)
        k.finish()
        return k.nc


def _fm(v):
    return np.ascontiguousarray(np.asarray(v, np.float32).reshape(-1, 128).T)


def _panels(W):
    K, N = W.shape
    kt, mt = K // 128, N // 128
    return np.ascontiguousarray(W.reshape(kt, 128, mt, 128).transpose(2, 1, 0, 3).reshape(mt, 128, kt * 128))


def _rope_tables():
    nf = 16
    inv = (10000.0 ** (-np.arange(nf, dtype=np.float32) / nf)).astype(np.float32)
    pos = np.arange(LAT)
    rows = (pos // 64).astype(np.float32)
    cols = (pos % 64).astype(np.float32)
    cos = np.ones((64, NTOK), np.float32)
    sin = np.zeros((64, NTOK), np.float32)
    for d in range(64):
        blk = d // 16
        p = rows if blk < 2 else cols
        ang = (p * inv[d % 16]).astype(np.float32)
        cos[d, CTX:] = np.cos(ang)
        sg = -1.0 if blk % 2 == 0 else 1.0
        sin[d, CTX:] = sg * np.sin(ang)
    return np.ascontiguousarray(np.stack([np.concatenate([cos, cos], 0), np.concatenate([sin, sin], 0)], 0))


def prep_shared(inp):
    f = lambda a: np.asarray(a, np.float32)
    w_in = f(inp["w_in"])
    partner = np.array([d + 16 if (d % 32) < 16 else d - 16 for d in range(64)])
    perm = (np.arange(16)[:, None] * 64 + partner[None, :]).reshape(-1)
    wins, wvs = [], []
    for l in range(2):
        W = w_in[l]
        u, q, kk, v, gs, gn = W[:, :1024], W[:, 1024:2048], W[:, 2048:3072], W[:, 3072:4096], W[:, 4096:6144], W[:, 6144:]
        comb = np.concatenate([u, q, q[:, perm], kk, kk[:, perm], gs, gn], axis=1)
        wins.append(_panels(comb))
        wvs.append(np.ascontiguousarray(v.reshape(KT, 128, 2, 512).transpose(2, 1, 0, 3).reshape(2, 128, KT * 512)))
    sh = {
        "wmod": np.stack([_panels(f(inp["w_mod"])[l]) for l in range(2)]),
        "win": np.stack(wins),
        "wv": np.stack(wvs),
        "rope": _rope_tables(),
        "rpb": np.ascontiguousarray(f(inp["na_rpb"])),
        "wval": np.stack([_panels(f(inp["w_ssm_val"])[l]) for l in range(2)]),
        "wglu": np.stack([_panels(f(inp["w_ssm_glu"])[l]) for l in range(2)]),
        "wna": np.stack([_panels(f(inp["w_na_proj"])[l]) for l in range(2)]),
        "wout": np.stack([_panels(f(inp["w_out"])[l]) for l in range(2)]),
        "wfc1": np.stack([_panels(f(inp["w_fc1"])[l]) for l in range(2)]),
        "wfc2": np.stack([_panels(f(inp["w_fc2"])[l]) for l in range(2)]),
        "ident": np.eye(128, dtype=np.float32),
    }
    pm = np.zeros((128, 128), np.float32)
    for mcol in range(128):
        pm[(mcol // 64) * 64 + partner[mcol % 64], mcol] = 1.0
    sh["perm"] = pm
    jj = np.arange(128) // 16
    cc = np.arange(128) % 16
    s5c = np.zeros((128, 400), np.float32)
    for kk in range(8):
        s5c[:, kk] = (jj == kk)
        s5c[:, 8 + kk] = (7 - jj == kk)
    s5c[:, 16:144] = (jj[:, None] <= jj[None, :])
    s5c[:, 144:272] = (jj[:, None] >= jj[None, :])
    s5c[:, 272:400] = np.eye(128)
    sel = np.zeros((128, 8, 8, 128), np.float32)
    selT = np.zeros((128, 8, 8, 128), np.float32)
    for gi in range(8):
        for j in range(8):
            for c in range(16):
                sel[gi * 16 + c, gi, j, j * 16 + c] = 1.0
                selT[j * 16 + c, gi, j, gi * 16 + c] = 1.0
    sh["s5c"] = s5c
    sh["sel"] = sel.reshape(128, 8192)
    sh["selT"] = selT.reshape(128, 8192)
    lre, lim, ldt = f(inp["ssm_lam_re"]), f(inp["ssm_lam_im"]), f(inp["ssm_log_dt"])
    bre, bim, cre, cim = f(inp["ssm_b_re"]), f(inp["ssm_b_im"]), f(inp["ssm_c_re"]), f(inp["ssm_c_im"])
    sC = np.empty((2, 2, 7, 128, 512), np.float32)
    for l in range(2):
        for d in range(2):
            def cl(A):
                t = A.reshape(2, 32, 64).transpose(0, 2, 1)
                return np.repeat(t.reshape(128, 32), 16, axis=1)
            sC[l, d, 0] = cl(lre[l, d])
            sC[l, d, 1] = cl(lim[l, d])
            sC[l, d, 2] = cl(np.repeat(ldt[l, d][:, None], 64, axis=1))
            for a, Cc in ((3, cre), (4, cim)):
                sC[l, d, a] = Cc[l, d].reshape(2, 32, 16, 64).transpose(0, 3, 1, 2).reshape(128, 512)
            for a, B in ((5, bre), (6, bim)):
                sC[l, d, a] = B[l, d].reshape(2, 32, 64, 16).transpose(0, 2, 1, 3).reshape(128, 512)
    sh["sC"] = sC
    dd = f(inp["ssm_d"])
    sh["drep"] = np.stack([np.tile(dd[l].reshape(64, 16).T, (8, 1)) for l in range(2)]).astype(np.float32)
    return sh


def prep_core(inp, b):
    f = lambda a: np.asarray(a, np.float32)
    X = np.concatenate([f(inp["ctx"])[b], f(inp["x"])[b]], axis=0)
    xT = np.ascontiguousarray(X.T.reshape(KT, 128, NTOK))
    cols = []
    for l in range(2):
        cols += [_fm(inp["g_pre_mix"][l]), _fm(inp["g_post_mix"][l]), _fm(inp["g_pre_mlp"][l]), _fm(inp["g_post_mlp"][l]),
                 _fm(inp["b_mod"][l]), _fm(inp["ssm_d"][l])]
    cols += [_fm(inp["c"][b]), _fm(inp["c_ctx"])]
    vecs = np.ascontiguousarray(np.concatenate(cols, axis=1))
    assert vecs.shape == (128, NVEC)
    return {"xT": xT, "vecs": vecs}


def _cmul(k, eng, o_re, o_im, a_re, a_im, b_re, b_im, t1, t2, rd=(), wr=()):
    E = k.op
    E(eng, lambda e: e.tensor_tensor(out=t1, in0=a_re, in1=b_re, op=ALU.mult), reads=rd, writes=wr)
    E(eng, lambda e: e.tensor_tensor(out=t2, in0=a_im, in1=b_im, op=ALU.mult), reads=rd, writes=wr)
    E(eng, lambda e: e.tensor_tensor(out=t2, in0=t1, in1=t2, op=ALU.subtract), reads=rd, writes=wr)
    E(eng, lambda e: e.tensor_tensor(out=t1, in0=a_re, in1=b_im, op=ALU.mult), reads=rd, writes=wr)
    E(eng, lambda e: e.tensor_tensor(out=o_im, in0=a_im, in1=b_re, op=ALU.mult), reads=rd, writes=wr)
    E(eng, lambda e: e.tensor_tensor(out=o_im, in0=o_im, in1=t1, op=ALU.add), reads=rd, writes=wr)
    E(eng, lambda e: e.tensor_copy(out=o_re, in_=t2), reads=rd, writes=wr)


def _lam_base(P, st, src, F, eng):
    k = P.k
    nb = lambda nm: k.sb([128, F], F32, nm, st)
    W = Buf(None, "Wtok")
    tok = [W, src]
    dt, ar, ai, c, s, m, t1, t2 = [nb(n) for n in ("dt", "ar", "ai", "c", "s", "m", "t1", "t2")]
    o = {n: nb(n) for n in ("lbr", "lbi", "lir", "lii", "gr", "gi")}
    E = lambda en, fn: k.op(en, fn, reads=tok + [P.hpib], writes=[W])
    E("act", lambda e: e.activation(out=dt[:, :], in_=src[:, 2, :], func=AF.Exp))
    E(eng, lambda e: e.tensor_tensor(out=ar[:, :], in0=src[:, 0, :], in1=dt[:, :], op=ALU.mult))
    E(eng, lambda e: e.tensor_tensor(out=ai[:, :], in0=src[:, 1, :], in1=dt[:, :], op=ALU.mult))
    E("act", lambda e: e.activation(out=s[:, :], in_=ai[:, :], func=AF.Sin, scale=1.0 / 16))
    E("act", lambda e: e.activation(out=c[:, :], in_=ai[:, :], func=AF.Sin, scale=1.0 / 16, bias=P.hpib[:, 0:1]))
    for _ in range(4):
        E(eng, lambda e: e.tensor_tensor(out=t1[:, :], in0=c[:, :], in1=s[:, :], op=ALU.mult))
        E(eng, lambda e: e.tensor_tensor(out=c[:, :], in0=c[:, :], in1=c[:, :], op=ALU.mult))
        E(eng, lambda e: e.tensor_tensor(out=s[:, :], in0=s[:, :], in1=s[:, :], op=ALU.mult))
        E(eng, lambda e: e.tensor_tensor(out=c[:, :], in0=c[:, :], in1=s[:, :], op=ALU.subtract))
        E(eng, lambda e: e.tensor_scalar(out=s[:, :], in0=t1[:, :], scalar1=2.0, scalar2=None, op0=ALU.mult))
    E("act", lambda e: e.activation(out=m[:, :], in_=ar[:, :], func=AF.Exp))
    E(eng, lambda e: e.tensor_tensor(out=o["lbr"][:, :], in0=m[:, :], in1=c[:, :], op=ALU.mult))
    E(eng, lambda e: e.tensor_tensor(out=o["lbi"][:, :], in0=m[:, :], in1=s[:, :], op=ALU.mult))
    E("act", lambda e: e.activation(out=m[:, :], in_=ar[:, :], func=AF.Exp, scale=-1.0))
    E(eng, lambda e: e.tensor_tensor(out=o["lir"][:, :], in0=m[:, :], in1=c[:, :], op=ALU.mult))
    E("dve", lambda e: e.scalar_tensor_tensor(out=o["lii"][:, :], in0=m[:, :], scalar=-1.0, in1=s[:, :], op0=ALU.mult, op1=ALU.mult))
    lr, li = src[:, 0, :], src[:, 1, :]
    E(eng, lambda e: e.tensor_tensor(out=t1[:, :], in0=lr, in1=lr, op=ALU.mult))
    E(eng, lambda e: e.tensor_tensor(out=t2[:, :], in0=li, in1=li, op=ALU.mult))
    E(eng, lambda e: e.tensor_tensor(out=t1[:, :], in0=t1[:, :], in1=t2[:, :], op=ALU.add))
    E("dve", lambda e: e.reciprocal(out=t1[:, :], in_=t1[:, :]))
    E(eng, lambda e: e.tensor_scalar(out=c[:, :], in0=o["lbr"][:, :], scalar1=-1.0, scalar2=None, op0=ALU.add))
    E(eng, lambda e: e.tensor_tensor(out=t2[:, :], in0=c[:, :], in1=lr, op=ALU.mult))
    E(eng, lambda e: e.tensor_tensor(out=s[:, :], in0=o["lbi"][:, :], in1=li, op=ALU.mult))
    E(eng, lambda e: e.tensor_tensor(out=t2[:, :], in0=t2[:, :], in1=s[:, :], op=ALU.add))
    E(eng, lambda e: e.tensor_tensor(out=o["gr"][:, :], in0=t2[:, :], in1=t1[:, :], op=ALU.mult))
    E(eng, lambda e: e.tensor_tensor(out=t2[:, :], in0=o["lbi"][:, :], in1=lr, op=ALU.mult))
    E(eng, lambda e: e.tensor_tensor(out=s[:, :], in0=c[:, :], in1=li, op=ALU.mult))
    E(eng, lambda e: e.tensor_tensor(out=t2[:, :], in0=t2[:, :], in1=s[:, :], op=ALU.subtract))
    E(eng, lambda e: e.tensor_tensor(out=o["gi"][:, :], in0=t2[:, :], in1=t1[:, :], op=ALU.mult))
    return o, W, (t1, t2, c, s, m, dt)


def phaseD(self, l):
    k = self.k
    st = contextlib.ExitStack()
    NG = 64
    R = k.sb([128, 17408], BF16, "Rraw", st)
    self._R = R
    BS = [Buf(R.ap[:, d * 8192:(d + 1) * 8192].rearrange("q (g m) -> q g m", m=128), "BS%d" % d) for d in range(2)]
    T = k.sb([128, NG, 128], BF16, "Tg", st)
    CS = [k.sb([128, 2, 32, 128], BF16, "CS%d" % d, st) for d in range(2)]
    MU = k.sb([128, 2, 2, 64], F32, "MU", st)
    cst = k.sb([128, 400], F32, "s5c", st)
    k.dma("sp", cst[:, :], self.s5c_d[:, :], writes=[cst])
    Drep = k.sb([128, NG], F32, "Drep", st)
    k.dma("sp", Drep[:, :], self.drep_d[l], writes=[Drep])
    F = NG * 64
    for d in range(2):
        st2 = contextlib.ExitStack()
        Fc = 512
        src = k.sb([128, 7, Fc], F32, "srcC", st2)
        for a in range(7):
            k.dma("sp", src[:, a, :], self.sC_d[l, d, a], writes=[src])
        LT = k.sb([128, 2, 32, 128], BF16, "LT", st2)
        LB = k.sb([128, 2, 32, 128], BF16, "LB", st2)
        o, W, (t1, t2, c, s, m, dt) = _lam_base(self, st2, src, Fc, "dve")
        tok = [W, src]
        E = lambda en, fn, wr=(): k.op(en, fn, reads=tok, writes=[W] + list(wr))
        bbr, bbi = k.sb([128, Fc], F32, "bbr", st2), k.sb([128, Fc], F32, "bbi", st2)
        _cmul(k, "dve", bbr[:, :], bbi[:, :], o["gr"][:, :], o["gi"][:, :], src[:, 5, :], src[:, 6, :], t1[:, :], t2[:, :], rd=tok, wr=[W])
        pr, pi = o["gr"], o["gi"]
        qr, qi = k.sb([128, Fc], F32, "qr", st2), k.sb([128, Fc], F32, "qi", st2)
        E("dve", lambda e: e.tensor_copy(out=pr[:, :], in_=o["lbr"][:, :]))
        E("dve", lambda e: e.tensor_copy(out=pi[:, :], in_=o["lbi"][:, :]))
        E("dve", lambda e: e.tensor_copy(out=qr[:, :], in_=o["lir"][:, :]))
        E("dve", lambda e: e.tensor_copy(out=qi[:, :], in_=o["lii"][:, :]))
        cs5 = CS[d].ap.rearrange("q r g (j c) -> q r g j c", c=16)
        lt5 = LT.ap.rearrange("q r g (j c) -> q r g j c", c=16)
        lb5 = LB.ap.rearrange("q r g (j c) -> q r g j c", c=16)
        g3 = lambda ap: ap.rearrange("q (g c) -> q g c", c=16)
        jb0 = 7 if d == 0 else 0
        E("act", lambda e: e.activation(out=lb5[:, 0, :, jb0, :], in_=g3(bbr[:, :]), func=AF.Copy), wr=[LB])
        E("act", lambda e: e.activation(out=lb5[:, 1, :, jb0, :], in_=g3(bbi[:, :]), func=AF.Copy), wr=[LB])
        for kk in range(1, 9):
            if kk > 1:
                _cmul(k, "dve", pr[:, :], pi[:, :], pr[:, :], pi[:, :], o["lbr"][:, :], o["lbi"][:, :], t1[:, :], t2[:, :], rd=tok, wr=[W])
                _cmul(k, "dve", qr[:, :], qi[:, :], qr[:, :], qi[:, :], o["lir"][:, :], o["lii"][:, :], t1[:, :], t2[:, :], rd=tok, wr=[W])
            jj = kk - 1 if d == 0 else 8 - kk
            _cmul(k, "dve", c[:, :], s[:, :], src[:, 3, :], src[:, 4, :], pr[:, :], pi[:, :], t1[:, :], t2[:, :], rd=tok, wr=[W])
            E("act", lambda e: e.activation(out=cs5[:, 0, :, jj, :], in_=g3(c[:, :]), func=AF.Copy), wr=[CS[d]])
            E("act", lambda e: e.activation(out=cs5[:, 1, :, jj, :], in_=g3(s[:, :]), func=AF.Copy, scale=-1.0), wr=[CS[d]])
            _cmul(k, "dve", c[:, :], s[:, :], bbr[:, :], bbi[:, :], qr[:, :], qi[:, :], t1[:, :], t2[:, :], rd=tok, wr=[W])
            E("act", lambda e: e.activation(out=lt5[:, 0, :, jj, :], in_=g3(c[:, :]), func=AF.Copy), wr=[LT])
            E("act", lambda e: e.activation(out=lt5[:, 1, :, jj, :], in_=g3(s[:, :]), func=AF.Copy), wr=[LT])
            if kk <= 7:
                jb = 7 - kk if d == 0 else kk
                _cmul(k, "dve", c[:, :], s[:, :], bbr[:, :], bbi[:, :], pr[:, :], pi[:, :], t1[:, :], t2[:, :], rd=tok, wr=[W])
                E("act", lambda e: e.activation(out=lb5[:, 0, :, jb, :], in_=g3(c[:, :]), func=AF.Copy), wr=[LB])
                E("act", lambda e: e.activation(out=lb5[:, 1, :, jb, :], in_=g3(s[:, :]), func=AF.Copy), wr=[LB])
        prg = g3(pr[:, :])[:, :, 0]
        pig = g3(pi[:, :])[:, :, 0]
        gsl = slice(d * 32, d * 32 + 32)
        E("dve", lambda e: e.tensor_copy(out=MU[:, 0, 0, gsl], in_=prg), wr=[MU])
        E("dve", lambda e: e.tensor_copy(out=MU[:, 0, 1, gsl], in_=prg), wr=[MU])
        E("dve", lambda e: e.tensor_scalar(out=MU[:, 1, 0, gsl], in0=pig, scalar1=-1.0, scalar2=None, op0=ALU.mult), wr=[MU])
        E("dve", lambda e: e.tensor_copy(out=MU[:, 1, 1, gsl], in_=pig), wr=[MU])
        for g in range(64):
            hb, gl = (g // 32) * 64, g % 32
            pT = self.psb[g % 2]
            for ri in range(2):
                k.op("pe", lambda e: e.transpose(out=pT[:, ri * 64:(ri + 1) * 64], in_=LB[hb:hb + 64, ri, gl, :], identity=self.identb[hb:hb + 64, hb:hb + 64]),
                     reads=[LB, self.identb], writes=[pT], inc=(ri == 1))
            if g % 2:
                k.op("act", lambda e: e.activation(out=BS[d][:, g, :], in_=pT[:, 0:128], func=AF.Copy), reads=[pT], writes=[BS[d]])
            else:
                k.op("dve", lambda e: e.tensor_copy(out=BS[d][:, g, :], in_=pT[:, 0:128]), reads=[pT], writes=[BS[d]])
        tf = [k.sb([128, 128], F32, "tfT%d" % i, st2) for i in range(2)]
        mask = cst[:, 16 + 128 * d:16 + 128 * d + 128]
        ident = cst[:, 272:400]
        for g in range(64):
            hb, gl = (g // 32) * 64, g % 32
            ps = self.nps()
            for ri in range(2):
                k.op("pe", lambda e: e.matmul(ps[:, 0:128], lhsT=LT[hb:hb + 64, ri, gl, :], rhs=CS[d][hb:hb + 64, ri, gl, :],
                                              start=(ri == 0), stop=(ri == 1)),
                     reads=[LT, CS[d]], writes=[ps], inc=(ri == 1))
            t = tf[g % 2]
            k.op("dve", lambda e: e.tensor_tensor(out=t[:, :], in0=ps[:, 0:128], in1=mask, op=ALU.mult), reads=[ps, cst], writes=[t])
            if d == 0:
                k.op("dve", lambda e: e.scalar_tensor_tensor(out=T[:, g, :], in0=ident, scalar=Drep[:, g:g + 1], in1=t[:, :],
                                                              op0=ALU.mult, op1=ALU.add), reads=[t, cst, Drep], writes=[T])
            else:
                k.op("pool", lambda e: e.tensor_tensor(out=T[:, g, :], in0=T[:, g, :], in1=t[:, :], op=ALU.add), reads=[t, T], writes=[T])
        k.barrier()
        st2.close()
    self._s5_run(l, BS, T, CS, MU, st)
    k.barrier()
    st.close()


def _s5_run(self, l, BS, T, CS, MU, st):
    k = self.k
    NG = 64
    Xall = k.sb([128, NG, NCH8], BF16, "Xall", st)
    st1 = contextlib.ExitStack()
    uT = k.sb([128, 8, NTOK], BF16, "uT", st1)
    Sel = k.sb([128, 8, 8, 128], BF16, "Sel", st1)
    for h in range(8):
        k.dma("sp", uT[:, h, :], self.u_d[h], reads=[k.tok(("u", l, h, ci)) for ci in range(5)], writes=[uT])
    k.dma("pool", Sel[:, :, :, :].rearrange("q a b m -> q (a b m)"), self.sel_d[:, :], writes=[Sel])
    for g in range(NG):
        ps = self.nps()
        for j in range(8):
            k.op("pe", lambda e: e.matmul(ps[:, 0:NCH8], lhsT=Sel[:, g % 8, j, :], rhs=uT[:, g // 8, j:NTOK:8], start=(j == 0), stop=(j == 7)),
                 reads=[Sel, uT], writes=[ps], inc=(j == 7))
        k.op("act" if g % 2 else "dve", (lambda e: e.activation(out=Xall[:, g, :], in_=ps[:, 0:NCH8], func=AF.Copy)) if g % 2 else
             (lambda e: e.tensor_copy(out=Xall[:, g, :], in_=ps[:, 0:NCH8])), reads=[ps], writes=[Xall])
    k.barrier()
    st1.close()
    S = [k.sb([128, 2, 32, NCH8], BF16, "S%d" % d, st) for d in range(2)]
    Srd = [Buf(S[d].ap, "Srd%d" % d) for d in range(2)]
    Swr = [Buf(S[d].ap, "Swr%d" % d) for d in range(2)]
    for d in range(2):
        for g in range(NG):
            hb, gl = (g // 32) * 64, g % 32
            ps = self.nps()
            ps2 = self.nps()
            for ri in range(2):
                k.op("pe", lambda e: e.matmul(ps[hb:hb + 64, ri * 256:(ri + 1) * 256], lhsT=BS[d][:, g, ri * 64:(ri + 1) * 64], rhs=Xall[:, g, 0:256],
                                              start=True, stop=True),
                     reads=[BS[d], Xall], writes=[ps], inc=(ri == 1))
            for ri in range(2):
                k.op("pe", lambda e: e.matmul(ps2[hb:hb + 64, ri * 32:(ri + 1) * 32], lhsT=BS[d][:, g, ri * 64:(ri + 1) * 64], rhs=Xall[:, g, 256:NCH8],
                                              start=True, stop=True),
                     reads=[BS[d], Xall], writes=[ps2], inc=(ri == 1))
            if g % 2:
                k.op("act", lambda e: e.activation(out=S[d][hb:hb + 64, :, gl, 0:256], in_=ps[hb:hb + 64, 0:512].rearrange("q (r n) -> q r n", r=2), func=AF.Copy),
                     reads=[ps], writes=[Srd[d]])
                k.op("act", lambda e: e.activation(out=S[d][hb:hb + 64, :, gl, 256:NCH8], in_=ps2[hb:hb + 64, 0:64].rearrange("q (r n) -> q r n", r=2), func=AF.Copy),
                     reads=[ps2], writes=[Srd[d]])
            else:
                k.op("dve", lambda e: e.tensor_copy(out=S[d][hb:hb + 64, :, gl, 0:256], in_=ps[hb:hb + 64, 0:512].rearrange("q (r n) -> q r n", r=2)),
                     reads=[ps], writes=[Srd[d]])
                k.op("dve", lambda e: e.tensor_copy(out=S[d][hb:hb + 64, :, gl, 256:NCH8], in_=ps2[hb:hb + 64, 0:64].rearrange("q (r n) -> q r n", r=2)),
                     reads=[ps2], writes=[Srd[d]])
    H = [k.sb([128, 2, 64], F32, "H%d" % i, st) for i in range(2)]
    t1 = k.sb([128, 2, 64], F32, "rt1", st)
    t2 = k.sb([128, 2, 64], F32, "rt2", st)
    k.op("pool", lambda e: e.memset(H[0][:, :, :], 0.0), writes=[H[0]])
    for i in range(NCH8):
        nf = i
        nb = (31 - i) if i < 32 else (319 - i)
        Ho, Hn = H[i % 2], H[(i + 1) % 2]
        k.op("dve", lambda e: e.tensor_tensor(out=t1[:, :, :], in0=MU[:, 0, :, :], in1=Ho[:, :, :], op=ALU.mult), reads=[MU, Ho], writes=[t1])
        k.op("dve", lambda e: e.tensor_tensor(out=t2[:, 0, :], in0=MU[:, 1, 0, :], in1=Ho[:, 1, :], op=ALU.mult), reads=[MU, Ho], writes=[t2])
        k.op("dve", lambda e: e.tensor_tensor(out=t2[:, 1, :], in0=MU[:, 1, 1, :], in1=Ho[:, 0, :], op=ALU.mult), reads=[MU, Ho], writes=[t2])
        k.op("dve", lambda e: e.tensor_tensor(out=t1[:, :, :], in0=t1[:, :, :], in1=t2[:, :, :], op=ALU.add), reads=[t1, t2], writes=[t1])
        stk = Buf(None, "stk")
        k.op("dve", lambda e: e.tensor_tensor(out=Hn[:, :, 0:32], in0=t1[:, :, 0:32], in1=S[0][:, :, :, nf], op=ALU.add), reads=[t1, Srd[0]], writes=[Hn, stk])
        k.op("dve", lambda e: e.tensor_tensor(out=Hn[:, :, 32:64], in0=t1[:, :, 32:64], in1=S[1][:, :, :, nb], op=ALU.add), reads=[t1, Srd[1]], writes=[Hn, stk])
        k.op("act", lambda e: e.activation(out=S[0][:, :, :, nf], in_=Ho[:, :, 0:32], func=AF.Copy), reads=[Ho, stk], writes=[Swr[0]])
        k.op("act", lambda e: e.activation(out=S[1][:, :, :, nb], in_=Ho[:, :, 32:64], func=AF.Copy), reads=[Ho, stk], writes=[Swr[1]])
    k.barrier()
    R = self._R
    SelT = Buf(R.ap[:, 0:8192].rearrange("q (a b m) -> q a b m", a=8, b=8), "SelT")
    k.dma("pool", R.ap[:, 0:8192], self.selT_d[:, :], writes=[SelT])
    Ag = [Buf(R.ap[:, 8192 + i * 2304:8192 + (i + 1) * 2304].rearrange("q (a n) -> q a n", a=8), "Ag%d" % i) for i in range(2)]
    ast = [Buf(R.ap[:, 12800 + i * 2304:12800 + (i + 1) * 2304], "ast%d" % i) for i in range(2)]
    ga = [k.sb([128, NCH8], F32, "ga%d" % i, st) for i in range(2)]
    gb = [k.sb([128, NCH8], F32, "gb%d" % i, st) for i in range(2)]
    for tl in range(8):
        A = Ag[tl % 2]
        for gi in range(8):
            g = tl * 8 + gi
            hb, gl = (g // 32) * 64, g % 32
            ps = self.nps()
            k.op("pe", lambda e: e.matmul(ps[:, 0:NCH8], lhsT=T[:, g, :], rhs=Xall[:, g, :], start=True, stop=False), reads=[T, Xall], writes=[ps], inc=False)
            for d in range(2):
                for ri in range(2):
                    last = (d == 1 and ri == 1)
                    k.op("pe", lambda e: e.matmul(ps[:, 0:NCH8], lhsT=CS[d][hb:hb + 64, ri, gl, :], rhs=S[d][hb:hb + 64, ri, gl, :], start=False, stop=last),
                         reads=[CS[d], Swr[d]], writes=[ps], inc=last)
            a_, b_ = ga[gi % 2], gb[gi % 2]
            k.op("act", lambda e: e.activation(out=a_[:, :], in_=ps[:, 0:NCH8], func=AF.Square), reads=[ps], writes=[a_])
            k.op("dve", lambda e: e.tensor_scalar(out=a_[:, :], in0=a_[:, :], scalar1=0.044715, scalar2=1.0, op0=ALU.mult, op1=ALU.add), reads=[a_], writes=[a_])
            k.op("dve", lambda e: e.tensor_tensor(out=b_[:, :], in0=ps[:, 0:NCH8], in1=a_[:, :], op=ALU.mult), reads=[ps, a_], writes=[b_])
            k.op("act", lambda e: e.activation(out=b_[:, :], in_=b_[:, :], func=AF.Sigmoid, scale=1.5957691216057308), reads=[b_], writes=[b_])
            k.op("dve", lambda e: e.tensor_tensor(out=A[:, gi, :], in0=ps[:, 0:NCH8], in1=b_[:, :], op=ALU.mult), reads=[ps, b_], writes=[A])
        o = ast[tl % 2]
        for j in range(8):
            ps = self.nps()
            for gi in range(8):
                k.op("pe", lambda e: e.matmul(ps[:, 0:NCH8], lhsT=SelT[:, gi, j, :], rhs=A[:, gi, :], start=(gi == 0), stop=(gi == 7)),
                     reads=[SelT, A], writes=[ps], inc=(gi == 7))
            k.op("act" if j % 2 else "dve", (lambda e: e.activation(out=o[:, j:NTOK:8], in_=ps[:, 0:NCH8], func=AF.Copy)) if j % 2 else
                 (lambda e: e.tensor_copy(out=o[:, j:NTOK:8], in_=ps[:, 0:NCH8])), reads=[ps], writes=[o])
        k.dma("sp", self.a_d[tl], o[:, :], reads=[o], writes=[k.tok(("a", l, tl))])


def _psl(self, ps, hb, ri):
    return ps[hb:hb + 64, ri * 256:(ri + 1) * 256]


Prog.phaseD = phaseD
Prog._s5_run = _s5_run
Prog._psl = _psl


_CACHE = {}


def kernel(**inputs):
    if "nc" not in _CACHE:
        _CACHE["nc"] = Prog().build()
    nc = _CACHE["nc"]
    sh = prep_shared(inputs)
    in_maps = []
    for core in range(8):
        m = dict(sh)
        m.update(prep_core(inputs, core % 4))
        in_maps.append(m)
    res = run_bass_kernel_spmd(nc, in_maps, core_ids=list(range(8)))
    out = np.empty((4, LAT, D), np.float32)
    for b in range(4):
        yT = np.asarray(res.results[b]["yT"]).reshape(D, LAT)
        out[b] = yT.T
    return out
```

```python
import contextlib
import math
import numpy as np
import concourse.bass as bass
import concourse.mybir as mybir
from concourse.bass_utils import run_bass_kernel_spmd

F32 = mybir.dt.float32
BF16 = mybir.dt.bfloat16
AF = mybir.ActivationFunctionType
ALU = mybir.AluOpType
AX = mybir.AxisListType

D = 2048
KT = 16
NTOK = 2304
CTX = 256
LAT = 2048
DFF = 8192
EPS = 1e-6
CHUNKS = [(0, 256), (256, 512), (768, 512), (1280, 512), (1792, 512)]
NEG = -30000.0
NCH8 = NTOK // 8
VEC_L = 168
C_OFF = 336
NVEC = 368


class Buf:
    __slots__ = ("ap", "w", "r", "name")

    def __init__(self, ap=None, name=""):
        self.ap = ap
        self.w = []
        self.r = []
        self.name = name

    def __getitem__(self, idx):
        return self.ap[idx]


class KB:
    NTICK = 40

    def __init__(self):
        self.nc = bass.Bass("TRN2", target_bir_lowering=False)
        nc = self.nc
        self.es = contextlib.ExitStack()
        self.eng = {}
        self.semid = {}
        for name, e in (("pe", nc.tensor), ("act", nc.scalar), ("dve", nc.vector),
                        ("pool", nc.gpsimd), ("sp", nc.sync)):
            sem = self.es.enter_context(nc.semaphore("s_" + name))
            self.eng[name] = [e, sem, 0]
        self.known = {n: {} for n in self.eng}
        self.ticks = []
        for i in range(self.NTICK):
            sem = self.es.enter_context(nc.semaphore("tk%d" % i))
            self.ticks.append([sem, 0, "tk%d" % i])
        self.tki = {"sp": 0, "pool": 0}
        self.tkr = {"sp": (0, 26), "pool": (26, 14)}
        self.out_evs = []
        self.phase_stack = None
        self.dbufs = {}
        self.uid = 0

    def sb(self, shape, dt, name=None, stack=None):
        self.uid += 1
        nm = (name or "t") + "_%d" % self.uid
        t = (stack or self.es).enter_context(self.nc.sbuf_tensor(nm, list(shape), dt))
        return Buf(t, nm)

    def pst(self, shape, dt, name=None, stack=None):
        self.uid += 1
        nm = (name or "p") + "_%d" % self.uid
        t = (stack or self.es).enter_context(self.nc.psum_tensor(nm, list(shape), dt))
        return Buf(t, nm)

    def dram(self, name, shape, dt, kind="Internal"):
        return self.nc.dram_tensor(name, list(shape), dt, kind=kind).ap()

    def tok(self, key):
        b = self.dbufs.get(key)
        if b is None:
            b = Buf(None, str(key))
            self.dbufs[key] = b
        return b

    def _wait(self, en, evs):
        e = self.eng[en][0]
        kn = self.known[en]
        best = {}
        for (key, sem, val) in evs:
            if val > kn.get(key, 0) and val > best.get(key, (None, 0))[1]:
                best[key] = (sem, val)
        for key, (sem, val) in best.items():
            e.wait_ge(sem, val)
            kn[key] = val

    def _deps(self, en, reads, writes):
        evs = []
        for b in reads:
            evs += b.w
        for b in writes:
            evs += b.w
            evs += b.r
        if en == "pe":
            evs = [x for x in evs if x[0] != "pe"]
        return evs

    def _record(self, ev, reads, writes):
        for b in reads:
            b.r = [x for x in b.r if x[0] != ev[0]]
            b.r.append(ev)
        for b in writes:
            b.w = [ev]
            b.r = []

    def op(self, en, fn, reads=(), writes=(), inc=True):
        self._wait(en, self._deps(en, reads, writes))
        ent = self.eng[en]
        ins = fn(ent[0])
        if inc:
            ent[2] += 1
            ins.then_inc(ent[1], 1)
            ev = (en, ent[1], ent[2])
            self._record(ev, reads, writes)
        return ins

    def dma(self, q, out_ap, in_ap, reads=(), writes=(), is_out=False):
        base, cnt = self.tkr[q]
        tk = self.ticks[base + self.tki[q]]
        self.tki[q] = (self.tki[q] + 1) % cnt
        evs = self._deps(q, reads, writes)
        if tk[1] > 0:
            evs.append((tk[2], tk[0], tk[1]))
        self._wait(q, evs)
        ins = self.eng[q][0].dma_start(out=out_ap, in_=in_ap)
        tk[1] += 16
        ins.then_inc(tk[0], 16)
        ev = (tk[2], tk[0], tk[1])
        self._record(ev, reads, writes)
        if is_out:
            self.out_evs.append(ev)
        return ev

    def pe_drain(self):
        ent = self.eng["pe"]
        if ent[2] > self.known["pe"].get("pe", 0):
            ent[0].wait_ge(ent[1], ent[2])
            self.known["pe"]["pe"] = ent[2]

    def barrier(self):
        evs = []
        for n, ent in self.eng.items():
            if ent[2] > 0:
                evs.append((n, ent[1], ent[2]))
        for tk in self.ticks:
            if tk[1] > 0:
                evs.append((tk[2], tk[0], tk[1]))
        for n in self.eng:
            self._wait(n, [x for x in evs if x[0] != n])

    def finish(self):
        self.barrier()
        self.es.close()


class Prog:
    def __init__(self, dbg=None, nlayers=2):
        self.k = KB()
        self.dbg = dbg or {}
        self.nlayers = nlayers
        k = self.k
        self.xT_d = k.dram("xT", [KT, 128, NTOK], F32, "ExternalInput")
        self.vecs_d = k.dram("vecs", [128, NVEC], F32, "ExternalInput")
        self.wmod_d = k.dram("wmod", [2, 96, 128, KT * 128], F32, "ExternalInput")
        self.win_d = k.dram("win", [2, 72, 128, KT * 128], F32, "ExternalInput")
        self.wv_d = k.dram("wv", [2, 2, 128, KT * 512], F32, "ExternalInput")
        self.rope_d = k.dram("rope", [2, 128, NTOK], F32, "ExternalInput")
        self.rpb_d = k.dram("rpb", [2, 16, 15, 31], F32, "ExternalInput")
        self.wval_d = k.dram("wval", [2, 16, 128, 8 * 128], F32, "ExternalInput")
        self.wglu_d = k.dram("wglu", [2, 16, 128, 8 * 128], F32, "ExternalInput")
        self.wna_d = k.dram("wna", [2, 16, 128, 8 * 128], F32, "ExternalInput")
        self.wout_d = k.dram("wout", [2, 16, 128, KT * 128], F32, "ExternalInput")
        self.wfc1_d = k.dram("wfc1", [2, 64, 128, KT * 128], F32, "ExternalInput")
        self.wfc2_d = k.dram("wfc2", [2, 16, 128, 64 * 128], F32, "ExternalInput")
        self.s5c_d = k.dram("s5c", [128, 400], F32, "ExternalInput")
        self.drep_d = k.dram("drep", [2, 128, 64], F32, "ExternalInput")
        self.sC_d = k.dram("sC", [2, 2, 7, 128, 512], F32, "ExternalInput")
        self.sel_d = k.dram("sel", [128, 8192], F32, "ExternalInput")
        self.selT_d = k.dram("selT", [128, 8192], F32, "ExternalInput")
        self.yT_d = k.dram("yT", [KT, 128, LAT], F32, "ExternalOutput")
        self.x1_d = k.dram("x1s", [KT, 128, NTOK], F32)
        self.u_d = k.dram("us", [8, 128, NTOK], BF16)
        self.q_d = k.dram("qs", [8, 128, NTOK], BF16)
        self.k_d = k.dram("ks", [8, 128, NTOK], BF16)
        self.v_d = k.dram("vs", [2, 18, 128, 1024], BF16)
        self.gs_d = k.dram("gss", [16, 128, NTOK], BF16)
        self.gn_d = k.dram("gns", [16, 128, NTOK], BF16)
        self.o_d = k.dram("os", [8, 128, NTOK], BF16)
        self.a_d = k.dram("as", [8, 128, NTOK], BF16)
        self.m_d = k.dram("ms", [16, 128, NTOK], BF16)
        self.wc_d = k.dram("wcache", [36, 128, 8192], BF16)
        self.vecs = k.sb([128, NVEC], F32, "vecs")
        self.sT = k.sb([128, KT, 2], F32, "sT")
        self.modT = k.sb([128, 2, 96, 2], F32, "modT")
        self.A = k.sb([128, 2, 2, KT, 2], F32, "Amod")
        self.G = k.sb([128, 2, 2, KT, 2], F32, "Gmod")
        self.ones = k.sb([128, 128], BF16, "ones")
        self.identb = k.sb([128, 128], BF16, "identb")
        self.ident_d = k.dram("ident", [128, 128], F32, "ExternalInput")
        self.perm_d = k.dram("perm", [128, 128], F32, "ExternalInput")
        self.permb = k.sb([128, 128], BF16, "permb")
        self.ps = [k.pst([128, 512], F32, "ps%d" % i) for i in range(6)]
        self.psb = [k.pst([128, 1024], BF16, "psb%d" % i) for i in range(2)]
        self.psi = 0
        self.wsl = None
        self.wsi = 0

    def nps(self):
        b = self.ps[self.psi % 6]
        self.psi += 1
        return b

    def alloc_ws(self, st):
        self.wsl = [self.k.sb([128, 8192], BF16, "wsl%d" % i, st) for i in range(3)]

    def nws(self):
        b = self.wsl[self.wsi % 3]
        self.wsi += 1
        return b

    def phase0(self):
        k = self.k
        k.dma("sp", self.vecs[:, :], self.vecs_d[:, :], writes=[self.vecs])
        st = contextlib.ExitStack()
        idf = k.sb([128, 128], F32, "idf", st)
        k.dma("sp", idf[:, :], self.ident_d[:, :], writes=[idf])
        k.op("dve", lambda e: e.tensor_copy(out=self.identb[:, :], in_=idf[:, :]), reads=[idf], writes=[self.identb])
        pmf = k.sb([128, 128], F32, "pmf", st)
        k.dma("sp", pmf[:, :], self.perm_d[:, :], writes=[pmf])
        k.op("dve", lambda e: e.tensor_copy(out=self.permb[:, :], in_=pmf[:, :]), reads=[pmf], writes=[self.permb])
        k.op("pool", lambda e: e.memset(self.ones[:, :], 1.0), writes=[self.ones])
        for col in range(2):
            k.op("act", lambda e: e.activation(out=self.sT[:, :, col], in_=self.vecs[:, C_OFF + 16 * col:C_OFF + 16 * col + 16],
                                               func=AF.Silu), reads=[self.vecs], writes=[self.sT])
        wm = [k.sb([128, KT * 128], F32, "wm%d" % i, st) for i in range(3)]
        for l in range(self.nlayers):
            for mt in range(96):
                slot = wm[mt % 3]
                k.dma("sp", slot[:, :], self.wmod_d[l, mt], writes=[slot])
                ps = self.nps()
                for kt in range(KT):
                    k.op("pe", lambda e: e.matmul(ps[:, 0:2], lhsT=slot[:, kt * 128:(kt + 1) * 128], rhs=self.sT[:, kt, :],
                                                  start=(kt == 0), stop=(kt == KT - 1)),
                         reads=[slot, self.sT], writes=[ps], inc=(kt == KT - 1))
                boff = l * VEC_L + 64 + mt
                k.op("dve", lambda e: e.tensor_scalar(out=self.modT[:, l, mt, :], in0=ps[:, 0:2],
                                                      scalar1=self.vecs[:, boff:boff + 1], scalar2=None, op0=ALU.add),
                     reads=[ps, self.vecs], writes=[self.modT])
            for s in range(2):
                sc0 = 16 + 48 * s
                gt0 = 32 + 48 * s
                gpre = l * VEC_L + (0 if s == 0 else 32)
                gpost = l * VEC_L + (16 if s == 0 else 48)
                for col in range(2):
                    k.op("dve", lambda e: e.scalar_tensor_tensor(out=self.A[:, l, s, :, col], in0=self.modT[:, l, sc0:sc0 + 16, col],
                                                                 scalar=1.0, in1=self.vecs[:, gpre:gpre + 16],
                                                                 op0=ALU.add, op1=ALU.mult),
                         reads=[self.modT, self.vecs], writes=[self.A])
                    k.op("dve", lambda e: e.tensor_tensor(out=self.G[:, l, s, :, col], in0=self.modT[:, l, gt0:gt0 + 16, col],
                                                          in1=self.vecs[:, gpost:gpost + 16], op=ALU.mult),
                         reads=[self.modT, self.vecs], writes=[self.G])
        k.barrier()
        st.close()

    def rstd_of(self, src, W, sq, rstd):
        k = self.k
        k.op("act", lambda e: e.activation(out=sq[:, :, :W], in_=src[:, :, :W], func=AF.Square), reads=[src], writes=[sq])
        ps = self.nps()
        for kt in range(KT):
            k.op("pe", lambda e: e.matmul(ps[:, :W], lhsT=self.ones[:, :], rhs=sq[:, kt, :W], start=(kt == 0), stop=(kt == KT - 1)),
                 reads=[self.ones, sq], writes=[ps], inc=(kt == KT - 1))
        k.op("act", lambda e: e.activation(out=rstd[:, :W], in_=ps[:, :W], func=AF.Sqrt, bias=self.epsb[:, 0:1], scale=1.0 / D),
             reads=[ps, self.epsb], writes=[rstd])
        k.op("dve", lambda e: e.reciprocal(out=rstd[:, :W], in_=rstd[:, :W]), reads=[rstd], writes=[rstd])

    def modulate(self, src, W, rstd, l, s, col, dst, dst_t0, tmps):
        k = self.k
        sh0 = 0 if s == 0 else 48
        for kt in range(KT):
            tmp = tmps[kt % 2]
            k.op("dve", lambda e: e.scalar_tensor_tensor(out=tmp[:, :W], in0=src[:, kt, :W], scalar=self.A[:, l, s, kt, col:col + 1],
                                                         in1=rstd[:, :W], op0=ALU.mult, op1=ALU.mult),
                 reads=[src, rstd, self.A], writes=[tmp])
            k.op("act", lambda e: e.activation(out=dst[:, kt, dst_t0:dst_t0 + W], in_=tmp[:, :W], func=AF.Identity,
                                               bias=self.modT[:, l, sh0 + kt, col:col + 1], scale=1.0),
                 reads=[tmp, self.modT], writes=[dst])

    def xsrc(self, l):
        return self.xT_d if l == 0 else self.x1_d

    def phaseAB(self, l):
        k = self.k
        st = contextlib.ExitStack()
        self.alloc_ws(st)
        hT = k.sb([128, KT, NTOK], BF16, "hT", st)
        xs = [k.sb([128, KT, 512], F32, "xs%d" % i, st) for i in range(1)]
        sq = k.sb([128, KT, 512], BF16, "sq", st)
        rstd = k.sb([128, 512], F32, "rstd", st)
        tmps = [k.sb([128, 512], F32, "tmp%d" % i, st) for i in range(4)]
        stg = [k.sb([128, 512], BF16, "stg%d" % i, st) for i in range(3)]
        rope = k.sb([128, 2, NTOK], F32, "rope", st)
        k.dma("sp", rope[:, 0, :], self.rope_d[0], writes=[rope])
        k.dma("sp", rope[:, 1, :], self.rope_d[1], writes=[rope])
        xd = self.xsrc(l)
        for ci, (t0, W) in enumerate(CHUNKS):
            col = 1 if ci == 0 else 0
            x = xs[0]
            k.dma("sp", x[:, :, :W], xd[:, :, t0:t0 + W].rearrange("k p t -> p k t"), reads=[k.tok(("x", l, ci))], writes=[x])
            self.rstd_of(x, W, sq, rstd)
            self.modulate(x, W, rstd, l, 0, col, hT, t0, tmps)
        if "hT" in self.dbg:
            k.dma("sp", self.dbg["hT"].rearrange("k p t -> p k t"), hT[:, :, :], reads=[hT])
        si = [0]

        def store(dst_ap, src_fn, reads_ps, tokkey, eng="act"):
            s = stg[si[0] % 3]
            si[0] += 1
            src_fn(s)
            k.dma("sp", dst_ap, s[:, :dst_ap.shape[-1]], reads=[s], writes=[k.tok(tokkey)])

        def mm_group(ps, wslot, woff, t0, W):
            for kt in range(KT):
                k.op("pe", lambda e: e.matmul(ps[:, :W], lhsT=wslot[:, woff + kt * 128:woff + (kt + 1) * 128], rhs=hT[:, kt, t0:t0 + W],
                                              start=(kt == 0), stop=(kt == KT - 1)),
                     reads=[wslot, hT], writes=[ps], inc=(kt == KT - 1))

        def simple(pbase, n, dst_d, key, func, skip_ctx=False):
            for m in range(0, n, 4):
                ws = self.nws()
                k.dma("pool", ws[:, :4 * 2048].rearrange("p (m x) -> p m x", m=4),
                      self.win_d[l, pbase + m:pbase + m + 4].rearrange("m p x -> p m x"), writes=[ws])
                for mm in range(4):
                    for ci, (t0, W) in enumerate(CHUNKS):
                        if skip_ctx and ci == 0:
                            continue
                        ps = self.nps()
                        mm_group(ps, ws, mm * 2048, t0, W)
                        store(dst_d[m + mm, :, t0:t0 + W],
                              lambda s: k.op("act", lambda e: e.activation(out=s[:, :W], in_=ps[:, :W], func=func), reads=[ps], writes=[s]),
                              None, (key, l, m + mm, ci))

        simple(0, 8, self.u_d, "u", AF.Copy)
        simple(40, 16, self.gs_d, "gs", AF.Sigmoid, skip_ctx=(l == 1))
        simple(56, 16, self.gn_d, "gn", AF.Sigmoid, skip_ctx=(l == 1))

        qbs = [k.sb([128, 512], BF16, "qb%d" % i, st) for i in range(2)]
        qi = [0]

        def roped(pbase, dst_d, key, skip_ctx=False):
            for m in range(0, 8, 4):
                ws = self.nws()
                k.dma("pool", ws[:, :].rearrange("p (m x) -> p m x", m=4),
                      self.win_d[l, pbase + m:pbase + m + 4].rearrange("m p x -> p m x"), writes=[ws])
                for mm in range(4):
                    for ci, (t0, W) in enumerate(CHUNKS):
                        if skip_ctx and ci == 0:
                            continue
                        ps1 = self.nps()
                        mm_group(ps1, ws, mm * 2048, t0, W)
                        qb = qbs[qi[0] % 2]
                        qi[0] += 1
                        k.op("act", lambda e: e.activation(out=qb[:, :W], in_=ps1[:, :W], func=AF.Copy), reads=[ps1], writes=[qb])
                        ps2 = self.nps()
                        k.op("pe", lambda e: e.matmul(ps2[:, :W], lhsT=self.permb[:, :], rhs=qb[:, :W], start=True, stop=True),
                             reads=[self.permb, qb], writes=[ps2])
                        ta, tb = tmps[2], tmps[3]
                        k.op("dve", lambda e: e.tensor_tensor(out=ta[:, :W], in0=ps1[:, :W], in1=rope[:, 0, t0:t0 + W], op=ALU.mult),
                             reads=[ps1, rope, qb], writes=[ta])
                        k.op("dve", lambda e: e.tensor_tensor(out=tb[:, :W], in0=ps2[:, :W], in1=rope[:, 1, t0:t0 + W], op=ALU.mult),
                             reads=[ps2, rope], writes=[tb])
                        store(dst_d[m + mm, :, t0:t0 + W],
                              lambda s: k.op("pool", lambda e: e.tensor_tensor(out=s[:, :W], in0=ta[:, :W], in1=tb[:, :W], op=ALU.add),
                                             reads=[ta, tb], writes=[s]),
                              None, (key, l, m + mm, ci))

        roped(8, self.q_d, "q", skip_ctx=(l == 1))
        roped(24, self.k_d, "k")
        for nt in range(2):
            ws = self.nws()
            k.dma("pool", ws[:, :], self.wv_d[l, nt], writes=[ws])
            for sh in range(2):
                for tt in range(18):
                    tk0 = tt * 128 + 64 * sh
                    if tk0 + 128 > NTOK:
                        continue
                    ps = self.nps()
                    for kt in range(KT):
                        k.op("pe", lambda e: e.matmul(ps[:, :], lhsT=hT[:, kt, tk0:tk0 + 128], rhs=ws[:, kt * 512:(kt + 1) * 512],
                                                      start=(kt == 0), stop=(kt == KT - 1)),
                             reads=[ws, hT], writes=[ps], inc=(kt == KT - 1))
                    store(self.v_d[sh, tt, :, nt * 512:(nt + 1) * 512],
                          lambda s: k.op("act", lambda e: e.activation(out=s[:, :], in_=ps[:, :], func=AF.Copy), reads=[ps], writes=[s]),
                          None, ("v", l, sh, tt, nt))
        k.barrier()
        st.close()

    def phaseC(self, l):
        k = self.k
        st = contextlib.ExitStack()
        kT = k.sb([128, 8, NTOK], BF16, "kT", st)
        vA = k.sb([128, 2, 18, 1024], BF16, "vA", st)
        tab = k.sb([128, 8, 15, 64], F32, "tab", st)
        qT = [k.sb([128, NTOK], BF16, "qT%d" % i, st) for i in range(2)]
        ost = [k.sb([128, NTOK], BF16, "ost%d" % i, st) for i in range(2)]
        S = [k.sb([128, 768], F32, "S%d" % i, st) for i in range(2)]
        P = [k.sb([128, 768], F32, "P%d" % i, st) for i in range(2)]
        Pb = [k.sb([128, 768], BF16, "Pb%d" % i, st) for i in range(2)]
        PT = [k.sb([128, 768], BF16, "PT%d" % i, st) for i in range(2)]
        sm = [k.sb([128, 4], F32, "sm%d" % i, st) for i in range(2)]
        for h in range(8):
            k.dma("sp", kT[:, h, :], self.k_d[h], reads=[k.tok(("k", l, h, ci)) for ci in range(5)], writes=[kT])
        for sh in range(2):
            for tt in range(18):
                if tt * 128 + 64 * sh + 128 > NTOK:
                    continue
                k.dma("sp", vA[:, sh, tt, :], self.v_d[sh, tt], reads=[k.tok(("v", l, sh, tt, nt)) for nt in range(2)], writes=[vA])
        k.op("pool", lambda e: e.memset(tab[:, :, :, :], NEG), writes=[tab])
        rp = self.rpb_d[l].rearrange("(hp hh) r c -> hh hp r c", hh=2)
        with self.k.nc.allow_non_contiguous_dma(reason="rpb window gather (tiny)"):
            for hh in range(2):
                for q in range(64):
                    c0 = min(max(q - 8, 0), 48)
                    off = c0 - q + 15
                    k.dma("sp", tab[hh * 64 + q:hh * 64 + q + 1, :, :, c0:c0 + 16], rp[hh:hh + 1, :, :, off:off + 16], writes=[tab])
        psS = [self.ps[0], self.ps[1]]
        psC = [self.ps[2], self.ps[3]]
        psO = [self.ps[4], self.ps[5]]
        items = []
        for hp in range(8):
            rows = list(range(32)) + ([-1, -2, -3, -4] if l < self.nlayers_total - 1 else [])
            for r in rows:
                items.append((hp, r))
        NI = len(items)

        def geo(idx):
            hp, r = items[idx]
            i2 = idx % 2
            d = dict(hp=hp, r=r, i2=i2, q=qT[hp % 2], o=ost[hp % 2], s=S[i2], p=P[i2], pb=Pb[i2], pt=PT[i2], sm=sm[i2],
                     pS=psS[i2], pC=psC[i2], pO=psO[i2], pT=self.psb[i2])
            if r >= 0:
                r0 = min(max(r - 4, 0), 24)
                d.update(qc=CTX + r * 64, r0=r0, ro0=r0 - r + 7, k0=CTX + r0 * 64, NK=768)
            else:
                d.update(qc=(-r - 1) * 64, NK=256)
            return d

        def stA(idx):
            g = geo(idx)
            hp, r, q = g["hp"], g["r"], g["q"]
            k.pe_drain()
            if idx == 0 or items[idx - 1][0] != hp:
                k.dma("sp", q[:, :], self.q_d[hp], reads=[k.tok(("q", l, hp, ci)) for ci in range(5)], writes=[q])
            for hh in range(2):
                pr = slice(hh * 64, hh * 64 + 64)
                if r >= 0:
                    k.op("pe", lambda e: e.matmul(g["pS"][pr, :], lhsT=q[pr, g["qc"]:g["qc"] + 64], rhs=kT[pr, hp, g["k0"]:g["k0"] + 512], start=True, stop=True),
                         reads=[q, kT], writes=[g["pS"]])
                k.op("pe", lambda e: e.matmul(g["pC"][pr, 0:256], lhsT=q[pr, g["qc"]:g["qc"] + 64], rhs=kT[pr, hp, 0:256], start=True, stop=True),
                     reads=[q, kT], writes=[g["pC"]])

        def stB(idx):
            g = geo(idx)
            hp, r, s_, p_, pb_, sm_, NK = g["hp"], g["r"], g["s"], g["p"], g["pb"], g["sm"], g["NK"]
            if r >= 0:
                ro0 = g["ro0"]
                k.op("dve", lambda e: e.scalar_tensor_tensor(out=s_[:, 0:512], in0=g["pS"][:, :], scalar=0.125,
                                                             in1=tab[:, hp, ro0:ro0 + 8, :].rearrange("p a b -> p (a b)"),
                                                             op0=ALU.mult, op1=ALU.add),
                     reads=[g["pS"], tab], writes=[s_])
                k.op("act", lambda e: e.activation(out=s_[:, 512:768], in_=g["pC"][:, 0:256], func=AF.Copy, scale=0.125),
                     reads=[g["pC"]], writes=[s_])
            else:
                k.op("act", lambda e: e.activation(out=s_[:, 0:256], in_=g["pC"][:, 0:256], func=AF.Copy, scale=0.125),
                     reads=[g["pC"]], writes=[s_])
            k.op("dve", lambda e: e.reduce_max(out=sm_[:, 0:1], in_=s_[:, :NK], axis=AX.X), reads=[s_], writes=[sm_])
            k.op("dve", lambda e: e.tensor_scalar(out=sm_[:, 1:2], in0=sm_[:, 0:1], scalar1=-1.0, scalar2=None, op0=ALU.mult),
                 reads=[sm_], writes=[sm_])
            k.op("act", lambda e: e.activation(out=p_[:, :NK], in_=s_[:, :NK], func=AF.Exp, bias=sm_[:, 1:2], scale=1.0,
                                               accum_out=sm_[:, 2:3]),
                 reads=[s_, sm_], writes=[p_, sm_])
            k.op("dve", lambda e: e.reciprocal(out=sm_[:, 3:4], in_=sm_[:, 2:3]), reads=[sm_], writes=[sm_])
            k.op("dve", lambda e: e.tensor_scalar(out=pb_[:, :NK], in0=p_[:, :NK], scalar1=sm_[:, 3:4], scalar2=None, op0=ALU.mult),
                 reads=[p_, sm_], writes=[pb_])

        def stC(idx):
            g = geo(idx)
            nj = g["NK"] // 128
            k.pe_drain()
            for j in range(nj):
                k.op("pe", lambda e: e.transpose(out=g["pT"][:, j * 128:(j + 1) * 128], in_=g["pb"][:, j * 128:(j + 1) * 128], identity=self.identb[:, :]),
                     reads=[g["pb"], self.identb], writes=[g["pT"]], inc=(j == nj - 1))

        def stD(idx):
            g = geo(idx)
            NK = g["NK"]
            k.op("act", lambda e: e.activation(out=g["pt"][:, :NK], in_=g["pT"][:, :NK], func=AF.Copy), reads=[g["pT"]], writes=[g["pt"]])

        def stE(idx):
            g = geo(idx)
            hp, r = g["hp"], g["r"]
            nj = g["NK"] // 128
            k.pe_drain()
            for hh in range(2):
                pr = slice(hh * 64, hh * 64 + 64)
                hc = (2 * hp + hh) * 64
                for j in range(nj):
                    if r >= 0 and j < 4:
                        sh = (g["r0"] % 2)
                        tt = (g["k0"] - 64 * sh) // 128 + j
                        vv = vA[:, sh, tt, hc:hc + 64]
                    else:
                        jj = j - 4 if r >= 0 else j
                        vv = vA[:, 0, jj, hc:hc + 64]
                    k.op("pe", lambda e: e.matmul(g["pO"][pr, 0:64], lhsT=vv, rhs=g["pt"][:, j * 128 + hh * 64:j * 128 + hh * 64 + 64],
                                                  start=(j == 0), stop=(j == nj - 1)),
                         reads=[vA, g["pt"]], writes=[g["pO"]], inc=(j == nj - 1))

        def stF(idx):
            g = geo(idx)
            hp, o = g["hp"], g["o"]
            k.op("dve", lambda e: e.tensor_copy(out=o[:, g["qc"]:g["qc"] + 64], in_=g["pO"][:, 0:64]), reads=[g["pO"]], writes=[o])
            if idx == NI - 1 or items[idx + 1][0] != hp:
                k.dma("sp", self.o_d[hp], o[:, :], reads=[o], writes=[k.tok(("o", l, hp))])

        for t in range(NI + 2):
            if t < NI:
                stA(t)
                stB(t)
            if 0 <= t - 1 < NI:
                stC(t - 1)
                stD(t - 1)
            if 0 <= t - 2 < NI:
                stE(t - 2)
                stF(t - 2)
        k.barrier()
        st.close()

    def phaseE(self, l, t_lo):
        k = self.k
        st = contextlib.ExitStack()
        self.alloc_ws(st)
        aT = k.sb([128, 8, NTOK], BF16, "aT", st)
        oT = k.sb([128, 8, NTOK], BF16, "oT", st)
        gsb = [k.sb([128, 512], BF16, "gsb%d" % i, st) for i in range(2)]
        gnb = [k.sb([128, 512], BF16, "gnb%d" % i, st) for i in range(2)]
        t1 = [k.sb([128, 512], F32, "t1_%d" % i, st) for i in range(2)]
        t2 = [k.sb([128, 512], F32, "t2_%d" % i, st) for i in range(2)]
        t3 = [k.sb([128, 512], F32, "t3_%d" % i, st) for i in range(2)]
        stg = [k.sb([128, 512], BF16, "stgE%d" % i, st) for i in range(3)]
        for h in range(8):
            k.dma("sp", aT[:, h, :], self.a_d[h], reads=[k.tok(("a", l, h))], writes=[aT])
            k.dma("sp", oT[:, h, :], self.o_d[h], reads=[k.tok(("o", l, h))], writes=[oT])
        chunks = [c for c in CHUNKS if c[0] >= t_lo]
        items = [(mt, t0, W) for mt in range(16) for (t0, W) in chunks]

        def gate_loads(idx):
            mt, t0, W = items[idx]
            ci = [c[0] for c in CHUNKS].index(t0)
            g1, g2 = gsb[idx % 2], gnb[idx % 2]
            k.dma("sp", g1[:, :W], self.gs_d[mt, :, t0:t0 + W], reads=[k.tok(("gs", l, mt, ci))], writes=[g1])
            k.dma("sp", g2[:, :W], self.gn_d[mt, :, t0:t0 + W], reads=[k.tok(("gn", l, mt, ci))], writes=[g2])

        gate_loads(0)
        ws = None
        for idx, (mt, t0, W) in enumerate(items):
            ci = [c[0] for c in CHUNKS].index(t0)
            if idx == 0 or items[idx - 1][0] != mt:
                ws = self.nws()
                k.dma("pool", ws[:, 0:1024], self.wval_d[l, mt], writes=[ws])
                k.dma("pool", ws[:, 1024:2048], self.wglu_d[l, mt], writes=[ws])
                k.dma("pool", ws[:, 2048:3072], self.wna_d[l, mt], writes=[ws])
            i2 = idx % 2
            pss = []
            for wi, src_ in ((0, aT), (1, aT), (2, oT)):
                ps = self.nps()
                for kt in range(8):
                    k.op("pe", lambda e: e.matmul(ps[:, :W], lhsT=ws[:, wi * 1024 + kt * 128:wi * 1024 + (kt + 1) * 128], rhs=src_[:, kt, t0:t0 + W],
                                                  start=(kt == 0), stop=(kt == 7)),
                         reads=[ws, src_], writes=[ps], inc=(kt == 7))
                pss.append(ps)
            g1, g2 = gsb[i2], gnb[i2]
            a1, a2, a3 = t1[i2], t2[i2], t3[i2]
            k.op("act", lambda e: e.activation(out=a1[:, :W], in_=pss[1][:, :W], func=AF.Sigmoid), reads=[pss[1]], writes=[a1])
            k.op("dve", lambda e: e.tensor_tensor(out=a2[:, :W], in0=pss[0][:, :W], in1=a1[:, :W], op=ALU.mult), reads=[pss[0], a1], writes=[a2])
            k.op("pool", lambda e: e.tensor_tensor(out=a2[:, :W], in0=a2[:, :W], in1=g1[:, :W], op=ALU.mult), reads=[a2, g1], writes=[a2])
            k.op("dve", lambda e: e.tensor_tensor(out=a3[:, :W], in0=pss[2][:, :W], in1=g2[:, :W], op=ALU.mult), reads=[pss[2], g2], writes=[a3])
            if idx + 1 < len(items):
                gate_loads(idx + 1)
            s = stg[idx % 3]
            k.op("pool", lambda e: e.tensor_tensor(out=s[:, :W], in0=a2[:, :W], in1=a3[:, :W], op=ALU.add), reads=[a2, a3], writes=[s])
            k.dma("sp", self.m_d[mt, :, t0:t0 + W], s[:, :W], reads=[s], writes=[k.tok(("m", l, mt, ci))])
        k.barrier()
        st.close()

    def phaseF(self, l, t_lo, last):
        k = self.k
        st = contextlib.ExitStack()
        self.alloc_ws(st)
        x = k.sb([128, KT, 512], F32, "xF", st)
        ob = k.sb([128, KT, 512], F32, "oF", st)
        mh = k.sb([128, KT, 512], BF16, "mh", st)
        ab = k.sb([128, 64, 512], BF16, "ab", st)
        rstd = k.sb([128, 512], F32, "rstdF", st)
        tmps = [k.sb([128, 512], F32, "tmpF%d" % i, st) for i in range(2)]
        sq = Buf(ab.ap, "sqalias")
        xd = self.xsrc(l)
        chunks = [c for c in CHUNKS if c[0] >= t_lo]

        def wload(first, fill, src_ap, rr=None):
            ws = self.nws()
            wtok = k.tok(("wc", fill))
            if first:
                k.dma("pool", ws[:, :].rearrange("p (m x) -> p m x", m=rr) if rr else ws[:, :], src_ap, writes=[ws])
                k.dma("sp", self.wc_d[fill], ws[:, :], reads=[ws], writes=[wtok])
            else:
                k.dma("sp", ws[:, :], self.wc_d[fill], reads=[wtok], writes=[ws])
            return ws

        for cidx, (t0, W) in enumerate(chunks):
            first = (cidx == 0)
            ci = [c[0] for c in CHUNKS].index(t0)
            col = 1 if ci == 0 else 0
            k.dma("sp", x[:, :, :W], xd[:, :, t0:t0 + W].rearrange("k p t -> p k t"), reads=[k.tok(("x", l, ci))], writes=[x])
            k.dma("sp", mh[:, :, :W], self.m_d[:, :, t0:t0 + W].rearrange("k p t -> p k t"),
                  reads=[k.tok(("m", l, mt, ci)) for mt in range(16)], writes=[mh])
            for m in range(0, 16, 4):
                ws = wload(first, m // 4, self.wout_d[l, m:m + 4].rearrange("m p x -> p m x"), 4)
                for mm in range(4):
                    ps = self.nps()
                    for kt in range(KT):
                        k.op("pe", lambda e: e.matmul(ps[:, :W], lhsT=ws[:, mm * 2048 + kt * 128:mm * 2048 + (kt + 1) * 128], rhs=mh[:, kt, :W],
                                                      start=(kt == 0), stop=(kt == KT - 1)),
                             reads=[ws, mh], writes=[ps], inc=(kt == KT - 1))
                    k.op("act", lambda e: e.activation(out=ob[:, m + mm, :W], in_=ps[:, :W], func=AF.Copy), reads=[ps], writes=[ob])
            self.residual(x, ob, W, sq, ab, rstd, l, 0, col, tmps)
            self.rstd_of(x, W, Buf(ab.ap, "sq2"), rstd) if False else None
            self._rstd_alias(x, W, ab, rstd)
            self.modulate(x, W, rstd, l, 1, col, mh, 0, tmps)
            for m in range(0, 64, 4):
                ws = wload(first, 4 + m // 4, self.wfc1_d[l, m:m + 4].rearrange("m p x -> p m x"), 4)
                for mm in range(4):
                    ps = self.nps()
                    for kt in range(KT):
                        k.op("pe", lambda e: e.matmul(ps[:, :W], lhsT=ws[:, mm * 2048 + kt * 128:mm * 2048 + (kt + 1) * 128], rhs=mh[:, kt, :W],
                                                      start=(kt == 0), stop=(kt == KT - 1)),
                             reads=[ws, mh], writes=[ps], inc=(kt == KT - 1))
                    tm = tmps[mm % 2]
                    k.op("act", lambda e: e.activation(out=tm[:, :W], in_=ps[:, :W], func=AF.Relu), reads=[ps], writes=[tm])
                    k.op("pool", lambda e: e.tensor_tensor(out=ab[:, m + mm, :W], in0=tm[:, :W], in1=tm[:, :W], op=ALU.mult), reads=[tm], writes=[ab])
            for m in range(16):
                ws = wload(first, 20 + m, self.wfc2_d[l, m])
                ps = self.nps()
                for kt in range(64):
                    k.op("pe", lambda e: e.matmul(ps[:, :W], lhsT=ws[:, kt * 128:(kt + 1) * 128], rhs=ab[:, kt, :W], start=(kt == 0), stop=(kt == 63)),
                         reads=[ws, ab], writes=[ps], inc=(kt == 63))
                k.op("act", lambda e: e.activation(out=ob[:, m, :W], in_=ps[:, :W], func=AF.Copy), reads=[ps], writes=[ob])
            self.residual(x, ob, W, sq, ab, rstd, l, 1, col, tmps)
            if last:
                k.dma("sp", self.yT_d[:, :, t0 - CTX:t0 - CTX + W].rearrange("k p t -> p k t"), x[:, :, :W], reads=[x],
                      writes=[k.tok(("y", ci))], is_out=True)
            else:
                k.dma("sp", self.x1_d[:, :, t0:t0 + W].rearrange("k p t -> p k t"), x[:, :, :W], reads=[x], writes=[k.tok(("x", l + 1, ci))])
        k.barrier()
        st.close()

    def _rstd_alias(self, src, W, ab, rstd):
        k = self.k
        sqv = ab.ap[:, 0:KT, :]
        k.op("act", lambda e: e.activation(out=sqv[:, :, :W], in_=src[:, :, :W], func=AF.Square), reads=[src], writes=[ab])
        ps = self.nps()
        for kt in range(KT):
            k.op("pe", lambda e: e.matmul(ps[:, :W], lhsT=self.ones[:, :], rhs=sqv[:, kt, :W], start=(kt == 0), stop=(kt == KT - 1)),
                 reads=[self.ones, ab], writes=[ps], inc=(kt == KT - 1))
        k.op("act", lambda e: e.activation(out=rstd[:, :W], in_=ps[:, :W], func=AF.Sqrt, bias=self.epsb[:, 0:1], scale=1.0 / D),
             reads=[ps, self.epsb], writes=[rstd])
        k.op("dve", lambda e: e.reciprocal(out=rstd[:, :W], in_=rstd[:, :W]), reads=[rstd], writes=[rstd])

    def residual(self, x, ob, W, sq, ab, rstd, l, s, col, tmps):
        k = self.k
        self._rstd_alias(ob, W, ab, rstd)
        for kt in range(KT):
            tmp = tmps[kt % 2]
            k.op("dve", lambda e: e.scalar_tensor_tensor(out=tmp[:, :W], in0=ob[:, kt, :W], scalar=self.G[:, l, s, kt, col:col + 1],
                                                         in1=rstd[:, :W], op0=ALU.mult, op1=ALU.mult),
                 reads=[ob, rstd, self.G], writes=[tmp])
            k.op("pool", lambda e: e.tensor_tensor(out=x[:, kt, :W], in0=x[:, kt, :W], in1=tmp[:, :W], op=ALU.add), reads=[x, tmp], writes=[x])

    def build(self, upto=None):
        k = self.k
        self.nlayers_total = 2
        self.epsb = k.sb([128, 1], F32, "epsb")
        k.op("pool", lambda e: e.memset(self.epsb[:, :], EPS), writes=[self.epsb])
        self.hpib = k.sb([128, 1], F32, "hpib")
        k.op("pool", lambda e: e.memset(self.hpib[:, :], float(np.pi / 2)), writes=[self.hpib])
        self.phase0()
        for l in range(self.nlayers):
            self.phaseAB(l)
            if upto == "AB":
                break
            self.phaseC(l)
            if upto == "C":
                break
            self.phaseD(l)
            last = (l == 1)
            t_lo = CTX if last else 0
            self.phaseE(l, t_lo)
            self.phaseF(l, t_lo, last)
        k.finish()
        return k.nc


def _fm(v):
    return np.ascontiguousarray(np.asarray(v, np.float32).reshape(-1, 128).T)


def _panels(W):
    K, N = W.shape
    kt, mt = K // 128, N // 128
    return np.ascontiguousarray(W.reshape(kt, 128, mt, 128).transpose(2, 1, 0, 3).reshape(mt, 128, kt * 128))


def _rope_tables():
    nf = 16
    inv = (10000.0 ** (-np.arange(nf, dtype=np.float32) / nf)).astype(np.float32)
    pos = np.arange(LAT)
    rows = (pos // 64).astype(np.float32)
    cols = (pos % 64).astype(np.float32)
    cos = np.ones((64, NTOK), np.float32)
    sin = np.zeros((64, NTOK), np.float32)
    for d in range(64):
        blk = d // 16
        p = rows if blk < 2 else cols
        ang = (p * inv[d % 16]).astype(np.float32)
        cos[d, CTX:] = np.cos(ang)
        sg = -1.0 if blk % 2 == 0 else 1.0
        sin[d, CTX:] = sg * np.sin(ang)
    return np.ascontiguousarray(np.stack([np.concatenate([cos, cos], 0), np.concatenate([sin, sin], 0)], 0))


def prep_shared(inp):
    f = lambda a: np.asarray(a, np.float32)
    w_in = f(inp["w_in"])
    partner = np.array([d + 16 if (d % 32) < 16 else d - 16 for d in range(64)])
    perm = (np.arange(16)[:, None] * 64 + partner[None, :]).reshape(-1)
    wins, wvs = [], []
    for l in range(2):
        W = w_in[l]
        u, q, kk, v, gs, gn = W[:, :1024], W[:, 1024:2048], W[:, 2048:3072], W[:, 3072:4096], W[:, 4096:6144], W[:, 6144:]
        comb = np.concatenate([u, q, q[:, perm], kk, kk[:, perm], gs, gn], axis=1)
        wins.append(_panels(comb))
        wvs.append(np.ascontiguousarray(v.reshape(KT, 128, 2, 512).transpose(2, 1, 0, 3).reshape(2, 128, KT * 512)))
    sh = {
        "wmod": np.stack([_panels(f(inp["w_mod"])[l]) for l in range(2)]),
        "win": np.stack(wins),
        "wv": np.stack(wvs),
        "rope": _rope_tables(),
        "rpb": np.ascontiguousarray(f(inp["na_rpb"])),
        "wval": np.stack([_panels(f(inp["w_ssm_val"])[l]) for l in range(2)]),
        "wglu": np.stack([_panels(f(inp["w_ssm_glu"])[l]) for l in range(2)]),
        "wna": np.stack([_panels(f(inp["w_na_proj"])[l]) for l in range(2)]),
        "wout": np.stack([_panels(f(inp["w_out"])[l]) for l in range(2)]),
        "wfc1": np.stack([_panels(f(inp["w_fc1"])[l]) for l in range(2)]),
        "wfc2": np.stack([_panels(f(inp["w_fc2"])[l]) for l in range(2)]),
        "ident": np.eye(128, dtype=np.float32),
    }
    pm = np.zeros((128, 128), np.float32)
    for mcol in range(128):
        pm[(mcol // 64) * 64 + partner[mcol % 64], mcol] = 1.0
    sh["perm"] = pm
    jj = np.arange(128) // 16
    cc = np.arange(128) % 16
    s5c = np.zeros((128, 400), np.float32)
    for kk in range(8):
        s5c[:, kk] = (jj == kk)
        s5c[:, 8 + kk] = (7 - jj == kk)
    s5c[:, 16:144] = (jj[:, None] <= jj[None, :])
    s5c[:, 144:272] = (jj[:, None] >= jj[None, :])
    s5c[:, 272:400] = np.eye(128)
    sel = np.zeros((128, 8, 8, 128), np.float32)
    selT = np.zeros((128, 8, 8, 128), np.float32)
    for gi in range(8):
        for j in range(8):
            for c in range(16):
                sel[gi * 16 + c, gi, j, j * 16 + c] = 1.0
                selT[j * 16 + c, gi, j, gi * 16 + c] = 1.0
    sh["s5c"] = s5c
    sh["sel"] = sel.reshape(128, 8192)
    sh["selT"] = selT.reshape(128, 8192)
    lre, lim, ldt = f(inp["ssm_lam_re"]), f(inp["ssm_lam_im"]), f(inp["ssm_log_dt"])
    bre, bim, cre, cim = f(inp["ssm_b_re"]), f(inp["ssm_b_im"]), f(inp["ssm_c_re"]), f(inp["ssm_c_im"])
    sC = np.empty((2, 2, 7, 128, 512), np.float32)
    for l in range(2):
        for d in range(2):
            def cl(A):
                t = A.reshape(2, 32, 64).transpose(0, 2, 1)
                return np.repeat(t.reshape(128, 32), 16, axis=1)
            sC[l, d, 0] = cl(lre[l, d])
            sC[l, d, 1] = cl(lim[l, d])
            sC[l, d, 2] = cl(np.repeat(ldt[l, d][:, None], 64, axis=1))
            for a, Cc in ((3, cre), (4, cim)):
                sC[l, d, a] = Cc[l, d].reshape(2, 32, 16, 64).transpose(0, 3, 1, 2).reshape(128, 512)
            for a, B in ((5, bre), (6, bim)):
                sC[l, d, a] = B[l, d].reshape(2, 32, 64, 16).transpose(0, 2, 1, 3).reshape(128, 512)
    sh["sC"] = sC
    dd = f(inp["ssm_d"])
    sh["drep"] = np.stack([np.tile(dd[l].reshape(64, 16).T, (8, 1)) for l in range(2)]).astype(np.float32)
    return sh


def prep_core(inp, b):
    f = lambda a: np.asarray(a, np.float32)
    X = np.concatenate([f(inp["ctx"])[b], f(inp["x"])[b]], axis=0)
    xT = np.ascontiguousarray(X.T.reshape(KT, 128, NTOK))
    cols = []
    for l in range(2):
        cols += [_fm(inp["g_pre_mix"][l]), _fm(inp["g_post_mix"][l]), _fm(inp["g_pre_mlp"][l]), _fm(inp["g_post_mlp"][l]),
                 _fm(inp["b_mod"][l]), _fm(inp["ssm_d"][l])]
    cols += [_fm(inp["c"][b]), _fm(inp["c_ctx"])]
    vecs = np.ascontiguousarray(np.concatenate(cols, axis=1))
    assert vecs.shape == (128, NVEC)
    return {"xT": xT, "vecs": vecs}


def _cmul(k, eng, o_re, o_im, a_re, a_im, b_re, b_im, t1, t2, rd=(), wr=()):
    E = k.op
    E(eng, lambda e: e.tensor_tensor(out=t1, in0=a_re, in1=b_re, op=ALU.mult), reads=rd, writes=wr)
    E(eng, lambda e: e.tensor_tensor(out=t2, in0=a_im, in1=b_im, op=ALU.mult), reads=rd, writes=wr)
    E(eng, lambda e: e.tensor_tensor(out=t2, in0=t1, in1=t2, op=ALU.subtract), reads=rd, writes=wr)
    E(eng, lambda e: e.tensor_tensor(out=t1, in0=a_re, in1=b_im, op=ALU.mult), reads=rd, writes=wr)
    E(eng, lambda e: e.tensor_tensor(out=o_im, in0=a_im, in1=b_re, op=ALU.mult), reads=rd, writes=wr)
    E(eng, lambda e: e.tensor_tensor(out=o_im, in0=o_im, in1=t1, op=ALU.add), reads=rd, writes=wr)
    E(eng, lambda e: e.tensor_copy(out=o_re, in_=t2), reads=rd, writes=wr)


def _lam_base(P, st, src, F, eng):
    k = P.k
    nb = lambda nm: k.sb([128, F], F32, nm, st)
    W = Buf(None, "Wtok")
    tok = [W, src]
    dt, ar, ai, c, s, m, t1, t2 = [nb(n) for n in ("dt", "ar", "ai", "c", "s", "m", "t1", "t2")]
    o = {n: nb(n) for n in ("lbr", "lbi", "lir", "lii", "gr", "gi")}
    E = lambda en, fn: k.op(en, fn, reads=tok + [P.hpib], writes=[W])
    E("act", lambda e: e.activation(out=dt[:, :], in_=src[:, 2, :], func=AF.Exp))
    E(eng, lambda e: e.tensor_tensor(out=ar[:, :], in0=src[:, 0, :], in1=dt[:, :], op=ALU.mult))
    E(eng, lambda e: e.tensor_tensor(out=ai[:, :], in0=src[:, 1, :], in1=dt[:, :], op=ALU.mult))
    E("act", lambda e: e.activation(out=s[:, :], in_=ai[:, :], func=AF.Sin, scale=1.0 / 16))
    E("act", lambda e: e.activation(out=c[:, :], in_=ai[:, :], func=AF.Sin, scale=1.0 / 16, bias=P.hpib[:, 0:1]))
    for _ in range(4):
        E(eng, lambda e: e.tensor_tensor(out=t1[:, :], in0=c[:, :], in1=s[:, :], op=ALU.mult))
        E(eng, lambda e: e.tensor_tensor(out=c[:, :], in0=c[:, :], in1=c[:, :], op=ALU.mult))
        E(eng, lambda e: e.tensor_tensor(out=s[:, :], in0=s[:, :], in1=s[:, :], op=ALU.mult))
        E(eng, lambda e: e.tensor_tensor(out=c[:, :], in0=c[:, :], in1=s[:, :], op=ALU.subtract))
        E(eng, lambda e: e.tensor_scalar(out=s[:, :], in0=t1[:, :], scalar1=2.0, scalar2=None, op0=ALU.mult))
    E("act", lambda e: e.activation(out=m[:, :], in_=ar[:, :], func=AF.Exp))
    E(eng, lambda e: e.tensor_tensor(out=o["lbr"][:, :], in0=m[:, :], in1=c[:, :], op=ALU.mult))
    E(eng, lambda e: e.tensor_tensor(out=o["lbi"][:, :], in0=m[:, :], in1=s[:, :], op=ALU.mult))
    E("act", lambda e: e.activation(out=m[:, :], in_=ar[:, :], func=AF.Exp, scale=-1.0))
    E(eng, lambda e: e.tensor_tensor(out=o["lir"][:, :], in0=m[:, :], in1=c[:, :], op=ALU.mult))
    E("dve", lambda e: e.scalar_tensor_tensor(out=o["lii"][:, :], in0=m[:, :], scalar=-1.0, in1=s[:, :], op0=ALU.mult, op1=ALU.mult))
    lr, li = src[:, 0, :], src[:, 1, :]
    E(eng, lambda e: e.tensor_tensor(out=t1[:, :], in0=lr, in1=lr, op=ALU.mult))
    E(eng, lambda e: e.tensor_tensor(out=t2[:, :], in0=li, in1=li, op=ALU.mult))
    E(eng, lambda e: e.tensor_tensor(out=t1[:, :], in0=t1[:, :], in1=t2[:, :], op=ALU.add))
    E("dve", lambda e: e.reciprocal(out=t1[:, :], in_=t1[:, :]))
    E(eng, lambda e: e.tensor_scalar(out=c[:, :], in0=o["lbr"][:, :], scalar1=-1.0, scalar2=None, op0=ALU.add))
    E(eng, lambda e: e.tensor_tensor(out=t2[:, :], in0=c[:, :], in1=lr, op=ALU.mult))
    E(eng, lambda e: e.tensor_tensor(out=s[:, :], in0=o["lbi"][:, :], in1=li, op=ALU.mult))
    E(eng, lambda e: e.tensor_tensor(out=t2[:, :], in0=t2[:, :], in1=s[:, :], op=ALU.add))
    E(eng, lambda e: e.tensor_tensor(out=o["gr"][:, :], in0=t2[:, :], in1=t1[:, :], op=ALU.mult))
    E(eng, lambda e: e.tensor_tensor(out=t2[:, :], in0=o["lbi"][:, :], in1=lr, op=ALU.mult))
    E(eng, lambda e: e.tensor_tensor(out=s[:, :], in0=c[:, :], in1=li, op=ALU.mult))
    E(eng, lambda e: e.tensor_tensor(out=t2[:, :], in0=t2[:, :], in1=s[:, :], op=ALU.subtract))
    E(eng, lambda e: e.tensor_tensor(out=o["gi"][:, :], in0=t2[:, :], in1=t1[:, :], op=ALU.mult))
    return o, W, (t1, t2, c, s, m, dt)


def phaseD(self, l):
    k = self.k
    st = contextlib.ExitStack()
    NG = 64
    R = k.sb([128, 17408], BF16, "Rraw", st)
    self._R = R
    BS = [Buf(R.ap[:, d * 8192:(d + 1) * 8192].rearrange("q (g m) -> q g m", m=128), "BS%d" % d) for d in range(2)]
    T = k.sb([128, NG, 128], BF16, "Tg", st)
    CS = [k.sb([128, 2, 32, 128], BF16, "CS%d" % d, st) for d in range(2)]
    MU = k.sb([128, 2, 2, 64], F32, "MU", st)
    cst = k.sb([128, 400], F32, "s5c", st)
    k.dma("sp", cst[:, :], self.s5c_d[:, :], writes=[cst])
    Drep = k.sb([128, NG], F32, "Drep", st)
    k.dma("sp", Drep[:, :], self.drep_d[l], writes=[Drep])
    F = NG * 64
    for d in range(2):
        st2 = contextlib.ExitStack()
        Fc = 512
        src = k.sb([128, 7, Fc], F32, "srcC", st2)
        for a in range(7):
            k.dma("sp", src[:, a, :], self.sC_d[l, d, a], writes=[src])
        LT = k.sb([128, 2, 32, 128], BF16, "LT", st2)
        LB = k.sb([128, 2, 32, 128], BF16, "LB", st2)
        o, W, (t1, t2, c, s, m, dt) = _lam_base(self, st2, src, Fc, "dve")
        tok = [W, src]
        E = lambda en, fn, wr=(): k.op(en, fn, reads=tok, writes=[W] + list(wr))
        bbr, bbi = k.sb([128, Fc], F32, "bbr", st2), k.sb([128, Fc], F32, "bbi", st2)
        _cmul(k, "dve", bbr[:, :], bbi[:, :], o["gr"][:, :], o["gi"][:, :], src[:, 5, :], src[:, 6, :], t1[:, :], t2[:, :], rd=tok, wr=[W])
        pr, pi = o["gr"], o["gi"]
        qr, qi = k.sb([128, Fc], F32, "qr", st2), k.sb([128, Fc], F32, "qi", st2)
        E("dve", lambda e: e.tensor_copy(out=pr[:, :], in_=o["lbr"][:, :]))
        E("dve", lambda e: e.tensor_copy(out=pi[:, :], in_=o["lbi"][:, :]))
        E("dve", lambda e: e.tensor_copy(out=qr[:, :], in_=o["lir"][:, :]))
        E("dve", lambda e: e.tensor_copy(out=qi[:, :], in_=o["lii"][:, :]))
        cs5 = CS[d].ap.rearrange("q r g (j c) -> q r g j c", c=16)
        lt5 = LT.ap.rearrange("q r g (j c) -> q r g j c", c=16)
        lb5 = LB.ap.rearrange("q r g (j c) -> q r g j c", c=16)
        g3 = lambda ap: ap.rearrange("q (g c) -> q g c", c=16)
        jb0 = 7 if d == 0 else 0
        E("act", lambda e: e.activation(out=lb5[:, 0, :, jb0, :], in_=g3(bbr[:, :]), func=AF.Copy), wr=[LB])
        E("act", lambda e: e.activation(out=lb5[:, 1, :, jb0, :], in_=g3(bbi[:, :]), func=AF.Copy), wr=[LB])
        for kk in range(1, 9):
            if kk > 1:
                _cmul(k, "dve", pr[:, :], pi[:, :], pr[:, :], pi[:, :], o["lbr"][:, :], o["lbi"][:, :], t1[:, :], t2[:, :], rd=tok, wr=[W])
                _cmul(k, "dve", qr[:, :], qi[:, :], qr[:, :], qi[:, :], o["lir"][:, :], o["lii"][:, :], t1[:, :], t2[:, :], rd=tok, wr=[W])
            jj = kk - 1 if d == 0 else 8 - kk
            _cmul(k, "dve", c[:, :], s[:, :], src[:, 3, :], src[:, 4, :], pr[:, :], pi[:, :], t1[:, :], t2[:, :], rd=tok, wr=[W])
            E("act", lambda e: e.activation(out=cs5[:, 0, :, jj, :], in_=g3(c[:, :]), func=AF.Copy), wr=[CS[d]])
            E("act", lambda e: e.activation(out=cs5[:, 1, :, jj, :], in_=g3(s[:, :]), func=AF.Copy, scale=-1.0), wr=[CS[d]])
            _cmul(k, "dve", c[:, :], s[:, :], bbr[:, :], bbi[:, :], qr[:, :], qi[:, :], t1[:, :], t2[:, :], rd=tok, wr=[W])
            E("act", lambda e: e.activation(out=lt5[:, 0, :, jj, :], in_=g3(c[:, :]), func=AF.Copy), wr=[LT])
            E("act", lambda e: e.activation(out=lt5[:, 1, :, jj, :], in_=g3(s[:, :]), func=AF.Copy), wr=[LT])
            if kk <= 7:
                jb = 7 - kk if d == 0 else kk
                _cmul(k, "dve", c[:, :], s[:, :], bbr[:, :], bbi[:, :], pr[:, :], pi[:, :], t1[:, :], t2[:, :], rd=tok, wr=[W])
                E("act", lambda e: e.activation(out=lb5[:, 0, :, jb, :], in_=g3(c[:, :]), func=AF.Copy), wr=[LB])
                E("act", lambda e: e.activation(out=lb5[:, 1, :, jb, :], in_=g3(s[:, :]), func=AF.Copy), wr=[LB])
        prg = g3(pr[:, :])[:, :, 0]
        pig = g3(pi[:, :])[:, :, 0]
        gsl = slice(d * 32, d * 32 + 32)
        E("dve", lambda e: e.tensor_copy(out=MU[:, 0, 0, gsl], in_=prg), wr=[MU])
        E("dve", lambda e: e.tensor_copy(out=MU[:, 0, 1, gsl], in_=prg), wr=[MU])
        E("dve", lambda e: e.tensor_scalar(out=MU[:, 1, 0, gsl], in0=pig, scalar1=-1.0, scalar2=None, op0=ALU.mult), wr=[MU])
        E("dve", lambda e: e.tensor_copy(out=MU[:, 1, 1, gsl], in_=pig), wr=[MU])
        for g in range(64):
            hb, gl = (g // 32) * 64, g % 32
            pT = self.psb[g % 2]
            for ri in range(2):
                k.op("pe", lambda e: e.transpose(out=pT[:, ri * 64:(ri + 1) * 64], in_=LB[hb:hb + 64, ri, gl, :], identity=self.identb[hb:hb + 64, hb:hb + 64]),
                     reads=[LB, self.identb], writes=[pT], inc=(ri == 1))
            if g % 2:
                k.op("act", lambda e: e.activation(out=BS[d][:, g, :], in_=pT[:, 0:128], func=AF.Copy), reads=[pT], writes=[BS[d]])
            else:
                k.op("dve", lambda e: e.tensor_copy(out=BS[d][:, g, :], in_=pT[:, 0:128]), reads=[pT], writes=[BS[d]])
        tf = [k.sb([128, 128], F32, "tfT%d" % i, st2) for i in range(2)]
        mask = cst[:, 16 + 128 * d:16 + 128 * d + 128]
        ident = cst[:, 272:400]
        for g in range(64):
            hb, gl = (g // 32) * 64, g % 32
            ps = self.nps()
            for ri in range(2):
                k.op("pe", lambda e: e.matmul(ps[:, 0:128], lhsT=LT[hb:hb + 64, ri, gl, :], rhs=CS[d][hb:hb + 64, ri, gl, :],
                                              start=(ri == 0), stop=(ri == 1)),
                     reads=[LT, CS[d]], writes=[ps], inc=(ri == 1))
            t = tf[g % 2]
            k.op("dve", lambda e: e.tensor_tensor(out=t[:, :], in0=ps[:, 0:128], in1=mask, op=ALU.mult), reads=[ps, cst], writes=[t])
            if d == 0:
                k.op("dve", lambda e: e.scalar_tensor_tensor(out=T[:, g, :], in0=ident, scalar=Drep[:, g:g + 1], in1=t[:, :],
                                                              op0=ALU.mult, op1=ALU.add), reads=[t, cst, Drep], writes=[T])
            else:
                k.op("pool", lambda e: e.tensor_tensor(out=T[:, g, :], in0=T[:, g, :], in1=t[:, :], op=ALU.add), reads=[t, T], writes=[T])
        k.barrier()
        st2.close()
    self._s5_run(l, BS, T, CS, MU, st)
    k.barrier()
    st.close()


def _s5_run(self, l, BS, T, CS, MU, st):
    k = self.k
    NG = 64
    Xall = k.sb([128, NG, NCH8], BF16, "Xall", st)
    st1 = contextlib.ExitStack()
    uT = k.sb([128, 8, NTOK], BF16, "uT", st1)
    Sel = k.sb([128, 8, 8, 128], BF16, "Sel", st1)
    for h in range(8):
        k.dma("sp", uT[:, h, :], self.u_d[h], reads=[k.tok(("u", l, h, ci)) for ci in range(5)], writes=[uT])
    k.dma("pool", Sel[:, :, :, :].rearrange("q a b m -> q (a b m)"), self.sel_d[:, :], writes=[Sel])
    for g in range(NG):
        ps = self.nps()
        for j in range(8):
            k.op("pe", lambda e: e.matmul(ps[:, 0:NCH8], lhsT=Sel[:, g % 8, j, :], rhs=uT[:, g // 8, j:NTOK:8], start=(j == 0), stop=(j == 7)),
                 reads=[Sel, uT], writes=[ps], inc=(j == 7))
        k.op("act" if g % 2 else "dve", (lambda e: e.activation(out=Xall[:, g, :], in_=ps[:, 0:NCH8], func=AF.Copy)) if g % 2 else
             (lambda e: e.tensor_copy(out=Xall[:, g, :], in_=ps[:, 0:NCH8])), reads=[ps], writes=[Xall])
    k.barrier()
    st1.close()
    S = [k.sb([128, 2, 32, NCH8], BF16, "S%d" % d, st) for d in range(2)]
    Srd = [Buf(S[d].ap, "Srd%d" % d) for d in range(2)]
    Swr = [Buf(S[d].ap, "Swr%d" % d) for d in range(2)]
    for d in range(2):
        for g in range(NG):
            hb, gl = (g // 32) * 64, g % 32
            ps = self.nps()
            ps2 = self.nps()
            for ri in range(2):
                k.op("pe", lambda e: e.matmul(ps[hb:hb + 64, ri * 256:(ri + 1) * 256], lhsT=BS[d][:, g, ri * 64:(ri + 1) * 64], rhs=Xall[:, g, 0:256],
                                              start=True, stop=True),
                     reads=[BS[d], Xall], writes=[ps], inc=(ri == 1))
            for ri in range(2):
                k.op("pe", lambda e: e.matmul(ps2[hb:hb + 64, ri * 32:(ri + 1) * 32], lhsT=BS[d][:, g, ri * 64:(ri + 1) * 64], rhs=Xall[:, g, 256:NCH8],
                                              start=True, stop=True),
                     reads=[BS[d], Xall], writes=[ps2], inc=(ri == 1))
            if g % 2:
                k.op("act", lambda e: e.activation(out=S[d][hb:hb + 64, :, gl, 0:256], in_=ps[hb:hb + 64, 0:512].rearrange("q (r n) -> q r n", r=2), func=AF.Copy),
                     reads=[ps], writes=[Srd[d]])
                k.op("act", lambda e: e.activation(out=S[d][hb:hb + 64, :, gl, 256:NCH8], in_=ps2[hb:hb + 64, 0:64].rearrange("q (r n) -> q r n", r=2), func=AF.Copy),
                     reads=[ps2], writes=[Srd[d]])
            else:
                k.op("dve", lambda e: e.tensor_copy(out=S[d][hb:hb + 64, :, gl, 0:256], in_=ps[hb:hb + 64, 0:512].rearrange("q (r n) -> q r n", r=2)),
                     reads=[ps], writes=[Srd[d]])
                k.op("dve", lambda e: e.tensor_copy(out=S[d][hb:hb + 64, :, gl, 256:NCH8], in_=ps2[hb:hb + 64, 0:64].rearrange("q (r n) -> q r n", r=2)),
                     reads=[ps2], writes=[Srd[d]])
    H = [k.sb([128, 2, 64], F32, "H%d" % i, st) for i in range(2)]
    t1 = k.sb([128, 2, 64], F32, "rt1", st)
    t2 = k.sb([128, 2, 64], F32, "rt2", st)
    k.op("pool", lambda e: e.memset(H[0][:, :, :], 0.0), writes=[H[0]])
    for i in range(NCH8):
        nf = i
        nb = (31 - i) if i < 32 else (319 - i)
        Ho, Hn = H[i % 2], H[(i + 1) % 2]
        k.op("dve", lambda e: e.tensor_tensor(out=t1[:, :, :], in0=MU[:, 0, :, :], in1=Ho[:, :, :], op=ALU.mult), reads=[MU, Ho], writes=[t1])
        k.op("dve", lambda e: e.tensor_tensor(out=t2[:, 0, :], in0=MU[:, 1, 0, :], in1=Ho[:, 1, :], op=ALU.mult), reads=[MU, Ho], writes=[t2])
        k.op("dve", lambda e: e.tensor_tensor(out=t2[:, 1, :], in0=MU[:, 1, 1, :], in1=Ho[:, 0, :], op=ALU.mult), reads=[MU, Ho], writes=[t2])
        k.op("dve", lambda e: e.tensor_tensor(out=t1[:, :, :], in0=t1[:, :, :], in1=t2[:, :, :], op=ALU.add), reads=[t1, t2], writes=[t1])
        stk = Buf(None, "stk")
        k.op("dve", lambda e: e.tensor_tensor(out=Hn[:, :, 0:32], in0=t1[:, :, 0:32], in1=S[0][:, :, :, nf], op=ALU.add), reads=[t1, Srd[0]], writes=[Hn, stk])
        k.op("dve", lambda e: e.tensor_tensor(out=Hn[:, :, 32:64], in0=t1[:, :, 32:64], in1=S[1][:, :, :, nb], op=ALU.add), reads=[t1, Srd[1]], writes=[Hn, stk])
        k.op("act", lambda e: e.activation(out=S[0][:, :, :, nf], in_=Ho[:, :, 0:32], func=AF.Copy), reads=[Ho, stk], writes=[Swr[0]])
        k.op("act", lambda e: e.activation(out=S[1][:, :, :, nb], in_=Ho[:, :, 32:64], func=AF.Copy), reads=[Ho, stk], writes=[Swr[1]])
    k.barrier()
    R = self._R
    SelT = Buf(R.ap[:, 0:8192].rearrange("q (a b m) -> q a b m", a=8, b=8), "SelT")
    k.dma("pool", R.ap[:, 0:8192], self.selT_d[:, :], writes=[SelT])
    Ag = [Buf(R.ap[:, 8192 + i * 2304:8192 + (i + 1) * 2304].rearrange("q (a n) -> q a n", a=8), "Ag%d" % i) for i in range(2)]
    ast = [Buf(R.ap[:, 12800 + i * 2304:12800 + (i + 1) * 2304], "ast%d" % i) for i in range(2)]
    ga = [k.sb([128, NCH8], F32, "ga%d" % i, st) for i in range(2)]
    gb = [k.sb([128, NCH8], F32, "gb%d" % i, st) for i in range(2)]
    for tl in range(8):
        A = Ag[tl % 2]
        for gi in range(8):
            g = tl * 8 + gi
            hb, gl = (g // 32) * 64, g % 32
            ps = self.nps()
            k.op("pe", lambda e: e.matmul(ps[:, 0:NCH8], lhsT=T[:, g, :], rhs=Xall[:, g, :], start=True, stop=False), reads=[T, Xall], writes=[ps], inc=False)
            for d in range(2):
                for ri in range(2):
                    last = (d == 1 and ri == 1)
                    k.op("pe", lambda e: e.matmul(ps[:, 0:NCH8], lhsT=CS[d][hb:hb + 64, ri, gl, :], rhs=S[d][hb:hb + 64, ri, gl, :], start=False, stop=last),
                         reads=[CS[d], Swr[d]], writes=[ps], inc=last)
            a_, b_ = ga[gi % 2], gb[gi % 2]
            k.op("act", lambda e: e.activation(out=a_[:, :], in_=ps[:, 0:NCH8], func=AF.Square), reads=[ps], writes=[a_])
            k.op("dve", lambda e: e.tensor_scalar(out=a_[:, :], in0=a_[:, :], scalar1=0.044715, scalar2=1.0, op0=ALU.mult, op1=ALU.add), reads=[a_], writes=[a_])
            k.op("dve", lambda e: e.tensor_tensor(out=b_[:, :], in0=ps[:, 0:NCH8], in1=a_[:, :], op=ALU.mult), reads=[ps, a_], writes=[b_])
            k.op("act", lambda e: e.activation(out=b_[:, :], in_=b_[:, :], func=AF.Sigmoid, scale=1.5957691216057308), reads=[b_], writes=[b_])
            k.op("dve", lambda e: e.tensor_tensor(out=A[:, gi, :], in0=ps[:, 0:NCH8], in1=b_[:, :], op=ALU.mult), reads=[ps, b_], writes=[A])
        o = ast[tl % 2]
        for j in range(8):
            ps = self.nps()
            for gi in range(8):
                k.op("pe", lambda e: e.matmul(ps[:, 0:NCH8], lhsT=SelT[:, gi, j, :], rhs=A[:, gi, :], start=(gi == 0), stop=(gi == 7)),
                     reads=[SelT, A], writes=[ps], inc=(gi == 7))
            k.op("act" if j % 2 else "dve", (lambda e: e.activation(out=o[:, j:NTOK:8], in_=ps[:, 0:NCH8], func=AF.Copy)) if j % 2 else
                 (lambda e: e.tensor_copy(out=o[:, j:NTOK:8], in_=ps[:, 0:NCH8])), reads=[ps], writes=[o])
        k.dma("sp", self.a_d[tl], o[:, :], reads=[o], writes=[k.tok(("a", l, tl))])


def _psl(self, ps, hb, ri):
    return ps[hb:hb + 64, ri * 256:(ri + 1) * 256]


Prog.phaseD = phaseD
Prog._s5_run = _s5_run
Prog._psl = _psl


_CACHE = {}


def kernel(**inputs):
    if "nc" not in _CACHE:
        _CACHE["nc"] = Prog().build()
    nc = _CACHE["nc"]
    sh = prep_shared(inputs)
    in_maps = []
    for core in range(8):
        m = dict(sh)
        m.update(prep_core(inputs, core % 4))
        in_maps.append(m)
    res = run_bass_kernel_spmd(nc, in_maps, core_ids=list(range(8)))
    out = np.empty((4, LAT, D), np.float32)
    for b in range(4):
        yT = np.asarray(res.results[b]["yT"]).reshape(D, LAT)
        out[b] = yT.T
    return out
```

```python
import contextlib
import math
import numpy as np
import concourse.bass as bass
import concourse.mybir as mybir
from concourse.bass_utils import run_bass_kernel_spmd

F32 = mybir.dt.float32
BF16 = mybir.dt.bfloat16
AF = mybir.ActivationFunctionType
ALU = mybir.AluOpType
AX = mybir.AxisListType

D = 2048
KT = 16
NTOK = 2304
CTX = 256
LAT = 2048
DFF = 8192
EPS = 1e-6
CHUNKS = [(0, 256), (256, 512), (768, 512), (1280, 512), (1792, 512)]
NEG = -30000.0
NCH8 = NTOK // 8
VEC_L = 168
C_OFF = 336
NVEC = 368


class Buf:
    __slots__ = ("ap", "w", "r", "name")

    def __init__(self, ap=None, name=""):
        self.ap = ap
        self.w = []
        self.r = []
        self.name = name

    def __getitem__(self, idx):
        return self.ap[idx]


class KB:
    NTICK = 40

    def __init__(self):
        self.nc = bass.Bass("TRN2", target_bir_lowering=False)
        nc = self.nc
        self.es = contextlib.ExitStack()
        self.eng = {}
        self.semid = {}
        for name, e in (("pe", nc.tensor), ("act", nc.scalar), ("dve", nc.vector),
                        ("pool", nc.gpsimd), ("sp", nc.sync)):
            sem = self.es.enter_context(nc.semaphore("s_" + name))
            self.eng[name] = [e, sem, 0]
        self.known = {n: {} for n in self.eng}
        self.ticks = []
        for i in range(self.NTICK):
            sem = self.es.enter_context(nc.semaphore("tk%d" % i))
            self.ticks.append([sem, 0, "tk%d" % i])
        self.tki = {"sp": 0, "pool": 0}
        self.tkr = {"sp": (0, 26), "pool": (26, 14)}
        self.out_evs = []
        self.phase_stack = None
        self.dbufs = {}
        self.uid = 0

    def sb(self, shape, dt, name=None, stack=None):
        self.uid += 1
        nm = (name or "t") + "_%d" % self.uid
        t = (stack or self.es).enter_context(self.nc.sbuf_tensor(nm, list(shape), dt))
        return Buf(t, nm)

    def pst(self, shape, dt, name=None, stack=None):
        self.uid += 1
        nm = (name or "p") + "_%d" % self.uid
        t = (stack or self.es).enter_context(self.nc.psum_tensor(nm, list(shape), dt))
        return Buf(t, nm)

    def dram(self, name, shape, dt, kind="Internal"):
        return self.nc.dram_tensor(name, list(shape), dt, kind=kind).ap()

    def tok(self, key):
        b = self.dbufs.get(key)
        if b is None:
            b = Buf(None, str(key))
            self.dbufs[key] = b
        return b

    def _wait(self, en, evs):
        e = self.eng[en][0]
        kn = self.known[en]
        best = {}
        for (key, sem, val) in evs:
            if val > kn.get(key, 0) and val > best.get(key, (None, 0))[1]:
                best[key] = (sem, val)
        for key, (sem, val) in best.items():
            e.wait_ge(sem, val)
            kn[key] = val

    def _deps(self, en, reads, writes):
        evs = []
        for b in reads:
            evs += b.w
        for b in writes:
            evs += b.w
            evs += b.r
        if en == "pe":
            evs = [x for x in evs if x[0] != "pe"]
        return evs

    def _record(self, ev, reads, writes):
        for b in reads:
            b.r = [x for x in b.r if x[0] != ev[0]]
            b.r.append(ev)
        for b in writes:
            b.w = [ev]
            b.r = []

    def op(self, en, fn, reads=(), writes=(), inc=True):
        self._wait(en, self._deps(en, reads, writes))
        ent = self.eng[en]
        ins = fn(ent[0])
        if inc:
            ent[2] += 1
            ins.then_inc(ent[1], 1)
            ev = (en, ent[1], ent[2])
            self._record(ev, reads, writes)
        return ins

    def dma(self, q, out_ap, in_ap, reads=(), writes=(), is_out=False):
        base, cnt = self.tkr[q]
        tk = self.ticks[base + self.tki[q]]
        self.tki[q] = (self.tki[q] + 1) % cnt
        evs = self._deps(q, reads, writes)
        if tk[1] > 0:
            evs.append((tk[2], tk[0], tk[1]))
        self._wait(q, evs)
        ins = self.eng[q][0].dma_start(out=out_ap, in_=in_ap)
        tk[1] += 16
        ins.then_inc(tk[0], 16)
        ev = (tk[2], tk[0], tk[1])
        self._record(ev, reads, writes)
        if is_out:
            self.out_evs.append(ev)
        return ev

    def pe_drain(self):
        ent = self.eng["pe"]
        if ent[2] > self.known["pe"].get("pe", 0):
            ent[0].wait_ge(ent[1], ent[2])
            self.known["pe"]["pe"] = ent[2]

    def barrier(self):
        evs = []
        for n, ent in self.eng.items():
            if ent[2] > 0:
                evs.append((n, ent[1], ent[2]))
        for tk in self.ticks:
            if tk[1] > 0:
                evs.append((tk[2], tk[0], tk[1]))
        for n in self.eng:
            self._wait(n, [x for x in evs if x[0] != n])

    def finish(self):
        self.barrier()
        self.es.close()


class Prog:
    def __init__(self, dbg=None, nlayers=2):
        self.k = KB()
        self.dbg = dbg or {}
        self.nlayers = nlayers
        k = self.k
        self.xT_d = k.dram("xT", [KT, 128, NTOK], F32, "ExternalInput")
        self.vecs_d = k.dram("vecs", [128, NVEC], F32, "ExternalInput")
        self.wmod_d = k.dram("wmod", [2, 96, 128, KT * 128], F32, "ExternalInput")
        self.win_d = k.dram("win", [2, 72, 128, KT * 128], F32, "ExternalInput")
        self.wv_d = k.dram("wv", [2, 2, 128, KT * 512], F32, "ExternalInput")
        self.rope_d = k.dram("rope", [2, 128, NTOK], F32, "ExternalInput")
        self.rpb_d = k.dram("rpb", [2, 16, 15, 31], F32, "ExternalInput")
        self.wval_d = k.dram("wval", [2, 16, 128, 8 * 128], F32, "ExternalInput")
        self.wglu_d = k.dram("wglu", [2, 16, 128, 8 * 128], F32, "ExternalInput")
        self.wna_d = k.dram("wna", [2, 16, 128, 8 * 128], F32, "ExternalInput")
        self.wout_d = k.dram("wout", [2, 16, 128, KT * 128], F32, "ExternalInput")
        self.wfc1_d = k.dram("wfc1", [2, 64, 128, KT * 128], F32, "ExternalInput")
        self.wfc2_d = k.dram("wfc2", [2, 16, 128, 64 * 128], F32, "ExternalInput")
        self.s5c_d = k.dram("s5c", [128, 400], F32, "ExternalInput")
        self.drep_d = k.dram("drep", [2, 128, 64], F32, "ExternalInput")
        self.sC_d = k.dram("sC", [2, 2, 7, 128, 512], F32, "ExternalInput")
        self.sel_d = k.dram("sel", [128, 8192], F32, "ExternalInput")
        self.selT_d = k.dram("selT", [128, 8192], F32, "ExternalInput")
        self.yT_d = k.dram("yT", [KT, 128, LAT], F32, "ExternalOutput")
        self.x1_d = k.dram("x1s", [KT, 128, NTOK], F32)
        self.u_d = k.dram("us", [8, 128, NTOK], BF16)
        self.q_d = k.dram("qs", [8, 128, NTOK], BF16)
        self.k_d = k.dram("ks", [8, 128, NTOK], BF16)
        self.v_d = k.dram("vs", [2, 18, 128, 1024], BF16)
        self.gs_d = k.dram("gss", [16, 128, NTOK], BF16)
        self.gn_d = k.dram("gns", [16, 128, NTOK], BF16)
        self.o_d = k.dram("os", [8, 128, NTOK], BF16)
        self.a_d = k.dram("as", [8, 128, NTOK], BF16)
        self.m_d = k.dram("ms", [16, 128, NTOK], BF16)
        self.wc_d = k.dram("wcache", [36, 128, 8192], BF16)
        self.vecs = k.sb([128, NVEC], F32, "vecs")
        self.sT = k.sb([128, KT, 2], F32, "sT")
        self.modT = k.sb([128, 2, 96, 2], F32, "modT")
        self.A = k.sb([128, 2, 2, KT, 2], F32, "Amod")
        self.G = k.sb([128, 2, 2, KT, 2], F32, "Gmod")
        self.ones = k.sb([128, 128], BF16, "ones")
        self.identb = k.sb([128, 128], BF16, "identb")
        self.ident_d = k.dram("ident", [128, 128], F32, "ExternalInput")
        self.perm_d = k.dram("perm", [128, 128], F32, "ExternalInput")
        self.permb = k.sb([128, 128], BF16, "permb")
        self.ps = [k.pst([128, 512], F32, "ps%d" % i) for i in range(6)]
        self.psb = [k.pst([128, 1024], BF16, "psb%d" % i) for i in range(2)]
        self.psi = 0
        self.wsl = None
        self.wsi = 0

    def nps(self):
        b = self.ps[self.psi % 6]
        self.psi += 1
        return b

    def alloc_ws(self, st):
        self.wsl = [self.k.sb([128, 8192], BF16, "wsl%d" % i, st) for i in range(3)]

    def nws(self):
        b = self.wsl[self.wsi % 3]
        self.wsi += 1
        return b

    def phase0(self):
        k = self.k
        k.dma("sp", self.vecs[:, :], self.vecs_d[:, :], writes=[self.vecs])
        st = contextlib.ExitStack()
        idf = k.sb([128, 128], F32, "idf", st)
        k.dma("sp", idf[:, :], self.ident_d[:, :], writes=[idf])
        k.op("dve", lambda e: e.tensor_copy(out=self.identb[:, :], in_=idf[:, :]), reads=[idf], writes=[self.identb])
        pmf = k.sb([128, 128], F32, "pmf", st)
        k.dma("sp", pmf[:, :], self.perm_d[:, :], writes=[pmf])
        k.op("dve", lambda e: e.tensor_copy(out=self.permb[:, :], in_=pmf[:, :]), reads=[pmf], writes=[self.permb])
        k.op("pool", lambda e: e.memset(self.ones[:, :], 1.0), writes=[self.ones])
        for col in range(2):
            k.op("act", lambda e: e.activation(out=self.sT[:, :, col], in_=self.vecs[:, C_OFF + 16 * col:C_OFF + 16 * col + 16],
                                               func=AF.Silu), reads=[self.vecs], writes=[self.sT])
        wm = [k.sb([128, KT * 128], F32, "wm%d" % i, st) for i in range(3)]
        for l in range(self.nlayers):
            for mt in range(96):
                slot = wm[mt % 3]
                k.dma("sp", slot[:, :], self.wmod_d[l, mt], writes=[slot])
                ps = self.nps()
                for kt in range(KT):
                    k.op("pe", lambda e: e.matmul(ps[:, 0:2], lhsT=slot[:, kt * 128:(kt + 1) * 128], rhs=self.sT[:, kt, :],
                                                  start=(kt == 0), stop=(kt == KT - 1)),
                         reads=[slot, self.sT], writes=[ps], inc=(kt == KT - 1))
                boff = l * VEC_L + 64 + mt
                k.op("dve", lambda e: e.tensor_scalar(out=self.modT[:, l, mt, :], in0=ps[:, 0:2],
                                                      scalar1=self.vecs[:, boff:boff + 1], scalar2=None, op0=ALU.add),
                     reads=[ps, self.vecs], writes=[self.modT])
            for s in range(2):
                sc0 = 16 + 48 * s
                gt0 = 32 + 48 * s
                gpre = l * VEC_L + (0 if s == 0 else 32)
                gpost = l * VEC_L + (16 if s == 0 else 48)
                for col in range(2):
                    k.op("dve", lambda e: e.scalar_tensor_tensor(out=self.A[:, l, s, :, col], in0=self.modT[:, l, sc0:sc0 + 16, col],
                                                                 scalar=1.0, in1=self.vecs[:, gpre:gpre + 16],
                                                                 op0=ALU.add, op1=ALU.mult),
                         reads=[self.modT, self.vecs], writes=[self.A])
                    k.op("dve", lambda e: e.tensor_tensor(out=self.G[:, l, s, :, col], in0=self.modT[:, l, gt0:gt0 + 16, col],
                                                          in1=self.vecs[:, gpost:gpost + 16], op=ALU.mult),
                         reads=[self.modT, self.vecs], writes=[self.G])
        k.barrier()
        st.close()

    def rstd_of(self, src, W, sq, rstd):
        k = self.k
        k.op("act", lambda e: e.activation(out=sq[:, :, :W], in_=src[:, :, :W], func=AF.Square), reads=[src], writes=[sq])
        ps = self.nps()
        for kt in range(KT):
            k.op("pe", lambda e: e.matmul(ps[:, :W], lhsT=self.ones[:, :], rhs=sq[:, kt, :W], start=(kt == 0), stop=(kt == KT - 1)),
                 reads=[self.ones, sq], writes=[ps], inc=(kt == KT - 1))
        k.op("act", lambda e: e.activation(out=rstd[:, :W], in_=ps[:, :W], func=AF.Sqrt, bias=self.epsb[:, 0:1], scale=1.0 / D),
             reads=[ps, self.epsb], writes=[rstd])
        k.op("dve", lambda e: e.reciprocal(out=rstd[:, :W], in_=rstd[:, :W]), reads=[rstd], writes=[rstd])

    def modulate(self, src, W, rstd, l, s, col, dst, dst_t0, tmps):
        k = self.k
        sh0 = 0 if s == 0 else 48
        for kt in range(KT):
            tmp = tmps[kt % 2]
            k.op("dve", lambda e: e.scalar_tensor_tensor(out=tmp[:, :W], in0=src[:, kt, :W], scalar=self.A[:, l, s, kt, col:col + 1],
                                                         in1=rstd[:, :W], op0=ALU.mult, op1=ALU.mult),
                 reads=[src, rstd, self.A], writes=[tmp])
            k.op("act", lambda e: e.activation(out=dst[:, kt, dst_t0:dst_t0 + W], in_=tmp[:, :W], func=AF.Identity,
                                               bias=self.modT[:, l, sh0 + kt, col:col + 1], scale=1.0),
                 reads=[tmp, self.modT], writes=[dst])

    def xsrc(self, l):
        return self.xT_d if l == 0 else self.x1_d

    def phaseAB(self, l):
        k = self.k
        st = contextlib.ExitStack()
        self.alloc_ws(st)
        hT = k.sb([128, KT, NTOK], BF16, "hT", st)
        xs = [k.sb([128, KT, 512], F32, "xs%d" % i, st) for i in range(1)]
        sq = k.sb([128, KT, 512], BF16, "sq", st)
        rstd = k.sb([128, 512], F32, "rstd", st)
        tmps = [k.sb([128, 512], F32, "tmp%d" % i, st) for i in range(4)]
        stg = [k.sb([128, 512], BF16, "stg%d" % i, st) for i in range(3)]
        rope = k.sb([128, 2, NTOK], F32, "rope", st)
        k.dma("sp", rope[:, 0, :], self.rope_d[0], writes=[rope])
        k.dma("sp", rope[:, 1, :], self.rope_d[1], writes=[rope])
        xd = self.xsrc(l)
        for ci, (t0, W) in enumerate(CHUNKS):
            col = 1 if ci == 0 else 0
            x = xs[0]
            k.dma("sp", x[:, :, :W], xd[:, :, t0:t0 + W].rearrange("k p t -> p k t"), reads=[k.tok(("x", l, ci))], writes=[x])
            self.rstd_of(x, W, sq, rstd)
            self.modulate(x, W, rstd, l, 0, col, hT, t0, tmps)
        if "hT" in self.dbg:
            k.dma("sp", self.dbg["hT"].rearrange("k p t -> p k t"), hT[:, :, :], reads=[hT])
        si = [0]

        def store(dst_ap, src_fn, reads_ps, tokkey, eng="act"):
            s = stg[si[0] % 3]
            si[0] += 1
            src_fn(s)
            k.dma("sp", dst_ap, s[:, :dst_ap.shape[-1]], reads=[s], writes=[k.tok(tokkey)])

        def mm_group(ps, wslot, woff, t0, W):
            for kt in range(KT):
                k.op("pe", lambda e: e.matmul(ps[:, :W], lhsT=wslot[:, woff + kt * 128:woff + (kt + 1) * 128], rhs=hT[:, kt, t0:t0 + W],
                                              start=(kt == 0), stop=(kt == KT - 1)),
                     reads=[wslot, hT], writes=[ps], inc=(kt == KT - 1))

        def simple(pbase, n, dst_d, key, func, skip_ctx=False):
            for m in range(0, n, 4):
                ws = self.nws()
                k.dma("pool", ws[:, :4 * 2048].rearrange("p (m x) -> p m x", m=4),
                      self.win_d[l, pbase + m:pbase + m + 4].rearrange("m p x -> p m x"), writes=[ws])
                for mm in range(4):
                    for ci, (t0, W) in enumerate(CHUNKS):
                        if skip_ctx and ci == 0:
                            continue
                        ps = self.nps()
                        mm_group(ps, ws, mm * 2048, t0, W)
                        store(dst_d[m + mm, :, t0:t0 + W],
                              lambda s: k.op("act", lambda e: e.activation(out=s[:, :W], in_=ps[:, :W], func=func), reads=[ps], writes=[s]),
                              None, (key, l, m + mm, ci))

        simple(0, 8, self.u_d, "u", AF.Copy)
        simple(40, 16, self.gs_d, "gs", AF.Sigmoid, skip_ctx=(l == 1))
        simple(56, 16, self.gn_d, "gn", AF.Sigmoid, skip_ctx=(l == 1))

        qbs = [k.sb([128, 512], BF16, "qb%d" % i, st) for i in range(2)]
        qi = [0]

        def roped(pbase, dst_d, key, skip_ctx=False):
            for m in range(0, 8, 4):
                ws = self.nws()
                k.dma("pool", ws[:, :].rearrange("p (m x) -> p m x", m=4),
                      self.win_d[l, pbase + m:pbase + m + 4].rearrange("m p x -> p m x"), writes=[ws])
                for mm in range(4):
                    for ci, (t0, W) in enumerate(CHUNKS):
                        if skip_ctx and ci == 0:
                            continue
                        ps1 = self.nps()
                        mm_group(ps1, ws, mm * 2048, t0, W)
                        qb = qbs[qi[0] % 2]
                        qi[0] += 1
                        k.op("act", lambda e: e.activation(out=qb[:, :W], in_=ps1[:, :W], func=AF.Copy), reads=[ps1], writes=[qb])
                        ps2 = self.nps()
                        k.op("pe", lambda e: e.matmul(ps2[:, :W], lhsT=self.permb[:, :], rhs=qb[:, :W], start=True, stop=True),
                             reads=[self.permb, qb], writes=[ps2])
                        ta, tb = tmps[2], tmps[3]
                        k.op("dve", lambda e: e.tensor_tensor(out=ta[:, :W], in0=ps1[:, :W], in1=rope[:, 0, t0:t0 + W], op=ALU.mult),
                             reads=[ps1, rope, qb], writes=[ta])
                        k.op("dve", lambda e: e.tensor_tensor(out=tb[:, :W], in0=ps2[:, :W], in1=rope[:, 1, t0:t0 + W], op=ALU.mult),
                             reads=[ps2, rope], writes=[tb])
                        store(dst_d[m + mm, :, t0:t0 + W],
                              lambda s: k.op("pool", lambda e: e.tensor_tensor(out=s[:, :W], in0=ta[:, :W], in1=tb[:, :W], op=ALU.add),
                                             reads=[ta, tb], writes=[s]),
                              None, (key, l, m + mm, ci))

        roped(8, self.q_d, "q", skip_ctx=(l == 1))
        roped(24, self.k_d, "k")
        for nt in range(2):
            ws = self.nws()
            k.dma("pool", ws[:, :], self.wv_d[l, nt], writes=[ws])
            for sh in range(2):
                for tt in range(18):
                    tk0 = tt * 128 + 64 * sh
                    if tk0 + 128 > NTOK:
                        continue
                    ps = self.nps()
                    for kt in range(KT):
                        k.op("pe", lambda e: e.matmul(ps[:, :], lhsT=hT[:, kt, tk0:tk0 + 128], rhs=ws[:, kt * 512:(kt + 1) * 512],
                                                      start=(kt == 0), stop=(kt == KT - 1)),
                             reads=[ws, hT], writes=[ps], inc=(kt == KT - 1))
                    store(self.v_d[sh, tt, :, nt * 512:(nt + 1) * 512],
                          lambda s: k.op("act", lambda e: e.activation(out=s[:, :], in_=ps[:, :], func=AF.Copy), reads=[ps], writes=[s]),
                          None, ("v", l, sh, tt, nt))
        k.barrier()
        st.close()

    def phaseC(self, l):
        k = self.k
        st = contextlib.ExitStack()
        kT = k.sb([128, 8, NTOK], BF16, "kT", st)
        vA = k.sb([128, 2, 18, 1024], BF16, "vA", st)
        tab = k.sb([128, 8, 15, 64], F32, "tab", st)
        qT = [k.sb([128, NTOK], BF16, "qT%d" % i, st) for i in range(2)]
        ost = [k.sb([128, NTOK], BF16, "ost%d" % i, st) for i in range(2)]
        S = [k.sb([128, 768], F32, "S%d" % i, st) for i in range(2)]
        P = [k.sb([128, 768], F32, "P%d" % i, st) for i in range(2)]
        Pb = [k.sb([128, 768], BF16, "Pb%d" % i, st) for i in range(2)]
        PT = [k.sb([128, 768], BF16, "PT%d" % i, st) for i in range(2)]
        sm = [k.sb([128, 4], F32, "sm%d" % i, st) for i in range(2)]
        for h in range(8):
            k.dma("sp", kT[:, h, :], self.k_d[h], reads=[k.tok(("k", l, h, ci)) for ci in range(5)], writes=[kT])
        for sh in range(2):
            for tt in range(18):
                if tt * 128 + 64 * sh + 128 > NTOK:
                    continue
                k.dma("sp", vA[:, sh, tt, :], self.v_d[sh, tt], reads=[k.tok(("v", l, sh, tt, nt)) for nt in range(2)], writes=[vA])
        k.op("pool", lambda e: e.memset(tab[:, :, :, :], NEG), writes=[tab])
        rp = self.rpb_d[l].rearrange("(hp hh) r c -> hh hp r c", hh=2)
        with self.k.nc.allow_non_contiguous_dma(reason="rpb window gather (tiny)"):
            for hh in range(2):
                for q in range(64):
                    c0 = min(max(q - 8, 0), 48)
                    off = c0 - q + 15
                    k.dma("sp", tab[hh * 64 + q:hh * 64 + q + 1, :, :, c0:c0 + 16], rp[hh:hh + 1, :, :, off:off + 16], writes=[tab])
        psS = [self.ps[0], self.ps[1]]
        psC = [self.ps[2], self.ps[3]]
        psO = [self.ps[4], self.ps[5]]
        items = []
        for hp in range(8):
            rows = list(range(32)) + ([-1, -2, -3, -4] if l < self.nlayers_total - 1 else [])
            for r in rows:
                items.append((hp, r))
        NI = len(items)

        def geo(idx):
            hp, r = items[idx]
            i2 = idx % 2
            d = dict(hp=hp, r=r, i2=i2, q=qT[hp % 2], o=ost[hp % 2], s=S[i2], p=P[i2], pb=Pb[i2], pt=PT[i2], sm=sm[i2],
                     pS=psS[i2], pC=psC[i2], pO=psO[i2], pT=self.psb[i2])
            if r >= 0:
                r0 = min(max(r - 4, 0), 24)
                d.update(qc=CTX + r * 64, r0=r0, ro0=r0 - r + 7, k0=CTX + r0 * 64, NK=768)
            else:
                d.update(qc=(-r - 1) * 64, NK=256)
            return d

        def stA(idx):
            g = geo(idx)
            hp, r, q = g["hp"], g["r"], g["q"]
            k.pe_drain()
            if idx == 0 or items[idx - 1][0] != hp:
                k.dma("sp", q[:, :], self.q_d[hp], reads=[k.tok(("q", l, hp, ci)) for ci in range(5)], writes=[q])
            for hh in range(2):
                pr = slice(hh * 64, hh * 64 + 64)
                if r >= 0:
                    k.op("pe", lambda e: e.matmul(g["pS"][pr, :], lhsT=q[pr, g["qc"]:g["qc"] + 64], rhs=kT[pr, hp, g["k0"]:g["k0"] + 512], start=True, stop=True),
                         reads=[q, kT], writes=[g["pS"]])
                k.op("pe", lambda e: e.matmul(g["pC"][pr, 0:256], lhsT=q[pr, g["qc"]:g["qc"] + 64], rhs=kT[pr, hp, 0:256], start=True, stop=True),
                     reads=[q, kT], writes=[g["pC"]])

        def stB(idx):
            g = geo(idx)
            hp, r, s_, p_, pb_, sm_, NK = g["hp"], g["r"], g["s"], g["p"], g["pb"], g["sm"], g["NK"]
            if r >= 0:
                ro0 = g["ro0"]
                k.op("dve", lambda e: e.scalar_tensor_tensor(out=s_[:, 0:512], in0=g["pS"][:, :], scalar=0.125,
                                                             in1=tab[:, hp, ro0:ro0 + 8, :].rearrange("p a b -> p (a b)"),
                                                             op0=ALU.mult, op1=ALU.add),
                     reads=[g["pS"], tab], writes=[s_])
                k.op("act", lambda e: e.activation(out=s_[:, 512:768], in_=g["pC"][:, 0:256], func=AF.Copy, scale=0.125),
                     reads=[g["pC"]], writes=[s_])
            else:
                k.op("act", lambda e: e.activation(out=s_[:, 0:256], in_=g["pC"][:, 0:256], func=AF.Copy, scale=0.125),
                     reads=[g["pC"]], writes=[s_])
            k.op("dve", lambda e: e.reduce_max(out=sm_[:, 0:1], in_=s_[:, :NK], axis=AX.X), reads=[s_], writes=[sm_])
            k.op("dve", lambda e: e.tensor_scalar(out=sm_[:, 1:2], in0=sm_[:, 0:1], scalar1=-1.0, scalar2=None, op0=ALU.mult),
                 reads=[sm_], writes=[sm_])
            k.op("act", lambda e: e.activation(out=p_[:, :NK], in_=s_[:, :NK], func=AF.Exp, bias=sm_[:, 1:2], scale=1.0,
                                               accum_out=sm_[:, 2:3]),
                 reads=[s_, sm_], writes=[p_, sm_])
            k.op("dve", lambda e: e.reciprocal(out=sm_[:, 3:4], in_=sm_[:, 2:3]), reads=[sm_], writes=[sm_])
            k.op("dve", lambda e: e.tensor_scalar(out=pb_[:, :NK], in0=p_[:, :NK], scalar1=sm_[:, 3:4], scalar2=None, op0=ALU.mult),
                 reads=[p_, sm_], writes=[pb_])

        def stC(idx):
            g = geo(idx)
            nj = g["NK"] // 128
            k.pe_drain()
            for j in range(nj):
                k.op("pe", lambda e: e.transpose(out=g["pT"][:, j * 128:(j + 1) * 128], in_=g["pb"][:, j * 128:(j + 1) * 128], identity=self.identb[:, :]),
                     reads=[g["pb"], self.identb], writes=[g["pT"]], inc=(j == nj - 1))

        def stD(idx):
            g = geo(idx)
            NK = g["NK"]
            k.op("act", lambda e: e.activation(out=g["pt"][:, :NK], in_=g["pT"][:, :NK], func=AF.Copy), reads=[g["pT"]], writes=[g["pt"]])

        def stE(idx):
            g = geo(idx)
            hp, r = g["hp"], g["r"]
            nj = g["NK"] // 128
            k.pe_drain()
            for hh in range(2):
                pr = slice(hh * 64, hh * 64 + 64)
                hc = (2 * hp + hh) * 64
                for j in range(nj):
                    if r >= 0 and j < 4:
                        sh = (g["r0"] % 2)
                        tt = (g["k0"] - 64 * sh) // 128 + j
                        vv = vA[:, sh, tt, hc:hc + 64]
                    else:
                        jj = j - 4 if r >= 0 else j
                        vv = vA[:, 0, jj, hc:hc + 64]
                    k.op("pe", lambda e: e.matmul(g["pO"][pr, 0:64], lhsT=vv, rhs=g["pt"][:, j * 128 + hh * 64:j * 128 + hh * 64 + 64],
                                                  start=(j == 0), stop=(j == nj - 1)),
                         reads=[vA, g["pt"]], writes=[g["pO"]], inc=(j == nj - 1))

        def stF(idx):
            g = geo(idx)
            hp, o = g["hp"], g["o"]
            k.op("dve", lambda e: e.tensor_copy(out=o[:, g["qc"]:g["qc"] + 64], in_=g["pO"][:, 0:64]), reads=[g["pO"]], writes=[o])
            if idx == NI - 1 or items[idx + 1][0] != hp:
                k.dma("sp", self.o_d[hp], o[:, :], reads=[o], writes=[k.tok(("o", l, hp))])

        for t in range(NI + 2):
            if t < NI:
                stA(t)
                stB(t)
            if 0 <= t - 1 < NI:
                stC(t - 1)
                stD(t - 1)
            if 0 <= t - 2 < NI:
                stE(t - 2)
                stF(t - 2)
        k.barrier()
        st.close()

    def phaseE(self, l, t_lo):
        k = self.k
        st = contextlib.ExitStack()
        self.alloc_ws(st)
        aT = k.sb([128, 8, NTOK], BF16, "aT", st)
        oT = k.sb([128, 8, NTOK], BF16, "oT", st)
        gsb = [k.sb([128, 512], BF16, "gsb%d" % i, st) for i in range(2)]
        gnb = [k.sb([128, 512], BF16, "gnb%d" % i, st) for i in range(2)]
        t1 = [k.sb([128, 512], F32, "t1_%d" % i, st) for i in range(2)]
        t2 = [k.sb([128, 512], F32, "t2_%d" % i, st) for i in range(2)]
        t3 = [k.sb([128, 512], F32, "t3_%d" % i, st) for i in range(2)]
        stg = [k.sb([128, 512], BF16, "stgE%d" % i, st) for i in range(3)]
        for h in range(8):
            k.dma("sp", aT[:, h, :], self.a_d[h], reads=[k.tok(("a", l, h))], writes=[aT])
            k.dma("sp", oT[:, h, :], self.o_d[h], reads=[k.tok(("o", l, h))], writes=[oT])
        chunks = [c for c in CHUNKS if c[0] >= t_lo]
        items = [(mt, t0, W) for mt in range(16) for (t0, W) in chunks]

        def gate_loads(idx):
            mt, t0, W = items[idx]
            ci = [c[0] for c in CHUNKS].index(t0)
            g1, g2 = gsb[idx % 2], gnb[idx % 2]
            k.dma("sp", g1[:, :W], self.gs_d[mt, :, t0:t0 + W], reads=[k.tok(("gs", l, mt, ci))], writes=[g1])
            k.dma("sp", g2[:, :W], self.gn_d[mt, :, t0:t0 + W], reads=[k.tok(("gn", l, mt, ci))], writes=[g2])

        gate_loads(0)
        ws = None
        for idx, (mt, t0, W) in enumerate(items):
            ci = [c[0] for c in CHUNKS].index(t0)
            if idx == 0 or items[idx - 1][0] != mt:
                ws = self.nws()
                k.dma("pool", ws[:, 0:1024], self.wval_d[l, mt], writes=[ws])
                k.dma("pool", ws[:, 1024:2048], self.wglu_d[l, mt], writes=[ws])
                k.dma("pool", ws[:, 2048:3072], self.wna_d[l, mt], writes=[ws])
            i2 = idx % 2
            pss = []
            for wi, src_ in ((0, aT), (1, aT), (2, oT)):
                ps = self.nps()
                for kt in range(8):
                    k.op("pe", lambda e: e.matmul(ps[:, :W], lhsT=ws[:, wi * 1024 + kt * 128:wi * 1024 + (kt + 1) * 128], rhs=src_[:, kt, t0:t0 + W],
                                                  start=(kt == 0), stop=(kt == 7)),
                         reads=[ws, src_], writes=[ps], inc=(kt == 7))
                pss.append(ps)
            g1, g2 = gsb[i2], gnb[i2]
            a1, a2, a3 = t1[i2], t2[i2], t3[i2]
            k.op("act", lambda e: e.activation(out=a1[:, :W], in_=pss[1][:, :W], func=AF.Sigmoid), reads=[pss[1]], writes=[a1])
            k.op("dve", lambda e: e.tensor_tensor(out=a2[:, :W], in0=pss[0][:, :W], in1=a1[:, :W], op=ALU.mult), reads=[pss[0], a1], writes=[a2])
            k.op("dve", lambda e: e.tensor_tensor(out=a2[:, :W], in0=a2[:, :W], in1=g1[:, :W], op=ALU.mult), reads=[a2, g1], writes=[a2])
            k.op("dve", lambda e: e.tensor_tensor(out=a3[:, :W], in0=pss[2][:, :W], in1=g2[:, :W], op=ALU.mult), reads=[pss[2], g2], writes=[a3])
            if idx + 1 < len(items):
                gate_loads(idx + 1)
            s = stg[idx % 3]
            k.op("dve", lambda e: e.tensor_tensor(out=s[:, :W], in0=a2[:, :W], in1=a3[:, :W], op=ALU.add), reads=[a2, a3], writes=[s])
            k.dma("sp", self.m_d[mt, :, t0:t0 + W], s[:, :W], reads=[s], writes=[k.tok(("m", l, mt, ci))])
        k.barrier()
        st.close()

    def phaseF(self, l, t_lo, last):
        k = self.k
        st = contextlib.ExitStack()
        self.alloc_ws(st)
        x = k.sb([128, KT, 512], F32, "xF", st)
        ob = k.sb([128, KT, 512], F32, "oF", st)
        mh = k.sb([128, KT, 512], BF16, "mh", st)
        ab = k.sb([128, 64, 512], BF16, "ab", st)
        rstd = k.sb([128, 512], F32, "rstdF", st)
        tmps = [k.sb([128, 512], F32, "tmpF%d" % i, st) for i in range(2)]
        sq = Buf(ab.ap, "sqalias")
        xd = self.xsrc(l)
        chunks = [c for c in CHUNKS if c[0] >= t_lo]

        def wload(first, fill, src_ap, rr=None):
            ws = self.nws()
            wtok = k.tok(("wc", fill))
            if first:
                k.dma("pool", ws[:, :].rearrange("p (m x) -> p m x", m=rr) if rr else ws[:, :], src_ap, writes=[ws])
                k.dma("sp", self.wc_d[fill], ws[:, :], reads=[ws], writes=[wtok])
            else:
                k.dma("sp", ws[:, :], self.wc_d[fill], reads=[wtok], writes=[ws])
            return ws

        for cidx, (t0, W) in enumerate(chunks):
            first = (cidx == 0)
            ci = [c[0] for c in CHUNKS].index(t0)
            col = 1 if ci == 0 else 0
            k.dma("sp", x[:, :, :W], xd[:, :, t0:t0 + W].rearrange("k p t -> p k t"), reads=[k.tok(("x", l, ci))], writes=[x])
            k.dma("sp", mh[:, :, :W], self.m_d[:, :, t0:t0 + W].rearrange("k p t -> p k t"),
                  reads=[k.tok(("m", l, mt, ci)) for mt in range(16)], writes=[mh])
            for m in range(0, 16, 4):
                ws = wload(first, m // 4, self.wout_d[l, m:m + 4].rearrange("m p x -> p m x"), 4)
                for mm in range(4):
                    ps = self.nps()
                    for kt in range(KT):
                        k.op("pe", lambda e: e.matmul(ps[:, :W], lhsT=ws[:, mm * 2048 + kt * 128:mm * 2048 + (kt + 1) * 128], rhs=mh[:, kt, :W],
                                                      start=(kt == 0), stop=(kt == KT - 1)),
                             reads=[ws, mh], writes=[ps], inc=(kt == KT - 1))
                    k.op("act", lambda e: e.activation(out=ob[:, m + mm, :W], in_=ps[:, :W], func=AF.Copy), reads=[ps], writes=[ob])
            self.residual(x, ob, W, sq, ab, rstd, l, 0, col, tmps)
            self.rstd_of(x, W, Buf(ab.ap, "sq2"), rstd) if False else None
            self._rstd_alias(x, W, ab, rstd)
            self.modulate(x, W, rstd, l, 1, col, mh, 0, tmps)
            for m in range(0, 64, 4):
                ws = wload(first, 4 + m // 4, self.wfc1_d[l, m:m + 4].rearrange("m p x -> p m x"), 4)
                for mm in range(4):
                    ps = self.nps()
                    for kt in range(KT):
                        k.op("pe", lambda e: e.matmul(ps[:, :W], lhsT=ws[:, mm * 2048 + kt * 128:mm * 2048 + (kt + 1) * 128], rhs=mh[:, kt, :W],
                                                      start=(kt == 0), stop=(kt == KT - 1)),
                             reads=[ws, mh], writes=[ps], inc=(kt == KT - 1))
                    tm = tmps[mm % 2]
                    k.op("act", lambda e: e.activation(out=tm[:, :W], in_=ps[:, :W], func=AF.Relu), reads=[ps], writes=[tm])
                    k.op("pool", lambda e: e.tensor_tensor(out=ab[:, m + mm, :W], in0=tm[:, :W], in1=tm[:, :W], op=ALU.mult), reads=[tm], writes=[ab])
            for m in range(16):
                ws = wload(first, 20 + m, self.wfc2_d[l, m])
                ps = self.nps()
                for kt in range(64):
                    k.op("pe", lambda e: e.matmul(ps[:, :W], lhsT=ws[:, kt * 128:(kt + 1) * 128], rhs=ab[:, kt, :W], start=(kt == 0), stop=(kt == 63)),
                         reads=[ws, ab], writes=[ps], inc=(kt == 63))
                k.op("act", lambda e: e.activation(out=ob[:, m, :W], in_=ps[:, :W], func=AF.Copy), reads=[ps], writes=[ob])
            self.residual(x, ob, W, sq, ab, rstd, l, 1, col, tmps)
            if last:
                k.dma("sp", self.yT_d[:, :, t0 - CTX:t0 - CTX + W].rearrange("k p t -> p k t"), x[:, :, :W], reads=[x],
                      writes=[k.tok(("y", ci))], is_out=True)
            else:
                k.dma("sp", self.x1_d[:, :, t0:t0 + W].rearrange("k p t -> p k t"), x[:, :, :W], reads=[x], writes=[k.tok(("x", l + 1, ci))])
        k.barrier()
        st.close()

    def _rstd_alias(self, src, W, ab, rstd):
        k = self.k
        sqv = ab.ap[:, 0:KT, :]
        k.op("act", lambda e: e.activation(out=sqv[:, :, :W], in_=src[:, :, :W], func=AF.Square), reads=[src], writes=[ab])
        ps = self.nps()
        for kt in range(KT):
            k.op("pe", lambda e: e.matmul(ps[:, :W], lhsT=self.ones[:, :], rhs=sqv[:, kt, :W], start=(kt == 0), stop=(kt == KT - 1)),
                 reads=[self.ones, ab], writes=[ps], inc=(kt == KT - 1))
        k.op("act", lambda e: e.activation(out=rstd[:, :W], in_=ps[:, :W], func=AF.Sqrt, bias=self.epsb[:, 0:1], scale=1.0 / D),
             reads=[ps, self.epsb], writes=[rstd])
        k.op("dve", lambda e: e.reciprocal(out=rstd[:, :W], in_=rstd[:, :W]), reads=[rstd], writes=[rstd])

    def residual(self, x, ob, W, sq, ab, rstd, l, s, col, tmps):
        k = self.k
        self._rstd_alias(ob, W, ab, rstd)
        for kt in range(KT):
            tmp = tmps[kt % 2]
            k.op("dve", lambda e: e.scalar_tensor_tensor(out=tmp[:, :W], in0=ob[:, kt, :W], scalar=self.G[:, l, s, kt, col:col + 1],
                                                         in1=rstd[:, :W], op0=ALU.mult, op1=ALU.mult),
                 reads=[ob, rstd, self.G], writes=[tmp])
            k.op("pool", lambda e: e.tensor_tensor(out=x[:, kt, :W], in0=x[:, kt, :W], in1=tmp[:, :W], op=ALU.add), reads=[x, tmp], writes=[x])

    def build(self, upto=None):
        k = self.k
        self.nlayers_total = 2
        self.epsb = k.sb([128, 1], F32, "epsb")
        k.op("pool", lambda e: e.memset(self.epsb[:, :], EPS), writes=[self.epsb])
        self.hpib = k.sb([128, 1], F32, "hpib")
        k.op("pool", lambda e: e.memset(self.hpib[:, :], float(np.pi / 2)), writes=[self.hpib])
        self.phase0()
        for l in range(self.nlayers):
            self.phaseAB(l)
            if upto == "AB":
                break
            self.phaseC(l)
            if upto == "C":
                break
            self.phaseD(l)
            last = (l == 1)
            t_lo = CTX if last else 0
            self.phaseE(l, t_lo)
            self.phaseF(l, t_lo, last)
        k.finish()
        return k.nc


def _fm(v):
    return np.ascontiguousarray(np.asarray(v, np.float32).reshape(-1, 128).T)


def _panels(W):
    K, N = W.shape
    kt, mt = K // 128, N // 128
    return np.ascontiguousarray(W.reshape(kt, 128, mt, 128).transpose(2, 1, 0, 3).reshape(mt, 128, kt * 128))


def _rope_tables():
    nf = 16
    inv = (10000.0 ** (-np.arange(nf, dtype=np.float32) / nf)).astype(np.float32)
    pos = np.arange(LAT)
    rows = (pos // 64).astype(np.float32)
    cols = (pos % 64).astype(np.float32)
    cos = np.ones((64, NTOK), np.float32)
    sin = np.zeros((64, NTOK), np.float32)
    for d in range(64):
        blk = d // 16
        p = rows if blk < 2 else cols
        ang = (p * inv[d % 16]).astype(np.float32)
        cos[d, CTX:] = np.cos(ang)
        sg = -1.0 if blk % 2 == 0 else 1.0
        sin[d, CTX:] = sg * np.sin(ang)
    return np.ascontiguousarray(np.stack([np.concatenate([cos, cos], 0), np.concatenate([sin, sin], 0)], 0))


def prep_shared(inp):
    f = lambda a: np.asarray(a, np.float32)
    w_in = f(inp["w_in"])
    partner = np.array([d + 16 if (d % 32) < 16 else d - 16 for d in range(64)])
    perm = (np.arange(16)[:, None] * 64 + partner[None, :]).reshape(-1)
    wins, wvs = [], []
    for l in range(2):
        W = w_in[l]
        u, q, kk, v, gs, gn = W[:, :1024], W[:, 1024:2048], W[:, 2048:3072], W[:, 3072:4096], W[:, 4096:6144], W[:, 6144:]
        comb = np.concatenate([u, q, q[:, perm], kk, kk[:, perm], gs, gn], axis=1)
        wins.append(_panels(comb))
        wvs.append(np.ascontiguousarray(v.reshape(KT, 128, 2, 512).transpose(2, 1, 0, 3).reshape(2, 128, KT * 512)))
    sh = {
        "wmod": np.stack([_panels(f(inp["w_mod"])[l]) for l in range(2)]),
        "win": np.stack(wins),
        "wv": np.stack(wvs),
        "rope": _rope_tables(),
        "rpb": np.ascontiguousarray(f(inp["na_rpb"])),
        "wval": np.stack([_panels(f(inp["w_ssm_val"])[l]) for l in range(2)]),
        "wglu": np.stack([_panels(f(inp["w_ssm_glu"])[l]) for l in range(2)]),
        "wna": np.stack([_panels(f(inp["w_na_proj"])[l]) for l in range(2)]),
        "wout": np.stack([_panels(f(inp["w_out"])[l]) for l in range(2)]),
        "wfc1": np.stack([_panels(f(inp["w_fc1"])[l]) for l in range(2)]),
        "wfc2": np.stack([_panels(f(inp["w_fc2"])[l]) for l in range(2)]),
        "ident": np.eye(128, dtype=np.float32),
    }
    pm = np.zeros((128, 128), np.float32)
    for mcol in range(128):
        pm[(mcol // 64) * 64 + partner[mcol % 64], mcol] = 1.0
    sh["perm"] = pm
    jj = np.arange(128) // 16
    cc = np.arange(128) % 16
    s5c = np.zeros((128, 400), np.float32)
    for kk in range(8):
        s5c[:, kk] = (jj == kk)
        s5c[:, 8 + kk] = (7 - jj == kk)
    s5c[:, 16:144] = (jj[:, None] <= jj[None, :])
    s5c[:, 144:272] = (jj[:, None] >= jj[None, :])
    s5c[:, 272:400] = np.eye(128)
    sel = np.zeros((128, 8, 8, 128), np.float32)
    selT = np.zeros((128, 8, 8, 128), np.float32)
    for gi in range(8):
        for j in range(8):
            for c in range(16):
                sel[gi * 16 + c, gi, j, j * 16 + c] = 1.0
                selT[j * 16 + c, gi, j, gi * 16 + c] = 1.0
    sh["s5c"] = s5c
    sh["sel"] = sel.reshape(128, 8192)
    sh["selT"] = selT.reshape(128, 8192)
    lre, lim, ldt = f(inp["ssm_lam_re"]), f(inp["ssm_lam_im"]), f(inp["ssm_log_dt"])
    bre, bim, cre, cim = f(inp["ssm_b_re"]), f(inp["ssm_b_im"]), f(inp["ssm_c_re"]), f(inp["ssm_c_im"])
    sC = np.empty((2, 2, 7, 128, 512), np.float32)
    for l in range(2):
        for d in range(2):
            def cl(A):
                t = A.reshape(2, 32, 64).transpose(0, 2, 1)
                return np.repeat(t.reshape(128, 32), 16, axis=1)
            sC[l, d, 0] = cl(lre[l, d])
            sC[l, d, 1] = cl(lim[l, d])
            sC[l, d, 2] = cl(np.repeat(ldt[l, d][:, None], 64, axis=1))
            for a, Cc in ((3, cre), (4, cim)):
                sC[l, d, a] = Cc[l, d].reshape(2, 32, 16, 64).transpose(0, 3, 1, 2).reshape(128, 512)
            for a, B in ((5, bre), (6, bim)):
                sC[l, d, a] = B[l, d].reshape(2, 32, 64, 16).transpose(0, 2, 1, 3).reshape(128, 512)
    sh["sC"] = sC
    dd = f(inp["ssm_d"])
    sh["drep"] = np.stack([np.tile(dd[l].reshape(64, 16).T, (8, 1)) for l in range(2)]).astype(np.float32)
    return sh


def prep_core(inp, b):
    f = lambda a: np.asarray(a, np.float32)
    X = np.concatenate([f(inp["ctx"])[b], f(inp["x"])[b]], axis=0)
    xT = np.ascontiguousarray(X.T.reshape(KT, 128, NTOK))
    cols = []
    for l in range(2):
        cols += [_fm(inp["g_pre_mix"][l]), _fm(inp["g_post_mix"][l]), _fm(inp["g_pre_mlp"][l]), _fm(inp["g_post_mlp"][l]),
                 _fm(inp["b_mod"][l]), _fm(inp["ssm_d"][l])]
    cols += [_fm(inp["c"][b]), _fm(inp["c_ctx"])]
    vecs = np.ascontiguousarray(np.concatenate(cols, axis=1))
    assert vecs.shape == (128, NVEC)
    return {"xT": xT, "vecs": vecs}


def _cmul(k, eng, o_re, o_im, a_re, a_im, b_re, b_im, t1, t2, rd=(), wr=()):
    E = k.op
    E(eng, lambda e: e.tensor_tensor(out=t1, in0=a_re, in1=b_re, op=ALU.mult), reads=rd, writes=wr)
    E(eng, lambda e: e.tensor_tensor(out=t2, in0=a_im, in1=b_im, op=ALU.mult), reads=rd, writes=wr)
    E(eng, lambda e: e.tensor_tensor(out=t2, in0=t1, in1=t2, op=ALU.subtract), reads=rd, writes=wr)
    E(eng, lambda e: e.tensor_tensor(out=t1, in0=a_re, in1=b_im, op=ALU.mult), reads=rd, writes=wr)
    E(eng, lambda e: e.tensor_tensor(out=o_im, in0=a_im, in1=b_re, op=ALU.mult), reads=rd, writes=wr)
    E(eng, lambda e: e.tensor_tensor(out=o_im, in0=o_im, in1=t1, op=ALU.add), reads=rd, writes=wr)
    E(eng, lambda e: e.tensor_copy(out=o_re, in_=t2), reads=rd, writes=wr)


def _lam_base(P, st, src, F, eng):
    k = P.k
    nb = lambda nm: k.sb([128, F], F32, nm, st)
    W = Buf(None, "Wtok")
    tok = [W, src]
    dt, ar, ai, c, s, m, t1, t2 = [nb(n) for n in ("dt", "ar", "ai", "c", "s", "m", "t1", "t2")]
    o = {n: nb(n) for n in ("lbr", "lbi", "lir", "lii", "gr", "gi")}
    E = lambda en, fn: k.op(en, fn, reads=tok + [P.hpib], writes=[W])
    E("act", lambda e: e.activation(out=dt[:, :], in_=src[:, 2, :], func=AF.Exp))
    E(eng, lambda e: e.tensor_tensor(out=ar[:, :], in0=src[:, 0, :], in1=dt[:, :], op=ALU.mult))
    E(eng, lambda e: e.tensor_tensor(out=ai[:, :], in0=src[:, 1, :], in1=dt[:, :], op=ALU.mult))
    E("act", lambda e: e.activation(out=s[:, :], in_=ai[:, :], func=AF.Sin, scale=1.0 / 16))
    E("act", lambda e: e.activation(out=c[:, :], in_=ai[:, :], func=AF.Sin, scale=1.0 / 16, bias=P.hpib[:, 0:1]))
    for _ in range(4):
        E(eng, lambda e: e.tensor_tensor(out=t1[:, :], in0=c[:, :], in1=s[:, :], op=ALU.mult))
        E(eng, lambda e: e.tensor_tensor(out=c[:, :], in0=c[:, :], in1=c[:, :], op=ALU.mult))
        E(eng, lambda e: e.tensor_tensor(out=s[:, :], in0=s[:, :], in1=s[:, :], op=ALU.mult))
        E(eng, lambda e: e.tensor_tensor(out=c[:, :], in0=c[:, :], in1=s[:, :], op=ALU.subtract))
        E(eng, lambda e: e.tensor_scalar(out=s[:, :], in0=t1[:, :], scalar1=2.0, scalar2=None, op0=ALU.mult))
    E("act", lambda e: e.activation(out=m[:, :], in_=ar[:, :], func=AF.Exp))
    E(eng, lambda e: e.tensor_tensor(out=o["lbr"][:, :], in0=m[:, :], in1=c[:, :], op=ALU.mult))
    E(eng, lambda e: e.tensor_tensor(out=o["lbi"][:, :], in0=m[:, :], in1=s[:, :], op=ALU.mult))
    E("act", lambda e: e.activation(out=m[:, :], in_=ar[:, :], func=AF.Exp, scale=-1.0))
    E(eng, lambda e: e.tensor_tensor(out=o["lir"][:, :], in0=m[:, :], in1=c[:, :], op=ALU.mult))
    E("dve", lambda e: e.scalar_tensor_tensor(out=o["lii"][:, :], in0=m[:, :], scalar=-1.0, in1=s[:, :], op0=ALU.mult, op1=ALU.mult))
    lr, li = src[:, 0, :], src[:, 1, :]
    E(eng, lambda e: e.tensor_tensor(out=t1[:, :], in0=lr, in1=lr, op=ALU.mult))
    E(eng, lambda e: e.tensor_tensor(out=t2[:, :], in0=li, in1=li, op=ALU.mult))
    E(eng, lambda e: e.tensor_tensor(out=t1[:, :], in0=t1[:, :], in1=t2[:, :], op=ALU.add))
    E("dve", lambda e: e.reciprocal(out=t1[:, :], in_=t1[:, :]))
    E(eng, lambda e: e.tensor_scalar(out=c[:, :], in0=o["lbr"][:, :], scalar1=-1.0, scalar2=None, op0=ALU.add))
    E(eng, lambda e: e.tensor_tensor(out=t2[:, :], in0=c[:, :], in1=lr, op=ALU.mult))
    E(eng, lambda e: e.tensor_tensor(out=s[:, :], in0=o["lbi"][:, :], in1=li, op=ALU.mult))
    E(eng, lambda e: e.tensor_tensor(out=t2[:, :], in0=t2[:, :], in1=s[:, :], op=ALU.add))
    E(eng, lambda e: e.tensor_tensor(out=o["gr"][:, :], in0=t2[:, :], in1=t1[:, :], op=ALU.mult))
    E(eng, lambda e: e.tensor_tensor(out=t2[:, :], in0=o["lbi"][:, :], in1=lr, op=ALU.mult))
    E(eng, lambda e: e.tensor_tensor(out=s[:, :], in0=c[:, :], in1=li, op=ALU.mult))
    E(eng, lambda e: e.tensor_tensor(out=t2[:, :], in0=t2[:, :], in1=s[:, :], op=ALU.subtract))
    E(eng, lambda e: e.tensor_tensor(out=o["gi"][:, :], in0=t2[:, :], in1=t1[:, :], op=ALU.mult))
    return o, W, (t1, t2, c, s, m, dt)


def phaseD(self, l):
    k = self.k
    st = contextlib.ExitStack()
    NG = 64
    R = k.sb([128, 17408], BF16, "Rraw", st)
    self._R = R
    BS = [Buf(R.ap[:, d * 8192:(d + 1) * 8192].rearrange("q (g m) -> q g m", m=128), "BS%d" % d) for d in range(2)]
    T = k.sb([128, NG, 128], BF16, "Tg", st)
    CS = [k.sb([128, 2, 32, 128], BF16, "CS%d" % d, st) for d in range(2)]
    MU = k.sb([128, 2, 2, 64], F32, "MU", st)
    cst = k.sb([128, 400], F32, "s5c", st)
    k.dma("sp", cst[:, :], self.s5c_d[:, :], writes=[cst])
    Drep = k.sb([128, NG], F32, "Drep", st)
    k.dma("sp", Drep[:, :], self.drep_d[l], writes=[Drep])
    F = NG * 64
    for d in range(2):
        st2 = contextlib.ExitStack()
        Fc = 512
        src = k.sb([128, 7, Fc], F32, "srcC", st2)
        for a in range(7):
            k.dma("sp", src[:, a, :], self.sC_d[l, d, a], writes=[src])
        LT = k.sb([128, 2, 32, 128], BF16, "LT", st2)
        LB = k.sb([128, 2, 32, 128], BF16, "LB", st2)
        o, W, (t1, t2, c, s, m, dt) = _lam_base(self, st2, src, Fc, "dve")
        tok = [W, src]
        E = lambda en, fn, wr=(): k.op(en, fn, reads=tok, writes=[W] + list(wr))
        bbr, bbi = k.sb([128, Fc], F32, "bbr", st2), k.sb([128, Fc], F32, "bbi", st2)
        _cmul(k, "dve", bbr[:, :], bbi[:, :], o["gr"][:, :], o["gi"][:, :], src[:, 5, :], src[:, 6, :], t1[:, :], t2[:, :], rd=tok, wr=[W])
        pr, pi = o["gr"], o["gi"]
        qr, qi = k.sb([128, Fc], F32, "qr", st2), k.sb([128, Fc], F32, "qi", st2)
        E("dve", lambda e: e.tensor_copy(out=pr[:, :], in_=o["lbr"][:, :]))
        E("dve", lambda e: e.tensor_copy(out=pi[:, :], in_=o["lbi"][:, :]))
        E("dve", lambda e: e.tensor_copy(out=qr[:, :], in_=o["lir"][:, :]))
        E("dve", lambda e: e.tensor_copy(out=qi[:, :], in_=o["lii"][:, :]))
        cs5 = CS[d].ap.rearrange("q r g (j c) -> q r g j c", c=16)
        lt5 = LT.ap.rearrange("q r g (j c) -> q r g j c", c=16)
        lb5 = LB.ap.rearrange("q r g (j c) -> q r g j c", c=16)
        g3 = lambda ap: ap.rearrange("q (g c) -> q g c", c=16)
        jb0 = 7 if d == 0 else 0
        E("act", lambda e: e.activation(out=lb5[:, 0, :, jb0, :], in_=g3(bbr[:, :]), func=AF.Copy), wr=[LB])
        E("act", lambda e: e.activation(out=lb5[:, 1, :, jb0, :], in_=g3(bbi[:, :]), func=AF.Copy), wr=[LB])
        for kk in range(1, 9):
            if kk > 1:
                _cmul(k, "dve", pr[:, :], pi[:, :], pr[:, :], pi[:, :], o["lbr"][:, :], o["lbi"][:, :], t1[:, :], t2[:, :], rd=tok, wr=[W])
                _cmul(k, "dve", qr[:, :], qi[:, :], qr[:, :], qi[:, :], o["lir"][:, :], o["lii"][:, :], t1[:, :], t2[:, :], rd=tok, wr=[W])
            jj = kk - 1 if d == 0 else 8 - kk
            _cmul(k, "dve", c[:, :], s[:, :], src[:, 3, :], src[:, 4, :], pr[:, :], pi[:, :], t1[:, :], t2[:, :], rd=tok, wr=[W])
            E("act", lambda e: e.activation(out=cs5[:, 0, :, jj, :], in_=g3(c[:, :]), func=AF.Copy), wr=[CS[d]])
            E("act", lambda e: e.activation(out=cs5[:, 1, :, jj, :], in_=g3(s[:, :]), func=AF.Copy, scale=-1.0), wr=[CS[d]])
            _cmul(k, "dve", c[:, :], s[:, :], bbr[:, :], bbi[:, :], qr[:, :], qi[:, :], t1[:, :], t2[:, :], rd=tok, wr=[W])
            E("act", lambda e: e.activation(out=lt5[:, 0, :, jj, :], in_=g3(c[:, :]), func=AF.Copy), wr=[LT])
            E("act", lambda e: e.activation(out=lt5[:, 1, :, jj, :], in_=g3(s[:, :]), func=AF.Copy), wr=[LT])
            if kk <= 7:
                jb = 7 - kk if d == 0 else kk
                _cmul(k, "dve", c[:, :], s[:, :], bbr[:, :], bbi[:, :], pr[:, :], pi[:, :], t1[:, :], t2[:, :], rd=tok, wr=[W])
                E("act", lambda e: e.activation(out=lb5[:, 0, :, jb, :], in_=g3(c[:, :]), func=AF.Copy), wr=[LB])
                E("act", lambda e: e.activation(out=lb5[:, 1, :, jb, :], in_=g3(s[:, :]), func=AF.Copy), wr=[LB])
        prg = g3(pr[:, :])[:, :, 0]
        pig = g3(pi[:, :])[:, :, 0]
        gsl = slice(d * 32, d * 32 + 32)
        E("dve", lambda e: e.tensor_copy(out=MU[:, 0, 0, gsl], in_=prg), wr=[MU])
        E("dve", lambda e: e.tensor_copy(out=MU[:, 0, 1, gsl], in_=prg), wr=[MU])
        E("dve", lambda e: e.tensor_scalar(out=MU[:, 1, 0, gsl], in0=pig, scalar1=-1.0, scalar2=None, op0=ALU.mult), wr=[MU])
        E("dve", lambda e: e.tensor_copy(out=MU[:, 1, 1, gsl], in_=pig), wr=[MU])
        for g in range(64):
            hb, gl = (g // 32) * 64, g % 32
            pT = self.psb[g % 2]
            for ri in range(2):
                k.op("pe", lambda e: e.transpose(out=pT[:, ri * 64:(ri + 1) * 64], in_=LB[hb:hb + 64, ri, gl, :], identity=self.identb[hb:hb + 64, hb:hb + 64]),
                     reads=[LB, self.identb], writes=[pT], inc=(ri == 1))
            if g % 2:
                k.op("act", lambda e: e.activation(out=BS[d][:, g, :], in_=pT[:, 0:128], func=AF.Copy), reads=[pT], writes=[BS[d]])
            else:
                k.op("dve", lambda e: e.tensor_copy(out=BS[d][:, g, :], in_=pT[:, 0:128]), reads=[pT], writes=[BS[d]])
        tf = [k.sb([128, 128], F32, "tfT%d" % i, st2) for i in range(2)]
        mask = cst[:, 16 + 128 * d:16 + 128 * d + 128]
        ident = cst[:, 272:400]
        for g in range(64):
            hb, gl = (g // 32) * 64, g % 32
            ps = self.nps()
            for ri in range(2):
                k.op("pe", lambda e: e.matmul(ps[:, 0:128], lhsT=LT[hb:hb + 64, ri, gl, :], rhs=CS[d][hb:hb + 64, ri, gl, :],
                                              start=(ri == 0), stop=(ri == 1)),
                     reads=[LT, CS[d]], writes=[ps], inc=(ri == 1))
            t = tf[g % 2]
            k.op("dve", lambda e: e.tensor_tensor(out=t[:, :], in0=ps[:, 0:128], in1=mask, op=ALU.mult), reads=[ps, cst], writes=[t])
            if d == 0:
                k.op("dve", lambda e: e.scalar_tensor_tensor(out=T[:, g, :], in0=ident, scalar=Drep[:, g:g + 1], in1=t[:, :],
                                                              op0=ALU.mult, op1=ALU.add), reads=[t, cst, Drep], writes=[T])
            else:
                k.op("pool", lambda e: e.tensor_tensor(out=T[:, g, :], in0=T[:, g, :], in1=t[:, :], op=ALU.add), reads=[t, T], writes=[T])
        k.barrier()
        st2.close()
    self._s5_run(l, BS, T, CS, MU, st)
    k.barrier()
    st.close()


def _s5_run(self, l, BS, T, CS, MU, st):
    k = self.k
    NG = 64
    Xall = k.sb([128, NG, NCH8], BF16, "Xall", st)
    st1 = contextlib.ExitStack()
    uT = k.sb([128, 8, NTOK], BF16, "uT", st1)
    Sel = k.sb([128, 8, 8, 128], BF16, "Sel", st1)
    for h in range(8):
        k.dma("sp", uT[:, h, :], self.u_d[h], reads=[k.tok(("u", l, h, ci)) for ci in range(5)], writes=[uT])
    k.dma("pool", Sel[:, :, :, :].rearrange("q a b m -> q (a b m)"), self.sel_d[:, :], writes=[Sel])
    for g in range(NG):
        ps = self.nps()
        for j in range(8):
            k.op("pe", lambda e: e.matmul(ps[:, 0:NCH8], lhsT=Sel[:, g % 8, j, :], rhs=uT[:, g // 8, j:NTOK:8], start=(j == 0), stop=(j == 7)),
                 reads=[Sel, uT], writes=[ps], inc=(j == 7))
        k.op("act" if g % 2 else "dve", (lambda e: e.activation(out=Xall[:, g, :], in_=ps[:, 0:NCH8], func=AF.Copy)) if g % 2 else
             (lambda e: e.tensor_copy(out=Xall[:, g, :], in_=ps[:, 0:NCH8])), reads=[ps], writes=[Xall])
    k.barrier()
    st1.close()
    S = [k.sb([128, 2, 32, NCH8], BF16, "S%d" % d, st) for d in range(2)]
    Srd = [Buf(S[d].ap, "Srd%d" % d) for d in range(2)]
    Swr = [Buf(S[d].ap, "Swr%d" % d) for d in range(2)]
    for d in range(2):
        for g in range(NG):
            hb, gl = (g // 32) * 64, g % 32
            ps = self.nps()
            ps2 = self.nps()
            for ri in range(2):
                k.op("pe", lambda e: e.matmul(ps[hb:hb + 64, ri * 256:(ri + 1) * 256], lhsT=BS[d][:, g, ri * 64:(ri + 1) * 64], rhs=Xall[:, g, 0:256],
                                              start=True, stop=True),
                     reads=[BS[d], Xall], writes=[ps], inc=(ri == 1))
            for ri in range(2):
                k.op("pe", lambda e: e.matmul(ps2[hb:hb + 64, ri * 32:(ri + 1) * 32], lhsT=BS[d][:, g, ri * 64:(ri + 1) * 64], rhs=Xall[:, g, 256:NCH8],
                                              start=True, stop=True),
                     reads=[BS[d], Xall], writes=[ps2], inc=(ri == 1))
            if g % 2:
                k.op("act", lambda e: e.activation(out=S[d][hb:hb + 64, :, gl, 0:256], in_=ps[hb:hb + 64, 0:512].rearrange("q (r n) -> q r n", r=2), func=AF.Copy),
                     reads=[ps], writes=[Srd[d]])
                k.op("act", lambda e: e.activation(out=S[d][hb:hb + 64, :, gl, 256:NCH8], in_=ps2[hb:hb + 64, 0:64].rearrange("q (r n) -> q r n", r=2), func=AF.Copy),
                     reads=[ps2], writes=[Srd[d]])
            else:
                k.op("dve", lambda e: e.tensor_copy(out=S[d][hb:hb + 64, :, gl, 0:256], in_=ps[hb:hb + 64, 0:512].rearrange("q (r n) -> q r n", r=2)),
                     reads=[ps], writes=[Srd[d]])
                k.op("dve", lambda e: e.tensor_copy(out=S[d][hb:hb + 64, :, gl, 256:NCH8], in_=ps2[hb:hb + 64, 0:64].rearrange("q (r n) -> q r n", r=2)),
                     reads=[ps2], writes=[Srd[d]])
    H = [k.sb([128, 2, 64], F32, "H%d" % i, st) for i in range(2)]
    t1 = k.sb([128, 2, 64], F32, "rt1", st)
    t2 = k.sb([128, 2, 64], F32, "rt2", st)
    k.op("pool", lambda e: e.memset(H[0][:, :, :], 0.0), writes=[H[0]])
    for i in range(NCH8):
        nf = i
        nb = (31 - i) if i < 32 else (319 - i)
        Ho, Hn = H[i % 2], H[(i + 1) % 2]
        k.op("dve", lambda e: e.tensor_tensor(out=t1[:, :, :], in0=MU[:, 0, :, :], in1=Ho[:, :, :], op=ALU.mult), reads=[MU, Ho], writes=[t1])
        k.op("dve", lambda e: e.tensor_tensor(out=t2[:, 0, :], in0=MU[:, 1, 0, :], in1=Ho[:, 1, :], op=ALU.mult), reads=[MU, Ho], writes=[t2])
        k.op("dve", lambda e: e.tensor_tensor(out=t2[:, 1, :], in0=MU[:, 1, 1, :], in1=Ho[:, 0, :], op=ALU.mult), reads=[MU, Ho], writes=[t2])
        k.op("dve", lambda e: e.tensor_tensor(out=t1[:, :, :], in0=t1[:, :, :], in1=t2[:, :, :], op=ALU.add), reads=[t1, t2], writes=[t1])
        stk = Buf(None, "stk")
        k.op("dve", lambda e: e.tensor_tensor(out=Hn[:, :, 0:32], in0=t1[:, :, 0:32], in1=S[0][:, :, :, nf], op=ALU.add), reads=[t1, Srd[0]], writes=[Hn, stk])
        k.op("dve", lambda e: e.tensor_tensor(out=Hn[:, :, 32:64], in0=t1[:, :, 32:64], in1=S[1][:, :, :, nb], op=ALU.add), reads=[t1, Srd[1]], writes=[Hn, stk])
        k.op("act", lambda e: e.activation(out=S[0][:, :, :, nf], in_=Ho[:, :, 0:32], func=AF.Copy), reads=[Ho, stk], writes=[Swr[0]])
        k.op("act", lambda e: e.activation(out=S[1][:, :, :, nb], in_=Ho[:, :, 32:64], func=AF.Copy), reads=[Ho, stk], writes=[Swr[1]])
    k.barrier()
    R = self._R
    SelT = Buf(R.ap[:, 0:8192].rearrange("q (a b m) -> q a b m", a=8, b=8), "SelT")
    k.dma("pool", R.ap[:, 0:8192], self.selT_d[:, :], writes=[SelT])
    Ag = [Buf(R.ap[:, 8192 + i * 2304:8192 + (i + 1) * 2304].rearrange("q (a n) -> q a n", a=8), "Ag%d" % i) for i in range(2)]
    ast = [Buf(R.ap[:, 12800 + i * 2304:12800 + (i + 1) * 2304], "ast%d" % i) for i in range(2)]
    ga = [k.sb([128, NCH8], F32, "ga%d" % i, st) for i in range(2)]
    gb = [k.sb([128, NCH8], F32, "gb%d" % i, st) for i in range(2)]
    for tl in range(8):
        A = Ag[tl % 2]
        for gi in range(8):
            g = tl * 8 + gi
            hb, gl = (g // 32) * 64, g % 32
            ps = self.nps()
            k.op("pe", lambda e: e.matmul(ps[:, 0:NCH8], lhsT=T[:, g, :], rhs=Xall[:, g, :], start=True, stop=False), reads=[T, Xall], writes=[ps], inc=False)
            for d in range(2):
                for ri in range(2):
                    last = (d == 1 and ri == 1)
                    k.op("pe", lambda e: e.matmul(ps[:, 0:NCH8], lhsT=CS[d][hb:hb + 64, ri, gl, :], rhs=S[d][hb:hb + 64, ri, gl, :], start=False, stop=last),
                         reads=[CS[d], Swr[d]], writes=[ps], inc=last)
            a_, b_ = ga[gi % 2], gb[gi % 2]
            k.op("act", lambda e: e.activation(out=a_[:, :], in_=ps[:, 0:NCH8], func=AF.Square), reads=[ps], writes=[a_])
            k.op("dve", lambda e: e.tensor_scalar(out=a_[:, :], in0=a_[:, :], scalar1=0.044715, scalar2=1.0, op0=ALU.mult, op1=ALU.add), reads=[a_], writes=[a_])
            k.op("dve", lambda e: e.tensor_tensor(out=b_[:, :], in0=ps[:, 0:NCH8], in1=a_[:, :], op=ALU.mult), reads=[ps, a_], writes=[b_])
            k.op("act", lambda e: e.activation(out=b_[:, :], in_=b_[:, :], func=AF.Sigmoid, scale=1.5957691216057308), reads=[b_], writes=[b_])
            k.op("dve", lambda e: e.tensor_tensor(out=A[:, gi, :], in0=ps[:, 0:NCH8], in1=b_[:, :], op=ALU.mult), reads=[ps, b_], writes=[A])
        o = ast[tl % 2]
        for j in range(8):
            ps = self.nps()
            for gi in range(8):
                k.op("pe", lambda e: e.matmul(ps[:, 0:NCH8], lhsT=SelT[:, gi, j, :], rhs=A[:, gi, :], start=(gi == 0), stop=(gi == 7)),
                     reads=[SelT, A], writes=[ps], inc=(gi == 7))
            k.op("act" if j % 2 else "dve", (lambda e: e.activation(out=o[:, j:NTOK:8], in_=ps[:, 0:NCH8], func=AF.Copy)) if j % 2 else
                 (lambda e: e.tensor_copy(out=o[:, j:NTOK:8], in_=ps[:, 0:NCH8])), reads=[ps], writes=[o])
        k.dma("sp", self.a_d[tl], o[:, :], reads=[o], writes=[k.tok(("a", l, tl))])


def _psl(self, ps, hb, ri):
    return ps[hb:hb + 64, ri * 256:(ri + 1) * 256]


Prog.phaseD = phaseD
Prog._s5_run = _s5_run
Prog._psl = _psl


_CACHE = {}


def kernel(**inputs):
    if "nc" not in _CACHE:
        _CACHE["nc"] = Prog().build()
    nc = _CACHE["nc"]
    sh = prep_shared(inputs)
    in_maps = []
    for core in range(8):
        m = dict(sh)
        m.update(prep_core(inputs, core % 4))
        in_maps.append(m)
    res = run_bass_kernel_spmd(nc, in_maps, core_ids=list(range(8)))
    out = np.empty((4, LAT, D), np.float32)
    for b in range(4):
        yT = np.asarray(res.results[b]["yT"]).reshape(D, LAT)
        out[b] = yT.T
    return out
```
